# Optimizing a Trainium2 kernel written in Bass

```python
import math
import jax, jax.numpy as jnp
from jax import lax
import numpy as np

D_MODEL = 1024
BATCH = 2
SEQ = 8192
DEPTH = 2

D_FF = 2816
NORM_EPS = 1e-6
NSA_HEADS = 8
NSA_KV_HEADS = 2
NSA_GROUP = NSA_HEADS // NSA_KV_HEADS
NSA_HEAD_DIM = 64
CMP_BLOCK = 32
CMP_STRIDE = 16
SEL_BLOCK = 64
SEL_TOPK = 16
WINDOW = 512
Q_BLOCK = 128
ROPE_THETA = 10000.0
FORCE_SCORE = 1e4
SSD_HEADS = 16
SSD_HEAD_DIM = 64
SSD_D_INNER = SSD_HEADS * SSD_HEAD_DIM
SSD_GROUPS = 2
SSD_STATE = 128
SSD_CONV = 4
SSD_CHUNK = 128
SSD_NORM_EPS = 1e-5
RWKV_HEAD_DIM = 64
RWKV_HEADS = D_MODEL // RWKV_HEAD_DIM
DECAY_LORA = 64
AAA_LORA = 64
GATE_LORA = 160
RWKV_GN_EPS = 64e-5

N_EVEN = (DEPTH + 1) // 2
N_ODD = DEPTH // 2

NSA_Q_W = NSA_HEADS * NSA_HEAD_DIM
NSA_KV_W = NSA_KV_HEADS * NSA_HEAD_DIM
SSD_XBC = SSD_D_INNER + 2 * SSD_GROUPS * SSD_STATE
IN_SPLITS = (NSA_Q_W, NSA_KV_W, NSA_KV_W, NSA_KV_W, NSA_KV_W, NSA_KV_W, NSA_KV_W,
             NSA_HEADS * 3, SSD_D_INNER, SSD_XBC, SSD_HEADS)
IN_WIDTH = sum(IN_SPLITS)
MIX_OUT_WIDTH = NSA_Q_W + SSD_D_INNER

kernel_name = 'hybrid_nsa_ssd_rwkv7_macaron'


def rmsnorm(x, g, eps=NORM_EPS):
    xf = x.astype(jnp.float32)
    y = xf * lax.rsqrt(jnp.mean(xf * xf, -1, keepdims=True) + eps)
    return (y * g.astype(jnp.float32)).astype(x.dtype)


def swiglu(x, w_gate, w_up, w_down):
    return (jax.nn.silu(x @ w_gate) * (x @ w_up)) @ w_down


def rope_tables(seq, dim):
    inv = ROPE_THETA ** (-jnp.arange(0, dim, 2, dtype=jnp.float32) / dim)
    ang = jnp.arange(seq, dtype=jnp.float32)[:, None] * inv[None, :]
    return jnp.cos(ang), jnp.sin(ang)


def apply_rope(x, cos, sin):
    shape = (1, x.shape[1]) + (1,) * (x.ndim - 3) + (cos.shape[-1],)
    c = cos.reshape(shape).astype(x.dtype)
    s = sin.reshape(shape).astype(x.dtype)
    x1, x2 = jnp.split(x, 2, axis=-1)
    return jnp.concatenate([x1 * c - x2 * s, x2 * c + x1 * s], -1)


def masked_softmax(s, mask):
    s = jnp.where(mask, s.astype(jnp.float32), -1e30)
    p = jnp.where(mask, jnp.exp(s - jnp.max(s, -1, keepdims=True)), 0.0)
    return p / jnp.maximum(jnp.sum(p, -1, keepdims=True), 1e-30)


def compress_blocks(k, pe, w1, w2):
    b, s, h, d = k.shape
    n_cmp = (s - CMP_BLOCK) // CMP_STRIDE + 1
    idx = jnp.arange(n_cmp)[:, None] * CMP_STRIDE + jnp.arange(CMP_BLOCK)[None, :]
    blk = k[:, idx] + pe[None, None, :, None, :]
    blk = jnp.moveaxis(blk, 3, 2).reshape(b, n_cmp, h, CMP_BLOCK * d)
    return jax.nn.silu(blk @ w1) @ w2


def nsa_attention(q, k_cmp, v_cmp, k_sel, v_sel, k_win, v_win, gates,
                  pe_k, w1_k, w2_k, pe_v, w1_v, w2_v):
    b, s, hkv, g, d = q.shape
    kc = compress_blocks(k_cmp, pe_k, w1_k, w2_k)
    vc = compress_blocks(v_cmp, pe_v, w1_v, w2_v)
    n_cmp = kc.shape[1]
    n_sel = s // SEL_BLOCK
    topk = min(SEL_TOPK, n_sel)
    cmp_start = jnp.arange(n_cmp) * CMP_STRIDE
    cmp_end = cmp_start + CMP_BLOCK - 1
    sel_start = jnp.arange(n_sel) * SEL_BLOCK
    overlap = ((cmp_start[:, None] < sel_start[None, :] + SEL_BLOCK)
               & (cmp_end[:, None] >= sel_start[None, :])).astype(jnp.float32)
    ks_blk = jnp.moveaxis(k_sel, 2, 1).reshape(b, hkv, n_sel, SEL_BLOCK, d)
    vs_blk = jnp.moveaxis(v_sel, 2, 1).reshape(b, hkv, n_sel, SEL_BLOCK, d)
    pad = ((0, 0), (WINDOW, 0), (0, 0), (0, 0))
    kw_pad = jnp.pad(k_win, pad)
    vw_pad = jnp.pad(v_win, pad)
    bi = jnp.arange(b)[:, None, None, None]
    hi = jnp.arange(hkv)[None, :, None, None]
    blk_ids = jnp.arange(n_sel)

    def block(qi):
        s0 = qi * Q_BLOCK
        qb = lax.dynamic_slice_in_dim(q, s0, Q_BLOCK, 1)
        gb = lax.dynamic_slice_in_dim(gates, s0, Q_BLOCK, 1)
        t = s0 + jnp.arange(Q_BLOCK)
        p_c = masked_softmax(jnp.einsum('bqhgd,bchd->bhgqc', qb, kc),
                             cmp_end[None, :] <= t[:, None])
        o_c = jnp.einsum('bhgqc,bchd->bqhgd', p_c.astype(vc.dtype), vc)
        imp = jnp.einsum('bhgqc,cj->bhqj', p_c, overlap)
        cur = t // SEL_BLOCK
        forced = ((blk_ids[None, :] == 0) | (blk_ids[None, :] == cur[:, None])
                  | (blk_ids[None, :] == cur[:, None] - 1))
        valid = sel_start[None, :] <= t[:, None]
        imp = jnp.where(valid, jnp.where(forced, FORCE_SCORE, imp), -jnp.inf)
        _, sel = lax.top_k(imp, topk)
        kg = ks_blk[bi, hi, sel].reshape(b, hkv, Q_BLOCK, topk * SEL_BLOCK, d)
        vg = vs_blk[bi, hi, sel].reshape(b, hkv, Q_BLOCK, topk * SEL_BLOCK, d)
        tok = (sel[..., None] * SEL_BLOCK + jnp.arange(SEL_BLOCK)).reshape(
            b, hkv, Q_BLOCK, topk * SEL_BLOCK)
        p_s = masked_softmax(jnp.einsum('bqhgd,bhqnd->bhgqn', qb, kg),
                             (tok <= t[:, None])[:, :, None])
        o_s = jnp.einsum('bhgqn,bhqnd->bqhgd', p_s.astype(vg.dtype), vg)
        kw = lax.dynamic_slice_in_dim(kw_pad, s0, Q_BLOCK + WINDOW, 1)
        vw = lax.dynamic_slice_in_dim(vw_pad, s0, Q_BLOCK + WINDOW, 1)
        kp = s0 - WINDOW + jnp.arange(Q_BLOCK + WINDOW)
        mask_w = ((kp[None, :] <= t[:, None]) & (kp[None, :] > t[:, None] - WINDOW)
                  & (kp[None, :] >= 0))
        p_w = masked_softmax(jnp.einsum('bqhgd,bkhd->bhgqk', qb, kw), mask_w)
        o_w = jnp.einsum('bhgqk,bkhd->bqhgd', p_w.astype(vw.dtype), vw)
        return gb[..., 0:1] * o_c + gb[..., 1:2] * o_s + gb[..., 2:3] * o_w

    out = lax.map(block, jnp.arange(s // Q_BLOCK))
    return jnp.moveaxis(out, 0, 1).reshape(b, s, hkv * g * d)


def causal_depthwise_conv(x, w, bias):
    y = lax.conv_general_dilated(x, w, window_strides=(1,), padding=[(SSD_CONV - 1, 0)],
                                 dimension_numbers=('NWC', 'WIO', 'NWC'),
                                 feature_group_count=x.shape[-1])
    return y + bias


def ssd_chunked(x, dt, a, bmat, cmat):
    b, s, h, p = x.shape
    g, n = bmat.shape[2], bmat.shape[3]
    r = h // g
    l = SSD_CHUNK
    c = s // l
    xd = (x * dt[..., None]).reshape(b, c, l, g, r, p)
    a_cs = jnp.cumsum(jnp.moveaxis((dt * a).reshape(b, c, l, g, r), 2, -1), axis=-1)
    bm = bmat.reshape(b, c, l, g, n)
    cm = cmat.reshape(b, c, l, g, n)
    causal = jnp.tril(jnp.ones((l, l), bool))
    seg = jnp.exp(jnp.where(causal, a_cs[..., :, None] - a_cs[..., None, :], -jnp.inf))
    cb = jnp.einsum('bclgn,bcsgn->bcgls', cm, bm)
    y_diag = jnp.einsum('bcgrls,bcsgrp->bclgrp', cb[:, :, :, None] * seg, xd)
    decay_states = jnp.exp(a_cs[..., -1:] - a_cs)
    states = jnp.einsum('bclgn,bcgrl,bclgrp->bcgrpn', bm, decay_states, xd)
    chunk_decay = jnp.exp(a_cs[..., -1])

    def step(carry, inp):
        st, dec = inp
        return carry * dec[..., None, None] + st, carry

    _, states_in = lax.scan(step, jnp.zeros_like(states[:, 0]),
                            (jnp.moveaxis(states, 1, 0), jnp.moveaxis(chunk_decay, 1, 0)))
    states_in = jnp.moveaxis(states_in, 0, 1)
    y_off = jnp.einsum('bclgn,bcgrpn,bcgrl->bclgrp', cm, states_in, jnp.exp(a_cs))
    return (y_diag + y_off).reshape(b, s, h, p)


def mamba2_branch(z, xbc, dt_raw, conv_w, conv_b, dt_bias, a_log, d_skip, norm_w):
    b, s, _ = z.shape
    f32 = jnp.float32
    xbc = jax.nn.silu(causal_depthwise_conv(xbc, conv_w, conv_b))
    xs, bm, cm = jnp.split(xbc, [SSD_D_INNER, SSD_D_INNER + SSD_GROUPS * SSD_STATE], -1)
    xs = xs.reshape(b, s, SSD_HEADS, SSD_HEAD_DIM).astype(f32)
    dt = jax.nn.softplus(dt_raw.astype(f32) + dt_bias.astype(f32))
    a = -jnp.exp(a_log.astype(f32))
    y = ssd_chunked(xs, dt, a, bm.reshape(b, s, SSD_GROUPS, SSD_STATE).astype(f32),
                    cm.reshape(b, s, SSD_GROUPS, SSD_STATE).astype(f32))
    y = y + xs * d_skip.astype(f32)[:, None]
    y = y.reshape(b, s, SSD_D_INNER) * jax.nn.silu(z.astype(f32))
    yg = y.reshape(b, s, SSD_GROUPS, SSD_D_INNER // SSD_GROUPS)
    yg = yg * lax.rsqrt(jnp.mean(yg * yg, -1, keepdims=True) + SSD_NORM_EPS)
    return (yg.reshape(b, s, SSD_D_INNER) * norm_w.astype(f32)).astype(z.dtype)


def nsa_ssd_mixer(h, cos, sin, w_in, pe_k, w1_k, w2_k, pe_v, w1_v, w2_v,
                  conv_w, conv_b, dt_bias, a_log, d_skip, ssd_norm_w, w_out):
    b, s, _ = h.shape
    proj = h @ w_in
    offs = np.cumsum(IN_SPLITS)[:-1].tolist()
    q, kc, vc, ks, vs, kw, vw, gl, z, xbc, dt = jnp.split(proj, offs, -1)
    kvshape = (b, s, NSA_KV_HEADS, NSA_HEAD_DIM)
    q = apply_rope(q.reshape(b, s, NSA_KV_HEADS, NSA_GROUP, NSA_HEAD_DIM), cos, sin) * (NSA_HEAD_DIM ** -0.5)
    kc = apply_rope(kc.reshape(kvshape), cos, sin)
    ks = apply_rope(ks.reshape(kvshape), cos, sin)
    kw = apply_rope(kw.reshape(kvshape), cos, sin)
    gates = jax.nn.sigmoid(gl).reshape(b, s, NSA_KV_HEADS, NSA_GROUP, 3)
    o_a = nsa_attention(q, kc, vc.reshape(kvshape), ks, vs.reshape(kvshape), kw, vw.reshape(kvshape),
                        gates, pe_k, w1_k, w2_k, pe_v, w1_v, w2_v)
    o_b = mamba2_branch(z, xbc, dt, conv_w, conv_b, dt_bias, a_log, d_skip, ssd_norm_w)
    return jnp.concatenate([o_a, o_b], -1) @ w_out


def rwkv7_time_mix(h, mu, w_r, w_k, w_v, w_o, w0, w1, w2, a0, a1, a2, g1, g2,
                   k_k, k_a, r_k, ln_g, ln_b):
    b, s, d = h.shape
    f32 = jnp.float32
    xx = jnp.pad(h, ((0, 0), (1, 0), (0, 0)))[:, :-1] - h
    xr, xw, xk, xv, xa, xg = [h + xx * mu[i] for i in range(6)]
    r = xr @ w_r
    w = -jax.nn.softplus(-(w0 + jnp.tanh(xw @ w1) @ w2)) - 0.5
    k = xk @ w_k
    v = xv @ w_v
    a = jax.nn.sigmoid(a0 + (xa @ a1) @ a2)
    g = jax.nn.sigmoid(xg @ g1) @ g2
    heads = lambda t: t.reshape(b, s, RWKV_HEADS, RWKV_HEAD_DIM).astype(f32)
    kk = heads(k * k_k)
    kk = kk / jnp.maximum(jnp.sqrt(jnp.sum(kk * kk, -1, keepdims=True)), 1e-12)
    k = k * (1 + (a - 1) * k_a)
    r_, k_, v_, a_ = heads(r), heads(k), heads(v), heads(a)
    decay = jnp.exp(-jnp.exp(heads(w)))

    def step(state, inp):
        r_t, d_t, k_t, v_t, kk_t, a_t = inp
        sa = jnp.einsum('bhij,bhj->bhi', state, -kk_t)
        state = (state * d_t[:, :, None, :] + sa[..., None] * (kk_t * a_t)[:, :, None, :]
                 + v_t[..., None] * k_t[:, :, None, :])
        return state, jnp.einsum('bhij,bhj->bhi', state, r_t)

    xs = tuple(jnp.moveaxis(t, 1, 0) for t in (r_, decay, k_, v_, kk, a_))
    _, y = lax.scan(step, jnp.zeros((b, RWKV_HEADS, RWKV_HEAD_DIM, RWKV_HEAD_DIM), f32), xs)
    y = jnp.moveaxis(y, 0, 1)
    mean = jnp.mean(y, -1, keepdims=True)
    var = jnp.mean(jnp.square(y - mean), -1, keepdims=True)
    y = ((y - mean) * lax.rsqrt(var + RWKV_GN_EPS)).reshape(b, s, d)
    y = y * ln_g.astype(f32) + ln_b.astype(f32)
    bonus = jnp.sum(r_ * k_ * r_k.astype(f32), -1, keepdims=True) * v_
    y = y + bonus.reshape(b, s, d)
    return (y * g.astype(f32)).astype(h.dtype) @ w_o


def setup_inputs(seed: int = 0) -> dict:
    key = jax.random.key(seed)
    ks = iter(jax.random.split(key, 48))
    nrm = lambda shape, scale: scale * jax.random.normal(next(ks), shape, jnp.float32)
    uni = lambda shape, lo, hi: jax.random.uniform(next(ks), shape, jnp.float32, lo, hi)
    D, F, E, O = D_MODEL, D_FF, N_EVEN, N_ODD
    L, KD = CMP_BLOCK, NSA_HEAD_DIM
    x = nrm((BATCH, SEQ, D), 1.0)
    norm_gains = 1.0 + nrm((DEPTH, 6, D), 0.05)
    ffn1_w_gate = nrm((DEPTH, D, F), D ** -0.5)
    ffn1_w_up = nrm((DEPTH, D, F), D ** -0.5)
    ffn1_w_down = nrm((DEPTH, F, D), F ** -0.5)
    ffn2_w_gate = nrm((DEPTH, D, F), D ** -0.5)
    ffn2_w_up = nrm((DEPTH, D, F), D ** -0.5)
    ffn2_w_down = nrm((DEPTH, F, D), F ** -0.5)
    ab_w_in = nrm((E, D, IN_WIDTH), D ** -0.5)
    a_cmp_pe_k = nrm((E, L, KD), 0.02)
    a_cmp_w1_k = nrm((E, L * KD, KD), (L * KD) ** -0.5)
    a_cmp_w2_k = nrm((E, KD, KD), KD ** -0.5)
    a_cmp_pe_v = nrm((E, L, KD), 0.02)
    a_cmp_w1_v = nrm((E, L * KD, KD), (L * KD) ** -0.5)
    a_cmp_w2_v = nrm((E, KD, KD), KD ** -0.5)
    b_conv_w = nrm((E, SSD_CONV, 1, SSD_XBC), SSD_CONV ** -0.5)
    b_conv_b = nrm((E, SSD_XBC), 0.02)
    dt0 = jnp.exp(uni((E, SSD_HEADS), math.log(1e-3), math.log(1e-1)))
    b_dt_bias = dt0 + jnp.log(-jnp.expm1(-dt0))
    b_a_log = jnp.log(uni((E, SSD_HEADS), 1.0, 16.0))
    b_d_skip = 1.0 + nrm((E, SSD_HEADS), 0.1)
    b_norm_w = 1.0 + nrm((E, SSD_D_INNER), 0.05)
    ab_w_out = nrm((E, MIX_OUT_WIDTH, D), MIX_OUT_WIDTH ** -0.5)
    c_mu = uni((O, 6, D), 0.0, 1.0)
    c_w_r = nrm((O, D, D), D ** -0.5)
    c_w_k = nrm((O, D, D), D ** -0.5)
    c_w_v = nrm((O, D, D), D ** -0.5)
    c_w_o = nrm((O, D, D), D ** -0.5)
    c_w0 = uni((O, D), -6.0, 1.0)
    c_w1 = nrm((O, D, DECAY_LORA), D ** -0.5)
    c_w2 = nrm((O, DECAY_LORA, D), 0.1 * DECAY_LORA ** -0.5)
    c_a0 = nrm((O, D), 0.1)
    c_a1 = nrm((O, D, AAA_LORA), D ** -0.5)
    c_a2 = nrm((O, AAA_LORA, D), 0.1 * AAA_LORA ** -0.5)
    c_g1 = nrm((O, D, GATE_LORA), D ** -0.5)
    c_g2 = nrm((O, GATE_LORA, D), GATE_LORA ** -0.5)
    c_k_k = 0.85 + nrm((O, D), 0.05)
    c_k_a = 1.0 + nrm((O, D), 0.05)
    c_r_k = nrm((O, RWKV_HEADS, RWKV_HEAD_DIM), 0.1)
    c_ln_g = 1.0 + nrm((O, D), 0.05)
    c_ln_b = nrm((O, D), 0.02)
    return {'x': x, 'norm_gains': norm_gains,
            'ffn1_w_gate': ffn1_w_gate, 'ffn1_w_up': ffn1_w_up, 'ffn1_w_down': ffn1_w_down,
            'ffn2_w_gate': ffn2_w_gate, 'ffn2_w_up': ffn2_w_up, 'ffn2_w_down': ffn2_w_down,
            'ab_w_in': ab_w_in, 'a_cmp_pe_k': a_cmp_pe_k, 'a_cmp_w1_k': a_cmp_w1_k,
            'a_cmp_w2_k': a_cmp_w2_k, 'a_cmp_pe_v': a_cmp_pe_v, 'a_cmp_w1_v': a_cmp_w1_v,
            'a_cmp_w2_v': a_cmp_w2_v, 'b_conv_w': b_conv_w, 'b_conv_b': b_conv_b,
            'b_dt_bias': b_dt_bias, 'b_a_log': b_a_log, 'b_d_skip': b_d_skip,
            'b_norm_w': b_norm_w, 'ab_w_out': ab_w_out,
            'c_mu': c_mu, 'c_w_r': c_w_r, 'c_w_k': c_w_k, 'c_w_v': c_w_v, 'c_w_o': c_w_o,
            'c_w0': c_w0, 'c_w1': c_w1, 'c_w2': c_w2, 'c_a0': c_a0, 'c_a1': c_a1, 'c_a2': c_a2,
            'c_g1': c_g1, 'c_g2': c_g2, 'c_k_k': c_k_k, 'c_k_a': c_k_a, 'c_r_k': c_r_k,
            'c_ln_g': c_ln_g, 'c_ln_b': c_ln_b}


def reference(x, norm_gains, ffn1_w_gate, ffn1_w_up, ffn1_w_down, ffn2_w_gate, ffn2_w_up,
              ffn2_w_down, ab_w_in, a_cmp_pe_k, a_cmp_w1_k, a_cmp_w2_k, a_cmp_pe_v, a_cmp_w1_v,
              a_cmp_w2_v, b_conv_w, b_conv_b, b_dt_bias, b_a_log, b_d_skip, b_norm_w, ab_w_out,
              c_mu, c_w_r, c_w_k, c_w_v, c_w_o, c_w0, c_w1, c_w2, c_a0, c_a1, c_a2, c_g1, c_g2,
              c_k_k, c_k_a, c_r_k, c_ln_g, c_ln_b):
    cos, sin = rope_tables(x.shape[1], NSA_HEAD_DIM)
    for layer in range(DEPTH):
        ng = norm_gains[layer]
        hdn = swiglu(rmsnorm(x, ng[0]), ffn1_w_gate[layer], ffn1_w_up[layer], ffn1_w_down[layer])
        x = x + 0.5 * rmsnorm(hdn, ng[1])
        hdn = rmsnorm(x, ng[2])
        i = layer // 2
        if layer % 2 == 0:
            hdn = nsa_ssd_mixer(hdn, cos, sin, ab_w_in[i], a_cmp_pe_k[i], a_cmp_w1_k[i],
                                a_cmp_w2_k[i], a_cmp_pe_v[i], a_cmp_w1_v[i], a_cmp_w2_v[i],
                                b_conv_w[i], b_conv_b[i], b_dt_bias[i], b_a_log[i], b_d_skip[i],
                                b_norm_w[i], ab_w_out[i])
        else:
            hdn = rwkv7_time_mix(hdn, c_mu[i], c_w_r[i], c_w_k[i], c_w_v[i], c_w_o[i], c_w0[i],
                                 c_w1[i], c_w2[i], c_a0[i], c_a1[i], c_a2[i], c_g1[i], c_g2[i],
                                 c_k_k[i], c_k_a[i], c_r_k[i], c_ln_g[i], c_ln_b[i])
        x = x + rmsnorm(hdn, ng[3])
        hdn = swiglu(rmsnorm(x, ng[4]), ffn2_w_gate[layer], ffn2_w_up[layer], ffn2_w_down[layer])
        x = x + 0.5 * rmsnorm(hdn, ng[5])
    return x
```

```python
import numpy as np
import concourse.bass as bass
import concourse.mybir as mybir
from concourse.bass_utils import run_bass_kernel_spmd

F32 = mybir.dt.float32
BF16 = mybir.dt.bfloat16
AF = mybir.ActivationFunctionType
ALU = mybir.AluOpType
AX = mybir.AxisListType

D = 1024
DFF = 2816
NCORE = 8
TOK = 2048
EPS = 1e-6


class Buf:
    __slots__ = ("name", "lw", "rd", "dsem", "lw_dma")

    def __init__(self, name):
        self.name = name
        self.lw = None
        self.rd = []
        self.dsem = None
        self.lw_dma = False


class Sem:
    __slots__ = ("h", "total", "name")

    def __init__(self, h, name):
        self.h = h
        self.total = 0
        self.name = name


class K:
    def __init__(self, nc):
        self.nc = nc
        self.engs = {"pe": nc.tensor, "act": nc.scalar, "dve": nc.vector,
                     "pool": nc.gpsimd, "sp": nc.sync}
        self.esem = {n: Sem(nc.alloc_semaphore(name="es_" + n), n) for n in self.engs}
        self.waited = {n: {} for n in self.engs}
        self.nsem = 0
        self.ninstr = 0
        self.nwait = 0
        self.rings = {}

    def begin_phase(self):
        import contextlib
        self.phase = getattr(self, "phase", 0) + 1
        self.stack = contextlib.ExitStack()
        self.rings = {}
        self.dfree = getattr(self, "dfree", [])

    def end_phase(self):
        self.barrier()
        self.stack.close()
        self.stack = None
        self.dfree = list(self.dused)
        self.dused = []

    def barrier(self):
        sems = list(self.esem.values()) + list(getattr(self, "dall", []))
        for eng in self.engs:
            needs = {s_: s_.total for s_ in sems if s_.total > 0}
            self._emit_waits(eng, needs)

    def sb(self, name, shape, dt):
        if getattr(self, "stack", None) is not None:
            return self.stack.enter_context(self.nc.sbuf_tensor("s%d_%s" % (self.phase, name), list(shape), dt)).ap()
        return self._sb_static(name, shape, dt)

    def _sb_static(self, name, shape, dt):
        return self.nc.alloc_sbuf_tensor("s_" + name, list(shape), dt).ap()

    def ps(self, name, shape, dt=F32):
        if getattr(self, "stack", None) is not None:
            return self.stack.enter_context(self.nc.psum_tensor("p%d_%s" % (self.phase, name), list(shape), dt)).ap()
        return self.nc.alloc_psum_tensor("p_" + name, list(shape), dt).ap()

    def tile(self, name, shape, dt):
        return self.sb(name, shape, dt), Buf(name)

    def ring(self, name, shape, dt, n, psum=False):
        if name not in self.rings:
            sl = []
            for i in range(n):
                nm = "%s_%d" % (name, i)
                ap = self.ps(nm, shape, dt) if psum else self.sb(nm, shape, dt)
                sl.append((ap, Buf(nm)))
            self.rings[name] = [sl, 0]
        r = self.rings[name]
        s = r[0][r[1] % len(r[0])]
        r[1] += 1
        return s

    def _dsem(self, b):
        if b.dsem is None:
            if not hasattr(self, "dall"):
                self.dall, self.dused, self.dfree = [], [], getattr(self, "dfree", [])
            if self.dfree:
                b.dsem = self.dfree.pop()
            else:
                b.dsem = Sem(self.nc.alloc_semaphore(name="ds%d" % self.nsem), b.name)
                self.nsem += 1
                self.dall.append(b.dsem)
            self.dused.append(b.dsem)
        return b.dsem

    def _need(self, eng, tok, needs):
        if tok is None:
            return
        s, v = tok
        if v is None:
            v = s.total
        if eng == "pe" and s is self.esem["pe"]:
            return
        if v > needs.get(s, 0):
            needs[s] = v

    def _emit_waits(self, eng, needs):
        e = self.engs[eng]
        w = self.waited[eng]
        for s, v in needs.items():
            if w.get(s, 0) >= v:
                continue
            e.wait_ge(s.h, v)
            self.nwait += 1
            w[s] = v

    def op(self, eng, fn, r=(), w=(), inc=True):
        needs = {}
        for b in r:
            self._need(eng, b.lw, needs)
        for b in w:
            self._need(eng, b.lw, needs)
            for t in b.rd:
                self._need(eng, t, needs)
        self._emit_waits(eng, needs)
        ins = fn(self.engs[eng])
        s = self.esem[eng]
        if inc:
            s.total += 1
            ins.then_inc(s.h, 1)
            tok = (s, s.total)
        else:
            tok = (s, s.total + 1)
        for b in r:
            b.rd.append(tok)
            if len(b.rd) > 24:
                b.rd = self._compact(b.rd)
        for b in w:
            b.lw = tok
            b.rd = []
            b.lw_dma = False
        self.ninstr += 1
        return ins

    def _compact(self, toks):
        best = {}
        for s, v in toks:
            if v is None:
                best[s] = None
            elif s not in best or (best[s] is not None and v > best[s]):
                best[s] = v
        return [(s, v) for s, v in best.items()]

    def dma(self, q, out, in_, r=(), w=(), sb=None, **kw):
        s = self._dsem(sb)
        needs = {}
        for b in r:
            self._need(q, b.lw, needs)
        for b in w:
            if not (b.lw_dma and b.lw is not None and b.lw[0] is s and not b.rd):
                self._need(q, b.lw, needs)
            for t in b.rd:
                self._need(q, t, needs)
        self._emit_waits(q, needs)
        ins = self.engs[q].dma_start(out=out, in_=in_, **kw)
        s.total += 16
        ins.then_inc(s.h, 16)
        tok = (s, None)
        for b in r:
            b.rd.append(tok)
        for b in w:
            b.lw = tok
            b.rd = []
            b.lw_dma = True
        self.ninstr += 1
        return ins

    def wait_all(self, eng, bufs):
        needs = {}
        for b in bufs:
            self._need(eng, b.lw, needs)
            for t in b.rd:
                self._need(eng, t, needs)
        self._emit_waits(eng, needs)


class Ctx:
    def __init__(self, nc, k=None):
        self.nc = nc
        self.k = k if k is not None else K(nc)
        k = self.k
        self.ones_d, self.b_ones_d = k.tile("ones_d", [128, 128], F32)
        k.op("pool", lambda e: e.memset(self.ones_d, 1.0 / D), w=[self.b_ones_d])
        self.ident, self.b_ident = k.tile("ident", [128, 128], F32)
        k.op("pool", lambda e: e.memset(self.ident, 1.0), w=[self.b_ident])
        k.op("pool", lambda e: e.affine_select(out=self.ident, in_=self.ident, pattern=[[-1, 128]],
                                               compare_op=ALU.is_equal, fill=0.0, base=0,
                                               channel_multiplier=1), r=[self.b_ident], w=[self.b_ident])
        self.dq = 0
        self.consts, self.b_consts = k.tile("consts", [128, 16], F32)
        self.cvals = {}

    def const(self, v):
        if v not in self.cvals:
            i = len(self.cvals)
            self.cvals[v] = i
            self.k.op("pool", lambda e: e.memset(self.consts[:, i:i + 1], float(v)), w=[self.b_consts])
        i = self.cvals[v]
        return self.consts[:, i:i + 1]

    def psum(self):
        return self.k.ring("psum", [128, 512], F32, 8, psum=True)

    def wload(self, dram_ap, shape, alt=False):
        k = self.k
        ap, b = k.ring("wring", [128, 5632], BF16, getattr(self, "wring_n", 3))
        n = shape[1] * shape[2]
        v = ap[:, 0:n].rearrange("p (a b) -> p a b", a=shape[1])
        if alt and n <= 2048:
            st, stb = k.ring("wstg", [128, 2048], F32, getattr(self, "wstg_n", 2))
            sv = st[:, 0:n].rearrange("p (a b) -> p a b", a=shape[1])
            k.dma("sp", sv, dram_ap, w=[stb], sb=stb)
            k.op("dve", lambda e: e.tensor_copy(out=v, in_=sv), r=[stb], w=[b])
        else:
            k.dma("pool", v, dram_ap, w=[b], sb=b)
        return v, b


class IO:
    def __init__(self, nc, pre="", ext=None):
        self.nc, self.pre, self.ext = nc, pre, dict(ext or {})

    def inp(self, name, shape, dt=F32):
        if name in self.ext:
            return self.ext[name]
        return self.nc.dram_tensor(self.pre + name, list(shape), dt, kind="ExternalInput").ap()

    def out(self, name, shape, dt=F32):
        if name in self.ext:
            return self.ext[name]
        return self.nc.dram_tensor(self.pre + name, list(shape), dt, kind="ExternalOutput").ap()


def _std(nc, cx, io):
    if nc is None:
        nc = bass.Bass("TRN2", target_bir_lowering=False)
    if cx is None:
        cx = Ctx(nc)
    if io is None:
        io = IO(nc)
    return nc, cx, io


def hall_tile(hall, t0, n):
    r, col = t0 // TOK, t0 % TOK
    v = hall.rearrange("(i r kk p) t -> r p i kk t", i=4, r=4, kk=2, p=128)
    return v[r][:, :, :, col:col + n]


def dma_hall(k, dst, hall, t0, n, buf, **kw):
    src = hall_tile(hall, t0, n)
    for i in range(4):
        k.dma("sp", dst[:, 2 * i:2 * i + 2, :], src[:, i], w=[buf], sb=buf, **kw)


def rms_rstd(cx, x_ap, xb, eps=EPS):
    k = cx.k
    T = x_ap.shape[2]
    sq, sqb = k.ring("sq", [128, 8, 512], F32, 1)
    k.op("act", lambda e: e.activation(out=sq[:, :, 0:T], in_=x_ap, func=AF.Square), r=[xb], w=[sqb])
    ps, pb = cx.psum()
    for c in range(8):
        k.op("pe", lambda e: e.matmul(ps[:, 0:T], lhsT=cx.ones_d, rhs=sq[:, c, 0:T], start=(c == 0), stop=(c == 7)),
             r=[sqb, cx.b_ones_d], w=[pb], inc=(c == 7))
    rs, rsb = k.ring("rstd", [128, 512], F32, 2)
    c_eps = cx.const(eps)
    k.op("act", lambda e: e.activation(out=rs[:, 0:T], in_=ps[:, 0:T], func=AF.Sqrt, bias=c_eps, scale=1.0),
         r=[pb, cx.b_consts], w=[rsb])
    k.op("dve", lambda e: e.reciprocal(out=rs[:, 0:T], in_=rs[:, 0:T]), r=[rsb], w=[rsb])
    return rs[:, 0:T], rsb


def norm_bf16(cx, x_ap, xb, g_ap, gb, out_ap, outb):
    k = cx.k
    rs, rsb = rms_rstd(cx, x_ap, xb)
    for c in range(8):
        eng = "dve"
        k.op(eng, lambda e: e.scalar_tensor_tensor(out=out_ap[:, c, :], in0=x_ap[:, c, :], scalar=g_ap[:, c:c + 1],
                                                   in1=rs, op0=ALU.mult, op1=ALU.mult),
             r=[xb, gb, rsb], w=[outb])


def ffn_half(cx, x, xb, NT, wg, wu, wd, g_in, g_out, gb):
    k = cx.k
    T = NT * 512
    h, hb = k.ring("h_bf", [128, 8, 1024], BF16, 1)
    act, actb = k.ring("act_bf", [128, 22, 1024], BF16, 1)
    for tt in range(NT):
        sl = slice(tt * 512, (tt + 1) * 512)
        norm_bf16(cx, x[:, :, sl], xb, g_in, gb, h[:, :, sl], hb)
    wgv = wg.rearrange("(kc p) f -> p kc f", p=128)
    wuv = wu.rearrange("(kc p) f -> p kc f", p=128)
    for fb in range(11):
        gw, gwb = cx.wload(wgv[:, :, fb * 256:(fb + 1) * 256], [128, 8, 256])
        uw, uwb = cx.wload(wuv[:, :, fb * 256:(fb + 1) * 256], [128, 8, 256], alt=True)
        for tt in range(NT):
            sl = slice(tt * 512, (tt + 1) * 512)
            for fc in range(2):
                f = fb * 2 + fc
                pg, pgb = cx.psum()
                pu, pub = cx.psum()
                for c in range(8):
                    k.op("pe", lambda e: e.matmul(pg, lhsT=gw[:, c, fc * 128:(fc + 1) * 128], rhs=h[:, c, sl],
                                                  start=(c == 0), stop=(c == 7)), r=[gwb, hb], w=[pgb], inc=(c == 7))
                for c in range(8):
                    k.op("pe", lambda e: e.matmul(pu, lhsT=uw[:, c, fc * 128:(fc + 1) * 128], rhs=h[:, c, sl],
                                                  start=(c == 0), stop=(c == 7)), r=[uwb, hb], w=[pub], inc=(c == 7))
                sg, sgb = k.ring("sg", [128, 512], F32, 3)
                k.op("act", lambda e: e.activation(out=sg, in_=pg, func=AF.Silu), r=[pgb], w=[sgb])
                k.op("dve", lambda e: e.tensor_tensor(out=act[:, f, sl], in0=sg, in1=pu, op=ALU.mult),
                     r=[sgb, pub], w=[actb])
    wdv = wd.rearrange("(fc p) d -> p fc d", p=128)
    o, ob = k.ring("ffn_o", [128, 8, 1024], F32, 1)
    for db in range(4):
        dw, dwb = cx.wload(wdv[:, :, db * 256:(db + 1) * 256], [128, 22, 256])
        for tt in range(NT):
            sl = slice(tt * 512, (tt + 1) * 512)
            for dc in range(2):
                d = db * 2 + dc
                po, pob = cx.psum()
                for f in range(22):
                    k.op("pe", lambda e: e.matmul(po, lhsT=dw[:, f, dc * 128:(dc + 1) * 128], rhs=act[:, f, sl],
                                                  start=(f == 0), stop=(f == 21)), r=[dwb, actb], w=[pob], inc=(f == 21))
                k.op("act", lambda e: e.activation(out=o[:, d, sl], in_=po, func=AF.Copy), r=[pob], w=[ob])
    for tt in range(NT):
        sl = slice(tt * 512, (tt + 1) * 512)
        post_norm_add(cx, x[:, :, sl], xb, o[:, :, sl], ob, g_out, gb, 0.5)


def post_norm_add(cx, x_ap, xb, o_ap, ob, g_ap, gb, coef):
    k = cx.k
    rs, rsb = rms_rstd(cx, o_ap, ob)
    for c in range(8):
        eng = "dve"
        tmp, tb = k.ring("pn_tmp" + eng, [128, 512], F32, 2)
        T = o_ap.shape[2]
        k.op(eng, lambda e: e.scalar_tensor_tensor(out=tmp[:, 0:T], in0=o_ap[:, c, :], scalar=g_ap[:, c:c + 1], in1=rs,
                                                   op0=ALU.mult, op1=ALU.mult), r=[ob, gb, rsb], w=[tb])
        k.op(eng, lambda e: e.scalar_tensor_tensor(out=x_ap[:, c, :], in0=tmp[:, 0:T], scalar=float(coef), in1=x_ap[:, c, :],
                                                   op0=ALU.mult, op1=ALU.add), r=[tb, xb], w=[xb])


def load_gains(cx, gains_dram):
    k = cx.k
    g, gb = k.tile("gains", [128, 12, 8], F32)
    k.dma("sp", g, gains_dram, w=[gb], sb=gb)
    return g, gb


def build_l1(nc=None, cx=None, io=None):
    nc, cx, io = _std(nc, cx, io)
    xT = io.inp("xT", [128, 8, TOK]); gains = io.inp("gains", [128, 12, 8])
    wg = io.inp("wg", [D, DFF]); wu = io.inp("wu", [D, DFF]); wd = io.inp("wd", [DFF, D])
    x1T = io.out("x1T", [128, 8, TOK], F32)
    h0T = io.out("h0T", [1024, TOK], BF16).rearrange("(kc p) t -> p kc t", p=128)
    k = cx.k
    g, gb = load_gains(cx, gains)
    outs = []
    for half in range(2):
        hs = slice(half * 1024, (half + 1) * 1024)
        x, xb = k.ring("x_res", [128, 8, 1024], F32, 1)
        k.dma("sp", x, xT[:, :, hs], w=[xb], sb=xb)
        ffn_half(cx, x, xb, 2, wg, wu, wd, g[:, 0, :], g[:, 1, :], gb)
        k.dma("sp", x1T[:, :, hs], x, r=[xb], sb=xb)
        hh, hhb = k.ring("h_bf", [128, 8, 1024], BF16, 1)
        for tt in range(2):
            sl = slice(tt * 512, (tt + 1) * 512)
            norm_bf16(cx, x[:, :, sl], xb, g[:, 2, :], gb, hh[:, :, sl], hhb)
        k.dma("sp", h0T[:, :, hs], hh, r=[hhb], sb=hhb)
        outs += [xb, hhb]
    k.wait_all("sp", outs)
    return nc


def fm(a):
    t = a.shape[0]
    return np.ascontiguousarray(a.T.reshape(8, 128, t).transpose(1, 0, 2))


def unfm(a):
    t = a.shape[2]
    return np.ascontiguousarray(a.transpose(1, 0, 2).reshape(1024, t).T)


def gains_layout(norm_gains):
    g = norm_gains.reshape(12, 8, 128).transpose(2, 0, 1)
    return np.ascontiguousarray(g)


SEQ = 8192
RW_EXPC = 0.6065306597126334
GN_EPS = 64e-5


def mm(k, ps_ap, pairs, rbufs, wbuf):
    n = len(pairs)
    for i, (l, r) in enumerate(pairs):
        k.op("pe", lambda e: e.matmul(ps_ap, lhsT=l, rhs=r, start=(i == 0), stop=(i == n - 1)),
             r=rbufs, w=[wbuf], inc=(i == n - 1))


class _Stop(Exception):
    pass


def build_l4(ntiles=16, stop=99, nc=None, cx=None, io=None):
    try:
        return _build_l4(ntiles, stop, nc, cx, io)
    except _Stop as e:
        return e.args[0]


def _build_l4(ntiles=16, stop=99, nc=None, cx=None, io=None):
    nc, cx, io = _std(nc, cx, io)
    dt_in = io.inp
    hall = dt_in("hall", [4096, TOK], BF16)
    hzero = dt_in("hzero", [1024, 1], BF16)
    W = {"r": dt_in("wr", [D, 256]), "k": dt_in("wk", [D, 256]), "v": dt_in("wv", [D, 256]),
         "w1": dt_in("w1", [D, 64]), "a1": dt_in("a1", [D, 64]), "g1": dt_in("g1", [D, 160])}
    w2d = dt_in("w2", [64, 256]); a2d = dt_in("a2", [64, 256]); g2d = dt_in("g2", [160, 256])
    mud = dt_in("mu", [128, 6, 8])
    vecd = dt_in("vecs", [128, 7, 2])
    maskd = dt_in("masks", [128, 3, 128])
    bonesd = dt_in("bones", [128, 128])
    resetd = dt_in("resetm", [128, 512])
    ygq = io.out("ygT", [1024, TOK], BF16).rearrange("(j c p) t -> j p c t", j=4, c=2, p=128)
    k = cx.k
    PS = lambda: k.ring("psr", [128, 512], F32, 4, psum=True)
    ybank = [k.ps("ybank%d" % i, [128, 512]) for i in range(2)]
    ybb = [Buf("ybank%d" % i) for i in range(2)]
    ybank2 = [k.ps("ybankb%d" % i, [128, 512]) for i in range(2)]
    ybb2 = [Buf("ybankb%d" % i) for i in range(2)]

    mu, mub = k.tile("mu", [128, 6, 8], F32); k.dma("sp", mu, mud, w=[mub], sb=mub)
    vec, vecb = k.tile("vecs", [128, 7, 2], F32); k.dma("sp", vec, vecd, w=[vecb], sb=vecb)
    msk, mskb = k.tile("masks", [128, 3, 128], F32); k.dma("sp", msk, maskd, w=[mskb], sb=mskb)
    bones, bonesb = k.tile("bones", [128, 128], F32); k.dma("sp", bones, bonesd, w=[bonesb], sb=bonesb)
    rstm, rstmb = k.tile("resetm", [128, 512], F32); k.dma("sp", rstm, resetd, w=[rstmb], sb=rstmb)
    W0, A0, KK, KA, RK, LNG, LNB = range(7)
    m4 = lambda i: msk[:, i, :].unsqueeze(1).to_broadcast([128, 4, 128])
    id4 = cx.ident.unsqueeze(1).to_broadcast([128, 4, 128])
    c_tiny = cx.const(1e-24)
    c_gneps = cx.const(GN_EPS)

    order = {"r": 0, "w1": 1, "k": 2, "v": 3, "a1": 4, "g1": 5}
    Wa, Wb, Wbuf = {}, {}, {}
    for nm, wd_ in W.items():
        n = wd_.shape[1]
        wa, wab = k.tile("wa_" + nm, [128, 8, n], BF16)
        wb, wbb = k.tile("wb_" + nm, [128, 8, n], BF16)
        Wa[nm], Wb[nm], Wbuf[nm] = wa, wb, [wab, wbb]
    import contextlib
    _outer = getattr(k, "stack", None)
    k.stack = contextlib.ExitStack()
    k.phase = getattr(k, "phase", 0)
    for nm, wd_ in W.items():
        n = wd_.shape[1]
        wa, wb = Wa[nm], Wb[nm]
        wab, wbb = Wbuf[nm]
        st, stb = k.ring("wstage", [128, 8, 256], F32, 1)
        k.dma("sp", st[:, :, 0:n], wd_.rearrange("(kc p) n -> p kc n", p=128), w=[stb], sb=stb)
        tmp, tmpb = k.ring("wstage2", [128, 8, 256], F32, 1)
        i = order[nm]
        k.op("dve", lambda e: e.tensor_tensor(out=tmp[:, :, 0:n], in0=st[:, :, 0:n],
                                              in1=mu[:, i, :].unsqueeze(2).to_broadcast([128, 8, n]), op=ALU.mult),
             r=[stb, mub], w=[tmpb])
        k.op("dve", lambda e: e.tensor_tensor(out=wa, in0=st[:, :, 0:n], in1=tmp[:, :, 0:n], op=ALU.subtract),
             r=[stb, tmpb], w=[wab])
        k.op("act", lambda e: e.activation(out=wb, in_=tmp[:, :, 0:n], func=AF.Copy), r=[tmpb], w=[wbb])
    k.barrier()
    k.stack.close()
    k.stack = _outer
    k.rings.pop("wstage"); k.rings.pop("wstage2")
    w2, w2b = k.tile("w2", [64, 256], BF16); k.dma("pool", w2, w2d, w=[w2b], sb=w2b)
    a2, a2b = k.tile("a2", [64, 256], BF16); k.dma("pool", a2, a2d, w=[a2b], sb=a2b)
    g2a, g2ab = k.tile("g2a", [128, 256], BF16); k.dma("pool", g2a, g2d[0:128, :], w=[g2ab], sb=g2ab)
    g2c, g2cb = k.tile("g2c", [32, 256], BF16); k.dma("pool", g2c, g2d[128:160, :], w=[g2cb], sb=g2cb)

    U = []
    for hd in range(4):
        pp = []
        for j in range(2):
            u, ub = k.tile("U%d_%d" % (hd, j), [128, 64], F32)
            k.op("pool", lambda e: e.memset(u, 0.0), w=[ub])
            pp.append((u, ub))
        U.append(pp)
    ucur = [0, 0, 0, 0]
    PL = []
    for pc in range(2):
        p_, pb_ = k.tile("PL%d" % pc, [128, 9], F32)
        k.op("pool", lambda e: e.memset(p_, 1.0), w=[pb_])
        PL.append((p_, pb_))
    outbufs = []
    if stop == 1:
        raise _Stop(nc)

    for ti in range(ntiles):
        t0 = ti * 512
        hb, hbb = k.ring("hb", [128, 8, 514], BF16, 2)
        dma_hall(k, hb[:, :, 1:513], hall, t0, 512, hbb)
        if t0 == 0:
            k.dma("sp", hb[:, :, 0:1], hzero.rearrange("(kc p) t -> p kc t", p=128), w=[hbb], sb=hbb, allow_slow_non_contiguous=True)
        else:
            dma_hall(k, hb[:, :, 0:1], hall, t0 - 1, 1, hbb, allow_slow_non_contiguous=True)

        def proj_pairs(nm, cols, tok=None):
            prs = []
            for kc in range(8):
                prs.append((Wa[nm][:, kc, cols], hb[:, kc, 1:513]))
                prs.append((Wb[nm][:, kc, cols], hb[:, kc, 0:512]))
            return prs

        FM = {}

        def evac(name, ps, pb, rows=128, func=AF.Copy, bias=None, dt=F32, extra=()):
            o, ob = k.ring("fm_" + name, [128, 512], dt, 1)
            kw = {}
            if bias is not None:
                kw["bias"] = bias
            k.op("act", lambda e: e.activation(out=o[0:rows, :], in_=ps[0:rows, :], func=func, **kw),
                 r=[pb] + list(extra), w=[ob])
            return o, ob

        for pc in range(2):
            cols = slice(pc * 128, (pc + 1) * 128)
            for nm in ("r", "k", "v"):
                ps, pb = PS()
                mm(k, ps, proj_pairs(nm, cols), [hbb] + Wbuf[nm], pb)
                FM[(nm, pc)] = evac("%s%d" % (nm, pc), ps, pb)
        vtok, vtokb = k.ring("vtok", [128, 4, 256], F32, 2)
        for half in range(2):
            ps, pb = PS()
            for bi in range(2):
                blk = half * 2 + bi
                prs = []
                for kc in range(8):
                    prs.append((hb[:, kc, 1 + blk * 128:1 + (blk + 1) * 128], Wa["v"][:, kc, :]))
                    prs.append((hb[:, kc, blk * 128:(blk + 1) * 128], Wb["v"][:, kc, :]))
                mm(k, ps[:, bi * 256:(bi + 1) * 256], prs, [hbb] + Wbuf["v"], pb)
            k.op("act", lambda e: e.activation(out=vtok[:, half * 2:half * 2 + 2, :],
                                               in_=ps.rearrange("p (a b) -> p a b", a=2), func=AF.Copy),
                 r=[pb], w=[vtokb])
        ps, pb = PS(); mm(k, ps[0:64, :], proj_pairs("w1", slice(0, 64)), [hbb] + Wbuf["w1"], pb)
        hw, hwb = evac("hw", ps, pb, rows=64, func=AF.Tanh, dt=BF16)
        ps, pb = PS(); mm(k, ps[0:64, :], proj_pairs("a1", slice(0, 64)), [hbb] + Wbuf["a1"], pb)
        ha, hab = evac("ha", ps, pb, rows=64, dt=BF16)
        ps, pb = PS(); mm(k, ps, proj_pairs("g1", slice(0, 128)), [hbb] + Wbuf["g1"], pb)
        hg0, hg0b = evac("hg0", ps, pb, func=AF.Sigmoid, dt=BF16)
        ps, pb = PS(); mm(k, ps[0:32, :], proj_pairs("g1", slice(128, 160)), [hbb] + Wbuf["g1"], pb)
        hg1, hg1b = evac("hg1", ps, pb, rows=32, func=AF.Sigmoid, dt=BF16)
        for pc in range(2):
            cols = slice(pc * 128, (pc + 1) * 128)
            ps, pb = PS(); mm(k, ps, [(w2[:, cols], hw[0:64, :])], [w2b, hwb], pb)
            FM[("sgw", pc)] = evac("sgw%d" % pc, ps, pb, func=AF.Sigmoid, bias=vec[:, W0, pc:pc + 1], extra=[vecb])
            ps, pb = PS(); mm(k, ps, [(a2[:, cols], ha[0:64, :])], [a2b, hab], pb)
            FM[("a", pc)] = evac("a%d" % pc, ps, pb, func=AF.Sigmoid, bias=vec[:, A0, pc:pc + 1], extra=[vecb])
            ps, pb = PS(); mm(k, ps, [(g2a[:, cols], hg0), (g2c[:, cols], hg1[0:32, :])], [g2ab, g2cb, hg0b, hg1b], pb)
            FM[("g", pc)] = evac("g%d" % pc, ps, pb)

        if stop == 2:
            raise _Stop(nc)
        def tmpt(name, dt=F32, n=2):
            return k.ring("tmp", [128, 512], F32, 9)

        PR = {}
        for pc in range(2):
            r_, rb_ = FM[("r", pc)]; k_, kb_ = FM[("k", pc)]; a_, ab_ = FM[("a", pc)]; sg_, sgb_ = FM[("sgw", pc)]
            kk0, kk0b = tmpt("kk0")
            k.op("dve", lambda e: e.tensor_scalar(out=kk0, in0=k_, scalar1=vec[:, KK, pc:pc + 1], scalar2=None, op0=ALU.mult),
                 r=[kb_, vecb], w=[kk0b])
            sq, sqb = tmpt("sq")
            k.op("pool", lambda e: e.tensor_tensor(out=sq, in0=kk0, in1=kk0, op=ALU.mult), r=[kk0b], w=[sqb])
            ps, pb = PS(); mm(k, ps, [(bones, sq)], [bonesb, sqb], pb)
            rn, rnb = tmpt("rn")
            k.op("act", lambda e: e.activation(out=rn, in_=ps, func=AF.Sqrt, bias=c_tiny, scale=1.0), r=[pb, cx.b_consts], w=[rnb])
            k.op("dve", lambda e: e.reciprocal(out=rn, in_=rn), r=[rnb], w=[rnb])
            kap, kapb = tmpt("kap")
            k.op("dve", lambda e: e.tensor_tensor(out=kap, in0=kk0, in1=rn, op=ALU.mult), r=[kk0b, rnb], w=[kapb])
            am, amb = tmpt("am")
            k.op("dve", lambda e: e.tensor_scalar(out=am, in0=a_, scalar1=-1.0, scalar2=vec[:, KA, pc:pc + 1],
                                                  op0=ALU.add, op1=ALU.mult), r=[ab_, vecb], w=[amb])
            kp, kpb = k.ring("kp%d" % pc, [128, 512], F32, 1)
            k.op("dve", lambda e: e.scalar_tensor_tensor(out=kp, in0=am, scalar=1.0, in1=k_, op0=ALU.add, op1=ALU.mult),
                 r=[amb, kb_], w=[kpb])
            logd, logdb = tmpt("logd")
            k.op("pool", lambda e: e.tensor_scalar(out=logd, in0=sg_, scalar1=-RW_EXPC, scalar2=None, op0=ALU.mult),
                 r=[sgb_], w=[logdb])
            Lc, Lcb = tmpt("Lc")
            k.op("dve", lambda e: e.tensor_tensor_scan(out=Lc, data0=rstm, data1=logd, initial=0.0, op0=ALU.mult, op1=ALU.add),
                 r=[rstmb, logdb], w=[Lcb])
            Lm, Lmb = tmpt("Lm")
            k.op("pool", lambda e: e.tensor_tensor(out=Lm, in0=Lc, in1=logd, op=ALU.subtract), r=[Lcb, logdb], w=[Lmb])
            P_, Pb_ = tmpt("P"); Pi, Pib = tmpt("Pi"); Pp, Ppb = tmpt("Pp")
            k.op("act", lambda e: e.activation(out=P_, in_=Lc, func=AF.Exp), r=[Lcb], w=[Pb_])
            k.op("act", lambda e: e.activation(out=Pi, in_=Lc, func=AF.Exp, scale=-1.0), r=[Lcb], w=[Pib])
            k.op("act", lambda e: e.activation(out=Pp, in_=Lm, func=AF.Exp), r=[Lmb], w=[Ppb])
            pl, plb = PL[pc]
            k.op("dve", lambda e: e.tensor_copy(out=pl[:, 0:1], in_=pl[:, 8:9]), r=[plb], w=[plb])
            k.op("dve", lambda e: e.tensor_copy(out=pl[:, 1:9], in_=P_.rearrange("p (c t) -> p c t", t=64)[:, :, 63]),
                 r=[Pb_, plb], w=[plb])
            rt, rtb = k.ring("rt%d" % pc, [128, 512], F32, 1)
            kt, ktb = k.ring("kt%d" % pc, [128, 512], F32, 1)
            bt, btb = k.ring("bt%d" % pc, [128, 512], F32, 1)
            kkt, kktb = k.ring("kkt%d" % pc, [128, 512], F32, 1)
            k.op("dve", lambda e: e.tensor_tensor(out=rt, in0=r_, in1=P_, op=ALU.mult), r=[rb_, Pb_], w=[rtb])
            k.op("pool", lambda e: e.tensor_tensor(out=kt, in0=kap, in1=Pp, op=ALU.mult), r=[kapb, Ppb], w=[ktb])
            ka_, kab_ = tmpt("ka")
            k.op("pool", lambda e: e.tensor_tensor(out=ka_, in0=kap, in1=a_, op=ALU.mult), r=[kapb, ab_], w=[kab_])
            k.op("dve", lambda e: e.tensor_tensor(out=bt, in0=ka_, in1=Pi, op=ALU.mult), r=[kab_, Pib], w=[btb])
            k.op("pool", lambda e: e.tensor_tensor(out=kkt, in0=kp, in1=Pi, op=ALU.mult), r=[kpb, Pib], w=[kktb])
            rts, rtsb = k.ring("rts%d" % pc, [128, 512], F32, 1)
            kts, ktsb = k.ring("kts%d" % pc, [128, 512], F32, 1)
            plbc = pl[:, 0:8].unsqueeze(2).to_broadcast([128, 8, 64])
            k.op("dve", lambda e: e.tensor_tensor(out=rts.rearrange("p (c t) -> p c t", t=64),
                                                  in0=rt.rearrange("p (c t) -> p c t", t=64), in1=plbc, op=ALU.mult),
                 r=[rtb, plb], w=[rtsb])
            k.op("dve", lambda e: e.tensor_tensor(out=kts.rearrange("p (c t) -> p c t", t=64),
                                                  in0=kt.rearrange("p (c t) -> p c t", t=64), in1=plbc, op=ALU.mult),
                 r=[ktb, plb], w=[ktsb])
            PR[pc] = dict(rt=(rt, rtb), kt=(kt, ktb), bt=(bt, btb), kkt=(kkt, kktb), rts=(rts, rtsb), kts=(kts, ktsb),
                          kp=(kp, kpb))
        if stop == 3:
            raise _Stop(nc)
        for pc in range(2):
            for nm_ in ("rt", "kt", "bt", "kkt"):
                src_, srcb_ = PR[pc][nm_]
                sh, shb = k.ring("sh_%s%d" % (nm_, pc), [128, 512], BF16, 1)
                k.op("act", lambda e: e.activation(out=sh, in_=src_, func=AF.Copy), r=[srcb_], w=[shb])
                PR[pc][nm_ + "_h"] = (sh, shb)
        btok, btokb = k.ring("btok", [128, 4, 256], F32, 1)
        ktok, ktokb = k.ring("ktok", [128, 4, 256], F32, 1)
        for (src, dst, dstb) in (("bt", btok, btokb), ("kkt", ktok, ktokb)):
            for pc in range(2):
                s_, sb_ = PR[pc][src]
                ps, pb = PS()
                for blk in range(4):
                    k.op("pe", lambda e: e.transpose(out=ps[:, blk * 128:(blk + 1) * 128], in_=s_[:, blk * 128:(blk + 1) * 128],
                                                     identity=cx.ident), r=[sb_, cx.b_ident], w=[pb], inc=(blk == 3))
                k.op("act", lambda e: e.activation(out=dst[:, :, pc * 128:(pc + 1) * 128],
                                                   in_=ps.rearrange("p (a b) -> p a b", a=4), func=AF.Copy), r=[pb], w=[dstb])

        if stop == 4:
            raise _Stop(nc)
        HD = {}
        for hd in range(4):
            pc, hp = hd // 2, hd % 2
            rows = slice(hp * 64, hp * 64 + 64)
            hcols = slice(hd * 64, hd * 64 + 64)
            pr = PR[pc]

            def intra(lname, rname, mi, nm, depth=2, odt=F32):
                l_, lb_ = pr[lname + "_h"]; r2, rb2 = pr[rname + "_h"]
                ps, pb = PS()
                for blk in range(4):
                    bs = slice(blk * 128, (blk + 1) * 128)
                    mm(k, ps[:, bs], [(l_[rows, bs], r2[rows, bs])], [lb_, rb2], pb)
                o, ob = k.ring("im_" + nm, [128, 4, 128], odt, depth)
                k.op("dve", lambda e: e.tensor_tensor(out=o, in0=ps.rearrange("p (a b) -> p a b", a=4), in1=m4(mi), op=ALU.mult),
                     r=[pb, mskb], w=[ob])
                return o, ob

            Pm, Pmb = intra("kt", "bt", 0, "P", 2, BF16)
            Qm, Qmb = intra("bt", "kt", 1, "Q", 2, BF16)
            AkT, AkTb = intra("kkt", "kt", 1, "AkT%d" % hd, 1)
            QBT, QBTb = intra("bt", "rt", 2, "QBT%d" % hd, 1)
            QKT, QKTb = intra("kkt", "rt", 2, "QKT%d" % hd, 1)
            Rm, Rmb = k.ring("im_R", [128, 4, 128], F32, 2)
            k.op("pool", lambda e: e.tensor_tensor(out=Rm, in0=id4, in1=Qm, op=ALU.subtract), r=[cx.b_ident, Qmb], w=[Rmb])
            Rh, Rhb = k.ring("im_Rh", [128, 4, 128], BF16, 2)
            k.op("act", lambda e: e.activation(out=Rh, in_=Rm, func=AF.Copy), r=[Rmb], w=[Rhb])
            for lev in range(1, 6):
                if lev < 5:
                    ps, pb = PS()
                    for blk in range(4):
                        bs = slice(blk * 128, (blk + 1) * 128)
                        mm(k, ps[:, bs], [(Pm[:, blk, :], Qm[:, blk, :])], [Pmb, Qmb], pb)
                    Qn, Qnb = k.ring("im_Q", [128, 4, 128], BF16, 2)
                    k.op("act", lambda e: e.activation(out=Qn, in_=ps.rearrange("p (a b) -> p a b", a=4), func=AF.Copy), r=[pb], w=[Qnb])
                ps, pb = PS()
                for blk in range(4):
                    bs = slice(blk * 128, (blk + 1) * 128)
                    mm(k, ps[:, bs], [(Qm[:, blk, :], Pm[:, blk, :])], [Pmb, Qmb], pb)
                Pn, Pnb = k.ring("im_P", [128, 4, 128], BF16, 2)
                k.op("act", lambda e: e.activation(out=Pn, in_=ps.rearrange("p (a b) -> p a b", a=4), func=AF.Copy), r=[pb], w=[Pnb])
                ps, pb = PS()
                for blk in range(4):
                    bs = slice(blk * 128, (blk + 1) * 128)
                    mm(k, ps[:, bs], [(Pn[:, blk, :], Rh[:, blk, :])], [Pnb, Rhb], pb)
                Rn, Rnb = k.ring("im_R", [128, 4, 128], F32, 2) if lev < 5 else k.ring("im_Rfin%d" % hd, [128, 4, 128], F32, 1)
                k.op("dve", lambda e: e.tensor_tensor(out=Rn, in0=ps.rearrange("p (a b) -> p a b", a=4), in1=Rm, op=ALU.add),
                     r=[pb, Rmb], w=[Rnb])
                Pm, Pmb = Pn, Pnb
                if lev < 5:
                    Qm, Qmb = Qn, Qnb
                    Rh, Rhb = k.ring("im_Rh", [128, 4, 128], BF16, 2)
                    k.op("act", lambda e: e.activation(out=Rh, in_=Rn, func=AF.Copy), r=[Rnb], w=[Rhb])
                Rm, Rmb = Rn, Rnb
            if stop == 5:
                raise _Stop(nc)
            HD[hd] = (AkT, AkTb, QBT, QBTb, QKT, QKTb, Rm, Rmb)
        XA = {}
        for hd in range(4):
            hcols = slice(hd * 64, hd * 64 + 64)
            AkT, AkTb = HD[hd][0], HD[hd][1]
            ps, pb = PS()
            for hf in range(2):
                trows = slice(hf * 64, hf * 64 + 64)
                for blk in range(4):
                    mm(k, ps[trows, blk * 64:(blk + 1) * 64], [(AkT[trows, blk, trows], vtok[trows, blk, hcols])], [AkTb, vtokb], pb)
            xa, xab = k.ring("XaAll%d" % hd, [128, 4, 64], F32, 1)
            k.op("act", lambda e: e.activation(out=xa, in_=ps[:, 0:256].rearrange("p (a b) -> p a b", a=4), func=AF.Copy, scale=-1.0),
                 r=[pb], w=[xab])
            XA[hd] = (xa, xab)
        for c in range(8):
            blk, hf = c // 2, c % 2
            trows = slice(hf * 64, hf * 64 + 64)
            diag = slice(hf * 64, hf * 64 + 64)
            tcol = slice(c * 64, c * 64 + 64)
            st = {}
            for hd in range(4):
                pc, hp = hd // 2, hd % 2
                rows = slice(hp * 64, hp * 64 + 64)
                uo, uob = U[hd][ucur[hd]]
                un, unb = U[hd][1 - ucur[hd]]
                ucur[hd] = 1 - ucur[hd]
                kts, ktsb = PR[pc]["kts"]
                ps1b, pb1b = PS()
                mm(k, ps1b[trows, 0:64], [(kts[rows, tcol], uo[rows, :])], [ktsb, uob], pb1b)
                st[hd] = dict(rows=rows, hcols=slice(hd * 64, hd * 64 + 64), pc=pc, uo=uo, uob=uob, un=un, unb=unb, ps1b=ps1b, pb1b=pb1b)
            for hd in range(4):
                d_ = st[hd]
                xa, xab = XA[hd]
                X, Xb = k.ring("X", [128, 64], F32, 4)
                k.op("dve", lambda e: e.tensor_tensor(out=X[trows, :], in0=xa[trows, blk, :], in1=d_["ps1b"][trows, 0:64], op=ALU.subtract),
                     r=[xab, d_["pb1b"]], w=[Xb])
                d_["X"], d_["Xb"] = X, Xb
            for hd in range(4):
                d_ = st[hd]
                Rm, Rmb = HD[hd][6], HD[hd][7]
                ps2, pb2 = PS()
                mm(k, ps2[trows, 0:64], [(Rm[trows, blk, diag], d_["X"][trows, :])], [Rmb, d_["Xb"]], pb2)
                d_["ps2"], d_["pb2"] = ps2, pb2
            for hd in range(4):
                d_ = st[hd]
                SA, SAb = k.ring("SA", [128, 64], F32, 4)
                eng = "dve" if hd % 2 == 0 else "act"
                if eng == "dve":
                    k.op("dve", lambda e: e.tensor_copy(out=SA[trows, :], in_=d_["ps2"][trows, 0:64]), r=[d_["pb2"]], w=[SAb])
                else:
                    k.op("act", lambda e: e.activation(out=SA[trows, :], in_=d_["ps2"][trows, 0:64], func=AF.Copy), r=[d_["pb2"]], w=[SAb])
                d_["SA"], d_["SAb"] = SA, SAb
            for hd in range(4):
                d_ = st[hd]
                rows, hcols = d_["rows"], d_["hcols"]
                ps3, pb3 = PS()
                mm(k, ps3[rows, 0:64], [(ktok[trows, blk, hcols], vtok[trows, blk, hcols]),
                                        (btok[trows, blk, hcols], d_["SA"][trows, :])], [ktokb, btokb, vtokb, d_["SAb"]], pb3)
                d_["ps3"], d_["pb3"] = ps3, pb3
            for hd in range(4):
                d_ = st[hd]
                pc, rows, hcols = d_["pc"], d_["rows"], d_["hcols"]
                rts, rtsb = PR[pc]["rts"]
                QBT, QBTb, QKT, QKTb = HD[hd][2], HD[hd][3], HD[hd][4], HD[hd][5]
                mm(k, ybank[pc][rows, tcol], [(d_["uo"][rows, :], rts[rows, tcol])], [d_["uob"], rtsb], ybb[pc])
                mm(k, ybank2[pc][rows, tcol], [(d_["SA"][trows, :], QBT[trows, blk, diag]),
                                               (vtok[trows, blk, hcols], QKT[trows, blk, diag])],
                   [d_["SAb"], QBTb, QKTb, vtokb], ybb2[pc])
            for hd in range(4):
                d_ = st[hd]
                pc, rows = d_["pc"], d_["rows"]
                pl, plb = PL[pc]
                k.op("dve", lambda e: e.scalar_tensor_tensor(out=d_["un"][rows, :], in0=d_["uo"][rows, :], scalar=pl[rows, c:c + 1],
                                                             in1=d_["ps3"][rows, 0:64], op0=ALU.mult, op1=ALU.add),
                     r=[d_["uob"], plb, d_["pb3"]], w=[d_["unb"]])
        if stop == 6:
            raise _Stop(nc)
        for pc in range(2):
            r_, rb_ = FM[("r", pc)]; v_, vb_ = FM[("v", pc)]; g_, gb_ = FM[("g", pc)]
            kp, kpb = PR[pc]["kp"]
            ysb, ysbb = tmpt("ysb")
            k.op("act", lambda e: e.activation(out=ysb, in_=ybank[pc], func=AF.Copy), r=[ybb[pc]], w=[ysbb])
            k.op("dve", lambda e: e.tensor_tensor(out=ysb, in0=ysb, in1=ybank2[pc], op=ALU.add), r=[ysbb, ybb2[pc]], w=[ysbb])
            ysq, ysqb = tmpt("ysq")
            k.op("act", lambda e: e.activation(out=ysq, in_=ysb, func=AF.Square), r=[ysbb], w=[ysqb])
            psm, pbm = PS(); mm(k, psm, [(bones, ysb)], [bonesb, ysbb], pbm)
            pse, pbe = PS(); mm(k, pse, [(bones, ysq)], [bonesb, ysqb], pbe)
            mean, meanb = tmpt("mean")
            k.op("act", lambda e: e.activation(out=mean, in_=psm, func=AF.Copy, scale=1.0 / 64), r=[pbm], w=[meanb])
            var, varb = tmpt("var")
            k.op("dve", lambda e: e.tensor_tensor(out=var, in0=mean, in1=mean, op=ALU.mult), r=[meanb], w=[varb])
            k.op("dve", lambda e: e.scalar_tensor_tensor(out=var, in0=pse, scalar=1.0 / 64, in1=var, op0=ALU.mult, op1=ALU.subtract),
                 r=[pbe, varb], w=[varb])
            k.op("act", lambda e: e.activation(out=var, in_=var, func=AF.Sqrt, bias=c_gneps, scale=1.0), r=[varb, cx.b_consts], w=[varb])
            k.op("dve", lambda e: e.reciprocal(out=var, in_=var), r=[varb], w=[varb])
            yn, ynb = tmpt("yn")
            k.op("dve", lambda e: e.tensor_tensor(out=yn, in0=ysb, in1=mean, op=ALU.subtract), r=[ysbb, meanb], w=[ynb])
            k.op("dve", lambda e: e.tensor_tensor(out=yn, in0=yn, in1=var, op=ALU.mult), r=[ynb, varb], w=[ynb])
            k.op("dve", lambda e: e.tensor_scalar(out=yn, in0=yn, scalar1=vec[:, LNG, pc:pc + 1], scalar2=vec[:, LNB, pc:pc + 1],
                                                  op0=ALU.mult, op1=ALU.add), r=[ynb, vecb], w=[ynb])
            rk, rkb = tmpt("rk")
            k.op("pool", lambda e: e.tensor_tensor(out=rk, in0=r_, in1=kp, op=ALU.mult), r=[rb_, kpb], w=[rkb])
            k.op("pool", lambda e: e.tensor_scalar(out=rk, in0=rk, scalar1=vec[:, RK, pc:pc + 1], scalar2=None, op0=ALU.mult),
                 r=[rkb, vecb], w=[rkb])
            psr, pbr = PS(); mm(k, psr, [(bones, rk)], [bonesb, rkb], pbr)
            bon, bonb = tmpt("bon")
            k.op("dve", lambda e: e.tensor_tensor(out=bon, in0=psr, in1=v_, op=ALU.mult), r=[pbr, vb_], w=[bonb])
            k.op("pool", lambda e: e.tensor_tensor(out=yn, in0=yn, in1=bon, op=ALU.add), r=[ynb, bonb], w=[ynb])
            yo, yob = k.ring("yo", [128, 512], BF16, 2)
            k.op("dve", lambda e: e.tensor_tensor(out=yo, in0=yn, in1=g_, op=ALU.mult), r=[ynb, gb_], w=[yob])
            k.dma("sp", ygq[ti // 4][:, pc, (ti % 4) * 512:(ti % 4) * 512 + 512], yo, r=[yob], sb=yob)
            outbufs.append(yob)
    k.wait_all("sp", list({id(b): b for b in outbufs}.values()))
    return nc


def rwkv_consts():
    t = np.arange(128)
    same = (t[:, None] // 64) == (t[None, :] // 64)
    m_sl = (same & (t[None, :] < t[:, None])).astype(np.float32)
    m_su = (same & (t[:, None] < t[None, :])).astype(np.float32)
    m_iu = (same & (t[:, None] <= t[None, :])).astype(np.float32)
    masks = np.ascontiguousarray(np.stack([m_sl, m_su, m_iu], axis=1))
    bones = same.astype(np.float32)
    resetm = np.ones((128, 512), np.float32)
    resetm[:, ::64] = 0.0
    return masks, np.ascontiguousarray(bones), resetm


def l4_inputs(h1T_b, core_hg, P):
    cs = slice(core_hg * 256, (core_hg + 1) * 256)
    masks, bones, resetm = rwkv_consts()
    import ml_dtypes
    col = lambda v: np.ascontiguousarray(v[cs].reshape(2, 128).T)
    vecs = np.stack([col(P["c_w0"]), col(P["c_a0"]), col(P["c_k_k"]), col(P["c_k_a"]), col(P["c_r_k"].reshape(-1)),
                     col(P["c_ln_g"]), col(P["c_ln_b"])], axis=1)
    mu = np.ascontiguousarray(P["c_mu"].reshape(6, 8, 128).transpose(2, 0, 1))
    return {"hall": h1T_b, "hzero": np.zeros((1024, 1), ml_dtypes.bfloat16), "wr": np.ascontiguousarray(P["c_w_r"][:, cs]), "wk": np.ascontiguousarray(P["c_w_k"][:, cs]),
            "wv": np.ascontiguousarray(P["c_w_v"][:, cs]), "w1": P["c_w1"], "a1": P["c_a1"], "g1": P["c_g1"],
            "w2": np.ascontiguousarray(P["c_w2"][:, cs]), "a2": np.ascontiguousarray(P["c_a2"][:, cs]),
            "g2": np.ascontiguousarray(P["c_g2"][:, cs]), "mu": mu, "vecs": np.ascontiguousarray(vecs.astype(np.float32)),
            "masks": masks, "bones": bones, "resetm": resetm}


NEG = -30000.0
NQB = 32


def build_l2a(nqb=NQB, ntile1=16, stop=99, nc=None, cx=None, io=None):
    try:
        return _build_l2a(nqb, ntile1, stop, nc, cx, io)
    except _Stop as e:
        return e.args[0]


def _build_l2a(nqb=NQB, ntile1=16, stop=99, nc=None, cx=None, io=None):
    nc, cx, io = _std(nc, cx, io)
    din = io.inp
    hall = din("hall", [4096, TOK], BF16)
    psel_d = din("psel", [128, 2])
    wq_d = din("wq", [D, 256]); wks_d = din("wks", [D, 64]); wkw_d = din("wkw", [D, 64])
    wkv_d = din("wkvc", [D, 128]); wv2_d = din("wv2", [D, 128]); wg_d = din("wgate", [D, 12])
    tabA_c = din("tabA_c", [64, SEQ]); tabA_s = din("tabA_s", [64, SEQ])
    tabB_c = din("tabB_c", [128, SEQ]); tabB_s = din("tabB_s", [128, SEQ])
    tabQ_c = din("tabQ_c", [64, NQB * 128]); tabQ_s = din("tabQ_s", [64, NQB * 128])
    rot_d = din("rotT", [128, 128])
    ebig_d = din("ebig", [128, SEQ], BF16)
    w1_d = din("w1kv", [128, 32, 64]); w2_d = din("w2kv", [128, 64]); pe_d = din("pekv", [128, 32, 2])
    vcx_d = din("vcx", [128, 4, 129], BF16)
    cm_d = din("cmask", [128, 9, 128], BF16)
    sm_d = din("smask", [128, 6, 128], BF16)
    fv_d = din("fv", [NQB, 128, 2, 128])
    oa = io.out("oa", [NQB * 128, 256], BF16)
    k = cx.k
    PS = lambda: k.ring("psr", [128, 512], F32, 4, psum=True)
    held = {}
    for nm in ("oc0", "oc1", "os", "ow"):
        held[nm] = (k.ps("h_" + nm, [128, 512]), Buf("h_" + nm))
    psel, pselb = k.tile("psel", [128, 2], F32)
    k.dma("sp", psel, psel_d, w=[pselb], sb=pselb)

    def cload(name, dram, shape, dt, q="sp"):
        t, b = k.tile(name, shape, dt)
        k.dma(q, t, dram, w=[b], sb=b)
        return t, b

    rot, rotb = cload("rot", rot_d, [128, 128], F32)
    ebig, ebigb = cload("ebig", ebig_d, [128, SEQ], BF16)
    cm, cmb = cload("cm", cm_d, [128, 9, 128], BF16)
    sm, smb = cload("sm", sm_d, [128, 6, 128], BF16)
    identb, identbb = k.tile("identb", [128, 128], BF16)
    k.op("act", lambda e: e.activation(out=identb, in_=cx.ident, func=AF.Copy), r=[cx.b_ident], w=[identbb])
    ones_f, ones_fb = k.tile("ones_f", [128, 128], F32)
    k.op("pool", lambda e: e.memset(ones_f, 1.0), w=[ones_fb])

    def wcast(name, dram, n, q="pool"):
        t, b = k.tile(name, [128, 8, n], BF16)
        k.dma(q, t, dram.rearrange("(kc p) n -> p kc n", p=128), w=[b], sb=b)
        return t, b

    wq, wqb = wcast("wq", wq_d, 256); wks, wksb = wcast("wks", wks_d, 64); wkw, wkwb = wcast("wkw", wkw_d, 64)
    wkv, wkvb = wcast("wkv", wkv_d, 128); wv2, wv2b = wcast("wv2", wv2_d, 128); wgt, wgtb = wcast("wgt", wg_d, 12)
    w1, w1b = k.tile("w1", [128, 32, 64], BF16); k.dma("pool", w1, w1_d, w=[w1b], sb=w1b)
    w2, w2b = k.tile("w2", [128, 64], BF16); k.dma("pool", w2, w2_d, w=[w2b], sb=w2b)
    pe, peb = k.tile("pe", [128, 32, 2], BF16); k.dma("pool", pe, pe_d, w=[peb], sb=peb)

    ksel, kselb = k.tile("ksel", [128, SEQ], BF16)
    kwin, kwinb = k.tile("kwin", [128, SEQ], BF16)
    kvc, kvcb = k.tile("kvc", [128, SEQ + 32], BF16)
    k.op("pool", lambda e: e.memset(kvc[:, SEQ:SEQ + 32], 0.0), w=[kvcb])
    vsel, vselb = k.tile("vsel", [128, 64, 96], BF16)
    vwin, vwinb = k.tile("vwin", [128, 64, 96], BF16)
    for t_, b_ in ((ksel, kselb), (kwin, kwinb)):
        k.op("pool", lambda e: e.memset(t_[64:128, :], 0.0), w=[b_])
        k.op("pool", lambda e: e.memset(t_[64:65, :], 1.0), w=[b_])
    for t_, b_ in ((vsel, vselb), (vwin, vwinb)):
        k.op("pool", lambda e: e.memset(t_[:, :, 64:65], 1.0), w=[b_])
    kmax2, kmax2b = k.tile("kmax2", [128, 1], F32)
    k.op("pool", lambda e: e.memset(kmax2, 0.0), w=[kmax2b])

    def upd_kmax(src, srcb, rows, n):
        sq, sqb = k.ring("ksq", [128, 512], F32, 2)
        k.op("pool", lambda e: e.tensor_tensor(out=sq[rows, 0:n], in0=src, in1=src, op=ALU.mult), r=[srcb], w=[sqb])
        ps, pb = PS()
        mm(k, ps[:, 0:n], [(ones_f[rows, :], sq[rows, 0:n])], [ones_fb, sqb], pb)
        k.op("act", lambda e: e.activation(out=sq[:, 0:n], in_=ps[:, 0:n], func=AF.Copy), r=[pb], w=[sqb])
        mx, mxb = k.ring("kmx", [128, 8], F32, 2)
        k.op("dve", lambda e: e.max(out=mx, in_=sq[:, 0:n]), r=[sqb], w=[mxb])
        k.op("dve", lambda e: e.tensor_tensor(out=kmax2, in0=kmax2, in1=mx[:, 0:1], op=ALU.max), r=[mxb, kmax2b], w=[kmax2b])

    def rope_store(ps, pb, rows, tc_d, ts_d, t0, n, dst, dstb, rbase=0):
        R = slice(rbase, rbase + rows)
        xk, xkb = k.ring("xk", [128, 512], F32, 2)
        k.op("act", lambda e: e.activation(out=xk[R, 0:n], in_=ps[R, 0:n], func=AF.Copy), r=[pb], w=[xkb])
        tc_, tcb = k.ring("tabc", [128, 512], F32, 2)
        ts_, tsb = k.ring("tabs", [128, 512], F32, 2)
        k.dma("sp", tc_[R, 0:n], tc_d[:, t0:t0 + n], w=[tcb], sb=tcb)
        k.dma("sp", ts_[R, 0:n], ts_d[:, t0:t0 + n], w=[tsb], sb=tsb)
        ps2, pb2 = PS()
        mm(k, ps2[R, 0:n], [(rot[R, R], xk[R, 0:n])], [rotb, xkb], pb2)
        t1, t1b = k.ring("rp1", [128, 512], F32, 2)
        t2, t2b = k.ring("rp2", [128, 512], F32, 2)
        k.op("pool", lambda e: e.tensor_tensor(out=t1[R, 0:n], in0=xk[R, 0:n], in1=tc_[R, 0:n], op=ALU.mult), r=[xkb, tcb], w=[t1b])
        k.op("dve", lambda e: e.tensor_tensor(out=t2[R, 0:n], in0=ps2[R, 0:n], in1=ts_[R, 0:n], op=ALU.mult), r=[pb2, tsb], w=[t2b])
        k.op("dve", lambda e: e.tensor_tensor(out=dst, in0=t1[R, 0:n], in1=t2[R, 0:n], op=ALU.add), r=[t1b, t2b], w=[dstb])

    if stop == 10:
        raise _Stop(nc)
    for ti in range(ntile1):
        t0 = ti * 512
        hb, hbb = k.ring("hb", [128, 8, 512], BF16, 2)
        dma_hall(k, hb, hall, t0, 512, hbb)
        for (w_, wb_, dst, dstb) in ((wks, wksb, ksel, kselb), (wkw, wkwb, kwin, kwinb)):
            ps, pb = PS()
            mm(k, ps[0:64, :], [(w_[:, kc, :], hb[:, kc, :]) for kc in range(8)], [wb_, hbb], pb)
            if stop == 11 + 100 * ti:
                raise _Stop(nc)
            rope_store(ps, pb, 64, tabA_c, tabA_s, t0, 512, dst[0:64, t0:t0 + 512], dstb)
            if stop == 12 + 100 * ti:
                raise _Stop(nc)
            upd_kmax(dst[0:64, t0:t0 + 512], dstb, slice(0, 64), 512)
            if stop == 13 + 100 * ti:
                raise _Stop(nc)
        if stop == 14 + 100 * ti:
            raise _Stop(nc)
        ps, pb = PS()
        mm(k, ps, [(wkv[:, kc, :], hb[:, kc, :]) for kc in range(8)], [wkvb, hbb], pb)
        rope_store(ps, pb, 128, tabB_c, tabB_s, t0, 512, kvc[:, t0:t0 + 512], kvcb)
        if stop == 15 + 100 * ti:
            raise _Stop(nc)
        ps, pb = PS()
        for blk in range(4):
            mm(k, ps[:, blk * 128:(blk + 1) * 128], [(hb[:, kc, blk * 128:(blk + 1) * 128], wv2[:, kc, :]) for kc in range(8)],
               [wv2b, hbb], pb)
        pv = ps.rearrange("p (a b) -> p a b", a=4)
        k.op("act", lambda e: e.activation(out=vsel[:, ti * 4:ti * 4 + 4, 0:64], in_=pv[:, :, 0:64], func=AF.Copy), r=[pb], w=[vselb])
        k.op("act", lambda e: e.activation(out=vwin[:, ti * 4:ti * 4 + 4, 0:64], in_=pv[:, :, 64:128], func=AF.Copy), r=[pb], w=[vwinb])

    if stop == 1:
        raise _Stop(nc)
    kc, kcb = k.tile("kc", [128, 512], BF16)
    k.op("pool", lambda e: e.memset(kc, 0.0), w=[kcb])
    k.op("pool", lambda e: e.memset(kc[64:65, :], 1.0), w=[kcb])
    vcx, vcxb = k.tile("vcx", [128, 4, 256], BF16)
    k.dma("sp", vcx[:, :, 64:193], vcx_d, w=[vcxb], sb=vcxb)
    kv16 = kvc.rearrange("p (n s) -> p n s", s=16)
    hid, hidb = k.tile("hid", [128, 512], BF16)
    k.op("pool", lambda e: e.memset(hid, 0.0), w=[hidb])
    for R in (slice(0, 64), slice(64, 128)):
        psb, pbb = PS()
        mm(k, psb[R, 0:2], [(w1[R, l, :], pe[R, l, :]) for l in range(32)], [w1b, peb], pbb)
        bia, biab = k.ring("cbias", [128, 1], F32, 2)
        k.op("act", lambda e: e.activation(out=bia[R, :], in_=psb[R, 0:1], func=AF.Copy), r=[pbb], w=[biab])
        ps, pb = PS()
        mm(k, ps[R, 0:512], [(w1[R, l, :], kv16[R, (l // 16):(l // 16) + 512, l % 16]) for l in range(32)], [w1b, kvcb], pb)
        k.op("act", lambda e: e.activation(out=hid[R, :], in_=ps[R, :], func=AF.Silu, bias=bia[R, :]), r=[pb, biab], w=[hidb])
    ps, pb = PS()
    mm(k, ps[0:64, :], [(w2[0:64, :], hid[0:64, :])], [w2b, hidb], pb)
    k.op("act", lambda e: e.activation(out=kc[0:64, :], in_=ps[0:64, :], func=AF.Copy), r=[pb], w=[kcb])
    upd_kmax(kc[0:64, 0:512], kcb, slice(0, 64), 512)
    ps, pb = PS()
    for ch in range(4):
        mm(k, ps[:, ch * 64:(ch + 1) * 64], [(hid[64:128, ch * 128:(ch + 1) * 128], w2[64:128, :])], [w2b, hidb], pb)
    k.op("act", lambda e: e.activation(out=vcx[:, :, 0:64], in_=ps[:, 0:256].rearrange("p (a b) -> p a b", a=4), func=AF.Copy),
         r=[pb], w=[vcxb])
    nkm, nkmb = k.tile("nkm", [128, 1], F32)
    k.op("act", lambda e: e.activation(out=nkm, in_=kmax2, func=AF.Sqrt), r=[kmax2b], w=[nkmb])
    k.op("dve", lambda e: e.tensor_scalar(out=nkm, in0=nkm, scalar1=-1.0, scalar2=None, op0=ALU.mult), r=[nkmb], w=[nkmb])

    if stop == 2:
        raise _Stop(nc)
    outb = []
    for i in range(nqb):
        q0 = i * 128
        hqc, hqcb = k.ring("hqc", [128, 2, 8, 128], BF16, 2)
        for pp in range(2):
            dma_hall(k, hqc[:, pp], hall, (2 * i + pp) * 128, 128, hqcb)
        hqt, hqtb = k.ring("hqt", [128, 8, 128], BF16, 2)
        hqb, hqbb = k.ring("hqb", [128, 8, 128], BF16, 2)
        k.op("dve", lambda e: e.tensor_scalar(out=hqt, in0=hqc[:, 0], scalar1=psel[:, 0:1], scalar2=None, op0=ALU.mult),
             r=[hqcb, pselb], w=[hqtb])
        k.op("dve", lambda e: e.scalar_tensor_tensor(out=hqb, in0=hqc[:, 1], scalar=psel[:, 1:2], in1=hqt, op0=ALU.mult, op1=ALU.add),
             r=[hqcb, pselb, hqtb], w=[hqbb])
        qa, qab = k.ring("qa", [128, 512], BF16, 2)
        k.op("pool", lambda e: e.memset(qa[64:128, :], 0.0), w=[qab])
        qf, qfb = k.ring("qf", [128, 512], F32, 2)
        tcq, tcqb = k.ring("tcq", [128, 128], F32, 2); tsq, tsqb = k.ring("tsq", [128, 128], F32, 2)
        k.dma("sp", tcq[0:64, :], tabQ_c[:, q0:q0 + 128], w=[tcqb], sb=tcqb)
        k.dma("sp", tsq[0:64, :], tabQ_s[:, q0:q0 + 128], w=[tsqb], sb=tsqb)
        ps, pb = PS()
        for g in range(4):
            mm(k, ps[0:64, g * 128:(g + 1) * 128], [(wq[:, kc, g * 64:(g + 1) * 64], hqb[:, kc, :]) for kc in range(8)], [wqb, hqbb], pb)
        xq, xqb = k.ring("xq", [128, 512], F32, 2)
        k.op("act", lambda e: e.activation(out=xq[0:64, :], in_=ps[0:64, :], func=AF.Copy), r=[pb], w=[xqb])
        ps2, pb2 = PS()
        mm(k, ps2[0:64, :], [(rot[0:64, 0:64], xq[0:64, :])], [rotb, xqb], pb2)
        v3 = lambda a: a.rearrange("p (g t) -> p g t", g=4)
        bc4 = lambda a: a.unsqueeze(1).to_broadcast([64, 4, 128])
        k.op("pool", lambda e: e.tensor_tensor(out=v3(xq[0:64, :]), in0=v3(xq[0:64, :]), in1=bc4(tcq[0:64, :]), op=ALU.mult),
             r=[xqb, tcqb], w=[xqb])
        k.op("dve", lambda e: e.tensor_tensor(out=v3(qf[0:64, :]), in0=v3(ps2[0:64, :]), in1=bc4(tsq[0:64, :]), op=ALU.mult),
             r=[pb2, tsqb], w=[qfb])
        k.op("dve", lambda e: e.tensor_tensor(out=qf[0:64, :], in0=qf[0:64, :], in1=xq[0:64, :], op=ALU.add), r=[qfb, xqb], w=[qfb])
        k.op("act", lambda e: e.activation(out=qa[0:64, :], in_=qf[0:64, :], func=AF.Copy), r=[qfb], w=[qab])
        k.op("pool", lambda e: e.tensor_tensor(out=xq[0:64, :], in0=qf[0:64, :], in1=qf[0:64, :], op=ALU.mult), r=[qfb, xqb], w=[xqb])
        ps3, pb3 = PS()
        mm(k, ps3[64:65, :], [(ones_f[0:64, 0:1], xq[0:64, :])], [ones_fb, xqb], pb3)
        mrow, mrowb = k.ring("mrow", [128, 512], F32, 2)
        k.op("act", lambda e: e.activation(out=mrow[64:65, :], in_=ps3[64:65, :], func=AF.Sqrt), r=[pb3], w=[mrowb])
        k.op("dve", lambda e: e.tensor_scalar(out=qa[64:65, :], in0=mrow[64:65, :], scalar1=nkm[64:65, 0:1], scalar2=None, op0=ALU.mult),
             r=[mrowb, nkmb], w=[qab])
        psg, pbg = PS()
        mm(k, psg[:, 0:12], [(hqb[:, kc, :], wgt[:, kc, :]) for kc in range(8)], [wgtb, hqbb], pbg)
        gt, gtb = k.ring("gates", [128, 12], F32, 2)
        k.op("act", lambda e: e.activation(out=gt, in_=psg[:, 0:12], func=AF.Sigmoid), r=[pbg], w=[gtb])

        if stop == 3:
            raise _Stop(nc)

        def back_to_token_major(bank, bankb):
            oT, oTb = k.ring("oTsb", [128, 512], F32, 2)
            k.op("act", lambda e: e.activation(out=oT[0:65, :], in_=bank[0:65, :], func=AF.Copy), r=[bankb], w=[oTb])
            for g in range(4):
                k.op("pe", lambda e: e.transpose(out=bank[:, g * 65:(g + 1) * 65], in_=oT[0:65, g * 128:(g + 1) * 128],
                                                 identity=cx.ident[0:65, 0:65]), r=[oTb, cx.b_ident], w=[bankb], inc=(g == 3))

        def score_tile(kaug, kaugb, ktile_cols, masks):
            ps, pb = PS()
            n = 1 + len(masks)
            k.op("pe", lambda e: e.matmul(ps, lhsT=kaug[:, ktile_cols], rhs=qa, start=True, stop=(n == 1)),
                 r=[kaugb, qab], w=[pb], inc=(n == 1))
            for mi, (ml, mlb, mr, mrb) in enumerate(masks):
                last = (mi == len(masks) - 1)
                k.op("pe", lambda e: e.matmul(ps.rearrange("p (g t) -> p g t", g=4), lhsT=ml,
                                              rhs=mr.unsqueeze(1).to_broadcast([128, 4, 128]), start=False, stop=last),
                     r=[mlb, mrb], w=[pb], inc=last)
            eT, eTb = k.ring("eT", [128, 512], BF16, 3)
            k.op("act", lambda e: e.activation(out=eT, in_=ps, func=AF.Exp), r=[pb], w=[eTb])
            return eT, eTb

        qi_e = 2 * i
        last = (8 * qi_e + 6) // 128
        oc = [held["oc0"], held["oc1"]]
        def cmp_pv(cc, eT, eTb):
            for g in range(4):
                o_, ob_ = oc[g // 2]
                k.op("pe", lambda e: e.matmul(o_[:, (g % 2) * 256:(g % 2) * 256 + 193], lhsT=eT[:, g * 128:(g + 1) * 128],
                                              rhs=vcx[:, cc, 0:193], start=(cc == 0 and g % 2 == 0), stop=(cc == last and g % 2 == 1),
                                              skip_group_check=True),
                     r=[eTb, vcxb], w=[ob_], inc=(g == 3))
        pend = None
        for cc in range(last + 1):
            masks = []
            if cc == last:
                masks.append((identb, identbb, cm[:, i % 8, :], cmb))
            elif cc == last - 1 and i % 8 == 0:
                masks.append((identb, identbb, cm[:, 8, :], cmb))
            eT, eTb = score_tile(kc, kcb, slice(cc * 128, (cc + 1) * 128), masks)
            if pend is not None:
                cmp_pv(*pend)
            pend = (cc, eT, eTb)
        cmp_pv(*pend)
        if stop == 6:
            raise _Stop(nc)
        ow_, owb_ = held["ow"]
        wt = [kt for kt in range(2 * i - 4, 2 * i + 2) if kt >= 0]
        def win_pv(kt, eT, eTb):
            k.op("pe", lambda e: e.matmul(ow_[0:65, :], lhsT=vwin[:, kt, 0:65], rhs=eT, start=(kt == wt[0]), stop=(kt == wt[-1])),
                 r=[eTb, vwinb], w=[owb_], inc=True)
        pend = None
        for kt in wt:
            cols = slice(kt * 128, (kt + 1) * 128)
            pos = kt - (2 * i - 4)
            masks = []
            mi = {0: 2, 1: 3, 4: 4, 5: 5}.get(pos)
            if mi is not None:
                masks.append((identb, identbb, sm[:, mi, :], smb))
            eT, eTb = score_tile(kwin, kwinb, cols, masks)
            if pend is not None:
                win_pv(*pend)
            pend = (kt, eT, eTb)
        win_pv(*pend)
        back_to_token_major(ow_, owb_)
        if stop == 4:
            raise _Stop(nc)
        zc, zcb = k.ring("zc", [128, 4], F32, 2)
        for g in range(4):
            o_, ob_ = oc[g // 2]
            k.op("dve", lambda e: e.tensor_scalar(out=zc[:, g:g + 1], in0=o_[:, (g % 2) * 256 + 64:(g % 2) * 256 + 65], scalar1=1e-30,
                                                  scalar2=None, op0=ALU.max), r=[ob_], w=[zcb])
        k.op("dve", lambda e: e.reciprocal(out=zc, in_=zc), r=[zcb], w=[zcb])
        imp, impb = k.ring("imp", [128, 128], F32, 2)
        for g in range(4):
            o_, ob_ = oc[g // 2]
            src = o_[:, (g % 2) * 256 + 65:(g % 2) * 256 + 193]
            if g == 0:
                k.op("dve", lambda e: e.tensor_scalar(out=imp, in0=src, scalar1=zc[:, 0:1], scalar2=None, op0=ALU.mult),
                     r=[ob_, zcb], w=[impb])
            else:
                k.op("dve", lambda e: e.scalar_tensor_tensor(out=imp, in0=src, scalar=zc[:, g:g + 1], in1=imp, op0=ALU.mult, op1=ALU.add),
                     r=[ob_, zcb, impb], w=[impb])
        fv, fvb = k.ring("fv", [128, 2, 128], F32, 2)
        k.dma("sp", fv, fv_d[i], w=[fvb], sb=fvb)
        k.op("dve", lambda e: e.tensor_tensor(out=imp, in0=imp, in1=fv[:, 0, :], op=ALU.mult), r=[impb, fvb], w=[impb])
        k.op("dve", lambda e: e.tensor_tensor(out=imp, in0=imp, in1=fv[:, 1, :], op=ALU.add), r=[impb, fvb], w=[impb])
        m8, m8b = k.ring("m8", [128, 16], F32, 2)
        wk_, wkb_ = k.ring("impw", [128, 128], F32, 2)
        k.op("dve", lambda e: e.max(out=m8[:, 0:8], in_=imp), r=[impb], w=[m8b])
        k.op("dve", lambda e: e.match_replace(out=wk_, in_to_replace=m8[:, 0:8], in_values=imp, imm_value=-3e38), r=[m8b, impb], w=[wkb_])
        k.op("dve", lambda e: e.max(out=m8[:, 8:16], in_=wk_), r=[wkb_], w=[m8b])
        k.op("dve", lambda e: e.tensor_scalar(out=wk_, in0=imp, scalar1=m8[:, 15:16], scalar2=None, op0=ALU.is_ge), r=[impb, m8b], w=[wkb_])
        k.op("dve", lambda e: e.tensor_scalar(out=wk_, in0=wk_, scalar1=-1.0, scalar2=-NEG, op0=ALU.add, op1=ALU.mult), r=[wkb_], w=[wkb_])
        pst, pbt = PS()
        k.op("pe", lambda e: e.transpose(out=pst[:, 0:128], in_=wk_, identity=cx.ident), r=[wkb_, cx.b_ident], w=[pbt])
        nbT, nbTb = k.ring("nbT", [128, 128], BF16, 2)
        k.op("act", lambda e: e.activation(out=nbT, in_=pst[:, 0:128], func=AF.Copy), r=[pbt], w=[nbTb])

        if stop == 5:
            raise _Stop(nc)
        os_, osb_ = held["os"]
        nkt = 2 * i + 2
        def sel_pv(kt, eT, eTb):
            k.op("pe", lambda e: e.matmul(os_[0:65, :], lhsT=vsel[:, kt, 0:65], rhs=eT, start=(kt == 0), stop=(kt == nkt - 1)),
                 r=[eTb, vselb], w=[osb_], inc=True)
        pend = None
        for kt in range(nkt):
            cols = slice(kt * 128, (kt + 1) * 128)
            masks = [(ebig[:, cols], ebigb, nbT, nbTb)]
            if kt == 2 * i:
                masks.append((identb, identbb, sm[:, 0, :], smb))
            elif kt == 2 * i + 1:
                masks.append((identb, identbb, sm[:, 1, :], smb))
            eT, eTb = score_tile(ksel, kselb, cols, masks)
            if pend is not None:
                sel_pv(*pend)
            pend = (kt, eT, eTb)
        sel_pv(*pend)
        back_to_token_major(os_, osb_)
        sc, scb = k.ring("sc", [128, 12], F32, 2)
        k.op("pool", lambda e: e.memset(sc, 1.0), w=[scb])
        for g in range(4):
            k.op("dve", lambda e: e.tensor_scalar(out=sc[:, g * 3 + 1:g * 3 + 2], in0=os_[:, g * 65 + 64:g * 65 + 65], scalar1=1e-30,
                                                  scalar2=None, op0=ALU.max), r=[osb_], w=[scb])
            k.op("dve", lambda e: e.tensor_scalar(out=sc[:, g * 3 + 2:g * 3 + 3], in0=ow_[:, g * 65 + 64:g * 65 + 65], scalar1=1e-30,
                                                  scalar2=None, op0=ALU.max), r=[owb_], w=[scb])
        k.op("dve", lambda e: e.reciprocal(out=sc, in_=sc), r=[scb], w=[scb])
        for g in range(4):
            k.op("dve", lambda e: e.tensor_copy(out=sc[:, g * 3:g * 3 + 1], in_=zc[:, g:g + 1]), r=[zcb], w=[scb])
        k.op("dve", lambda e: e.tensor_tensor(out=sc, in0=sc, in1=gt, op=ALU.mult), r=[scb, gtb], w=[scb])
        ot, otb = k.ring("oat", [128, 256], F32, 2)
        for g in range(4):
            o_, ob_ = oc[g // 2]
            dst = ot[:, g * 64:(g + 1) * 64]
            k.op("dve", lambda e: e.tensor_scalar(out=dst, in0=o_[:, (g % 2) * 256:(g % 2) * 256 + 64], scalar1=sc[:, g * 3:g * 3 + 1],
                                                  scalar2=None, op0=ALU.mult), r=[ob_, scb], w=[otb])
            k.op("dve", lambda e: e.scalar_tensor_tensor(out=dst, in0=os_[:, g * 65:g * 65 + 64], scalar=sc[:, g * 3 + 1:g * 3 + 2], in1=dst,
                                                         op0=ALU.mult, op1=ALU.add), r=[osb_, scb, otb], w=[otb])
            k.op("dve", lambda e: e.scalar_tensor_tensor(out=dst, in0=ow_[:, g * 65:g * 65 + 64], scalar=sc[:, g * 3 + 2:g * 3 + 3], in1=dst,
                                                         op0=ALU.mult, op1=ALU.add), r=[owb_, scb, otb], w=[otb])
        otc, otcb = k.ring("oatc", [128, 256], BF16, 2)
        k.op("act", lambda e: e.activation(out=otc, in_=ot, func=AF.Copy), r=[otb], w=[otcb])
        k.dma("sp", oa[q0:q0 + 128, :], otc, r=[otcb], sb=otcb)
        outb.append(otcb)
    k.wait_all("sp", list({id(b): b for b in outb}.values()))
    return nc


def _bf16(a):
    import ml_dtypes
    return np.ascontiguousarray(a).astype(ml_dtypes.bfloat16)


def nsa_consts():
    inv = (10000.0 ** (-np.arange(0, 64, 2, dtype=np.float32) / 64)).astype(np.float32)
    ang = np.arange(SEQ, dtype=np.float32)[:, None] * inv[None, :]
    cos, sin = np.cos(ang).astype(np.float32), np.sin(ang).astype(np.float32)
    tA_c = np.ascontiguousarray(np.concatenate([cos, cos], 1).T)
    tA_s = np.ascontiguousarray(np.concatenate([sin, sin], 1).T)
    tB_c = np.ascontiguousarray(np.concatenate([tA_c, np.ones_like(tA_c)], 0))
    tB_s = np.ascontiguousarray(np.concatenate([tA_s, np.zeros_like(tA_s)], 0))
    rot = np.zeros((128, 128), np.float32)
    for m in range(128):
        mm_ = m % 64
        if mm_ < 32:
            rot[m + 32, m] = -1.0
        else:
            rot[m - 32, m] = 1.0
    x = np.arange(SEQ)
    ebig = (np.arange(128)[:, None] == (x[None, :] // 64)).astype(np.float32)
    c = np.arange(512)
    j = np.arange(128)
    ov = ((16 * c[:, None] < 64 * j[None, :] + 64) & (16 * c[:, None] + 31 >= 64 * j[None, :])).astype(np.float32)
    vcx = np.zeros((512, 129), np.float32)
    vcx[:, 0] = 1.0
    vcx[:, 1:] = ov
    vcx[511, :] = 0.0
    vcx = np.ascontiguousarray(vcx.reshape(4, 128, 129).transpose(1, 0, 2))
    return dict(tA_c=tA_c, tA_s=tA_s, tB_c=tB_c, tB_s=tB_s, rot=rot, ebig=_bf16(ebig), vcx=_bf16(vcx))


def nsa_core_consts(par):
    p = np.arange(128)[:, None]
    tl = np.arange(128)[None, :]
    cm = np.zeros((128, 9, 128), np.float32)
    for s in range(8):
        off = 8 * ((2 * s + par) % 16)
        cm[:, s, :] = np.where(16 * (p - off) + 31 <= tl, 0.0, NEG)
    if par == 0:
        cm[:, 8, :] = np.where((p == 127) & (tl < 15), NEG, 0.0)
    causal = np.where(p <= tl, 0.0, NEG).astype(np.float32)
    winT = np.where(p > tl, 0.0, NEG).astype(np.float32)
    ALL = np.full((128, 128), NEG, np.float32)
    Z = np.zeros((128, 128), np.float32)
    sm = [causal, ALL, winT, Z, causal, ALL] if par == 0 else [Z, causal, ALL, winT, Z, causal]
    sm = np.stack(sm, axis=1)
    fv = np.zeros((NQB, 128, 2, 128), np.float32)
    jj = np.arange(128)[None, :]
    for i in range(NQB):
        qi = 2 * i + par
        t = 128 * qi + np.arange(128)[:, None]
        cur = t // 64
        valid = jj <= cur
        f0 = (jj == 0)
        f1 = (jj == cur)
        f2 = (jj == cur - 1)
        forced = f0 | f1 | f2
        V = (valid & ~forced).astype(np.float32)
        F = np.where(valid, 0.0, -1e30).astype(np.float32)
        F = np.where(f2 & valid, 1e4 + 2.0, F)
        F = np.where(f0, 1e4 + 1.0, F)
        F = np.where(f1, 1e4, F)
        fv[i, :, 0, :] = V
        fv[i, :, 1, :] = F
    return dict(cm=_bf16(cm), sm=_bf16(sm), fv=fv)


def l2a_inputs(h0T_b, kvh, par, P, C):
    w = P["ab_w_in"]
    cat = lambda *a: np.ascontiguousarray(np.concatenate(a, axis=1))
    sl = lambda o, n: w[:, o + kvh * n: o + (kvh + 1) * n]
    qtok = np.concatenate([np.arange((2 * i + par) * 128, (2 * i + par + 1) * 128) for i in range(NQB)])
    cc = nsa_core_consts(par)
    w1 = np.concatenate([P["a_cmp_w1_k"].reshape(32, 64, 64).transpose(1, 0, 2),
                         P["a_cmp_w1_v"].reshape(32, 64, 64).transpose(1, 0, 2)], 0)
    w2 = np.concatenate([P["a_cmp_w2_k"], P["a_cmp_w2_v"]], 0)
    pe = np.concatenate([P["a_cmp_pe_k"].T, P["a_cmp_pe_v"].T], 0)
    pselv = np.zeros((128, 2), np.float32); pselv[:, par] = 1.0
    return {"hall": h0T_b, "psel": pselv,
            "wq": np.ascontiguousarray(sl(0, 256)), "wks": np.ascontiguousarray(sl(768, 64)), "wkw": np.ascontiguousarray(sl(1024, 64)),
            "wkvc": cat(sl(512, 64), sl(640, 64)), "wv2": cat(sl(896, 64), sl(1152, 64)), "wgate": np.ascontiguousarray(sl(1280, 12)),
            "tabA_c": C["tA_c"], "tabA_s": C["tA_s"], "tabB_c": C["tB_c"], "tabB_s": C["tB_s"],
            "tabQ_c": np.ascontiguousarray(C["tA_c"][:, qtok] * np.float32(0.125)),
            "tabQ_s": np.ascontiguousarray(C["tA_s"][:, qtok] * np.float32(0.125)),
            "rotT": C["rot"], "ebig": C["ebig"], "w1kv": np.ascontiguousarray(w1), "w2kv": np.ascontiguousarray(w2),
            "pekv": np.ascontiguousarray(np.stack([pe, pe], axis=2)), "vcx": C["vcx"], "cmask": cc["cm"], "smask": cc["sm"], "fv": cc["fv"]}


def build_l2b(ntiles=16, nc=None, cx=None, io=None):
    nc, cx, io = _std(nc, cx, io)
    din = io.inp
    hall = din("hall", [4096, TOK], BF16)
    wx_d = din("wx", [D, 256]); wB_d = din("wB", [D, 128]); wC_d = din("wC", [D, 128]); wdt_d = din("wdt", [D, 4])
    cw_d = din("convw", [128, 4, 4]); cb_d = din("convb", [128, 4])
    dtb_d = din("dtb", [128, 16]); alog_d = din("alog", [128, 16]); dsk_d = din("dskip", [128, 4])
    tri_d = din("tri", [128, 128]); su_d = din("su", [128, 128])
    yout = io.out("y", [SEQ, 256], BF16)
    k = cx.k
    PS = lambda: k.ring("psr", [128, 512], F32, 8, psum=True)

    def cload(name, dram, shape, dt, q="sp"):
        t, b = k.tile(name, shape, dt)
        k.dma(q, t, dram, w=[b], sb=b)
        return t, b

    def wcast(name, dram, n):
        t, b = k.tile(name, [128, 8, n], BF16)
        k.dma("pool", t, dram.rearrange("(kc p) n -> p kc n", p=128), w=[b], sb=b)
        return t, b

    wx, wxb = wcast("wx", wx_d, 256); wB, wBb = wcast("wB", wB_d, 128); wC, wCb = wcast("wC", wC_d, 128)
    wdt, wdtb = wcast("wdt", wdt_d, 4)
    cw, cwb = cload("cw", cw_d, [128, 4, 4], F32); cb, cbb = cload("cb", cb_d, [128, 4], F32)
    dtb, dtbb = cload("dtb", dtb_d, [128, 16], F32); alog, alogb = cload("alog", alog_d, [128, 16], F32)
    dsk, dskb = cload("dsk", dsk_d, [128, 4], F32)
    tri, trib = cload("tri", tri_d, [128, 128], F32); su, sub_ = cload("su", su_d, [128, 128], F32)
    ones_f, ones_fb = k.tile("ones_f", [128, 128], F32)
    k.op("pool", lambda e: e.memset(ones_f, 1.0), w=[ones_fb])
    c_one = cx.const(1.0)
    arep, arepb = k.tile("arep", [128, 16], F32)
    k.op("act", lambda e: e.activation(out=arep, in_=alog, func=AF.Exp), r=[alogb], w=[arepb])
    k.op("dve", lambda e: e.tensor_scalar(out=arep, in0=arep, scalar1=-1.0, scalar2=None, op0=ALU.mult), r=[arepb], w=[arepb])
    S, Sb = k.tile("S", [128, 256], F32)
    k.op("pool", lambda e: e.memset(S, 0.0), w=[Sb])
    xbc = []
    for m in range(4):
        t, b = k.tile("xbc%d" % m, [128, 516], F32)
        k.op("pool", lambda e: e.memset(t, 0.0), w=[b])
        xbc.append((t, b))
    outb = []
    wsel = [(wx, wxb, slice(0, 128)), (wx, wxb, slice(128, 256)), (wB, wBb, slice(0, 128)), (wC, wCb, slice(0, 128))]
    bc64 = lambda a, n: a.unsqueeze(2).to_broadcast([128, n, 64])
    for ti in range(ntiles):
        t0 = ti * 512
        hb, hbb = k.ring("hb", [128, 8, 512], BF16, 2)
        dma_hall(k, hb, hall, t0, 512, hbb)
        xc = []
        for m in range(4):
            w_, wb_, cols = wsel[m]
            ps, pb = PS()
            mm(k, ps, [(w_[:, kc, cols], hb[:, kc, :]) for kc in range(8)], [wb_, hbb], pb)
            xt, xtb = xbc[m]
            k.op("act", lambda e: e.activation(out=xt[:, 3:515], in_=ps, func=AF.Copy), r=[pb], w=[xtb])
            acc, accb = k.ring("cacc", [128, 512], F32, 2)
            k.op("dve", lambda e: e.tensor_scalar(out=acc, in0=xt[:, 0:512], scalar1=cw[:, m, 0:1], scalar2=cb[:, m:m + 1],
                                                  op0=ALU.mult, op1=ALU.add), r=[xtb, cwb, cbb], w=[accb])
            for j in range(1, 4):
                k.op("dve", lambda e: e.scalar_tensor_tensor(out=acc, in0=xt[:, j:j + 512], scalar=cw[:, m, j:j + 1], in1=acc,
                                                             op0=ALU.mult, op1=ALU.add), r=[xtb, cwb, accb], w=[accb])
            o, ob = k.ring("xc%d" % m, [128, 512], F32, 2)
            k.op("act", lambda e: e.activation(out=o, in_=acc, func=AF.Silu), r=[accb], w=[ob])
            k.op("pool", lambda e: e.tensor_copy(out=xt[:, 0:3], in_=xt[:, 512:515]), r=[xtb], w=[xtb])
            xc.append((o, ob))
        psd, pbd = PS()
        for c in range(4):
            mm(k, psd[:, c * 4:(c + 1) * 4], [(hb[:, kc, c * 128:(c + 1) * 128], wdt[:, kc, :]) for kc in range(8)], [wdtb, hbb], pbd)
        dt, dtb_ = k.ring("dt", [128, 16], F32, 2)
        k.op("dve", lambda e: e.tensor_tensor(out=dt, in0=psd[:, 0:16], in1=dtb, op=ALU.add), r=[pbd, dtbb], w=[dtb_])
        k.op("act", lambda e: e.activation(out=dt, in_=dt, func=AF.Exp), r=[dtb_], w=[dtb_])
        k.op("act", lambda e: e.activation(out=dt, in_=dt, func=AF.Ln, bias=c_one, scale=1.0), r=[dtb_, cx.b_consts], w=[dtb_])
        da, dab = k.ring("da", [128, 16], F32, 2)
        k.op("dve", lambda e: e.tensor_tensor(out=da, in0=dt, in1=arep, op=ALU.mult), r=[dtb_, arepb], w=[dab])
        psa, pba = PS(); mm(k, psa[:, 0:16], [(tri, da)], [trib, dab], pba)
        pst_, pbt_ = PS(); mm(k, pst_[:, 0:16], [(ones_f, da)], [ones_fb, dab], pbt_)
        acs, acsb = k.ring("acs", [128, 16], F32, 2)
        k.op("act", lambda e: e.activation(out=acs, in_=psa[:, 0:16], func=AF.Copy), r=[pba], w=[acsb])
        eacs, eacsb = k.ring("eacs", [128, 16], F32, 2)
        k.op("act", lambda e: e.activation(out=eacs, in_=acs, func=AF.Exp), r=[acsb], w=[eacsb])
        decs, decsb = k.ring("decs", [128, 16], F32, 2)
        k.op("dve", lambda e: e.tensor_tensor(out=decs, in0=pst_[:, 0:16], in1=acs, op=ALU.subtract), r=[pbt_, acsb], w=[decsb])
        k.op("act", lambda e: e.activation(out=decs, in_=decs, func=AF.Exp), r=[decsb], w=[decsb])
        cd, cdb = k.ring("cd", [128, 16], F32, 2)
        k.op("act", lambda e: e.activation(out=cd, in_=pst_[:, 0:16], func=AF.Exp), r=[pbt_], w=[cdb])
        xtok, xtokb = k.ring("xtok", [128, 4, 256], F32, 2)
        btok, btokb = k.ring("btok", [128, 4, 128], F32, 2)
        for half in range(2):
            ps, pb = PS()
            for ci in range(2):
                c = half * 2 + ci
                for m in range(2):
                    k.op("pe", lambda e: e.transpose(out=ps[:, ci * 256 + m * 128:ci * 256 + (m + 1) * 128],
                                                     in_=xc[m][0][:, c * 128:(c + 1) * 128], identity=cx.ident),
                         r=[xc[m][1], cx.b_ident], w=[pb], inc=True)
            k.op("act", lambda e: e.activation(out=xtok[:, half * 2:half * 2 + 2, :], in_=ps.rearrange("p (a b) -> p a b", a=2), func=AF.Copy),
                 r=[pb], w=[xtokb])
        ps, pb = PS()
        for c in range(4):
            k.op("pe", lambda e: e.transpose(out=ps[:, c * 128:(c + 1) * 128], in_=xc[2][0][:, c * 128:(c + 1) * 128], identity=cx.ident),
                 r=[xc[2][1], cx.b_ident], w=[pb], inc=True)
        k.op("act", lambda e: e.activation(out=btok, in_=ps.rearrange("p (a b) -> p a b", a=4), func=AF.Copy), r=[pb], w=[btokb])
        xd, xdb = k.ring("xd", [128, 4, 256], F32, 2)
        xdd, xddb = k.ring("xdd", [128, 4, 256], F32, 2)
        v16 = lambda a: a.rearrange("p c (h d) -> p (c h) d", d=64)
        k.op("dve", lambda e: e.tensor_tensor(out=v16(xd), in0=v16(xtok), in1=bc64(dt, 16), op=ALU.mult), r=[xtokb, dtb_], w=[xdb])
        k.op("pool", lambda e: e.tensor_tensor(out=v16(xdd), in0=v16(xd), in1=bc64(decs, 16), op=ALU.mult), r=[xdb, decsb], w=[xddb])
        Bc, Bcb = xc[2]; Cc, Ccb = xc[3]
        v4 = lambda a_: a_.rearrange("p (h d) -> p h d", d=64)

        def stage_a(c):
            cs = slice(c * 128, (c + 1) * 128)
            ps, pb = PS(); mm(k, ps[:, 0:128], [(Bc[:, cs], Cc[:, cs])], [Bcb, Ccb], pb)
            cbm, cbmb = k.ring("cbm", [128, 128], F32, 3)
            k.op("dve", lambda e: e.tensor_tensor(out=cbm, in0=ps[:, 0:128], in1=tri, op=ALU.mult), r=[pb, trib], w=[cbmb])
            pdf, pdfb = PS()
            for h in range(4):
                lh, lhb = k.ring("lh", [128, 128], F32, 4)
                if h % 2 == 0:
                    k.op("dve", lambda e: e.tensor_scalar(out=lh, in0=su, scalar1=da[:, c * 4 + h:c * 4 + h + 1], scalar2=None, op0=ALU.mult),
                         r=[sub_, dab], w=[lhb])
                else:
                    k.op("act", lambda e: e.activation(out=lh, in_=su, func=AF.Copy, scale=da[:, c * 4 + h:c * 4 + h + 1]),
                         r=[sub_, dab], w=[lhb])
                mm(k, pdf[:, h * 128:(h + 1) * 128], [(lh, tri)], [lhb, trib], pdfb)
            seg, segb = k.ring("seg", [128, 4, 128], F32, 3)
            k.op("act", lambda e: e.activation(out=seg, in_=pdf.rearrange("p (a b) -> p a b", a=4), func=AF.Exp), r=[pdfb], w=[segb])
            k.op("dve", lambda e: e.tensor_tensor(out=seg, in0=seg, in1=cbm.unsqueeze(1).to_broadcast([128, 4, 128]), op=ALU.mult),
                 r=[segb, cbmb], w=[segb])
            return seg, segb

        def stage_b(c, seg, segb):
            cs = slice(c * 128, (c + 1) * 128)
            py, pyb = PS()
            for h in range(4):
                mm(k, py[:, h * 64:(h + 1) * 64], [(seg[:, h, :], xd[:, c, h * 64:(h + 1) * 64])], [segb, xdb], pyb)
            po, pob = PS(); mm(k, po[:, 0:256], [(Cc[:, cs], S)], [Ccb, Sb], pob)
            t1, t1b = k.ring("yt1", [128, 256], F32, 2)
            k.op("dve", lambda e: e.tensor_tensor(out=v4(t1), in0=v4(po[:, 0:256]), in1=bc64(eacs[:, c * 4:(c + 1) * 4], 4), op=ALU.mult),
                 r=[pob, eacsb], w=[t1b])
            k.op("dve", lambda e: e.tensor_tensor(out=t1, in0=t1, in1=py[:, 0:256], op=ALU.add), r=[t1b, pyb], w=[t1b])
            t2, t2b = k.ring("yt2", [128, 256], F32, 2)
            k.op("pool", lambda e: e.tensor_tensor(out=v4(t2), in0=v4(xtok[:, c, :]), in1=bc64(dsk, 4), op=ALU.mult), r=[xtokb, dskb], w=[t2b])
            yo, yob = k.ring("yo", [128, 256], BF16, 2)
            k.op("pool", lambda e: e.tensor_tensor(out=yo, in0=t1, in1=t2, op=ALU.add), r=[t1b, t2b], w=[yob])
            r0 = (ti * 4 + c) * 128
            k.dma("sp", yout[r0:r0 + 128, :], yo, r=[yob], sb=yob)
            outb.append(yob)
            pss, pssb = PS(); mm(k, pss[:, 0:256], [(btok[:, c, :], xdd[:, c, :])], [btokb, xddb], pssb)
            k.op("dve", lambda e: e.tensor_tensor(out=v4(S), in0=v4(S), in1=bc64(cd[:, c * 4:(c + 1) * 4], 4), op=ALU.mult), r=[Sb, cdb], w=[Sb])
            k.op("dve", lambda e: e.tensor_tensor(out=S, in0=S, in1=pss[:, 0:256], op=ALU.add), r=[Sb, pssb], w=[Sb])

        pend = None
        for c in range(4):
            cur = (c,) + stage_a(c)
            if pend is not None:
                stage_b(*pend)
            pend = cur
        stage_b(*pend)
    k.wait_all("sp", list({id(b): b for b in outb}.values()))
    return nc


def l2b_inputs(h0T_b, hg, P):
    w = P["ab_w_in"]
    g = hg // 2
    xo = 2328
    rep = lambda v, n: np.ascontiguousarray(np.tile(np.asarray(v, np.float32)[None, :], (128, n)))
    chans = [np.arange(hg * 256, hg * 256 + 128), np.arange(hg * 256 + 128, hg * 256 + 256),
             1024 + g * 128 + np.arange(128), 1280 + g * 128 + np.arange(128)]
    cwf = P["b_conv_w"][:, 0, :]
    convw = np.stack([cwf[:, ch].T for ch in chans], axis=1)
    convb = np.stack([P["b_conv_b"][ch] for ch in chans], axis=1)
    hs = slice(hg * 4, hg * 4 + 4)
    t = np.arange(128)
    return {"hall": h0T_b, "wx": np.ascontiguousarray(w[:, xo + hg * 256: xo + (hg + 1) * 256]),
            "wB": np.ascontiguousarray(w[:, xo + 1024 + g * 128: xo + 1024 + (g + 1) * 128]),
            "wC": np.ascontiguousarray(w[:, xo + 1280 + g * 128: xo + 1280 + (g + 1) * 128]),
            "wdt": np.ascontiguousarray(w[:, 3864 + hg * 4: 3864 + hg * 4 + 4]),
            "convw": np.ascontiguousarray(convw.astype(np.float32)), "convb": np.ascontiguousarray(convb.astype(np.float32)),
            "dtb": rep(P["b_dt_bias"][hs], 4), "alog": rep(P["b_a_log"][hs], 4), "dskip": rep(P["b_d_skip"][hs], 1),
            "tri": (t[:, None] <= t[None, :]).astype(np.float32), "su": (t[:, None] > t[None, :]).astype(np.float32)}


def linear_tile(cx, in_ap, inb, W_dram, KC, M, out_fn):
    k = cx.k
    Wv = W_dram.rearrange("(kc p) m -> p kc m", p=128)
    for mb in range(M // 256):
        w, wb = cx.wload(Wv[:, :, mb * 256:(mb + 1) * 256], [128, KC, 256])
        for m2 in range(2):
            ps, pb = cx.psum()
            mm(k, ps, [(w[:, kc, m2 * 128:(m2 + 1) * 128], in_ap[:, kc, :]) for kc in range(KC)], [wb, inb], pb)
            out_fn(mb * 2 + m2, ps, pb)


def build_l3(nc=None, cx=None, io=None):
    nc, cx, io = _std(nc, cx, io)
    din = io.inp
    x1T = din("x1T", [128, 8, TOK]); h0T = din("h0T", [1024, TOK], BF16).rearrange("(kc p) t -> p kc t", p=128)
    oaall = din("oaall", [4 * 4096, 256], BF16); yall = din("yall", [4 * SEQ, 256], BF16); qsel_d = din("qsel", [128, 4])
    gains = din("gains", [128, 12, 8]); normw_d = din("normw", [128, 8])
    wz = din("wz", [D, D]); wout = din("wout", [1536, D])
    f2 = [din("f2g", [D, DFF]), din("f2u", [D, DFF]), din("f2d", [DFF, D])]
    f1 = [din("f1g", [D, DFF]), din("f1u", [D, DFF]), din("f1d", [DFF, D])]
    x4T = io.out("x4T", [128, 8, TOK], F32)
    h1T = io.out("h1T", [1024, TOK], BF16).rearrange("(kc p) t -> p kc t", p=128)
    k = cx.k
    cx.wring_n = 2
    cx.wstg_n = 1
    g, gb = load_gains(cx, gains)
    nw, nwb = k.tile("normw", [128, 8], F32); k.dma("sp", nw, normw_d, w=[nwb], sb=nwb)
    qsel, qselb = k.tile("qsel", [128, 4], F32); k.dma("sp", qsel, qsel_d, w=[qselb], sb=qselb)
    ones512, ones512b = k.tile("ones512", [128, 128], F32)
    k.op("pool", lambda e: e.memset(ones512, 1.0 / 512), w=[ones512b])
    idq, idqb = k.tile("idq", [128, 4, 128], BF16)
    for j in range(4):
        k.op("dve", lambda e: e.tensor_scalar(out=idq[:, j, :], in0=cx.ident, scalar1=qsel[:, j:j + 1], scalar2=None, op0=ALU.mult),
             r=[cx.b_ident, qselb], w=[idqb])
    c_eps5 = cx.const(1e-5)
    outs = []
    for half in range(2):
        x, xb = k.ring("x_res", [128, 8, 1024], F32, 1)
        k.dma("sp", x, x1T[:, :, half * 1024:(half + 1) * 1024], w=[xb], sb=xb)
        for tt in range(2):
            t0 = half * 1024 + tt * 512
            tq = t0 // 512
            sl = slice(tt * 512, (tt + 1) * 512)
            o, ob = k.ring("ffn_o", [128, 8, 1024], F32, 1)
            act, actb = k.ring("act_bf", [128, 22, 1024], BF16, 1)
            ys = o[:, :, 0:512]
            mo = o[:, :, 512:1024]
            mixin = act[:, 0:6, :].rearrange("p a (b t) -> p (a b) t", t=512)
            h0 = act[:, 6:10, :].rearrange("p a (b t) -> p (a b) t", t=512)
            k.dma("sp", h0, h0T[:, :, t0:t0 + 512], w=[actb], sb=actb)
            for hg in range(4):
                cand, candb = k.ring("ycand", [128, 4, 4, 256], BF16, 1)
                for j in range(4):
                    r0 = (j * 4 + hg) * TOK + t0
                    k.dma("sp", cand[:, j], yall[r0:r0 + 512, :].rearrange("(tb p) c -> p tb c", p=128), w=[candb], sb=candb)
                for hh in range(2):
                    ps, pb = cx.psum()
                    for tb in range(4):
                        mm(k, ps[:, tb * 128:(tb + 1) * 128],
                           [(cand[:, j, tb, hh * 128:(hh + 1) * 128], idq[:, j, :]) for j in range(4)], [candb, idqb], pb)
                    k.op("act", lambda e: e.activation(out=ys[:, hg * 2 + hh, :], in_=ps, func=AF.Copy), r=[pb], w=[ob])
            for kvh in range(2):
                ocand, ocandb = k.ring("ocand", [128, 2, 4, 2, 256], BF16, 1)
                for par in range(2):
                    for j in range(4):
                        r0 = ((j // 2) * 4 + kvh * 2 + par) * 2048 + (j % 2) * 1024 + (2 * tq) * 128
                        k.dma("sp", ocand[:, par, j], oaall[r0:r0 + 256, :].rearrange("(i p) c -> p i c", p=128), w=[ocandb], sb=ocandb)
                for hh in range(2):
                    ps, pb = cx.psum()
                    for u in range(4):
                        par, i2 = u % 2, u // 2
                        mm(k, ps[:, u * 128:(u + 1) * 128],
                           [(ocand[:, par, j, i2, hh * 128:(hh + 1) * 128], idq[:, j, :]) for j in range(4)], [ocandb, idqb], pb)
                    k.op("act", lambda e: e.activation(out=mixin[:, kvh * 2 + hh, :], in_=ps, func=AF.Copy), r=[pb], w=[actb])

            def z_out(mc, ps, pb):
                zs, zsb = k.ring("sg", [128, 512], F32, 3)
                k.op("act", lambda e: e.activation(out=zs, in_=ps, func=AF.Silu), r=[pb], w=[zsb])
                k.op("dve", lambda e: e.tensor_tensor(out=ys[:, mc, :], in0=ys[:, mc, :], in1=zs, op=ALU.mult), r=[zsb, ob], w=[ob])
            linear_tile(cx, h0, actb, wz, 8, D, z_out)
            sq, sqb = k.ring("sq", [128, 8, 512], F32, 1)
            k.op("act", lambda e: e.activation(out=sq, in_=ys, func=AF.Square), r=[ob], w=[sqb])
            for gi in range(2):
                ps, pb = cx.psum()
                mm(k, ps, [(ones512, sq[:, gi * 4 + c, :]) for c in range(4)], [ones512b, sqb], pb)
                rs, rsb = k.ring("rstd", [128, 512], F32, 2)
                k.op("act", lambda e: e.activation(out=rs, in_=ps, func=AF.Sqrt, bias=c_eps5, scale=1.0), r=[pb, cx.b_consts], w=[rsb])
                k.op("dve", lambda e: e.reciprocal(out=rs, in_=rs), r=[rsb], w=[rsb])
                for c in range(4):
                    ch = gi * 4 + c
                    k.op("dve", lambda e: e.scalar_tensor_tensor(out=mixin[:, 4 + ch, :], in0=ys[:, ch, :], scalar=nw[:, ch:ch + 1], in1=rs,
                                                                 op0=ALU.mult, op1=ALU.mult), r=[ob, nwb, rsb], w=[actb])

            def mix_out(mc, ps, pb):
                k.op("act", lambda e: e.activation(out=mo[:, mc, :], in_=ps, func=AF.Copy), r=[pb], w=[ob])
            linear_tile(cx, mixin, actb, wout, 12, D, mix_out)
            post_norm_add(cx, x[:, :, sl], xb, mo, ob, g[:, 3, :], gb, 1.0)
        ffn_half(cx, x, xb, 2, f2[0], f2[1], f2[2], g[:, 4, :], g[:, 5, :], gb)
        ffn_half(cx, x, xb, 2, f1[0], f1[1], f1[2], g[:, 6, :], g[:, 7, :], gb)
        k.dma("sp", x4T[:, :, half * 1024:(half + 1) * 1024], x, r=[xb], sb=xb)
        hh_, hhb = k.ring("h_bf", [128, 8, 1024], BF16, 1)
        for tt in range(2):
            sl = slice(tt * 512, (tt + 1) * 512)
            norm_bf16(cx, x[:, :, sl], xb, g[:, 8, :], gb, hh_[:, :, sl], hhb)
        k.dma("sp", h1T[:, :, half * 1024:(half + 1) * 1024], hh_, r=[hhb], sb=hhb)
        outs += [xb, hhb]
    k.wait_all("sp", outs)
    return nc


def build_l5(nc=None, cx=None, io=None):
    nc, cx, io = _std(nc, cx, io)
    din = io.inp
    x4T = din("x4T", [128, 8, TOK])
    ygall = din("ygall", [4096, TOK], BF16).rearrange("(j hg pc p) t -> j p (hg pc) t", j=4, hg=4, pc=2, p=128)
    qsel_d = din("qsel", [128, 4])
    gains = din("gains", [128, 12, 8]); wo = din("wo", [D, D])
    f2 = [din("f2g", [D, DFF]), din("f2u", [D, DFF]), din("f2d", [DFF, D])]
    outT = io.out("outT", [128, 8, TOK], F32)
    k = cx.k
    cx.wstg_n = 1
    g, gb = load_gains(cx, gains)
    qsel, qselb = k.tile("qsel", [128, 4], F32); k.dma("sp", qsel, qsel_d, w=[qselb], sb=qselb)
    outs = []
    for half in range(2):
        x, xb = k.ring("x_res", [128, 8, 1024], F32, 1)
        k.dma("sp", x, x4T[:, :, half * 1024:(half + 1) * 1024], w=[xb], sb=xb)
        for tt in range(2):
            t0 = half * 1024 + tt * 512
            sl = slice(tt * 512, (tt + 1) * 512)
            o, ob = k.ring("ffn_o", [128, 8, 1024], F32, 1)
            act, actb = k.ring("act_bf", [128, 22, 1024], BF16, 1)
            mo = o[:, :, 512:1024]
            yg = act[:, 6:10, :].rearrange("p a (b t) -> p (a b) t", t=512)
            for j in range(4):
                cand, candb = k.ring("ygcand", [128, 8, 512], BF16, 1)
                k.dma("sp", cand, ygall[j][:, :, t0:t0 + 512], w=[candb], sb=candb)
                if j == 0:
                    k.op("dve", lambda e: e.tensor_scalar(out=yg, in0=cand, scalar1=qsel[:, 0:1], scalar2=None, op0=ALU.mult),
                         r=[candb, qselb], w=[actb])
                else:
                    k.op("dve", lambda e: e.scalar_tensor_tensor(out=yg, in0=cand, scalar=qsel[:, j:j + 1], in1=yg, op0=ALU.mult, op1=ALU.add),
                         r=[candb, qselb, actb], w=[actb])

            def mix_out(mc, ps, pb):
                k.op("act", lambda e: e.activation(out=mo[:, mc, :], in_=ps, func=AF.Copy), r=[pb], w=[ob])
            linear_tile(cx, yg, actb, wo, 8, D, mix_out)
            post_norm_add(cx, x[:, :, sl], xb, mo, ob, g[:, 9, :], gb, 1.0)
        ffn_half(cx, x, xb, 2, f2[0], f2[1], f2[2], g[:, 10, :], g[:, 11, :], gb)
        k.dma("sp", outT[:, :, half * 1024:(half + 1) * 1024], x, r=[xb], sb=xb)
        outs.append(xb)
    k.wait_all("sp", outs)
    return nc


def _run(nc, in_maps):
    res = run_bass_kernel_spmd(nc, in_maps, core_ids=list(range(NCORE)))
    return res.results


def kernel_unfused(**inp):
    import ml_dtypes
    f32 = lambda a: np.ascontiguousarray(np.asarray(a, dtype=np.float32))
    I = {k_: f32(v) for k_, v in inp.items()}
    x = I["x"].reshape(16384, D)
    g = gains_layout(I["norm_gains"])
    tok = [slice(c * TOK, (c + 1) * TOK) for c in range(NCORE)]
    grp = lambda lst, b: np.ascontiguousarray(np.concatenate(lst[4 * b:4 * b + 4], axis=0))
    qsel = []
    for c in range(NCORE):
        q_ = np.zeros((128, 4), np.float32); q_[:, c % 4] = 1.0
        qsel.append(q_)
    r1 = _run(build_l1(), [{"xT": fm(x[tok[c]]), "gains": g, "wg": I["ffn1_w_gate"][0], "wu": I["ffn1_w_up"][0],
                            "wd": I["ffn1_w_down"][0]} for c in range(NCORE)])
    x1T = [np.asarray(r["x1T"]) for r in r1]
    h0loc = [np.asarray(r["h0T"]) for r in r1]
    h0all = [grp(h0loc, b) for b in range(2)]
    PA = {k_: I[k_][0] for k_ in I if k_.startswith("a_") or k_.startswith("ab_") or k_.startswith("b_")}
    C = nsa_consts()
    r2a = _run(build_l2a(), [l2a_inputs(h0all[c // 4], (c % 4) // 2, c % 2, PA, C) for c in range(NCORE)])
    r2b = _run(build_l2b(), [l2b_inputs(h0all[c // 4], c % 4, PA) for c in range(NCORE)])
    oaall = [grp([np.asarray(r["oa"]) for r in r2a], b) for b in range(2)]
    yall = [grp([np.asarray(r["y"]) for r in r2b], b) for b in range(2)]
    w_in = I["ab_w_in"][0]
    m3 = []
    for c in range(NCORE):
        m3.append({"x1T": x1T[c], "h0T": h0loc[c], "oaall": oaall[c // 4], "yall": yall[c // 4], "qsel": qsel[c], "gains": g,
                   "normw": np.ascontiguousarray(I["b_norm_w"][0].reshape(8, 128).T),
                   "wz": np.ascontiguousarray(w_in[:, 1304:2328]), "wout": I["ab_w_out"][0],
                   "f2g": I["ffn2_w_gate"][0], "f2u": I["ffn2_w_up"][0], "f2d": I["ffn2_w_down"][0],
                   "f1g": I["ffn1_w_gate"][1], "f1u": I["ffn1_w_up"][1], "f1d": I["ffn1_w_down"][1]})
    r3 = _run(build_l3(), m3)
    x4T = [np.asarray(r["x4T"]) for r in r3]
    h1all = [grp([np.asarray(r["h1T"]) for r in r3], b) for b in range(2)]
    PC = {k_: I[k_][0] for k_ in I if k_.startswith("c_")}
    r4 = _run(build_l4(), [l4_inputs(h1all[c // 4], c % 4, PC) for c in range(NCORE)])
    ygall = [grp([np.asarray(r["ygT"]) for r in r4], b) for b in range(2)]
    m5 = []
    for c in range(NCORE):
        m5.append({"x4T": x4T[c], "ygall": ygall[c // 4], "qsel": qsel[c], "gains": g,
                   "wo": I["c_w_o"][0], "f2g": I["ffn2_w_gate"][1], "f2u": I["ffn2_w_up"][1], "f2d": I["ffn2_w_down"][1]})
    r5 = _run(build_l5(), m5)
    out = np.concatenate([unfm(np.asarray(r["outT"])) for r in r5], axis=0)
    return np.ascontiguousarray(out.reshape(2, SEQ, D).astype(np.float32))


RG = [[0, 1, 2, 3], [4, 5, 6, 7]]


def build_fused(upto=99):
    nc = bass.Bass("TRN2", target_bir_lowering=False, num_devices=NCORE)
    k = K(nc)
    idram = lambda n, sh, dt: nc.dram_tensor(n, list(sh), dt, kind="Internal").ap()
    x1T = idram("i_x1T", [128, 8, TOK], F32)
    h0loc = idram("i_h0loc", [1024, TOK], BF16); h0all = idram("i_h0all", [4096, TOK], BF16)
    oaloc = idram("i_oaloc", [NQB * 128, 256], BF16); oaall = idram("i_oaall", [4 * 4096, 256], BF16)
    yloc = idram("i_yloc", [SEQ, 256], BF16); yall = idram("i_yall", [4 * SEQ, 256], BF16)
    x4T = idram("i_x4T", [128, 8, TOK], F32)
    h1loc = idram("i_h1loc", [1024, TOK], BF16); h1all = idram("i_h1all", [4096, TOK], BF16)
    ygloc = idram("i_ygloc", [1024, TOK], BF16); ygall = idram("i_ygall", [4096, TOK], BF16)
    ccsem = Sem(nc.alloc_semaphore(name="ccsem"), "cc")
    k.dall = [ccsem]; k.dused = []; k.dfree = []

    def allgather(src, dst, wait=True):
        rows, cols = src.shape
        R = (1 << 20) // (cols * mybir.dt.size(src.dtype))
        k.barrier()
        for i in range(rows // R):
            ins = nc.gpsimd.collective_compute("AllGather", ALU.bypass, replica_groups=RG,
                                               ins=[src[i * R:(i + 1) * R, :]], outs=[dst[i * 4 * R:(i + 1) * 4 * R, :]])
            ins.then_inc(ccsem.h, 1)
            ccsem.total += 1
        if wait:
            k.barrier()

    def phase(fn, pre, ext, **kw):
        k.begin_phase()
        cx = Ctx(nc, k)
        fn(nc=nc, cx=cx, io=IO(nc, pre, ext), **kw)
        k.end_phase()

    steps = [lambda: phase(build_l1, "l1_", {"x1T": x1T, "h0T": h0loc}),
             lambda: allgather(h0loc, h0all),
             lambda: phase(build_l2a, "l2a_", {"hall": h0all, "oa": oaloc}),
             lambda: allgather(oaloc, oaall, wait=False),
             lambda: phase(build_l2b, "l2b_", {"hall": h0all, "y": yloc}),
             lambda: allgather(yloc, yall),
             lambda: phase(build_l3, "l3_", {"x1T": x1T, "h0T": h0loc, "oaall": oaall, "yall": yall, "x4T": x4T, "h1T": h1loc}),
             lambda: allgather(h1loc, h1all),
             lambda: phase(build_l4, "l4_", {"hall": h1all, "ygT": ygloc}),
             lambda: allgather(ygloc, ygall),
             lambda: phase(build_l5, "l5_", {"x4T": x4T, "ygall": ygall})]
    for st in steps[:upto]:
        st()
    if upto < len(steps):
        nc.dram_tensor("l5_outT", [128, 8, TOK], F32, kind="ExternalOutput")
    return nc


def kernel(**inp):
    f32 = lambda a: np.ascontiguousarray(np.asarray(a, dtype=np.float32))
    I = {k_: f32(v) for k_, v in inp.items()}
    x = I["x"].reshape(16384, D)
    g = gains_layout(I["norm_gains"])
    PA = {k_: I[k_][0] for k_ in I if k_.startswith("a_") or k_.startswith("ab_") or k_.startswith("b_")}
    PC = {k_: I[k_][0] for k_ in I if k_.startswith("c_")}
    C = nsa_consts()
    w_in = I["ab_w_in"][0]
    in_maps = []
    for c in range(NCORE):
        m = {}
        qs = np.zeros((128, 4), np.float32); qs[:, c % 4] = 1.0
        m.update({"l1_" + k_: v for k_, v in {"xT": fm(x[c * TOK:(c + 1) * TOK]), "gains": g, "wg": I["ffn1_w_gate"][0],
                                             "wu": I["ffn1_w_up"][0], "wd": I["ffn1_w_down"][0]}.items()})
        a = l2a_inputs(None, (c % 4) // 2, c % 2, PA, C); a.pop("hall")
        m.update({"l2a_" + k_: v for k_, v in a.items()})
        b_ = l2b_inputs(None, c % 4, PA); b_.pop("hall")
        m.update({"l2b_" + k_: v for k_, v in b_.items()})
        m.update({"l3_" + k_: v for k_, v in {"qsel": qs, "gains": g,
                  "normw": np.ascontiguousarray(I["b_norm_w"][0].reshape(8, 128).T),
                  "wz": np.ascontiguousarray(w_in[:, 1304:2328]), "wout": I["ab_w_out"][0],
                  "f2g": I["ffn2_w_gate"][0], "f2u": I["ffn2_w_up"][0], "f2d": I["ffn2_w_down"][0],
                  "f1g": I["ffn1_w_gate"][1], "f1u": I["ffn1_w_up"][1], "f1d": I["ffn1_w_down"][1]}.items()})
        d4 = l4_inputs(None, c % 4, PC); d4.pop("hall")
        m.update({"l4_" + k_: v for k_, v in d4.items()})
        m.update({"l5_" + k_: v for k_, v in {"qsel": qs, "gains": g, "wo": I["c_w_o"][0], "f2g": I["ffn2_w_gate"][1],
                                             "f2u": I["ffn2_w_up"][1], "f2d": I["ffn2_w_down"][1]}.items()})
        in_maps.append(m)
    import os
    upto = int(os.environ.get('FUSED_UPTO', '99'))
    nc_ = build_fused(upto)
    if upto < 99:
        names = {a.memorylocations[0].name for a in nc_.allocations if hasattr(a, 'memorylocations') and a.memorylocations}
        in_maps = [{k_: v for k_, v in m.items() if k_ in names} for m in in_maps]
    res = _run(nc_, in_maps)
    out = np.concatenate([unfm(np.asarray(r["l5_outT"])) for r in res], axis=0)
    return np.ascontiguousarray(out.reshape(2, SEQ, D).astype(np.float32))
```

```python
import numpy as np
import concourse.bass as bass
import concourse.mybir as mybir
from concourse.bass_utils import run_bass_kernel_spmd

F32 = mybir.dt.float32
BF16 = mybir.dt.bfloat16
AF = mybir.ActivationFunctionType
ALU = mybir.AluOpType
AX = mybir.AxisListType

D = 1024
DFF = 2816
NCORE = 8
TOK = 2048
EPS = 1e-6


class Buf:
    __slots__ = ("name", "lw", "rd", "dsem", "lw_dma")

    def __init__(self, name):
        self.name = name
        self.lw = None
        self.rd = []
        self.dsem = None
        self.lw_dma = False


class Sem:
    __slots__ = ("h", "total", "name")

    def __init__(self, h, name):
        self.h = h
        self.total = 0
        self.name = name


class K:
    def __init__(self, nc):
        self.nc = nc
        self.engs = {"pe": nc.tensor, "act": nc.scalar, "dve": nc.vector,
                     "pool": nc.gpsimd, "sp": nc.sync}
        self.esem = {n: Sem(nc.alloc_semaphore(name="es_" + n), n) for n in self.engs}
        self.waited = {n: {} for n in self.engs}
        self.nsem = 0
        self.ninstr = 0
        self.nwait = 0
        self.rings = {}

    def begin_phase(self):
        import contextlib
        self.phase = getattr(self, "phase", 0) + 1
        self.stack = contextlib.ExitStack()
        self.rings = {}
        self.dfree = getattr(self, "dfree", [])

    def end_phase(self):
        self.barrier()
        self.stack.close()
        self.stack = None
        self.dfree = list(self.dused)
        self.dused = []

    def barrier(self):
        sems = list(self.esem.values()) + list(getattr(self, "dall", []))
        for eng in self.engs:
            needs = {s_: s_.total for s_ in sems if s_.total > 0}
            self._emit_waits(eng, needs)

    def sb(self, name, shape, dt):
        if getattr(self, "stack", None) is not None:
            return self.stack.enter_context(self.nc.sbuf_tensor("s%d_%s" % (self.phase, name), list(shape), dt)).ap()
        return self._sb_static(name, shape, dt)

    def _sb_static(self, name, shape, dt):
        return self.nc.alloc_sbuf_tensor("s_" + name, list(shape), dt).ap()

    def ps(self, name, shape, dt=F32):
        if getattr(self, "stack", None) is not None:
            return self.stack.enter_context(self.nc.psum_tensor("p%d_%s" % (self.phase, name), list(shape), dt)).ap()
        return self.nc.alloc_psum_tensor("p_" + name, list(shape), dt).ap()

    def tile(self, name, shape, dt):
        return self.sb(name, shape, dt), Buf(name)

    def ring(self, name, shape, dt, n, psum=False):
        if name not in self.rings:
            sl = []
            for i in range(n):
                nm = "%s_%d" % (name, i)
                ap = self.ps(nm, shape, dt) if psum else self.sb(nm, shape, dt)
                sl.append((ap, Buf(nm)))
            self.rings[name] = [sl, 0]
        r = self.rings[name]
        s = r[0][r[1] % len(r[0])]
        r[1] += 1
        return s

    def _dsem(self, b, q="sp"):
        if b.dsem is None:
            if not hasattr(self, "dall"):
                self.dall, self.dused, self.dfree = [], [], getattr(self, "dfree", [])
            if self.dfree and q != "pool":
                b.dsem = self.dfree.pop()
            else:
                b.dsem = Sem(self.nc.alloc_semaphore(name="ds%d" % self.nsem), b.name)
                self.nsem += 1
                self.dall.append(b.dsem)
            if q != "pool":
                self.dused.append(b.dsem)
        return b.dsem

    def _need(self, eng, tok, needs):
        if tok is None:
            return
        s, v = tok
        if v is None:
            v = s.total
        if eng == "pe" and s is self.esem["pe"]:
            return
        if v > needs.get(s, 0):
            needs[s] = v

    def _emit_waits(self, eng, needs):
        e = self.engs[eng]
        w = self.waited[eng]
        for s, v in needs.items():
            if w.get(s, 0) >= v:
                continue
            e.wait_ge(s.h, v)
            self.nwait += 1
            w[s] = v

    def op(self, eng, fn, r=(), w=(), inc=True):
        needs = {}
        for b in r:
            self._need(eng, b.lw, needs)
        for b in w:
            self._need(eng, b.lw, needs)
            for t in b.rd:
                self._need(eng, t, needs)
        self._emit_waits(eng, needs)
        ins = fn(self.engs[eng])
        s = self.esem[eng]
        if inc:
            s.total += 1
            ins.then_inc(s.h, 1)
            tok = (s, s.total)
        else:
            tok = (s, s.total + 1)
        for b in r:
            b.rd.append(tok)
            if len(b.rd) > 24:
                b.rd = self._compact(b.rd)
        for b in w:
            b.lw = tok
            b.rd = []
            b.lw_dma = False
        self.ninstr += 1
        return ins

    def _compact(self, toks):
        best = {}
        for s, v in toks:
            if v is None:
                best[s] = None
            elif s not in best or (best[s] is not None and v > best[s]):
                best[s] = v
        return [(s, v) for s, v in best.items()]

    def dma(self, q, out, in_, r=(), w=(), sb=None, **kw):
        s = self._dsem(sb, q)
        needs = {}
        for b in r:
            self._need(q, b.lw, needs)
        for b in w:
            if not (b.lw_dma and b.lw is not None and b.lw[0] is s and not b.rd):
                self._need(q, b.lw, needs)
            for t in b.rd:
                self._need(q, t, needs)
        self._emit_waits(q, needs)
        ins = self.engs[q].dma_start(out=out, in_=in_, **kw)
        s.total += 16
        ins.then_inc(s.h, 16)
        tok = (s, None)
        for b in r:
            b.rd.append(tok)
        for b in w:
            b.lw = tok
            b.rd = []
            b.lw_dma = True
        self.ninstr += 1
        return ins

    def wait_all(self, eng, bufs):
        needs = {}
        for b in bufs:
            self._need(eng, b.lw, needs)
            for t in b.rd:
                self._need(eng, t, needs)
        self._emit_waits(eng, needs)


class Ctx:
    def __init__(self, nc, k=None):
        self.nc = nc
        self.k = k if k is not None else K(nc)
        k = self.k
        self.ones_d, self.b_ones_d = k.tile("ones_d", [128, 128], F32)
        k.op("pool", lambda e: e.memset(self.ones_d, 1.0 / D), w=[self.b_ones_d])
        self.ident, self.b_ident = k.tile("ident", [128, 128], F32)
        k.op("pool", lambda e: e.memset(self.ident, 1.0), w=[self.b_ident])
        k.op("pool", lambda e: e.affine_select(out=self.ident, in_=self.ident, pattern=[[-1, 128]],
                                               compare_op=ALU.is_equal, fill=0.0, base=0,
                                               channel_multiplier=1), r=[self.b_ident], w=[self.b_ident])
        self.dq = 0
        self.consts, self.b_consts = k.tile("consts", [128, 16], F32)
        self.cvals = {}

    def const(self, v):
        if v not in self.cvals:
            i = len(self.cvals)
            self.cvals[v] = i
            self.k.op("pool", lambda e: e.memset(self.consts[:, i:i + 1], float(v)), w=[self.b_consts])
        i = self.cvals[v]
        return self.consts[:, i:i + 1]

    def psum(self):
        return self.k.ring("psum", [128, 512], F32, 8, psum=True)

    def wload(self, dram_ap, shape, alt=False):
        k = self.k
        ap, b = k.ring("wring", [128, 5632], BF16, getattr(self, "wring_n", 3))
        n = shape[1] * shape[2]
        v = ap[:, 0:n].rearrange("p (a b) -> p a b", a=shape[1])
        if alt and n <= 2048:
            st, stb = k.ring("wstg", [128, 2048], F32, getattr(self, "wstg_n", 2))
            sv = st[:, 0:n].rearrange("p (a b) -> p a b", a=shape[1])
            k.dma("sp", sv, dram_ap, w=[stb], sb=stb)
            k.op("dve", lambda e: e.tensor_copy(out=v, in_=sv), r=[stb], w=[b])
        else:
            k.dma("pool", v, dram_ap, w=[b], sb=b)
        return v, b


class IO:
    def __init__(self, nc, pre="", ext=None):
        self.nc, self.pre, self.ext = nc, pre, dict(ext or {})

    def inp(self, name, shape, dt=F32):
        if name in self.ext:
            return self.ext[name]
        return self.nc.dram_tensor(self.pre + name, list(shape), dt, kind="ExternalInput").ap()

    def out(self, name, shape, dt=F32):
        if name in self.ext:
            return self.ext[name]
        return self.nc.dram_tensor(self.pre + name, list(shape), dt, kind="ExternalOutput").ap()


def _std(nc, cx, io):
    if nc is None:
        nc = bass.Bass("TRN2", target_bir_lowering=False)
    if cx is None:
        cx = Ctx(nc)
    if io is None:
        io = IO(nc)
    return nc, cx, io


def hall_tile(hall, t0, n):
    r, col = t0 // TOK, t0 % TOK
    v = hall.rearrange("(i r kk p) t -> r p i kk t", i=4, r=4, kk=2, p=128)
    return v[r][:, :, :, col:col + n]


def dma_hall(k, dst, hall, t0, n, buf, **kw):
    src = hall_tile(hall, t0, n)
    for i in range(4):
        k.dma("sp", dst[:, 2 * i:2 * i + 2, :], src[:, i], w=[buf], sb=buf, **kw)


def rms_rstd(cx, x_ap, xb, eps=EPS):
    k = cx.k
    T = x_ap.shape[2]
    sq, sqb = k.ring("sq", [128, 8, 512], F32, 1)
    k.op("act", lambda e: e.activation(out=sq[:, :, 0:T], in_=x_ap, func=AF.Square), r=[xb], w=[sqb])
    ps, pb = cx.psum()
    for c in range(8):
        k.op("pe", lambda e: e.matmul(ps[:, 0:T], lhsT=cx.ones_d, rhs=sq[:, c, 0:T], start=(c == 0), stop=(c == 7)),
             r=[sqb, cx.b_ones_d], w=[pb], inc=(c == 7))
    rs, rsb = k.ring("rstd", [128, 512], F32, 2)
    c_eps = cx.const(eps)
    k.op("act", lambda e: e.activation(out=rs[:, 0:T], in_=ps[:, 0:T], func=AF.Sqrt, bias=c_eps, scale=1.0),
         r=[pb, cx.b_consts], w=[rsb])
    k.op("dve", lambda e: e.reciprocal(out=rs[:, 0:T], in_=rs[:, 0:T]), r=[rsb], w=[rsb])
    return rs[:, 0:T], rsb


def norm_bf16(cx, x_ap, xb, g_ap, gb, out_ap, outb):
    k = cx.k
    rs, rsb = rms_rstd(cx, x_ap, xb)
    for c in range(8):
        eng = "dve"
        k.op(eng, lambda e: e.scalar_tensor_tensor(out=out_ap[:, c, :], in0=x_ap[:, c, :], scalar=g_ap[:, c:c + 1],
                                                   in1=rs, op0=ALU.mult, op1=ALU.mult),
             r=[xb, gb, rsb], w=[outb])


def ffn_half(cx, x, xb, NT, wg, wu, wd, g_in, g_out, gb):
    k = cx.k
    T = NT * 512
    h, hb = k.ring("h_bf", [128, 8, 1024], BF16, 1)
    act, actb = k.ring("act_bf", [128, 22, 1024], BF16, 1)
    for tt in range(NT):
        sl = slice(tt * 512, (tt + 1) * 512)
        norm_bf16(cx, x[:, :, sl], xb, g_in, gb, h[:, :, sl], hb)
    wgv = wg.rearrange("(kc p) f -> p kc f", p=128)
    wuv = wu.rearrange("(kc p) f -> p kc f", p=128)
    for fb in range(11):
        gw, gwb = cx.wload(wgv[:, :, fb * 256:(fb + 1) * 256], [128, 8, 256])
        uw, uwb = cx.wload(wuv[:, :, fb * 256:(fb + 1) * 256], [128, 8, 256], alt=True)
        for tt in range(NT):
            sl = slice(tt * 512, (tt + 1) * 512)
            for fc in range(2):
                f = fb * 2 + fc
                pg, pgb = cx.psum()
                pu, pub = cx.psum()
                for c in range(8):
                    k.op("pe", lambda e: e.matmul(pg, lhsT=gw[:, c, fc * 128:(fc + 1) * 128], rhs=h[:, c, sl],
                                                  start=(c == 0), stop=(c == 7)), r=[gwb, hb], w=[pgb], inc=(c == 7))
                for c in range(8):
                    k.op("pe", lambda e: e.matmul(pu, lhsT=uw[:, c, fc * 128:(fc + 1) * 128], rhs=h[:, c, sl],
                                                  start=(c == 0), stop=(c == 7)), r=[uwb, hb], w=[pub], inc=(c == 7))
                sg, sgb = k.ring("sg", [128, 512], F32, 3)
                k.op("act", lambda e: e.activation(out=sg, in_=pg, func=AF.Silu), r=[pgb], w=[sgb])
                k.op("dve", lambda e: e.tensor_tensor(out=act[:, f, sl], in0=sg, in1=pu, op=ALU.mult),
                     r=[sgb, pub], w=[actb])
    wdv = wd.rearrange("(fc p) d -> p fc d", p=128)
    o, ob = k.ring("ffn_o", [128, 8, 1024], F32, 1)
    for db in range(4):
        dw, dwb = cx.wload(wdv[:, :, db * 256:(db + 1) * 256], [128, 22, 256])
        for tt in range(NT):
            sl = slice(tt * 512, (tt + 1) * 512)
            for dc in range(2):
                d = db * 2 + dc
                po, pob = cx.psum()
                for f in range(22):
                    k.op("pe", lambda e: e.matmul(po, lhsT=dw[:, f, dc * 128:(dc + 1) * 128], rhs=act[:, f, sl],
                                                  start=(f == 0), stop=(f == 21)), r=[dwb, actb], w=[pob], inc=(f == 21))
                k.op("act", lambda e: e.activation(out=o[:, d, sl], in_=po, func=AF.Copy), r=[pob], w=[ob])
    for tt in range(NT):
        sl = slice(tt * 512, (tt + 1) * 512)
        post_norm_add(cx, x[:, :, sl], xb, o[:, :, sl], ob, g_out, gb, 0.5)


def post_norm_add(cx, x_ap, xb, o_ap, ob, g_ap, gb, coef):
    k = cx.k
    rs, rsb = rms_rstd(cx, o_ap, ob)
    for c in range(8):
        eng = "dve"
        tmp, tb = k.ring("pn_tmp" + eng, [128, 512], F32, 2)
        T = o_ap.shape[2]
        k.op(eng, lambda e: e.scalar_tensor_tensor(out=tmp[:, 0:T], in0=o_ap[:, c, :], scalar=g_ap[:, c:c + 1], in1=rs,
                                                   op0=ALU.mult, op1=ALU.mult), r=[ob, gb, rsb], w=[tb])
        k.op(eng, lambda e: e.scalar_tensor_tensor(out=x_ap[:, c, :], in0=tmp[:, 0:T], scalar=float(coef), in1=x_ap[:, c, :],
                                                   op0=ALU.mult, op1=ALU.add), r=[tb, xb], w=[xb])


def load_gains(cx, gains_dram):
    k = cx.k
    g, gb = k.tile("gains", [128, 12, 8], F32)
    k.dma("sp", g, gains_dram, w=[gb], sb=gb)
    return g, gb


def build_l1(nc=None, cx=None, io=None):
    nc, cx, io = _std(nc, cx, io)
    xT = io.inp("xT", [128, 8, TOK]); gains = io.inp("gains", [128, 12, 8])
    wg = io.inp("wg", [D, DFF]); wu = io.inp("wu", [D, DFF]); wd = io.inp("wd", [DFF, D])
    x1T = io.out("x1T", [128, 8, TOK], F32)
    h0T = io.out("h0T", [1024, TOK], BF16).rearrange("(kc p) t -> p kc t", p=128)
    k = cx.k
    g, gb = load_gains(cx, gains)
    outs = []
    for half in range(2):
        hs = slice(half * 1024, (half + 1) * 1024)
        x, xb = k.ring("x_res", [128, 8, 1024], F32, 1)
        k.dma("sp", x, xT[:, :, hs], w=[xb], sb=xb)
        ffn_half(cx, x, xb, 2, wg, wu, wd, g[:, 0, :], g[:, 1, :], gb)
        k.dma("sp", x1T[:, :, hs], x, r=[xb], sb=xb)
        hh, hhb = k.ring("h_bf", [128, 8, 1024], BF16, 1)
        for tt in range(2):
            sl = slice(tt * 512, (tt + 1) * 512)
            norm_bf16(cx, x[:, :, sl], xb, g[:, 2, :], gb, hh[:, :, sl], hhb)
        k.dma("sp", h0T[:, :, hs], hh, r=[hhb], sb=hhb)
        outs += [xb, hhb]
    k.wait_all("sp", outs)
    return nc


def fm(a):
    t = a.shape[0]
    return np.ascontiguousarray(a.T.reshape(8, 128, t).transpose(1, 0, 2))


def unfm(a):
    t = a.shape[2]
    return np.ascontiguousarray(a.transpose(1, 0, 2).reshape(1024, t).T)


def gains_layout(norm_gains):
    g = norm_gains.reshape(12, 8, 128).transpose(2, 0, 1)
    return np.ascontiguousarray(g)


SEQ = 8192
RW_EXPC = 0.6065306597126334
GN_EPS = 64e-5


def mm(k, ps_ap, pairs, rbufs, wbuf):
    n = len(pairs)
    for i, (l, r) in enumerate(pairs):
        k.op("pe", lambda e: e.matmul(ps_ap, lhsT=l, rhs=r, start=(i == 0), stop=(i == n - 1)),
             r=rbufs, w=[wbuf], inc=(i == n - 1))


class _Stop(Exception):
    pass


def build_l4(ntiles=16, stop=99, nc=None, cx=None, io=None):
    try:
        return _build_l4(ntiles, stop, nc, cx, io)
    except _Stop as e:
        return e.args[0]


def _build_l4(ntiles=16, stop=99, nc=None, cx=None, io=None):
    nc, cx, io = _std(nc, cx, io)
    dt_in = io.inp
    hall = dt_in("hall", [4096, TOK], BF16)
    hzero = dt_in("hzero", [1024, 1], BF16)
    W = {"r": dt_in("wr", [D, 256]), "k": dt_in("wk", [D, 256]), "v": dt_in("wv", [D, 256]),
         "w1": dt_in("w1", [D, 64]), "a1": dt_in("a1", [D, 64]), "g1": dt_in("g1", [D, 160])}
    w2d = dt_in("w2", [64, 256]); a2d = dt_in("a2", [64, 256]); g2d = dt_in("g2", [160, 256])
    mud = dt_in("mu", [128, 6, 8])
    vecd = dt_in("vecs", [128, 7, 2])
    maskd = dt_in("masks", [128, 3, 128])
    bonesd = dt_in("bones", [128, 128])
    resetd = dt_in("resetm", [128, 512])
    ygq = io.out("ygT", [1024, TOK], BF16).rearrange("(j c p) t -> j p c t", j=4, c=2, p=128)
    k = cx.k
    PS = lambda: k.ring("psr", [128, 512], F32, 4, psum=True)
    ybank = [k.ps("ybank%d" % i, [128, 512]) for i in range(2)]
    ybb = [Buf("ybank%d" % i) for i in range(2)]
    ybank2 = [k.ps("ybankb%d" % i, [128, 512]) for i in range(2)]
    ybb2 = [Buf("ybankb%d" % i) for i in range(2)]

    mu, mub = k.tile("mu", [128, 6, 8], F32); k.dma("sp", mu, mud, w=[mub], sb=mub)
    vec, vecb = k.tile("vecs", [128, 7, 2], F32); k.dma("sp", vec, vecd, w=[vecb], sb=vecb)
    msk, mskb = k.tile("masks", [128, 3, 128], F32); k.dma("sp", msk, maskd, w=[mskb], sb=mskb)
    bones, bonesb = k.tile("bones", [128, 128], F32); k.dma("sp", bones, bonesd, w=[bonesb], sb=bonesb)
    rstm, rstmb = k.tile("resetm", [128, 512], F32); k.dma("sp", rstm, resetd, w=[rstmb], sb=rstmb)
    W0, A0, KK, KA, RK, LNG, LNB = range(7)
    m4 = lambda i: msk[:, i, :].unsqueeze(1).to_broadcast([128, 4, 128])
    id4 = cx.ident.unsqueeze(1).to_broadcast([128, 4, 128])
    c_tiny = cx.const(1e-24)
    c_gneps = cx.const(GN_EPS)

    order = {"r": 0, "w1": 1, "k": 2, "v": 3, "a1": 4, "g1": 5}
    Wa, Wb, Wbuf = {}, {}, {}
    for nm, wd_ in W.items():
        n = wd_.shape[1]
        wa, wab = k.tile("wa_" + nm, [128, 8, n], BF16)
        wb, wbb = k.tile("wb_" + nm, [128, 8, n], BF16)
        Wa[nm], Wb[nm], Wbuf[nm] = wa, wb, [wab, wbb]
    import contextlib
    _outer = getattr(k, "stack", None)
    k.stack = contextlib.ExitStack()
    k.phase = getattr(k, "phase", 0)
    for nm, wd_ in W.items():
        n = wd_.shape[1]
        wa, wb = Wa[nm], Wb[nm]
        wab, wbb = Wbuf[nm]
        st, stb = k.ring("wstage", [128, 8, 256], F32, 1)
        k.dma("sp", st[:, :, 0:n], wd_.rearrange("(kc p) n -> p kc n", p=128), w=[stb], sb=stb)
        tmp, tmpb = k.ring("wstage2", [128, 8, 256], F32, 1)
        i = order[nm]
        k.op("dve", lambda e: e.tensor_tensor(out=tmp[:, :, 0:n], in0=st[:, :, 0:n],
                                              in1=mu[:, i, :].unsqueeze(2).to_broadcast([128, 8, n]), op=ALU.mult),
             r=[stb, mub], w=[tmpb])
        k.op("dve", lambda e: e.tensor_tensor(out=wa, in0=st[:, :, 0:n], in1=tmp[:, :, 0:n], op=ALU.subtract),
             r=[stb, tmpb], w=[wab])
        k.op("act", lambda e: e.activation(out=wb, in_=tmp[:, :, 0:n], func=AF.Copy), r=[tmpb], w=[wbb])
    k.barrier()
    k.stack.close()
    k.stack = _outer
    k.rings.pop("wstage"); k.rings.pop("wstage2")
    w2, w2b = k.tile("w2", [64, 256], BF16); k.dma("pool", w2, w2d, w=[w2b], sb=w2b)
    a2, a2b = k.tile("a2", [64, 256], BF16); k.dma("pool", a2, a2d, w=[a2b], sb=a2b)
    g2a, g2ab = k.tile("g2a", [128, 256], BF16); k.dma("pool", g2a, g2d[0:128, :], w=[g2ab], sb=g2ab)
    g2c, g2cb = k.tile("g2c", [32, 256], BF16); k.dma("pool", g2c, g2d[128:160, :], w=[g2cb], sb=g2cb)

    U = []
    for hd in range(4):
        pp = []
        for j in range(2):
            u, ub = k.tile("U%d_%d" % (hd, j), [128, 64], F32)
            k.op("pool", lambda e: e.memset(u, 0.0), w=[ub])
            pp.append((u, ub))
        U.append(pp)
    ucur = [0, 0, 0, 0]
    PL = []
    for pc in range(2):
        p_, pb_ = k.tile("PL%d" % pc, [128, 9], F32)
        k.op("pool", lambda e: e.memset(p_, 1.0), w=[pb_])
        PL.append((p_, pb_))
    outbufs = []
    if stop == 1:
        raise _Stop(nc)

    for ti in range(ntiles):
        t0 = ti * 512
        hb, hbb = k.ring("hb", [128, 8, 514], BF16, 2)
        dma_hall(k, hb[:, :, 1:513], hall, t0, 512, hbb)
        if t0 == 0:
            k.dma("sp", hb[:, :, 0:1], hzero.rearrange("(kc p) t -> p kc t", p=128), w=[hbb], sb=hbb, allow_slow_non_contiguous=True)
        else:
            dma_hall(k, hb[:, :, 0:1], hall, t0 - 1, 1, hbb, allow_slow_non_contiguous=True)

        def proj_pairs(nm, cols, tok=None):
            prs = []
            for kc in range(8):
                prs.append((Wa[nm][:, kc, cols], hb[:, kc, 1:513]))
                prs.append((Wb[nm][:, kc, cols], hb[:, kc, 0:512]))
            return prs

        FM = {}

        def evac(name, ps, pb, rows=128, func=AF.Copy, bias=None, dt=F32, extra=()):
            o, ob = k.ring("fm_" + name, [128, 512], dt, 1)
            kw = {}
            if bias is not None:
                kw["bias"] = bias
            k.op("act", lambda e: e.activation(out=o[0:rows, :], in_=ps[0:rows, :], func=func, **kw),
                 r=[pb] + list(extra), w=[ob])
            return o, ob

        for pc in range(2):
            cols = slice(pc * 128, (pc + 1) * 128)
            for nm in ("r", "k", "v"):
                ps, pb = PS()
                mm(k, ps, proj_pairs(nm, cols), [hbb] + Wbuf[nm], pb)
                FM[(nm, pc)] = evac("%s%d" % (nm, pc), ps, pb)
        vtok, vtokb = k.ring("vtok", [128, 4, 256], F32, 2)
        for half in range(2):
            ps, pb = PS()
            for bi in range(2):
                blk = half * 2 + bi
                prs = []
                for kc in range(8):
                    prs.append((hb[:, kc, 1 + blk * 128:1 + (blk + 1) * 128], Wa["v"][:, kc, :]))
                    prs.append((hb[:, kc, blk * 128:(blk + 1) * 128], Wb["v"][:, kc, :]))
                mm(k, ps[:, bi * 256:(bi + 1) * 256], prs, [hbb] + Wbuf["v"], pb)
            k.op("act", lambda e: e.activation(out=vtok[:, half * 2:half * 2 + 2, :],
                                               in_=ps.rearrange("p (a b) -> p a b", a=2), func=AF.Copy),
                 r=[pb], w=[vtokb])
        ps, pb = PS(); mm(k, ps[0:64, :], proj_pairs("w1", slice(0, 64)), [hbb] + Wbuf["w1"], pb)
        hw, hwb = evac("hw", ps, pb, rows=64, func=AF.Tanh, dt=BF16)
        ps, pb = PS(); mm(k, ps[0:64, :], proj_pairs("a1", slice(0, 64)), [hbb] + Wbuf["a1"], pb)
        ha, hab = evac("ha", ps, pb, rows=64, dt=BF16)
        ps, pb = PS(); mm(k, ps, proj_pairs("g1", slice(0, 128)), [hbb] + Wbuf["g1"], pb)
        hg0, hg0b = evac("hg0", ps, pb, func=AF.Sigmoid, dt=BF16)
        ps, pb = PS(); mm(k, ps[0:32, :], proj_pairs("g1", slice(128, 160)), [hbb] + Wbuf["g1"], pb)
        hg1, hg1b = evac("hg1", ps, pb, rows=32, func=AF.Sigmoid, dt=BF16)
        for pc in range(2):
            cols = slice(pc * 128, (pc + 1) * 128)
            ps, pb = PS(); mm(k, ps, [(w2[:, cols], hw[0:64, :])], [w2b, hwb], pb)
            FM[("sgw", pc)] = evac("sgw%d" % pc, ps, pb, func=AF.Sigmoid, bias=vec[:, W0, pc:pc + 1], extra=[vecb])
            ps, pb = PS(); mm(k, ps, [(a2[:, cols], ha[0:64, :])], [a2b, hab], pb)
            FM[("a", pc)] = evac("a%d" % pc, ps, pb, func=AF.Sigmoid, bias=vec[:, A0, pc:pc + 1], extra=[vecb])
            ps, pb = PS(); mm(k, ps, [(g2a[:, cols], hg0), (g2c[:, cols], hg1[0:32, :])], [g2ab, g2cb, hg0b, hg1b], pb)
            FM[("g", pc)] = evac("g%d" % pc, ps, pb)

        if stop == 2:
            raise _Stop(nc)
        def tmpt(name, dt=F32, n=2):
            return k.ring("tmp", [128, 512], F32, 9)

        PR = {}
        for pc in range(2):
            r_, rb_ = FM[("r", pc)]; k_, kb_ = FM[("k", pc)]; a_, ab_ = FM[("a", pc)]; sg_, sgb_ = FM[("sgw", pc)]
            kk0, kk0b = tmpt("kk0")
            k.op("dve", lambda e: e.tensor_scalar(out=kk0, in0=k_, scalar1=vec[:, KK, pc:pc + 1], scalar2=None, op0=ALU.mult),
                 r=[kb_, vecb], w=[kk0b])
            sq, sqb = tmpt("sq")
            k.op("pool", lambda e: e.tensor_tensor(out=sq, in0=kk0, in1=kk0, op=ALU.mult), r=[kk0b], w=[sqb])
            ps, pb = PS(); mm(k, ps, [(bones, sq)], [bonesb, sqb], pb)
            rn, rnb = tmpt("rn")
            k.op("act", lambda e: e.activation(out=rn, in_=ps, func=AF.Sqrt, bias=c_tiny, scale=1.0), r=[pb, cx.b_consts], w=[rnb])
            k.op("dve", lambda e: e.reciprocal(out=rn, in_=rn), r=[rnb], w=[rnb])
            kap, kapb = tmpt("kap")
            k.op("dve", lambda e: e.tensor_tensor(out=kap, in0=kk0, in1=rn, op=ALU.mult), r=[kk0b, rnb], w=[kapb])
            am, amb = tmpt("am")
            k.op("dve", lambda e: e.tensor_scalar(out=am, in0=a_, scalar1=-1.0, scalar2=vec[:, KA, pc:pc + 1],
                                                  op0=ALU.add, op1=ALU.mult), r=[ab_, vecb], w=[amb])
            kp, kpb = k.ring("kp%d" % pc, [128, 512], F32, 1)
            k.op("dve", lambda e: e.scalar_tensor_tensor(out=kp, in0=am, scalar=1.0, in1=k_, op0=ALU.add, op1=ALU.mult),
                 r=[amb, kb_], w=[kpb])
            logd, logdb = tmpt("logd")
            k.op("pool", lambda e: e.tensor_scalar(out=logd, in0=sg_, scalar1=-RW_EXPC, scalar2=None, op0=ALU.mult),
                 r=[sgb_], w=[logdb])
            Lc, Lcb = tmpt("Lc")
            k.op("dve", lambda e: e.tensor_tensor_scan(out=Lc, data0=rstm, data1=logd, initial=0.0, op0=ALU.mult, op1=ALU.add),
                 r=[rstmb, logdb], w=[Lcb])
            Lm, Lmb = tmpt("Lm")
            k.op("pool", lambda e: e.tensor_tensor(out=Lm, in0=Lc, in1=logd, op=ALU.subtract), r=[Lcb, logdb], w=[Lmb])
            P_, Pb_ = tmpt("P"); Pi, Pib = tmpt("Pi"); Pp, Ppb = tmpt("Pp")
            k.op("act", lambda e: e.activation(out=P_, in_=Lc, func=AF.Exp), r=[Lcb], w=[Pb_])
            k.op("act", lambda e: e.activation(out=Pi, in_=Lc, func=AF.Exp, scale=-1.0), r=[Lcb], w=[Pib])
            k.op("act", lambda e: e.activation(out=Pp, in_=Lm, func=AF.Exp), r=[Lmb], w=[Ppb])
            pl, plb = PL[pc]
            k.op("dve", lambda e: e.tensor_copy(out=pl[:, 0:1], in_=pl[:, 8:9]), r=[plb], w=[plb])
            k.op("dve", lambda e: e.tensor_copy(out=pl[:, 1:9], in_=P_.rearrange("p (c t) -> p c t", t=64)[:, :, 63]),
                 r=[Pb_, plb], w=[plb])
            rt, rtb = k.ring("rt%d" % pc, [128, 512], F32, 1)
            kt, ktb = k.ring("kt%d" % pc, [128, 512], F32, 1)
            bt, btb = k.ring("bt%d" % pc, [128, 512], F32, 1)
            kkt, kktb = k.ring("kkt%d" % pc, [128, 512], F32, 1)
            k.op("dve", lambda e: e.tensor_tensor(out=rt, in0=r_, in1=P_, op=ALU.mult), r=[rb_, Pb_], w=[rtb])
            k.op("pool", lambda e: e.tensor_tensor(out=kt, in0=kap, in1=Pp, op=ALU.mult), r=[kapb, Ppb], w=[ktb])
            ka_, kab_ = tmpt("ka")
            k.op("pool", lambda e: e.tensor_tensor(out=ka_, in0=kap, in1=a_, op=ALU.mult), r=[kapb, ab_], w=[kab_])
            k.op("dve", lambda e: e.tensor_tensor(out=bt, in0=ka_, in1=Pi, op=ALU.mult), r=[kab_, Pib], w=[btb])
            k.op("pool", lambda e: e.tensor_tensor(out=kkt, in0=kp, in1=Pi, op=ALU.mult), r=[kpb, Pib], w=[kktb])
            rts, rtsb = k.ring("rts%d" % pc, [128, 512], F32, 1)
            kts, ktsb = k.ring("kts%d" % pc, [128, 512], F32, 1)
            plbc = pl[:, 0:8].unsqueeze(2).to_broadcast([128, 8, 64])
            k.op("dve", lambda e: e.tensor_tensor(out=rts.rearrange("p (c t) -> p c t", t=64),
                                                  in0=rt.rearrange("p (c t) -> p c t", t=64), in1=plbc, op=ALU.mult),
                 r=[rtb, plb], w=[rtsb])
            k.op("dve", lambda e: e.tensor_tensor(out=kts.rearrange("p (c t) -> p c t", t=64),
                                                  in0=kt.rearrange("p (c t) -> p c t", t=64), in1=plbc, op=ALU.mult),
                 r=[ktb, plb], w=[ktsb])
            PR[pc] = dict(rt=(rt, rtb), kt=(kt, ktb), bt=(bt, btb), kkt=(kkt, kktb), rts=(rts, rtsb), kts=(kts, ktsb),
                          kp=(kp, kpb))
        if stop == 3:
            raise _Stop(nc)
        for pc in range(2):
            for nm_ in ("rt", "kt", "bt", "kkt"):
                src_, srcb_ = PR[pc][nm_]
                sh, shb = k.ring("sh_%s%d" % (nm_, pc), [128, 512], BF16, 1)
                k.op("act", lambda e: e.activation(out=sh, in_=src_, func=AF.Copy), r=[srcb_], w=[shb])
                PR[pc][nm_ + "_h"] = (sh, shb)
        btok, btokb = k.ring("btok", [128, 4, 256], F32, 1)
        ktok, ktokb = k.ring("ktok", [128, 4, 256], F32, 1)
        for (src, dst, dstb) in (("bt", btok, btokb), ("kkt", ktok, ktokb)):
            for pc in range(2):
                s_, sb_ = PR[pc][src]
                ps, pb = PS()
                for blk in range(4):
                    k.op("pe", lambda e: e.transpose(out=ps[:, blk * 128:(blk + 1) * 128], in_=s_[:, blk * 128:(blk + 1) * 128],
                                                     identity=cx.ident), r=[sb_, cx.b_ident], w=[pb], inc=(blk == 3))
                k.op("act", lambda e: e.activation(out=dst[:, :, pc * 128:(pc + 1) * 128],
                                                   in_=ps.rearrange("p (a b) -> p a b", a=4), func=AF.Copy), r=[pb], w=[dstb])

        if stop == 4:
            raise _Stop(nc)
        HD = {}
        for hd in range(4):
            pc, hp = hd // 2, hd % 2
            rows = slice(hp * 64, hp * 64 + 64)
            hcols = slice(hd * 64, hd * 64 + 64)
            pr = PR[pc]

            def intra(lname, rname, mi, nm, depth=2, odt=F32):
                l_, lb_ = pr[lname + "_h"]; r2, rb2 = pr[rname + "_h"]
                ps, pb = PS()
                for blk in range(4):
                    bs = slice(blk * 128, (blk + 1) * 128)
                    mm(k, ps[:, bs], [(l_[rows, bs], r2[rows, bs])], [lb_, rb2], pb)
                o, ob = k.ring("im_" + nm, [128, 4, 128], odt, depth)
                k.op("dve", lambda e: e.tensor_tensor(out=o, in0=ps.rearrange("p (a b) -> p a b", a=4), in1=m4(mi), op=ALU.mult),
                     r=[pb, mskb], w=[ob])
                return o, ob

            Pm, Pmb = intra("kt", "bt", 0, "P", 2, BF16)
            Qm, Qmb = intra("bt", "kt", 1, "Q", 2, BF16)
            AkT, AkTb = intra("kkt", "kt", 1, "AkT%d" % hd, 1)
            QBT, QBTb = intra("bt", "rt", 2, "QBT%d" % hd, 1)
            QKT, QKTb = intra("kkt", "rt", 2, "QKT%d" % hd, 1)
            Rm, Rmb = k.ring("im_R", [128, 4, 128], F32, 2)
            k.op("pool", lambda e: e.tensor_tensor(out=Rm, in0=id4, in1=Qm, op=ALU.subtract), r=[cx.b_ident, Qmb], w=[Rmb])
            Rh, Rhb = k.ring("im_Rh", [128, 4, 128], BF16, 2)
            k.op("act", lambda e: e.activation(out=Rh, in_=Rm, func=AF.Copy), r=[Rmb], w=[Rhb])
            for lev in range(1, 6):
                if lev < 5:
                    ps, pb = PS()
                    for blk in range(4):
                        bs = slice(blk * 128, (blk + 1) * 128)
                        mm(k, ps[:, bs], [(Pm[:, blk, :], Qm[:, blk, :])], [Pmb, Qmb], pb)
                    Qn, Qnb = k.ring("im_Q", [128, 4, 128], BF16, 2)
                    k.op("act", lambda e: e.activation(out=Qn, in_=ps.rearrange("p (a b) -> p a b", a=4), func=AF.Copy), r=[pb], w=[Qnb])
                ps, pb = PS()
                for blk in range(4):
                    bs = slice(blk * 128, (blk + 1) * 128)
                    mm(k, ps[:, bs], [(Qm[:, blk, :], Pm[:, blk, :])], [Pmb, Qmb], pb)
                Pn, Pnb = k.ring("im_P", [128, 4, 128], BF16, 2)
                k.op("act", lambda e: e.activation(out=Pn, in_=ps.rearrange("p (a b) -> p a b", a=4), func=AF.Copy), r=[pb], w=[Pnb])
                ps, pb = PS()
                for blk in range(4):
                    bs = slice(blk * 128, (blk + 1) * 128)
                    mm(k, ps[:, bs], [(Pn[:, blk, :], Rh[:, blk, :])], [Pnb, Rhb], pb)
                Rn, Rnb = k.ring("im_R", [128, 4, 128], F32, 2) if lev < 5 else k.ring("im_Rfin%d" % hd, [128, 4, 128], F32, 1)
                k.op("dve", lambda e: e.tensor_tensor(out=Rn, in0=ps.rearrange("p (a b) -> p a b", a=4), in1=Rm, op=ALU.add),
                     r=[pb, Rmb], w=[Rnb])
                Pm, Pmb = Pn, Pnb
                if lev < 5:
                    Qm, Qmb = Qn, Qnb
                    Rh, Rhb = k.ring("im_Rh", [128, 4, 128], BF16, 2)
                    k.op("act", lambda e: e.activation(out=Rh, in_=Rn, func=AF.Copy), r=[Rnb], w=[Rhb])
                Rm, Rmb = Rn, Rnb
            if stop == 5:
                raise _Stop(nc)
            HD[hd] = (AkT, AkTb, QBT, QBTb, QKT, QKTb, Rm, Rmb)
        XA = {}
        for hd in range(4):
            hcols = slice(hd * 64, hd * 64 + 64)
            AkT, AkTb = HD[hd][0], HD[hd][1]
            ps, pb = PS()
            for hf in range(2):
                trows = slice(hf * 64, hf * 64 + 64)
                for blk in range(4):
                    mm(k, ps[trows, blk * 64:(blk + 1) * 64], [(AkT[trows, blk, trows], vtok[trows, blk, hcols])], [AkTb, vtokb], pb)
            xa, xab = k.ring("XaAll%d" % hd, [128, 4, 64], F32, 1)
            k.op("act", lambda e: e.activation(out=xa, in_=ps[:, 0:256].rearrange("p (a b) -> p a b", a=4), func=AF.Copy, scale=-1.0),
                 r=[pb], w=[xab])
            XA[hd] = (xa, xab)
        for c in range(8):
            blk, hf = c // 2, c % 2
            trows = slice(hf * 64, hf * 64 + 64)
            diag = slice(hf * 64, hf * 64 + 64)
            tcol = slice(c * 64, c * 64 + 64)
            st = {}
            for hd in range(4):
                pc, hp = hd // 2, hd % 2
                rows = slice(hp * 64, hp * 64 + 64)
                uo, uob = U[hd][ucur[hd]]
                un, unb = U[hd][1 - ucur[hd]]
                ucur[hd] = 1 - ucur[hd]
                kts, ktsb = PR[pc]["kts"]
                ps1b, pb1b = PS()
                mm(k, ps1b[trows, 0:64], [(kts[rows, tcol], uo[rows, :])], [ktsb, uob], pb1b)
                st[hd] = dict(rows=rows, hcols=slice(hd * 64, hd * 64 + 64), pc=pc, uo=uo, uob=uob, un=un, unb=unb, ps1b=ps1b, pb1b=pb1b)
            for hd in range(4):
                d_ = st[hd]
                xa, xab = XA[hd]
                X, Xb = k.ring("X", [128, 64], F32, 4)
                k.op("dve", lambda e: e.tensor_tensor(out=X[trows, :], in0=xa[trows, blk, :], in1=d_["ps1b"][trows, 0:64], op=ALU.subtract),
                     r=[xab, d_["pb1b"]], w=[Xb])
                d_["X"], d_["Xb"] = X, Xb
            for hd in range(4):
                d_ = st[hd]
                Rm, Rmb = HD[hd][6], HD[hd][7]
                ps2, pb2 = PS()
                mm(k, ps2[trows, 0:64], [(Rm[trows, blk, diag], d_["X"][trows, :])], [Rmb, d_["Xb"]], pb2)
                d_["ps2"], d_["pb2"] = ps2, pb2
            for hd in range(4):
                d_ = st[hd]
                SA, SAb = k.ring("SA", [128, 64], F32, 4)
                eng = "dve" if hd % 2 == 0 else "act"
                if eng == "dve":
                    k.op("dve", lambda e: e.tensor_copy(out=SA[trows, :], in_=d_["ps2"][trows, 0:64]), r=[d_["pb2"]], w=[SAb])
                else:
                    k.op("act", lambda e: e.activation(out=SA[trows, :], in_=d_["ps2"][trows, 0:64], func=AF.Copy), r=[d_["pb2"]], w=[SAb])
                d_["SA"], d_["SAb"] = SA, SAb
            for hd in range(4):
                d_ = st[hd]
                rows, hcols = d_["rows"], d_["hcols"]
                ps3, pb3 = PS()
                mm(k, ps3[rows, 0:64], [(ktok[trows, blk, hcols], vtok[trows, blk, hcols]),
                                        (btok[trows, blk, hcols], d_["SA"][trows, :])], [ktokb, btokb, vtokb, d_["SAb"]], pb3)
                d_["ps3"], d_["pb3"] = ps3, pb3
            for hd in range(4):
                d_ = st[hd]
                pc, rows, hcols = d_["pc"], d_["rows"], d_["hcols"]
                rts, rtsb = PR[pc]["rts"]
                QBT, QBTb, QKT, QKTb = HD[hd][2], HD[hd][3], HD[hd][4], HD[hd][5]
                mm(k, ybank[pc][rows, tcol], [(d_["uo"][rows, :], rts[rows, tcol])], [d_["uob"], rtsb], ybb[pc])
                mm(k, ybank2[pc][rows, tcol], [(d_["SA"][trows, :], QBT[trows, blk, diag]),
                                               (vtok[trows, blk, hcols], QKT[trows, blk, diag])],
                   [d_["SAb"], QBTb, QKTb, vtokb], ybb2[pc])
            for hd in range(4):
                d_ = st[hd]
                pc, rows = d_["pc"], d_["rows"]
                pl, plb = PL[pc]
                k.op("dve", lambda e: e.scalar_tensor_tensor(out=d_["un"][rows, :], in0=d_["uo"][rows, :], scalar=pl[rows, c:c + 1],
                                                             in1=d_["ps3"][rows, 0:64], op0=ALU.mult, op1=ALU.add),
                     r=[d_["uob"], plb, d_["pb3"]], w=[d_["unb"]])
        if stop == 6:
            raise _Stop(nc)
        for pc in range(2):
            r_, rb_ = FM[("r", pc)]; v_, vb_ = FM[("v", pc)]; g_, gb_ = FM[("g", pc)]
            kp, kpb = PR[pc]["kp"]
            ysb, ysbb = tmpt("ysb")
            k.op("act", lambda e: e.activation(out=ysb, in_=ybank[pc], func=AF.Copy), r=[ybb[pc]], w=[ysbb])
            k.op("dve", lambda e: e.tensor_tensor(out=ysb, in0=ysb, in1=ybank2[pc], op=ALU.add), r=[ysbb, ybb2[pc]], w=[ysbb])
            ysq, ysqb = tmpt("ysq")
            k.op("act", lambda e: e.activation(out=ysq, in_=ysb, func=AF.Square), r=[ysbb], w=[ysqb])
            psm, pbm = PS(); mm(k, psm, [(bones, ysb)], [bonesb, ysbb], pbm)
            pse, pbe = PS(); mm(k, pse, [(bones, ysq)], [bonesb, ysqb], pbe)
            mean, meanb = tmpt("mean")
            k.op("act", lambda e: e.activation(out=mean, in_=psm, func=AF.Copy, scale=1.0 / 64), r=[pbm], w=[meanb])
            var, varb = tmpt("var")
            k.op("dve", lambda e: e.tensor_tensor(out=var, in0=mean, in1=mean, op=ALU.mult), r=[meanb], w=[varb])
            k.op("dve", lambda e: e.scalar_tensor_tensor(out=var, in0=pse, scalar=1.0 / 64, in1=var, op0=ALU.mult, op1=ALU.subtract),
                 r=[pbe, varb], w=[varb])
            k.op("act", lambda e: e.activation(out=var, in_=var, func=AF.Sqrt, bias=c_gneps, scale=1.0), r=[varb, cx.b_consts], w=[varb])
            k.op("dve", lambda e: e.reciprocal(out=var, in_=var), r=[varb], w=[varb])
            yn, ynb = tmpt("yn")
            k.op("dve", lambda e: e.tensor_tensor(out=yn, in0=ysb, in1=mean, op=ALU.subtract), r=[ysbb, meanb], w=[ynb])
            k.op("dve", lambda e: e.tensor_tensor(out=yn, in0=yn, in1=var, op=ALU.mult), r=[ynb, varb], w=[ynb])
            k.op("dve", lambda e: e.tensor_scalar(out=yn, in0=yn, scalar1=vec[:, LNG, pc:pc + 1], scalar2=vec[:, LNB, pc:pc + 1],
                                                  op0=ALU.mult, op1=ALU.add), r=[ynb, vecb], w=[ynb])
            rk, rkb = tmpt("rk")
            k.op("pool", lambda e: e.tensor_tensor(out=rk, in0=r_, in1=kp, op=ALU.mult), r=[rb_, kpb], w=[rkb])
            k.op("pool", lambda e: e.tensor_scalar(out=rk, in0=rk, scalar1=vec[:, RK, pc:pc + 1], scalar2=None, op0=ALU.mult),
                 r=[rkb, vecb], w=[rkb])
            psr, pbr = PS(); mm(k, psr, [(bones, rk)], [bonesb, rkb], pbr)
            bon, bonb = tmpt("bon")
            k.op("dve", lambda e: e.tensor_tensor(out=bon, in0=psr, in1=v_, op=ALU.mult), r=[pbr, vb_], w=[bonb])
            k.op("pool", lambda e: e.tensor_tensor(out=yn, in0=yn, in1=bon, op=ALU.add), r=[ynb, bonb], w=[ynb])
            yo, yob = k.ring("yo", [128, 512], BF16, 2)
            k.op("dve", lambda e: e.tensor_tensor(out=yo, in0=yn, in1=g_, op=ALU.mult), r=[ynb, gb_], w=[yob])
            k.dma("sp", ygq[ti // 4][:, pc, (ti % 4) * 512:(ti % 4) * 512 + 512], yo, r=[yob], sb=yob)
            outbufs.append(yob)
    k.wait_all("sp", list({id(b): b for b in outbufs}.values()))
    return nc


def rwkv_consts():
    t = np.arange(128)
    same = (t[:, None] // 64) == (t[None, :] // 64)
    m_sl = (same & (t[None, :] < t[:, None])).astype(np.float32)
    m_su = (same & (t[:, None] < t[None, :])).astype(np.float32)
    m_iu = (same & (t[:, None] <= t[None, :])).astype(np.float32)
    masks = np.ascontiguousarray(np.stack([m_sl, m_su, m_iu], axis=1))
    bones = same.astype(np.float32)
    resetm = np.ones((128, 512), np.float32)
    resetm[:, ::64] = 0.0
    return masks, np.ascontiguousarray(bones), resetm


def l4_inputs(h1T_b, core_hg, P):
    cs = slice(core_hg * 256, (core_hg + 1) * 256)
    masks, bones, resetm = rwkv_consts()
    import ml_dtypes
    col = lambda v: np.ascontiguousarray(v[cs].reshape(2, 128).T)
    vecs = np.stack([col(P["c_w0"]), col(P["c_a0"]), col(P["c_k_k"]), col(P["c_k_a"]), col(P["c_r_k"].reshape(-1)),
                     col(P["c_ln_g"]), col(P["c_ln_b"])], axis=1)
    mu = np.ascontiguousarray(P["c_mu"].reshape(6, 8, 128).transpose(2, 0, 1))
    return {"hall": h1T_b, "hzero": np.zeros((1024, 1), ml_dtypes.bfloat16), "wr": np.ascontiguousarray(P["c_w_r"][:, cs]), "wk": np.ascontiguousarray(P["c_w_k"][:, cs]),
            "wv": np.ascontiguousarray(P["c_w_v"][:, cs]), "w1": P["c_w1"], "a1": P["c_a1"], "g1": P["c_g1"],
            "w2": np.ascontiguousarray(P["c_w2"][:, cs]), "a2": np.ascontiguousarray(P["c_a2"][:, cs]),
            "g2": np.ascontiguousarray(P["c_g2"][:, cs]), "mu": mu, "vecs": np.ascontiguousarray(vecs.astype(np.float32)),
            "masks": masks, "bones": bones, "resetm": resetm}


NEG = -30000.0
NQB = 32


def build_l2a(nqb=NQB, ntile1=16, stop=99, nc=None, cx=None, io=None):
    try:
        return _build_l2a(nqb, ntile1, stop, nc, cx, io)
    except _Stop as e:
        return e.args[0]


def _build_l2a(nqb=NQB, ntile1=16, stop=99, nc=None, cx=None, io=None):
    nc, cx, io = _std(nc, cx, io)
    din = io.inp
    hall = din("hall", [4096, TOK], BF16)
    psel_d = din("psel", [128, 2])
    wq_d = din("wq", [D, 256]); wks_d = din("wks", [D, 64]); wkw_d = din("wkw", [D, 64])
    wkv_d = din("wkvc", [D, 128]); wv2_d = din("wv2", [D, 128]); wg_d = din("wgate", [D, 12])
    tabA_c = din("tabA_c", [64, SEQ]); tabA_s = din("tabA_s", [64, SEQ])
    tabB_c = din("tabB_c", [128, SEQ]); tabB_s = din("tabB_s", [128, SEQ])
    tabQ_c = din("tabQ_c", [64, NQB * 128]); tabQ_s = din("tabQ_s", [64, NQB * 128])
    rot_d = din("rotT", [128, 128])
    ebig_d = din("ebig", [128, SEQ], BF16)
    w1_d = din("w1kv", [128, 32, 64]); w2_d = din("w2kv", [128, 64]); pe_d = din("pekv", [128, 32, 2])
    vcx_d = din("vcx", [128, 4, 129], BF16)
    cm_d = din("cmask", [128, 9, 128], BF16)
    sm_d = din("smask", [128, 6, 128], BF16)
    fv_d = din("fv", [NQB, 128, 2, 128])
    oa = io.out("oa", [NQB * 128, 256], BF16)
    k = cx.k
    PS = lambda: k.ring("psr", [128, 512], F32, 4, psum=True)
    held = {}
    for nm in ("oc0", "oc1", "os", "ow"):
        held[nm] = (k.ps("h_" + nm, [128, 512]), Buf("h_" + nm))
    psel, pselb = k.tile("psel", [128, 2], F32)
    k.dma("sp", psel, psel_d, w=[pselb], sb=pselb)

    def cload(name, dram, shape, dt, q="sp"):
        t, b = k.tile(name, shape, dt)
        k.dma(q, t, dram, w=[b], sb=b)
        return t, b

    rot, rotb = cload("rot", rot_d, [128, 128], F32)
    ebig, ebigb = cload("ebig", ebig_d, [128, SEQ], BF16)
    cm, cmb = cload("cm", cm_d, [128, 9, 128], BF16)
    sm, smb = cload("sm", sm_d, [128, 6, 128], BF16)
    identb, identbb = k.tile("identb", [128, 128], BF16)
    k.op("act", lambda e: e.activation(out=identb, in_=cx.ident, func=AF.Copy), r=[cx.b_ident], w=[identbb])
    ones_f, ones_fb = k.tile("ones_f", [128, 128], F32)
    k.op("pool", lambda e: e.memset(ones_f, 1.0), w=[ones_fb])

    def wcast(name, dram, n, q="pool"):
        t, b = k.tile(name, [128, 8, n], BF16)
        k.dma(q, t, dram.rearrange("(kc p) n -> p kc n", p=128), w=[b], sb=b)
        return t, b

    wq, wqb = wcast("wq", wq_d, 256); wks, wksb = wcast("wks", wks_d, 64); wkw, wkwb = wcast("wkw", wkw_d, 64)
    wkv, wkvb = wcast("wkv", wkv_d, 128); wv2, wv2b = wcast("wv2", wv2_d, 128); wgt, wgtb = wcast("wgt", wg_d, 12)
    w1, w1b = k.tile("w1", [128, 32, 64], BF16); k.dma("pool", w1, w1_d, w=[w1b], sb=w1b)
    w2, w2b = k.tile("w2", [128, 64], BF16); k.dma("pool", w2, w2_d, w=[w2b], sb=w2b)
    pe, peb = k.tile("pe", [128, 32, 2], BF16); k.dma("pool", pe, pe_d, w=[peb], sb=peb)

    ksel, kselb = k.tile("ksel", [128, SEQ], BF16)
    kwin, kwinb = k.tile("kwin", [128, SEQ], BF16)
    kvc, kvcb = k.tile("kvc", [128, SEQ + 32], BF16)
    k.op("pool", lambda e: e.memset(kvc[:, SEQ:SEQ + 32], 0.0), w=[kvcb])
    vsel, vselb = k.tile("vsel", [128, 64, 96], BF16)
    vwin, vwinb = k.tile("vwin", [128, 64, 96], BF16)
    for t_, b_ in ((ksel, kselb), (kwin, kwinb)):
        k.op("pool", lambda e: e.memset(t_[64:128, :], 0.0), w=[b_])
        k.op("pool", lambda e: e.memset(t_[64:65, :], 1.0), w=[b_])
    for t_, b_ in ((vsel, vselb), (vwin, vwinb)):
        k.op("pool", lambda e: e.memset(t_[:, :, 64:65], 1.0), w=[b_])
    kmax2, kmax2b = k.tile("kmax2", [128, 1], F32)
    k.op("pool", lambda e: e.memset(kmax2, 0.0), w=[kmax2b])

    def upd_kmax(src, srcb, rows, n):
        sq, sqb = k.ring("ksq", [128, 512], F32, 2)
        k.op("pool", lambda e: e.tensor_tensor(out=sq[rows, 0:n], in0=src, in1=src, op=ALU.mult), r=[srcb], w=[sqb])
        ps, pb = PS()
        mm(k, ps[:, 0:n], [(ones_f[rows, :], sq[rows, 0:n])], [ones_fb, sqb], pb)
        k.op("act", lambda e: e.activation(out=sq[:, 0:n], in_=ps[:, 0:n], func=AF.Copy), r=[pb], w=[sqb])
        mx, mxb = k.ring("kmx", [128, 8], F32, 2)
        k.op("dve", lambda e: e.max(out=mx, in_=sq[:, 0:n]), r=[sqb], w=[mxb])
        k.op("dve", lambda e: e.tensor_tensor(out=kmax2, in0=kmax2, in1=mx[:, 0:1], op=ALU.max), r=[mxb, kmax2b], w=[kmax2b])

    def rope_store(ps, pb, rows, tc_d, ts_d, t0, n, dst, dstb, rbase=0):
        R = slice(rbase, rbase + rows)
        xk, xkb = k.ring("xk", [128, 512], F32, 2)
        k.op("act", lambda e: e.activation(out=xk[R, 0:n], in_=ps[R, 0:n], func=AF.Copy), r=[pb], w=[xkb])
        tc_, tcb = k.ring("tabc", [128, 512], F32, 2)
        ts_, tsb = k.ring("tabs", [128, 512], F32, 2)
        k.dma("sp", tc_[R, 0:n], tc_d[:, t0:t0 + n], w=[tcb], sb=tcb)
        k.dma("sp", ts_[R, 0:n], ts_d[:, t0:t0 + n], w=[tsb], sb=tsb)
        ps2, pb2 = PS()
        mm(k, ps2[R, 0:n], [(rot[R, R], xk[R, 0:n])], [rotb, xkb], pb2)
        t1, t1b = k.ring("rp1", [128, 512], F32, 2)
        t2, t2b = k.ring("rp2", [128, 512], F32, 2)
        k.op("pool", lambda e: e.tensor_tensor(out=t1[R, 0:n], in0=xk[R, 0:n], in1=tc_[R, 0:n], op=ALU.mult), r=[xkb, tcb], w=[t1b])
        k.op("dve", lambda e: e.tensor_tensor(out=t2[R, 0:n], in0=ps2[R, 0:n], in1=ts_[R, 0:n], op=ALU.mult), r=[pb2, tsb], w=[t2b])
        k.op("dve", lambda e: e.tensor_tensor(out=dst, in0=t1[R, 0:n], in1=t2[R, 0:n], op=ALU.add), r=[t1b, t2b], w=[dstb])

    if stop == 10:
        raise _Stop(nc)
    for ti in range(ntile1):
        t0 = ti * 512
        hb, hbb = k.ring("hb", [128, 8, 512], BF16, 2)
        dma_hall(k, hb, hall, t0, 512, hbb)
        for (w_, wb_, dst, dstb) in ((wks, wksb, ksel, kselb), (wkw, wkwb, kwin, kwinb)):
            ps, pb = PS()
            mm(k, ps[0:64, :], [(w_[:, kc, :], hb[:, kc, :]) for kc in range(8)], [wb_, hbb], pb)
            if stop == 11 + 100 * ti:
                raise _Stop(nc)
            rope_store(ps, pb, 64, tabA_c, tabA_s, t0, 512, dst[0:64, t0:t0 + 512], dstb)
            if stop == 12 + 100 * ti:
                raise _Stop(nc)
            upd_kmax(dst[0:64, t0:t0 + 512], dstb, slice(0, 64), 512)
            if stop == 13 + 100 * ti:
                raise _Stop(nc)
        if stop == 14 + 100 * ti:
            raise _Stop(nc)
        ps, pb = PS()
        mm(k, ps, [(wkv[:, kc, :], hb[:, kc, :]) for kc in range(8)], [wkvb, hbb], pb)
        rope_store(ps, pb, 128, tabB_c, tabB_s, t0, 512, kvc[:, t0:t0 + 512], kvcb)
        if stop == 15 + 100 * ti:
            raise _Stop(nc)
        ps, pb = PS()
        for blk in range(4):
            mm(k, ps[:, blk * 128:(blk + 1) * 128], [(hb[:, kc, blk * 128:(blk + 1) * 128], wv2[:, kc, :]) for kc in range(8)],
               [wv2b, hbb], pb)
        pv = ps.rearrange("p (a b) -> p a b", a=4)
        k.op("act", lambda e: e.activation(out=vsel[:, ti * 4:ti * 4 + 4, 0:64], in_=pv[:, :, 0:64], func=AF.Copy), r=[pb], w=[vselb])
        k.op("act", lambda e: e.activation(out=vwin[:, ti * 4:ti * 4 + 4, 0:64], in_=pv[:, :, 64:128], func=AF.Copy), r=[pb], w=[vwinb])

    if stop == 1:
        raise _Stop(nc)
    kc, kcb = k.tile("kc", [128, 512], BF16)
    k.op("pool", lambda e: e.memset(kc, 0.0), w=[kcb])
    k.op("pool", lambda e: e.memset(kc[64:65, :], 1.0), w=[kcb])
    vcx, vcxb = k.tile("vcx", [128, 4, 256], BF16)
    k.dma("sp", vcx[:, :, 64:193], vcx_d, w=[vcxb], sb=vcxb)
    kv16 = kvc.rearrange("p (n s) -> p n s", s=16)
    hid, hidb = k.tile("hid", [128, 512], BF16)
    k.op("pool", lambda e: e.memset(hid, 0.0), w=[hidb])
    for R in (slice(0, 64), slice(64, 128)):
        psb, pbb = PS()
        mm(k, psb[R, 0:2], [(w1[R, l, :], pe[R, l, :]) for l in range(32)], [w1b, peb], pbb)
        bia, biab = k.ring("cbias", [128, 1], F32, 2)
        k.op("act", lambda e: e.activation(out=bia[R, :], in_=psb[R, 0:1], func=AF.Copy), r=[pbb], w=[biab])
        ps, pb = PS()
        mm(k, ps[R, 0:512], [(w1[R, l, :], kv16[R, (l // 16):(l // 16) + 512, l % 16]) for l in range(32)], [w1b, kvcb], pb)
        k.op("act", lambda e: e.activation(out=hid[R, :], in_=ps[R, :], func=AF.Silu, bias=bia[R, :]), r=[pb, biab], w=[hidb])
    ps, pb = PS()
    mm(k, ps[0:64, :], [(w2[0:64, :], hid[0:64, :])], [w2b, hidb], pb)
    k.op("act", lambda e: e.activation(out=kc[0:64, :], in_=ps[0:64, :], func=AF.Copy), r=[pb], w=[kcb])
    upd_kmax(kc[0:64, 0:512], kcb, slice(0, 64), 512)
    ps, pb = PS()
    for ch in range(4):
        mm(k, ps[:, ch * 64:(ch + 1) * 64], [(hid[64:128, ch * 128:(ch + 1) * 128], w2[64:128, :])], [w2b, hidb], pb)
    k.op("act", lambda e: e.activation(out=vcx[:, :, 0:64], in_=ps[:, 0:256].rearrange("p (a b) -> p a b", a=4), func=AF.Copy),
         r=[pb], w=[vcxb])
    nkm, nkmb = k.tile("nkm", [128, 1], F32)
    k.op("act", lambda e: e.activation(out=nkm, in_=kmax2, func=AF.Sqrt), r=[kmax2b], w=[nkmb])
    k.op("dve", lambda e: e.tensor_scalar(out=nkm, in0=nkm, scalar1=-1.0, scalar2=None, op0=ALU.mult), r=[nkmb], w=[nkmb])

    if stop == 2:
        raise _Stop(nc)
    outb = []
    for i in range(nqb):
        q0 = i * 128
        hqc, hqcb = k.ring("hqc", [128, 2, 8, 128], BF16, 2)
        for pp in range(2):
            dma_hall(k, hqc[:, pp], hall, (2 * i + pp) * 128, 128, hqcb)
        hqt, hqtb = k.ring("hqt", [128, 8, 128], BF16, 2)
        hqb, hqbb = k.ring("hqb", [128, 8, 128], BF16, 2)
        k.op("dve", lambda e: e.tensor_scalar(out=hqt, in0=hqc[:, 0], scalar1=psel[:, 0:1], scalar2=None, op0=ALU.mult),
             r=[hqcb, pselb], w=[hqtb])
        k.op("dve", lambda e: e.scalar_tensor_tensor(out=hqb, in0=hqc[:, 1], scalar=psel[:, 1:2], in1=hqt, op0=ALU.mult, op1=ALU.add),
             r=[hqcb, pselb, hqtb], w=[hqbb])
        qa, qab = k.ring("qa", [128, 512], BF16, 2)
        k.op("pool", lambda e: e.memset(qa[64:128, :], 0.0), w=[qab])
        qf, qfb = k.ring("qf", [128, 512], F32, 2)
        tcq, tcqb = k.ring("tcq", [128, 128], F32, 2); tsq, tsqb = k.ring("tsq", [128, 128], F32, 2)
        k.dma("sp", tcq[0:64, :], tabQ_c[:, q0:q0 + 128], w=[tcqb], sb=tcqb)
        k.dma("sp", tsq[0:64, :], tabQ_s[:, q0:q0 + 128], w=[tsqb], sb=tsqb)
        ps, pb = PS()
        for g in range(4):
            mm(k, ps[0:64, g * 128:(g + 1) * 128], [(wq[:, kc, g * 64:(g + 1) * 64], hqb[:, kc, :]) for kc in range(8)], [wqb, hqbb], pb)
        xq, xqb = k.ring("xq", [128, 512], F32, 2)
        k.op("act", lambda e: e.activation(out=xq[0:64, :], in_=ps[0:64, :], func=AF.Copy), r=[pb], w=[xqb])
        ps2, pb2 = PS()
        mm(k, ps2[0:64, :], [(rot[0:64, 0:64], xq[0:64, :])], [rotb, xqb], pb2)
        v3 = lambda a: a.rearrange("p (g t) -> p g t", g=4)
        bc4 = lambda a: a.unsqueeze(1).to_broadcast([64, 4, 128])
        k.op("pool", lambda e: e.tensor_tensor(out=v3(xq[0:64, :]), in0=v3(xq[0:64, :]), in1=bc4(tcq[0:64, :]), op=ALU.mult),
             r=[xqb, tcqb], w=[xqb])
        k.op("dve", lambda e: e.tensor_tensor(out=v3(qf[0:64, :]), in0=v3(ps2[0:64, :]), in1=bc4(tsq[0:64, :]), op=ALU.mult),
             r=[pb2, tsqb], w=[qfb])
        k.op("dve", lambda e: e.tensor_tensor(out=qf[0:64, :], in0=qf[0:64, :], in1=xq[0:64, :], op=ALU.add), r=[qfb, xqb], w=[qfb])
        k.op("act", lambda e: e.activation(out=qa[0:64, :], in_=qf[0:64, :], func=AF.Copy), r=[qfb], w=[qab])
        k.op("pool", lambda e: e.tensor_tensor(out=xq[0:64, :], in0=qf[0:64, :], in1=qf[0:64, :], op=ALU.mult), r=[qfb, xqb], w=[xqb])
        ps3, pb3 = PS()
        mm(k, ps3[64:65, :], [(ones_f[0:64, 0:1], xq[0:64, :])], [ones_fb, xqb], pb3)
        mrow, mrowb = k.ring("mrow", [128, 512], F32, 2)
        k.op("act", lambda e: e.activation(out=mrow[64:65, :], in_=ps3[64:65, :], func=AF.Sqrt), r=[pb3], w=[mrowb])
        k.op("dve", lambda e: e.tensor_scalar(out=qa[64:65, :], in0=mrow[64:65, :], scalar1=nkm[64:65, 0:1], scalar2=None, op0=ALU.mult),
             r=[mrowb, nkmb], w=[qab])
        psg, pbg = PS()
        mm(k, psg[:, 0:12], [(hqb[:, kc, :], wgt[:, kc, :]) for kc in range(8)], [wgtb, hqbb], pbg)
        gt, gtb = k.ring("gates", [128, 12], F32, 2)
        k.op("act", lambda e: e.activation(out=gt, in_=psg[:, 0:12], func=AF.Sigmoid), r=[pbg], w=[gtb])

        if stop == 3:
            raise _Stop(nc)

        def score_tile(kaug, kaugb, ktile_cols, masks):
            ps, pb = PS()
            n = 1 + len(masks)
            k.op("pe", lambda e: e.matmul(ps, lhsT=kaug[:, ktile_cols], rhs=qa, start=True, stop=(n == 1)),
                 r=[kaugb, qab], w=[pb], inc=(n == 1))
            for mi, (ml, mlb, mr, mrb) in enumerate(masks):
                last = (mi == len(masks) - 1)
                k.op("pe", lambda e: e.matmul(ps.rearrange("p (g t) -> p g t", g=4), lhsT=ml,
                                              rhs=mr.unsqueeze(1).to_broadcast([128, 4, 128]), start=False, stop=last),
                     r=[mlb, mrb], w=[pb], inc=last)
            eT, eTb = k.ring("eT", [128, 512], BF16, 3)
            k.op("act", lambda e: e.activation(out=eT, in_=ps, func=AF.Exp), r=[pb], w=[eTb])
            return eT, eTb

        qi_e = 2 * i
        last = (8 * qi_e + 6) // 128
        oc = [held["oc0"], held["oc1"]]
        def cmp_pv(cc, eT, eTb):
            for g in range(4):
                o_, ob_ = oc[g // 2]
                k.op("pe", lambda e: e.matmul(o_[:, (g % 2) * 256:(g % 2) * 256 + 193], lhsT=eT[:, g * 128:(g + 1) * 128],
                                              rhs=vcx[:, cc, 0:193], start=(cc == 0 and g % 2 == 0), stop=(cc == last and g % 2 == 1),
                                              skip_group_check=True),
                     r=[eTb, vcxb], w=[ob_], inc=(g == 3))
        pend = None
        for cc in range(last + 1):
            masks = []
            if cc == last:
                masks.append((identb, identbb, cm[:, i % 8, :], cmb))
            elif cc == last - 1 and i % 8 == 0:
                masks.append((identb, identbb, cm[:, 8, :], cmb))
            eT, eTb = score_tile(kc, kcb, slice(cc * 128, (cc + 1) * 128), masks)
            if pend is not None:
                cmp_pv(*pend)
            pend = (cc, eT, eTb)
        cmp_pv(*pend)
        if stop == 6:
            raise _Stop(nc)
        ow_, owb_ = held["ow"]
        wt = [kt for kt in range(2 * i - 4, 2 * i + 2) if kt >= 0]
        def win_pv(kt, eT, eTb):
            for g in range(4):
                k.op("pe", lambda e: e.matmul(ow_[:, g * 65:(g + 1) * 65], lhsT=eT[:, g * 128:(g + 1) * 128], rhs=vwin[:, kt, 0:65],
                                              start=(kt == wt[0] and g == 0), stop=(kt == wt[-1] and g == 3), skip_group_check=True),
                     r=[eTb, vwinb], w=[owb_], inc=(g == 3))
        pend = None
        for kt in wt:
            cols = slice(kt * 128, (kt + 1) * 128)
            pos = kt - (2 * i - 4)
            masks = []
            mi = {0: 2, 1: 3, 4: 4, 5: 5}.get(pos)
            if mi is not None:
                masks.append((identb, identbb, sm[:, mi, :], smb))
            eT, eTb = score_tile(kwin, kwinb, cols, masks)
            if pend is not None:
                win_pv(*pend)
            pend = (kt, eT, eTb)
        win_pv(*pend)
        if stop == 4:
            raise _Stop(nc)
        zc, zcb = k.ring("zc", [128, 4], F32, 2)
        for g in range(4):
            o_, ob_ = oc[g // 2]
            k.op("dve", lambda e: e.tensor_scalar(out=zc[:, g:g + 1], in0=o_[:, (g % 2) * 256 + 64:(g % 2) * 256 + 65], scalar1=1e-30,
                                                  scalar2=None, op0=ALU.max), r=[ob_], w=[zcb])
        k.op("dve", lambda e: e.reciprocal(out=zc, in_=zc), r=[zcb], w=[zcb])
        imp, impb = k.ring("imp", [128, 128], F32, 2)
        for g in range(4):
            o_, ob_ = oc[g // 2]
            src = o_[:, (g % 2) * 256 + 65:(g % 2) * 256 + 193]
            if g == 0:
                k.op("dve", lambda e: e.tensor_scalar(out=imp, in0=src, scalar1=zc[:, 0:1], scalar2=None, op0=ALU.mult),
                     r=[ob_, zcb], w=[impb])
            else:
                k.op("dve", lambda e: e.scalar_tensor_tensor(out=imp, in0=src, scalar=zc[:, g:g + 1], in1=imp, op0=ALU.mult, op1=ALU.add),
                     r=[ob_, zcb, impb], w=[impb])
        fv, fvb = k.ring("fv", [128, 2, 128], F32, 2)
        k.dma("sp", fv, fv_d[i], w=[fvb], sb=fvb)
        k.op("dve", lambda e: e.tensor_tensor(out=imp, in0=imp, in1=fv[:, 0, :], op=ALU.mult), r=[impb, fvb], w=[impb])
        k.op("dve", lambda e: e.tensor_tensor(out=imp, in0=imp, in1=fv[:, 1, :], op=ALU.add), r=[impb, fvb], w=[impb])
        m8, m8b = k.ring("m8", [128, 16], F32, 2)
        wk_, wkb_ = k.ring("impw", [128, 128], F32, 2)
        k.op("dve", lambda e: e.max(out=m8[:, 0:8], in_=imp), r=[impb], w=[m8b])
        k.op("dve", lambda e: e.match_replace(out=wk_, in_to_replace=m8[:, 0:8], in_values=imp, imm_value=-3e38), r=[m8b, impb], w=[wkb_])
        k.op("dve", lambda e: e.max(out=m8[:, 8:16], in_=wk_), r=[wkb_], w=[m8b])
        k.op("dve", lambda e: e.tensor_scalar(out=wk_, in0=imp, scalar1=m8[:, 15:16], scalar2=None, op0=ALU.is_ge), r=[impb, m8b], w=[wkb_])
        k.op("dve", lambda e: e.tensor_scalar(out=wk_, in0=wk_, scalar1=-1.0, scalar2=-NEG, op0=ALU.add, op1=ALU.mult), r=[wkb_], w=[wkb_])
        pst, pbt = PS()
        k.op("pe", lambda e: e.transpose(out=pst[:, 0:128], in_=wk_, identity=cx.ident), r=[wkb_, cx.b_ident], w=[pbt])
        nbT, nbTb = k.ring("nbT", [128, 128], BF16, 2)
        k.op("act", lambda e: e.activation(out=nbT, in_=pst[:, 0:128], func=AF.Copy), r=[pbt], w=[nbTb])

        if stop == 5:
            raise _Stop(nc)
        os_, osb_ = held["os"]
        nkt = 2 * i + 2
        def sel_pv(kt, eT, eTb):
            for g in range(4):
                k.op("pe", lambda e: e.matmul(os_[:, g * 65:(g + 1) * 65], lhsT=eT[:, g * 128:(g + 1) * 128], rhs=vsel[:, kt, 0:65],
                                              start=(kt == 0 and g == 0), stop=(kt == nkt - 1 and g == 3), skip_group_check=True),
                     r=[eTb, vselb], w=[osb_], inc=(g == 3))
        pend = None
        for kt in range(nkt):
            cols = slice(kt * 128, (kt + 1) * 128)
            masks = [(ebig[:, cols], ebigb, nbT, nbTb)]
            if kt == 2 * i:
                masks.append((identb, identbb, sm[:, 0, :], smb))
            elif kt == 2 * i + 1:
                masks.append((identb, identbb, sm[:, 1, :], smb))
            eT, eTb = score_tile(ksel, kselb, cols, masks)
            if pend is not None:
                sel_pv(*pend)
            pend = (kt, eT, eTb)
        sel_pv(*pend)
        sc, scb = k.ring("sc", [128, 12], F32, 2)
        k.op("pool", lambda e: e.memset(sc, 1.0), w=[scb])
        for g in range(4):
            k.op("dve", lambda e: e.tensor_scalar(out=sc[:, g * 3 + 1:g * 3 + 2], in0=os_[:, g * 65 + 64:g * 65 + 65], scalar1=1e-30,
                                                  scalar2=None, op0=ALU.max), r=[osb_], w=[scb])
            k.op("dve", lambda e: e.tensor_scalar(out=sc[:, g * 3 + 2:g * 3 + 3], in0=ow_[:, g * 65 + 64:g * 65 + 65], scalar1=1e-30,
                                                  scalar2=None, op0=ALU.max), r=[owb_], w=[scb])
        k.op("dve", lambda e: e.reciprocal(out=sc, in_=sc), r=[scb], w=[scb])
        for g in range(4):
            k.op("dve", lambda e: e.tensor_copy(out=sc[:, g * 3:g * 3 + 1], in_=zc[:, g:g + 1]), r=[zcb], w=[scb])
        k.op("dve", lambda e: e.tensor_tensor(out=sc, in0=sc, in1=gt, op=ALU.mult), r=[scb, gtb], w=[scb])
        ot, otb = k.ring("oat", [128, 256], F32, 2)
        for g in range(4):
            o_, ob_ = oc[g // 2]
            dst = ot[:, g * 64:(g + 1) * 64]
            k.op("dve", lambda e: e.tensor_scalar(out=dst, in0=o_[:, (g % 2) * 256:(g % 2) * 256 + 64], scalar1=sc[:, g * 3:g * 3 + 1],
                                                  scalar2=None, op0=ALU.mult), r=[ob_, scb], w=[otb])
            k.op("dve", lambda e: e.scalar_tensor_tensor(out=dst, in0=os_[:, g * 65:g * 65 + 64], scalar=sc[:, g * 3 + 1:g * 3 + 2], in1=dst,
                                                         op0=ALU.mult, op1=ALU.add), r=[osb_, scb, otb], w=[otb])
            k.op("dve", lambda e: e.scalar_tensor_tensor(out=dst, in0=ow_[:, g * 65:g * 65 + 64], scalar=sc[:, g * 3 + 2:g * 3 + 3], in1=dst,
                                                         op0=ALU.mult, op1=ALU.add), r=[owb_, scb, otb], w=[otb])
        otc, otcb = k.ring("oatc", [128, 256], BF16, 2)
        k.op("act", lambda e: e.activation(out=otc, in_=ot, func=AF.Copy), r=[otb], w=[otcb])
        k.dma("sp", oa[q0:q0 + 128, :], otc, r=[otcb], sb=otcb)
        outb.append(otcb)
    k.wait_all("sp", list({id(b): b for b in outb}.values()))
    return nc


def _bf16(a):
    import ml_dtypes
    return np.ascontiguousarray(a).astype(ml_dtypes.bfloat16)


def nsa_consts():
    inv = (10000.0 ** (-np.arange(0, 64, 2, dtype=np.float32) / 64)).astype(np.float32)
    ang = np.arange(SEQ, dtype=np.float32)[:, None] * inv[None, :]
    cos, sin = np.cos(ang).astype(np.float32), np.sin(ang).astype(np.float32)
    tA_c = np.ascontiguousarray(np.concatenate([cos, cos], 1).T)
    tA_s = np.ascontiguousarray(np.concatenate([sin, sin], 1).T)
    tB_c = np.ascontiguousarray(np.concatenate([tA_c, np.ones_like(tA_c)], 0))
    tB_s = np.ascontiguousarray(np.concatenate([tA_s, np.zeros_like(tA_s)], 0))
    rot = np.zeros((128, 128), np.float32)
    for m in range(128):
        mm_ = m % 64
        if mm_ < 32:
            rot[m + 32, m] = -1.0
        else:
            rot[m - 32, m] = 1.0
    x = np.arange(SEQ)
    ebig = (np.arange(128)[:, None] == (x[None, :] // 64)).astype(np.float32)
    c = np.arange(512)
    j = np.arange(128)
    ov = ((16 * c[:, None] < 64 * j[None, :] + 64) & (16 * c[:, None] + 31 >= 64 * j[None, :])).astype(np.float32)
    vcx = np.zeros((512, 129), np.float32)
    vcx[:, 0] = 1.0
    vcx[:, 1:] = ov
    vcx[511, :] = 0.0
    vcx = np.ascontiguousarray(vcx.reshape(4, 128, 129).transpose(1, 0, 2))
    return dict(tA_c=tA_c, tA_s=tA_s, tB_c=tB_c, tB_s=tB_s, rot=rot, ebig=_bf16(ebig), vcx=_bf16(vcx))


def nsa_core_consts(par):
    p = np.arange(128)[:, None]
    tl = np.arange(128)[None, :]
    cm = np.zeros((128, 9, 128), np.float32)
    for s in range(8):
        off = 8 * ((2 * s + par) % 16)
        cm[:, s, :] = np.where(16 * (p - off) + 31 <= tl, 0.0, NEG)
    if par == 0:
        cm[:, 8, :] = np.where((p == 127) & (tl < 15), NEG, 0.0)
    causal = np.where(p <= tl, 0.0, NEG).astype(np.float32)
    winT = np.where(p > tl, 0.0, NEG).astype(np.float32)
    ALL = np.full((128, 128), NEG, np.float32)
    Z = np.zeros((128, 128), np.float32)
    sm = [causal, ALL, winT, Z, causal, ALL] if par == 0 else [Z, causal, ALL, winT, Z, causal]
    sm = np.stack(sm, axis=1)
    fv = np.zeros((NQB, 128, 2, 128), np.float32)
    jj = np.arange(128)[None, :]
    for i in range(NQB):
        qi = 2 * i + par
        t = 128 * qi + np.arange(128)[:, None]
        cur = t // 64
        valid = jj <= cur
        f0 = (jj == 0)
        f1 = (jj == cur)
        f2 = (jj == cur - 1)
        forced = f0 | f1 | f2
        V = (valid & ~forced).astype(np.float32)
        F = np.where(valid, 0.0, -1e30).astype(np.float32)
        F = np.where(f2 & valid, 1e4 + 2.0, F)
        F = np.where(f0, 1e4 + 1.0, F)
        F = np.where(f1, 1e4, F)
        fv[i, :, 0, :] = V
        fv[i, :, 1, :] = F
    return dict(cm=_bf16(cm), sm=_bf16(sm), fv=fv)


def l2a_inputs(h0T_b, kvh, par, P, C):
    w = P["ab_w_in"]
    cat = lambda *a: np.ascontiguousarray(np.concatenate(a, axis=1))
    sl = lambda o, n: w[:, o + kvh * n: o + (kvh + 1) * n]
    qtok = np.concatenate([np.arange((2 * i + par) * 128, (2 * i + par + 1) * 128) for i in range(NQB)])
    cc = nsa_core_consts(par)
    w1 = np.concatenate([P["a_cmp_w1_k"].reshape(32, 64, 64).transpose(1, 0, 2),
                         P["a_cmp_w1_v"].reshape(32, 64, 64).transpose(1, 0, 2)], 0)
    w2 = np.concatenate([P["a_cmp_w2_k"], P["a_cmp_w2_v"]], 0)
    pe = np.concatenate([P["a_cmp_pe_k"].T, P["a_cmp_pe_v"].T], 0)
    pselv = np.zeros((128, 2), np.float32); pselv[:, par] = 1.0
    return {"hall": h0T_b, "psel": pselv,
            "wq": np.ascontiguousarray(sl(0, 256)), "wks": np.ascontiguousarray(sl(768, 64)), "wkw": np.ascontiguousarray(sl(1024, 64)),
            "wkvc": cat(sl(512, 64), sl(640, 64)), "wv2": cat(sl(896, 64), sl(1152, 64)), "wgate": np.ascontiguousarray(sl(1280, 12)),
            "tabA_c": C["tA_c"], "tabA_s": C["tA_s"], "tabB_c": C["tB_c"], "tabB_s": C["tB_s"],
            "tabQ_c": np.ascontiguousarray(C["tA_c"][:, qtok] * np.float32(0.125)),
            "tabQ_s": np.ascontiguousarray(C["tA_s"][:, qtok] * np.float32(0.125)),
            "rotT": C["rot"], "ebig": C["ebig"], "w1kv": np.ascontiguousarray(w1), "w2kv": np.ascontiguousarray(w2),
            "pekv": np.ascontiguousarray(np.stack([pe, pe], axis=2)), "vcx": C["vcx"], "cmask": cc["cm"], "smask": cc["sm"], "fv": cc["fv"]}


def build_l2b(ntiles=16, nc=None, cx=None, io=None):
    nc, cx, io = _std(nc, cx, io)
    din = io.inp
    hall = din("hall", [4096, TOK], BF16)
    wx_d = din("wx", [D, 256]); wB_d = din("wB", [D, 128]); wC_d = din("wC", [D, 128]); wdt_d = din("wdt", [D, 4])
    cw_d = din("convw", [128, 4, 4]); cb_d = din("convb", [128, 4])
    dtb_d = din("dtb", [128, 16]); alog_d = din("alog", [128, 16]); dsk_d = din("dskip", [128, 4])
    tri_d = din("tri", [128, 128]); su_d = din("su", [128, 128])
    yout = io.out("y", [SEQ, 256], BF16)
    k = cx.k
    PS = lambda: k.ring("psr", [128, 512], F32, 8, psum=True)

    def cload(name, dram, shape, dt, q="sp"):
        t, b = k.tile(name, shape, dt)
        k.dma(q, t, dram, w=[b], sb=b)
        return t, b

    def wcast(name, dram, n):
        t, b = k.tile(name, [128, 8, n], BF16)
        k.dma("pool", t, dram.rearrange("(kc p) n -> p kc n", p=128), w=[b], sb=b)
        return t, b

    wx, wxb = wcast("wx", wx_d, 256); wB, wBb = wcast("wB", wB_d, 128); wC, wCb = wcast("wC", wC_d, 128)
    wdt, wdtb = wcast("wdt", wdt_d, 4)
    cw, cwb = cload("cw", cw_d, [128, 4, 4], F32); cb, cbb = cload("cb", cb_d, [128, 4], F32)
    dtb, dtbb = cload("dtb", dtb_d, [128, 16], F32); alog, alogb = cload("alog", alog_d, [128, 16], F32)
    dsk, dskb = cload("dsk", dsk_d, [128, 4], F32)
    tri, trib = cload("tri", tri_d, [128, 128], F32); su, sub_ = cload("su", su_d, [128, 128], F32)
    ones_f, ones_fb = k.tile("ones_f", [128, 128], F32)
    k.op("pool", lambda e: e.memset(ones_f, 1.0), w=[ones_fb])
    c_one = cx.const(1.0)
    arep, arepb = k.tile("arep", [128, 16], F32)
    k.op("act", lambda e: e.activation(out=arep, in_=alog, func=AF.Exp), r=[alogb], w=[arepb])
    k.op("dve", lambda e: e.tensor_scalar(out=arep, in0=arep, scalar1=-1.0, scalar2=None, op0=ALU.mult), r=[arepb], w=[arepb])
    S, Sb = k.tile("S", [128, 256], F32)
    k.op("pool", lambda e: e.memset(S, 0.0), w=[Sb])
    xbc = []
    for m in range(4):
        t, b = k.tile("xbc%d" % m, [128, 516], F32)
        k.op("pool", lambda e: e.memset(t, 0.0), w=[b])
        xbc.append((t, b))
    outb = []
    wsel = [(wx, wxb, slice(0, 128)), (wx, wxb, slice(128, 256)), (wB, wBb, slice(0, 128)), (wC, wCb, slice(0, 128))]
    bc64 = lambda a, n: a.unsqueeze(2).to_broadcast([128, n, 64])
    for ti in range(ntiles):
        t0 = ti * 512
        hb, hbb = k.ring("hb", [128, 8, 512], BF16, 2)
        dma_hall(k, hb, hall, t0, 512, hbb)
        xc = []
        for m in range(4):
            w_, wb_, cols = wsel[m]
            ps, pb = PS()
            mm(k, ps, [(w_[:, kc, cols], hb[:, kc, :]) for kc in range(8)], [wb_, hbb], pb)
            xt, xtb = xbc[m]
            k.op("act", lambda e: e.activation(out=xt[:, 3:515], in_=ps, func=AF.Copy), r=[pb], w=[xtb])
            acc, accb = k.ring("cacc", [128, 512], F32, 2)
            k.op("dve", lambda e: e.tensor_scalar(out=acc, in0=xt[:, 0:512], scalar1=cw[:, m, 0:1], scalar2=cb[:, m:m + 1],
                                                  op0=ALU.mult, op1=ALU.add), r=[xtb, cwb, cbb], w=[accb])
            for j in range(1, 4):
                k.op("dve", lambda e: e.scalar_tensor_tensor(out=acc, in0=xt[:, j:j + 512], scalar=cw[:, m, j:j + 1], in1=acc,
                                                             op0=ALU.mult, op1=ALU.add), r=[xtb, cwb, accb], w=[accb])
            o, ob = k.ring("xc%d" % m, [128, 512], F32, 2)
            k.op("act", lambda e: e.activation(out=o, in_=acc, func=AF.Silu), r=[accb], w=[ob])
            k.op("pool", lambda e: e.tensor_copy(out=xt[:, 0:3], in_=xt[:, 512:515]), r=[xtb], w=[xtb])
            xc.append((o, ob))
        psd, pbd = PS()
        for c in range(4):
            mm(k, psd[:, c * 4:(c + 1) * 4], [(hb[:, kc, c * 128:(c + 1) * 128], wdt[:, kc, :]) for kc in range(8)], [wdtb, hbb], pbd)
        dt, dtb_ = k.ring("dt", [128, 16], F32, 2)
        k.op("dve", lambda e: e.tensor_tensor(out=dt, in0=psd[:, 0:16], in1=dtb, op=ALU.add), r=[pbd, dtbb], w=[dtb_])
        k.op("act", lambda e: e.activation(out=dt, in_=dt, func=AF.Exp), r=[dtb_], w=[dtb_])
        k.op("act", lambda e: e.activation(out=dt, in_=dt, func=AF.Ln, bias=c_one, scale=1.0), r=[dtb_, cx.b_consts], w=[dtb_])
        da, dab = k.ring("da", [128, 16], F32, 2)
        k.op("dve", lambda e: e.tensor_tensor(out=da, in0=dt, in1=arep, op=ALU.mult), r=[dtb_, arepb], w=[dab])
        psa, pba = PS(); mm(k, psa[:, 0:16], [(tri, da)], [trib, dab], pba)
        pst_, pbt_ = PS(); mm(k, pst_[:, 0:16], [(ones_f, da)], [ones_fb, dab], pbt_)
        acs, acsb = k.ring("acs", [128, 16], F32, 2)
        k.op("act", lambda e: e.activation(out=acs, in_=psa[:, 0:16], func=AF.Copy), r=[pba], w=[acsb])
        eacs, eacsb = k.ring("eacs", [128, 16], F32, 2)
        k.op("act", lambda e: e.activation(out=eacs, in_=acs, func=AF.Exp), r=[acsb], w=[eacsb])
        decs, decsb = k.ring("decs", [128, 16], F32, 2)
        k.op("dve", lambda e: e.tensor_tensor(out=decs, in0=pst_[:, 0:16], in1=acs, op=ALU.subtract), r=[pbt_, acsb], w=[decsb])
        k.op("act", lambda e: e.activation(out=decs, in_=decs, func=AF.Exp), r=[decsb], w=[decsb])
        cd, cdb = k.ring("cd", [128, 16], F32, 2)
        k.op("act", lambda e: e.activation(out=cd, in_=pst_[:, 0:16], func=AF.Exp), r=[pbt_], w=[cdb])
        xtok, xtokb = k.ring("xtok", [128, 4, 256], F32, 2)
        btok, btokb = k.ring("btok", [128, 4, 128], F32, 2)
        for half in range(2):
            ps, pb = PS()
            for ci in range(2):
                c = half * 2 + ci
                for m in range(2):
                    k.op("pe", lambda e: e.transpose(out=ps[:, ci * 256 + m * 128:ci * 256 + (m + 1) * 128],
                                                     in_=xc[m][0][:, c * 128:(c + 1) * 128], identity=cx.ident),
                         r=[xc[m][1], cx.b_ident], w=[pb], inc=True)
            k.op("act", lambda e: e.activation(out=xtok[:, half * 2:half * 2 + 2, :], in_=ps.rearrange("p (a b) -> p a b", a=2), func=AF.Copy),
                 r=[pb], w=[xtokb])
        ps, pb = PS()
        for c in range(4):
            k.op("pe", lambda e: e.transpose(out=ps[:, c * 128:(c + 1) * 128], in_=xc[2][0][:, c * 128:(c + 1) * 128], identity=cx.ident),
                 r=[xc[2][1], cx.b_ident], w=[pb], inc=True)
        k.op("act", lambda e: e.activation(out=btok, in_=ps.rearrange("p (a b) -> p a b", a=4), func=AF.Copy), r=[pb], w=[btokb])
        xd, xdb = k.ring("xd", [128, 4, 256], F32, 2)
        xdd, xddb = k.ring("xdd", [128, 4, 256], F32, 2)
        v16 = lambda a: a.rearrange("p c (h d) -> p (c h) d", d=64)
        k.op("dve", lambda e: e.tensor_tensor(out=v16(xd), in0=v16(xtok), in1=bc64(dt, 16), op=ALU.mult), r=[xtokb, dtb_], w=[xdb])
        k.op("pool", lambda e: e.tensor_tensor(out=v16(xdd), in0=v16(xd), in1=bc64(decs, 16), op=ALU.mult), r=[xdb, decsb], w=[xddb])
        Bc, Bcb = xc[2]; Cc, Ccb = xc[3]
        v4 = lambda a_: a_.rearrange("p (h d) -> p h d", d=64)

        def stage_a(c):
            cs = slice(c * 128, (c + 1) * 128)
            ps, pb = PS(); mm(k, ps[:, 0:128], [(Bc[:, cs], Cc[:, cs])], [Bcb, Ccb], pb)
            cbm, cbmb = k.ring("cbm", [128, 128], F32, 3)
            k.op("dve", lambda e: e.tensor_tensor(out=cbm, in0=ps[:, 0:128], in1=tri, op=ALU.mult), r=[pb, trib], w=[cbmb])
            pdf, pdfb = PS()
            for h in range(4):
                lh, lhb = k.ring("lh", [128, 128], F32, 4)
                if h % 2 == 0:
                    k.op("dve", lambda e: e.tensor_scalar(out=lh, in0=su, scalar1=da[:, c * 4 + h:c * 4 + h + 1], scalar2=None, op0=ALU.mult),
                         r=[sub_, dab], w=[lhb])
                else:
                    k.op("act", lambda e: e.activation(out=lh, in_=su, func=AF.Copy, scale=da[:, c * 4 + h:c * 4 + h + 1]),
                         r=[sub_, dab], w=[lhb])
                mm(k, pdf[:, h * 128:(h + 1) * 128], [(lh, tri)], [lhb, trib], pdfb)
            seg, segb = k.ring("seg", [128, 4, 128], F32, 3)
            k.op("act", lambda e: e.activation(out=seg, in_=pdf.rearrange("p (a b) -> p a b", a=4), func=AF.Exp), r=[pdfb], w=[segb])
            k.op("dve", lambda e: e.tensor_tensor(out=seg, in0=seg, in1=cbm.unsqueeze(1).to_broadcast([128, 4, 128]), op=ALU.mult),
                 r=[segb, cbmb], w=[segb])
            return seg, segb

        def stage_b(c, seg, segb):
            cs = slice(c * 128, (c + 1) * 128)
            py, pyb = PS()
            for h in range(4):
                mm(k, py[:, h * 64:(h + 1) * 64], [(seg[:, h, :], xd[:, c, h * 64:(h + 1) * 64])], [segb, xdb], pyb)
            po, pob = PS(); mm(k, po[:, 0:256], [(Cc[:, cs], S)], [Ccb, Sb], pob)
            t1, t1b = k.ring("yt1", [128, 256], F32, 2)
            k.op("dve", lambda e: e.tensor_tensor(out=v4(t1), in0=v4(po[:, 0:256]), in1=bc64(eacs[:, c * 4:(c + 1) * 4], 4), op=ALU.mult),
                 r=[pob, eacsb], w=[t1b])
            k.op("dve", lambda e: e.tensor_tensor(out=t1, in0=t1, in1=py[:, 0:256], op=ALU.add), r=[t1b, pyb], w=[t1b])
            t2, t2b = k.ring("yt2", [128, 256], F32, 2)
            k.op("pool", lambda e: e.tensor_tensor(out=v4(t2), in0=v4(xtok[:, c, :]), in1=bc64(dsk, 4), op=ALU.mult), r=[xtokb, dskb], w=[t2b])
            yo, yob = k.ring("yo", [128, 256], BF16, 2)
            k.op("pool", lambda e: e.tensor_tensor(out=yo, in0=t1, in1=t2, op=ALU.add), r=[t1b, t2b], w=[yob])
            r0 = (ti * 4 + c) * 128
            k.dma("sp", yout[r0:r0 + 128, :], yo, r=[yob], sb=yob)
            outb.append(yob)
            pss, pssb = PS(); mm(k, pss[:, 0:256], [(btok[:, c, :], xdd[:, c, :])], [btokb, xddb], pssb)
            k.op("dve", lambda e: e.tensor_tensor(out=v4(S), in0=v4(S), in1=bc64(cd[:, c * 4:(c + 1) * 4], 4), op=ALU.mult), r=[Sb, cdb], w=[Sb])
            k.op("dve", lambda e: e.tensor_tensor(out=S, in0=S, in1=pss[:, 0:256], op=ALU.add), r=[Sb, pssb], w=[Sb])

        pend = None
        for c in range(4):
            cur = (c,) + stage_a(c)
            if pend is not None:
                stage_b(*pend)
            pend = cur
        stage_b(*pend)
    k.wait_all("sp", list({id(b): b for b in outb}.values()))
    return nc


def l2b_inputs(h0T_b, hg, P):
    w = P["ab_w_in"]
    g = hg // 2
    xo = 2328
    rep = lambda v, n: np.ascontiguousarray(np.tile(np.asarray(v, np.float32)[None, :], (128, n)))
    chans = [np.arange(hg * 256, hg * 256 + 128), np.arange(hg * 256 + 128, hg * 256 + 256),
             1024 + g * 128 + np.arange(128), 1280 + g * 128 + np.arange(128)]
    cwf = P["b_conv_w"][:, 0, :]
    convw = np.stack([cwf[:, ch].T for ch in chans], axis=1)
    convb = np.stack([P["b_conv_b"][ch] for ch in chans], axis=1)
    hs = slice(hg * 4, hg * 4 + 4)
    t = np.arange(128)
    return {"hall": h0T_b, "wx": np.ascontiguousarray(w[:, xo + hg * 256: xo + (hg + 1) * 256]),
            "wB": np.ascontiguousarray(w[:, xo + 1024 + g * 128: xo + 1024 + (g + 1) * 128]),
            "wC": np.ascontiguousarray(w[:, xo + 1280 + g * 128: xo + 1280 + (g + 1) * 128]),
            "wdt": np.ascontiguousarray(w[:, 3864 + hg * 4: 3864 + hg * 4 + 4]),
            "convw": np.ascontiguousarray(convw.astype(np.float32)), "convb": np.ascontiguousarray(convb.astype(np.float32)),
            "dtb": rep(P["b_dt_bias"][hs], 4), "alog": rep(P["b_a_log"][hs], 4), "dskip": rep(P["b_d_skip"][hs], 1),
            "tri": (t[:, None] <= t[None, :]).astype(np.float32), "su": (t[:, None] > t[None, :]).astype(np.float32)}


def linear_tile(cx, in_ap, inb, W_dram, KC, M, out_fn):
    k = cx.k
    Wv = W_dram.rearrange("(kc p) m -> p kc m", p=128)
    for mb in range(M // 256):
        w, wb = cx.wload(Wv[:, :, mb * 256:(mb + 1) * 256], [128, KC, 256])
        for m2 in range(2):
            ps, pb = cx.psum()
            mm(k, ps, [(w[:, kc, m2 * 128:(m2 + 1) * 128], in_ap[:, kc, :]) for kc in range(KC)], [wb, inb], pb)
            out_fn(mb * 2 + m2, ps, pb)


def build_l3(nc=None, cx=None, io=None):
    nc, cx, io = _std(nc, cx, io)
    din = io.inp
    x1T = din("x1T", [128, 8, TOK]); h0T = din("h0T", [1024, TOK], BF16).rearrange("(kc p) t -> p kc t", p=128)
    oaall = din("oaall", [4 * 4096, 256], BF16); yall = din("yall", [4 * SEQ, 256], BF16); qsel_d = din("qsel", [128, 4])
    gains = din("gains", [128, 12, 8]); normw_d = din("normw", [128, 8])
    wz = din("wz", [D, D]); wout = din("wout", [1536, D])
    f2 = [din("f2g", [D, DFF]), din("f2u", [D, DFF]), din("f2d", [DFF, D])]
    f1 = [din("f1g", [D, DFF]), din("f1u", [D, DFF]), din("f1d", [DFF, D])]
    x4T = io.out("x4T", [128, 8, TOK], F32)
    h1T = io.out("h1T", [1024, TOK], BF16).rearrange("(kc p) t -> p kc t", p=128)
    k = cx.k
    cx.wring_n = 2
    cx.wstg_n = 1
    g, gb = load_gains(cx, gains)
    nw, nwb = k.tile("normw", [128, 8], F32); k.dma("sp", nw, normw_d, w=[nwb], sb=nwb)
    qsel, qselb = k.tile("qsel", [128, 4], F32); k.dma("sp", qsel, qsel_d, w=[qselb], sb=qselb)
    ones512, ones512b = k.tile("ones512", [128, 128], F32)
    k.op("pool", lambda e: e.memset(ones512, 1.0 / 512), w=[ones512b])
    idq, idqb = k.tile("idq", [128, 4, 128], BF16)
    for j in range(4):
        k.op("dve", lambda e: e.tensor_scalar(out=idq[:, j, :], in0=cx.ident, scalar1=qsel[:, j:j + 1], scalar2=None, op0=ALU.mult),
             r=[cx.b_ident, qselb], w=[idqb])
    c_eps5 = cx.const(1e-5)
    outs = []
    for half in range(2):
        x, xb = k.ring("x_res", [128, 8, 1024], F32, 1)
        k.dma("sp", x, x1T[:, :, half * 1024:(half + 1) * 1024], w=[xb], sb=xb)
        for tt in range(2):
            t0 = half * 1024 + tt * 512
            tq = t0 // 512
            sl = slice(tt * 512, (tt + 1) * 512)
            o, ob = k.ring("ffn_o", [128, 8, 1024], F32, 1)
            act, actb = k.ring("act_bf", [128, 22, 1024], BF16, 1)
            ys = o[:, :, 0:512]
            mo = o[:, :, 512:1024]
            mixin = act[:, 0:6, :].rearrange("p a (b t) -> p (a b) t", t=512)
            h0 = act[:, 6:10, :].rearrange("p a (b t) -> p (a b) t", t=512)
            k.dma("sp", h0, h0T[:, :, t0:t0 + 512], w=[actb], sb=actb)
            for hg in range(4):
                cand, candb = k.ring("ycand", [128, 4, 4, 256], BF16, 1)
                for j in range(4):
                    r0 = (j * 4 + hg) * TOK + t0
                    k.dma("sp", cand[:, j], yall[r0:r0 + 512, :].rearrange("(tb p) c -> p tb c", p=128), w=[candb], sb=candb)
                for hh in range(2):
                    ps, pb = cx.psum()
                    for tb in range(4):
                        mm(k, ps[:, tb * 128:(tb + 1) * 128],
                           [(cand[:, j, tb, hh * 128:(hh + 1) * 128], idq[:, j, :]) for j in range(4)], [candb, idqb], pb)
                    k.op("act", lambda e: e.activation(out=ys[:, hg * 2 + hh, :], in_=ps, func=AF.Copy), r=[pb], w=[ob])
            for kvh in range(2):
                ocand, ocandb = k.ring("ocand", [128, 2, 4, 2, 256], BF16, 1)
                for par in range(2):
                    for j in range(4):
                        r0 = ((j // 2) * 4 + kvh * 2 + par) * 2048 + (j % 2) * 1024 + (2 * tq) * 128
                        k.dma("sp", ocand[:, par, j], oaall[r0:r0 + 256, :].rearrange("(i p) c -> p i c", p=128), w=[ocandb], sb=ocandb)
                for hh in range(2):
                    ps, pb = cx.psum()
                    for u in range(4):
                        par, i2 = u % 2, u // 2
                        mm(k, ps[:, u * 128:(u + 1) * 128],
                           [(ocand[:, par, j, i2, hh * 128:(hh + 1) * 128], idq[:, j, :]) for j in range(4)], [ocandb, idqb], pb)
                    k.op("act", lambda e: e.activation(out=mixin[:, kvh * 2 + hh, :], in_=ps, func=AF.Copy), r=[pb], w=[actb])

            def z_out(mc, ps, pb):
                zs, zsb = k.ring("sg", [128, 512], F32, 3)
                k.op("act", lambda e: e.activation(out=zs, in_=ps, func=AF.Silu), r=[pb], w=[zsb])
                k.op("dve", lambda e: e.tensor_tensor(out=ys[:, mc, :], in0=ys[:, mc, :], in1=zs, op=ALU.mult), r=[zsb, ob], w=[ob])
            linear_tile(cx, h0, actb, wz, 8, D, z_out)
            sq, sqb = k.ring("sq", [128, 8, 512], F32, 1)
            k.op("act", lambda e: e.activation(out=sq, in_=ys, func=AF.Square), r=[ob], w=[sqb])
            for gi in range(2):
                ps, pb = cx.psum()
                mm(k, ps, [(ones512, sq[:, gi * 4 + c, :]) for c in range(4)], [ones512b, sqb], pb)
                rs, rsb = k.ring("rstd", [128, 512], F32, 2)
                k.op("act", lambda e: e.activation(out=rs, in_=ps, func=AF.Sqrt, bias=c_eps5, scale=1.0), r=[pb, cx.b_consts], w=[rsb])
                k.op("dve", lambda e: e.reciprocal(out=rs, in_=rs), r=[rsb], w=[rsb])
                for c in range(4):
                    ch = gi * 4 + c
                    k.op("dve", lambda e: e.scalar_tensor_tensor(out=mixin[:, 4 + ch, :], in0=ys[:, ch, :], scalar=nw[:, ch:ch + 1], in1=rs,
                                                                 op0=ALU.mult, op1=ALU.mult), r=[ob, nwb, rsb], w=[actb])

            def mix_out(mc, ps, pb):
                k.op("act", lambda e: e.activation(out=mo[:, mc, :], in_=ps, func=AF.Copy), r=[pb], w=[ob])
            linear_tile(cx, mixin, actb, wout, 12, D, mix_out)
            post_norm_add(cx, x[:, :, sl], xb, mo, ob, g[:, 3, :], gb, 1.0)
        ffn_half(cx, x, xb, 2, f2[0], f2[1], f2[2], g[:, 4, :], g[:, 5, :], gb)
        ffn_half(cx, x, xb, 2, f1[0], f1[1], f1[2], g[:, 6, :], g[:, 7, :], gb)
        k.dma("sp", x4T[:, :, half * 1024:(half + 1) * 1024], x, r=[xb], sb=xb)
        hh_, hhb = k.ring("h_bf", [128, 8, 1024], BF16, 1)
        for tt in range(2):
            sl = slice(tt * 512, (tt + 1) * 512)
            norm_bf16(cx, x[:, :, sl], xb, g[:, 8, :], gb, hh_[:, :, sl], hhb)
        k.dma("sp", h1T[:, :, half * 1024:(half + 1) * 1024], hh_, r=[hhb], sb=hhb)
        outs += [xb, hhb]
    k.wait_all("sp", outs)
    return nc


def build_l5(nc=None, cx=None, io=None):
    nc, cx, io = _std(nc, cx, io)
    din = io.inp
    x4T = din("x4T", [128, 8, TOK])
    ygall = din("ygall", [4096, TOK], BF16).rearrange("(j hg pc p) t -> j p (hg pc) t", j=4, hg=4, pc=2, p=128)
    qsel_d = din("qsel", [128, 4])
    gains = din("gains", [128, 12, 8]); wo = din("wo", [D, D])
    f2 = [din("f2g", [D, DFF]), din("f2u", [D, DFF]), din("f2d", [DFF, D])]
    outT = io.out("outT", [128, 8, TOK], F32)
    k = cx.k
    cx.wstg_n = 1
    g, gb = load_gains(cx, gains)
    qsel, qselb = k.tile("qsel", [128, 4], F32); k.dma("sp", qsel, qsel_d, w=[qselb], sb=qselb)
    outs = []
    for half in range(2):
        x, xb = k.ring("x_res", [128, 8, 1024], F32, 1)
        k.dma("sp", x, x4T[:, :, half * 1024:(half + 1) * 1024], w=[xb], sb=xb)
        for tt in range(2):
            t0 = half * 1024 + tt * 512
            sl = slice(tt * 512, (tt + 1) * 512)
            o, ob = k.ring("ffn_o", [128, 8, 1024], F32, 1)
            act, actb = k.ring("act_bf", [128, 22, 1024], BF16, 1)
            mo = o[:, :, 512:1024]
            yg = act[:, 6:10, :].rearrange("p a (b t) -> p (a b) t", t=512)
            for j in range(4):
                cand, candb = k.ring("ygcand", [128, 8, 512], BF16, 1)
                k.dma("sp", cand, ygall[j][:, :, t0:t0 + 512], w=[candb], sb=candb)
                if j == 0:
                    k.op("dve", lambda e: e.tensor_scalar(out=yg, in0=cand, scalar1=qsel[:, 0:1], scalar2=None, op0=ALU.mult),
                         r=[candb, qselb], w=[actb])
                else:
                    k.op("dve", lambda e: e.scalar_tensor_tensor(out=yg, in0=cand, scalar=qsel[:, j:j + 1], in1=yg, op0=ALU.mult, op1=ALU.add),
                         r=[candb, qselb, actb], w=[actb])

            def mix_out(mc, ps, pb):
                k.op("act", lambda e: e.activation(out=mo[:, mc, :], in_=ps, func=AF.Copy), r=[pb], w=[ob])
            linear_tile(cx, yg, actb, wo, 8, D, mix_out)
            post_norm_add(cx, x[:, :, sl], xb, mo, ob, g[:, 9, :], gb, 1.0)
        ffn_half(cx, x, xb, 2, f2[0], f2[1], f2[2], g[:, 10, :], g[:, 11, :], gb)
        k.dma("sp", outT[:, :, half * 1024:(half + 1) * 1024], x, r=[xb], sb=xb)
        outs.append(xb)
    k.wait_all("sp", outs)
    return nc


def _run(nc, in_maps):
    res = run_bass_kernel_spmd(nc, in_maps, core_ids=list(range(NCORE)))
    return res.results


def kernel_unfused(**inp):
    import ml_dtypes
    f32 = lambda a: np.ascontiguousarray(np.asarray(a, dtype=np.float32))
    I = {k_: f32(v) for k_, v in inp.items()}
    x = I["x"].reshape(16384, D)
    g = gains_layout(I["norm_gains"])
    tok = [slice(c * TOK, (c + 1) * TOK) for c in range(NCORE)]
    grp = lambda lst, b: np.ascontiguousarray(np.concatenate(lst[4 * b:4 * b + 4], axis=0))
    qsel = []
    for c in range(NCORE):
        q_ = np.zeros((128, 4), np.float32); q_[:, c % 4] = 1.0
        qsel.append(q_)
    r1 = _run(build_l1(), [{"xT": fm(x[tok[c]]), "gains": g, "wg": I["ffn1_w_gate"][0], "wu": I["ffn1_w_up"][0],
                            "wd": I["ffn1_w_down"][0]} for c in range(NCORE)])
    x1T = [np.asarray(r["x1T"]) for r in r1]
    h0loc = [np.asarray(r["h0T"]) for r in r1]
    h0all = [grp(h0loc, b) for b in range(2)]
    PA = {k_: I[k_][0] for k_ in I if k_.startswith("a_") or k_.startswith("ab_") or k_.startswith("b_")}
    C = nsa_consts()
    r2a = _run(build_l2a(), [l2a_inputs(h0all[c // 4], (c % 4) // 2, c % 2, PA, C) for c in range(NCORE)])
    r2b = _run(build_l2b(), [l2b_inputs(h0all[c // 4], c % 4, PA) for c in range(NCORE)])
    oaall = [grp([np.asarray(r["oa"]) for r in r2a], b) for b in range(2)]
    yall = [grp([np.asarray(r["y"]) for r in r2b], b) for b in range(2)]
    w_in = I["ab_w_in"][0]
    m3 = []
    for c in range(NCORE):
        m3.append({"x1T": x1T[c], "h0T": h0loc[c], "oaall": oaall[c // 4], "yall": yall[c // 4], "qsel": qsel[c], "gains": g,
                   "normw": np.ascontiguousarray(I["b_norm_w"][0].reshape(8, 128).T),
                   "wz": np.ascontiguousarray(w_in[:, 1304:2328]), "wout": I["ab_w_out"][0],
                   "f2g": I["ffn2_w_gate"][0], "f2u": I["ffn2_w_up"][0], "f2d": I["ffn2_w_down"][0],
                   "f1g": I["ffn1_w_gate"][1], "f1u": I["ffn1_w_up"][1], "f1d": I["ffn1_w_down"][1]})
    r3 = _run(build_l3(), m3)
    x4T = [np.asarray(r["x4T"]) for r in r3]
    h1all = [grp([np.asarray(r["h1T"]) for r in r3], b) for b in range(2)]
    PC = {k_: I[k_][0] for k_ in I if k_.startswith("c_")}
    r4 = _run(build_l4(), [l4_inputs(h1all[c // 4], c % 4, PC) for c in range(NCORE)])
    ygall = [grp([np.asarray(r["ygT"]) for r in r4], b) for b in range(2)]
    m5 = []
    for c in range(NCORE):
        m5.append({"x4T": x4T[c], "ygall": ygall[c // 4], "qsel": qsel[c], "gains": g,
                   "wo": I["c_w_o"][0], "f2g": I["ffn2_w_gate"][1], "f2u": I["ffn2_w_up"][1], "f2d": I["ffn2_w_down"][1]})
    r5 = _run(build_l5(), m5)
    out = np.concatenate([unfm(np.asarray(r["outT"])) for r in r5], axis=0)
    return np.ascontiguousarray(out.reshape(2, SEQ, D).astype(np.float32))


RG = [[0, 1, 2, 3], [4, 5, 6, 7]]


def build_fused(upto=99):
    nc = bass.Bass("TRN2", target_bir_lowering=False, num_devices=NCORE)
    k = K(nc)
    idram = lambda n, sh, dt: nc.dram_tensor(n, list(sh), dt, kind="Internal").ap()
    x1T = idram("i_x1T", [128, 8, TOK], F32)
    h0loc = idram("i_h0loc", [1024, TOK], BF16); h0all = idram("i_h0all", [4096, TOK], BF16)
    oaloc = idram("i_oaloc", [NQB * 128, 256], BF16); oaall = idram("i_oaall", [4 * 4096, 256], BF16)
    yloc = idram("i_yloc", [SEQ, 256], BF16); yall = idram("i_yall", [4 * SEQ, 256], BF16)
    x4T = idram("i_x4T", [128, 8, TOK], F32)
    h1loc = idram("i_h1loc", [1024, TOK], BF16); h1all = idram("i_h1all", [4096, TOK], BF16)
    ygloc = idram("i_ygloc", [1024, TOK], BF16); ygall = idram("i_ygall", [4096, TOK], BF16)
    ccsem = Sem(nc.alloc_semaphore(name="ccsem"), "cc")
    k.dall = [ccsem]; k.dused = []; k.dfree = []

    def allgather(src, dst, wait=True):
        rows, cols = src.shape
        R = (1 << 20) // (cols * mybir.dt.size(src.dtype))
        k.barrier()
        for i in range(rows // R):
            ins = nc.gpsimd.collective_compute("AllGather", ALU.bypass, replica_groups=RG,
                                               ins=[src[i * R:(i + 1) * R, :]], outs=[dst[i * 4 * R:(i + 1) * 4 * R, :]])
            ins.then_inc(ccsem.h, 1)
            ccsem.total += 1
        if wait:
            k.barrier()

    def phase(fn, pre, ext, **kw):
        k.begin_phase()
        cx = Ctx(nc, k)
        fn(nc=nc, cx=cx, io=IO(nc, pre, ext), **kw)
        k.end_phase()

    steps = [lambda: phase(build_l1, "l1_", {"x1T": x1T, "h0T": h0loc}),
             lambda: allgather(h0loc, h0all),
             lambda: phase(build_l2a, "l2a_", {"hall": h0all, "oa": oaloc}),
             lambda: allgather(oaloc, oaall, wait=False),
             lambda: phase(build_l2b, "l2b_", {"hall": h0all, "y": yloc}),
             lambda: allgather(yloc, yall),
             lambda: phase(build_l3, "l3_", {"x1T": x1T, "h0T": h0loc, "oaall": oaall, "yall": yall, "x4T": x4T, "h1T": h1loc}),
             lambda: allgather(h1loc, h1all),
             lambda: phase(build_l4, "l4_", {"hall": h1all, "ygT": ygloc}),
             lambda: allgather(ygloc, ygall),
             lambda: phase(build_l5, "l5_", {"x4T": x4T, "ygall": ygall})]
    for st in steps[:upto]:
        st()
    if upto < len(steps):
        nc.dram_tensor("l5_outT", [128, 8, TOK], F32, kind="ExternalOutput")
    return nc


def kernel(**inp):
    f32 = lambda a: np.ascontiguousarray(np.asarray(a, dtype=np.float32))
    I = {k_: f32(v) for k_, v in inp.items()}
    x = I["x"].reshape(16384, D)
    g = gains_layout(I["norm_gains"])
    PA = {k_: I[k_][0] for k_ in I if k_.startswith("a_") or k_.startswith("ab_") or k_.startswith("b_")}
    PC = {k_: I[k_][0] for k_ in I if k_.startswith("c_")}
    C = nsa_consts()
    w_in = I["ab_w_in"][0]
    in_maps = []
    for c in range(NCORE):
        m = {}
        qs = np.zeros((128, 4), np.float32); qs[:, c % 4] = 1.0
        m.update({"l1_" + k_: v for k_, v in {"xT": fm(x[c * TOK:(c + 1) * TOK]), "gains": g, "wg": I["ffn1_w_gate"][0],
                                             "wu": I["ffn1_w_up"][0], "wd": I["ffn1_w_down"][0]}.items()})
        a = l2a_inputs(None, (c % 4) // 2, c % 2, PA, C); a.pop("hall")
        m.update({"l2a_" + k_: v for k_, v in a.items()})
        b_ = l2b_inputs(None, c % 4, PA); b_.pop("hall")
        m.update({"l2b_" + k_: v for k_, v in b_.items()})
        m.update({"l3_" + k_: v for k_, v in {"qsel": qs, "gains": g,
                  "normw": np.ascontiguousarray(I["b_norm_w"][0].reshape(8, 128).T),
                  "wz": np.ascontiguousarray(w_in[:, 1304:2328]), "wout": I["ab_w_out"][0],
                  "f2g": I["ffn2_w_gate"][0], "f2u": I["ffn2_w_up"][0], "f2d": I["ffn2_w_down"][0],
                  "f1g": I["ffn1_w_gate"][1], "f1u": I["ffn1_w_up"][1], "f1d": I["ffn1_w_down"][1]}.items()})
        d4 = l4_inputs(None, c % 4, PC); d4.pop("hall")
        m.update({"l4_" + k_: v for k_, v in d4.items()})
        m.update({"l5_" + k_: v for k_, v in {"qsel": qs, "gains": g, "wo": I["c_w_o"][0], "f2g": I["ffn2_w_gate"][1],
                                             "f2u": I["ffn2_w_up"][1], "f2d": I["ffn2_w_down"][1]}.items()})
        in_maps.append(m)
    import os
    upto = int(os.environ.get('FUSED_UPTO', '99'))
    nc_ = build_fused(upto)
    if upto < 99:
        names = {a.memorylocations[0].name for a in nc_.allocations if hasattr(a, 'memorylocations') and a.memorylocations}
        in_maps = [{k_: v for k_, v in m.items() if k_ in names} for m in in_maps]
    res = _run(nc_, in_maps)
    out = np.concatenate([unfm(np.asarray(r["l5_outT"])) for r in res], axis=0)
    return np.ascontiguousarray(out.reshape(2, SEQ, D).astype(np.float32))
```

```python
import numpy as np
import concourse.bass as bass
import concourse.mybir as mybir
from concourse.bass_utils import run_bass_kernel_spmd

F32 = mybir.dt.float32
BF16 = mybir.dt.bfloat16
AF = mybir.ActivationFunctionType
ALU = mybir.AluOpType
AX = mybir.AxisListType

D = 1024
DFF = 2816
NCORE = 8
TOK = 2048
EPS = 1e-6


class Buf:
    __slots__ = ("name", "lw", "rd", "dsem", "lw_dma")

    def __init__(self, name):
        self.name = name
        self.lw = None
        self.rd = []
        self.dsem = None
        self.lw_dma = False


class Sem:
    __slots__ = ("h", "total", "name")

    def __init__(self, h, name):
        self.h = h
        self.total = 0
        self.name = name


class K:
    def __init__(self, nc):
        self.nc = nc
        self.engs = {"pe": nc.tensor, "act": nc.scalar, "dve": nc.vector,
                     "pool": nc.gpsimd, "sp": nc.sync}
        self.esem = {n: Sem(nc.alloc_semaphore(name="es_" + n), n) for n in self.engs}
        self.waited = {n: {} for n in self.engs}
        self.nsem = 0
        self.ninstr = 0
        self.nwait = 0
        self.rings = {}

    def begin_phase(self):
        import contextlib
        self.phase = getattr(self, "phase", 0) + 1
        self.stack = contextlib.ExitStack()
        self.rings = {}
        self.dfree = getattr(self, "dfree", [])

    def end_phase(self):
        self.barrier()
        self.stack.close()
        self.stack = None
        self.dfree = list(self.dused)
        self.dused = []

    def barrier(self):
        sems = list(self.esem.values()) + list(getattr(self, "dall", []))
        for eng in self.engs:
            needs = {s_: s_.total for s_ in sems if s_.total > 0}
            self._emit_waits(eng, needs)

    def sb(self, name, shape, dt):
        if getattr(self, "stack", None) is not None:
            return self.stack.enter_context(self.nc.sbuf_tensor("s%d_%s" % (self.phase, name), list(shape), dt)).ap()
        return self._sb_static(name, shape, dt)

    def _sb_static(self, name, shape, dt):
        return self.nc.alloc_sbuf_tensor("s_" + name, list(shape), dt).ap()

    def ps(self, name, shape, dt=F32):
        if getattr(self, "stack", None) is not None:
            return self.stack.enter_context(self.nc.psum_tensor("p%d_%s" % (self.phase, name), list(shape), dt)).ap()
        return self.nc.alloc_psum_tensor("p_" + name, list(shape), dt).ap()

    def tile(self, name, shape, dt):
        return self.sb(name, shape, dt), Buf(name)

    def ring(self, name, shape, dt, n, psum=False):
        if name not in self.rings:
            sl = []
            for i in range(n):
                nm = "%s_%d" % (name, i)
                ap = self.ps(nm, shape, dt) if psum else self.sb(nm, shape, dt)
                sl.append((ap, Buf(nm)))
            self.rings[name] = [sl, 0]
        r = self.rings[name]
        s = r[0][r[1] % len(r[0])]
        r[1] += 1
        return s

    def _dsem(self, b, q="sp"):
        if b.dsem is None:
            if not hasattr(self, "dall"):
                self.dall, self.dused, self.dfree = [], [], getattr(self, "dfree", [])
            if self.dfree and q != "pool":
                b.dsem = self.dfree.pop()
            else:
                b.dsem = Sem(self.nc.alloc_semaphore(name="ds%d" % self.nsem), b.name)
                self.nsem += 1
                self.dall.append(b.dsem)
            if q != "pool":
                self.dused.append(b.dsem)
        return b.dsem

    def _need(self, eng, tok, needs):
        if tok is None:
            return
        s, v = tok
        if v is None:
            v = s.total
        if eng == "pe" and s is self.esem["pe"]:
            return
        if v > needs.get(s, 0):
            needs[s] = v

    def _emit_waits(self, eng, needs):
        e = self.engs[eng]
        w = self.waited[eng]
        for s, v in needs.items():
            if w.get(s, 0) >= v:
                continue
            e.wait_ge(s.h, v)
            self.nwait += 1
            w[s] = v

    def op(self, eng, fn, r=(), w=(), inc=True):
        needs = {}
        for b in r:
            self._need(eng, b.lw, needs)
        for b in w:
            self._need(eng, b.lw, needs)
            for t in b.rd:
                self._need(eng, t, needs)
        self._emit_waits(eng, needs)
        ins = fn(self.engs[eng])
        s = self.esem[eng]
        if inc:
            s.total += 1
            ins.then_inc(s.h, 1)
            tok = (s, s.total)
        else:
            tok = (s, s.total + 1)
        for b in r:
            b.rd.append(tok)
            if len(b.rd) > 24:
                b.rd = self._compact(b.rd)
        for b in w:
            b.lw = tok
            b.rd = []
            b.lw_dma = False
        self.ninstr += 1
        return ins

    def _compact(self, toks):
        best = {}
        for s, v in toks:
            if v is None:
                best[s] = None
            elif s not in best or (best[s] is not None and v > best[s]):
                best[s] = v
        return [(s, v) for s, v in best.items()]

    def dma(self, q, out, in_, r=(), w=(), sb=None, **kw):
        s = self._dsem(sb, q)
        needs = {}
        for b in r:
            self._need(q, b.lw, needs)
        for b in w:
            if not (b.lw_dma and b.lw is not None and b.lw[0] is s and not b.rd):
                self._need(q, b.lw, needs)
            for t in b.rd:
                self._need(q, t, needs)
        self._emit_waits(q, needs)
        ins = self.engs[q].dma_start(out=out, in_=in_, **kw)
        s.total += 16
        ins.then_inc(s.h, 16)
        tok = (s, None)
        for b in r:
            b.rd.append(tok)
        for b in w:
            b.lw = tok
            b.rd = []
            b.lw_dma = True
        self.ninstr += 1
        return ins

    def wait_all(self, eng, bufs):
        needs = {}
        for b in bufs:
            self._need(eng, b.lw, needs)
            for t in b.rd:
                self._need(eng, t, needs)
        self._emit_waits(eng, needs)


class Ctx:
    def __init__(self, nc, k=None):
        self.nc = nc
        self.k = k if k is not None else K(nc)
        k = self.k
        self.ones_d, self.b_ones_d = k.tile("ones_d", [128, 128], F32)
        k.op("pool", lambda e: e.memset(self.ones_d, 1.0 / D), w=[self.b_ones_d])
        self.ident, self.b_ident = k.tile("ident", [128, 128], F32)
        k.op("pool", lambda e: e.memset(self.ident, 1.0), w=[self.b_ident])
        k.op("pool", lambda e: e.affine_select(out=self.ident, in_=self.ident, pattern=[[-1, 128]],
                                               compare_op=ALU.is_equal, fill=0.0, base=0,
                                               channel_multiplier=1), r=[self.b_ident], w=[self.b_ident])
        self.dq = 0
        self.consts, self.b_consts = k.tile("consts", [128, 16], F32)
        self.cvals = {}

    def const(self, v):
        if v not in self.cvals:
            i = len(self.cvals)
            self.cvals[v] = i
            self.k.op("pool", lambda e: e.memset(self.consts[:, i:i + 1], float(v)), w=[self.b_consts])
        i = self.cvals[v]
        return self.consts[:, i:i + 1]

    def psum(self):
        return self.k.ring("psum", [128, 512], F32, 8, psum=True)

    def wload(self, dram_ap, shape, alt=False):
        k = self.k
        ap, b = k.ring("wring", [128, 5632], BF16, getattr(self, "wring_n", 3))
        n = shape[1] * shape[2]
        v = ap[:, 0:n].rearrange("p (a b) -> p a b", a=shape[1])
        if alt and n <= 2048:
            st, stb = k.ring("wstg", [128, 2048], F32, getattr(self, "wstg_n", 2))
            sv = st[:, 0:n].rearrange("p (a b) -> p a b", a=shape[1])
            k.dma("sp", sv, dram_ap, w=[stb], sb=stb)
            k.op("dve", lambda e: e.tensor_copy(out=v, in_=sv), r=[stb], w=[b])
        else:
            k.dma("pool", v, dram_ap, w=[b], sb=b)
        return v, b


class IO:
    def __init__(self, nc, pre="", ext=None):
        self.nc, self.pre, self.ext = nc, pre, dict(ext or {})

    def inp(self, name, shape, dt=F32):
        if name in self.ext:
            return self.ext[name]
        return self.nc.dram_tensor(self.pre + name, list(shape), dt, kind="ExternalInput").ap()

    def out(self, name, shape, dt=F32):
        if name in self.ext:
            return self.ext[name]
        return self.nc.dram_tensor(self.pre + name, list(shape), dt, kind="ExternalOutput").ap()


def _std(nc, cx, io):
    if nc is None:
        nc = bass.Bass("TRN2", target_bir_lowering=False)
    if cx is None:
        cx = Ctx(nc)
    if io is None:
        io = IO(nc)
    return nc, cx, io


def hall_tile(hall, t0, n):
    r, col = t0 // TOK, t0 % TOK
    v = hall.rearrange("(i r kk p) t -> r p i kk t", i=4, r=4, kk=2, p=128)
    return v[r][:, :, :, col:col + n]


def dma_hall(k, dst, hall, t0, n, buf, **kw):
    src = hall_tile(hall, t0, n)
    for i in range(4):
        k.dma("sp", dst[:, 2 * i:2 * i + 2, :], src[:, i], w=[buf], sb=buf, **kw)


def rms_rstd(cx, x_ap, xb, eps=EPS):
    k = cx.k
    T = x_ap.shape[2]
    sq, sqb = k.ring("sq", [128, 8, 512], F32, 1)
    k.op("act", lambda e: e.activation(out=sq[:, :, 0:T], in_=x_ap, func=AF.Square), r=[xb], w=[sqb])
    ps, pb = cx.psum()
    for c in range(8):
        k.op("pe", lambda e: e.matmul(ps[:, 0:T], lhsT=cx.ones_d, rhs=sq[:, c, 0:T], start=(c == 0), stop=(c == 7)),
             r=[sqb, cx.b_ones_d], w=[pb], inc=(c == 7))
    rs, rsb = k.ring("rstd", [128, 512], F32, 2)
    c_eps = cx.const(eps)
    k.op("act", lambda e: e.activation(out=rs[:, 0:T], in_=ps[:, 0:T], func=AF.Sqrt, bias=c_eps, scale=1.0),
         r=[pb, cx.b_consts], w=[rsb])
    k.op("dve", lambda e: e.reciprocal(out=rs[:, 0:T], in_=rs[:, 0:T]), r=[rsb], w=[rsb])
    return rs[:, 0:T], rsb


def norm_bf16(cx, x_ap, xb, g_ap, gb, out_ap, outb):
    k = cx.k
    rs, rsb = rms_rstd(cx, x_ap, xb)
    for c in range(8):
        eng = "dve"
        k.op(eng, lambda e: e.scalar_tensor_tensor(out=out_ap[:, c, :], in0=x_ap[:, c, :], scalar=g_ap[:, c:c + 1],
                                                   in1=rs, op0=ALU.mult, op1=ALU.mult),
             r=[xb, gb, rsb], w=[outb])


def ffn_half(cx, x, xb, NT, wg, wu, wd, g_in, g_out, gb):
    k = cx.k
    T = NT * 512
    h, hb = k.ring("h_bf", [128, 8, 1024], BF16, 1)
    act, actb = k.ring("act_bf", [128, 22, 1024], BF16, 1)
    for tt in range(NT):
        sl = slice(tt * 512, (tt + 1) * 512)
        norm_bf16(cx, x[:, :, sl], xb, g_in, gb, h[:, :, sl], hb)
    wgv = wg.rearrange("(kc p) f -> p kc f", p=128)
    wuv = wu.rearrange("(kc p) f -> p kc f", p=128)
    for fb in range(11):
        gw, gwb = cx.wload(wgv[:, :, fb * 256:(fb + 1) * 256], [128, 8, 256])
        uw, uwb = cx.wload(wuv[:, :, fb * 256:(fb + 1) * 256], [128, 8, 256], alt=True)
        for tt in range(NT):
            sl = slice(tt * 512, (tt + 1) * 512)
            for fc in range(2):
                f = fb * 2 + fc
                pg, pgb = cx.psum()
                pu, pub = cx.psum()
                for c in range(8):
                    k.op("pe", lambda e: e.matmul(pg, lhsT=gw[:, c, fc * 128:(fc + 1) * 128], rhs=h[:, c, sl],
                                                  start=(c == 0), stop=(c == 7)), r=[gwb, hb], w=[pgb], inc=(c == 7))
                for c in range(8):
                    k.op("pe", lambda e: e.matmul(pu, lhsT=uw[:, c, fc * 128:(fc + 1) * 128], rhs=h[:, c, sl],
                                                  start=(c == 0), stop=(c == 7)), r=[uwb, hb], w=[pub], inc=(c == 7))
                sg, sgb = k.ring("sg", [128, 512], F32, 3)
                k.op("act", lambda e: e.activation(out=sg, in_=pg, func=AF.Silu), r=[pgb], w=[sgb])
                k.op("dve", lambda e: e.tensor_tensor(out=act[:, f, sl], in0=sg, in1=pu, op=ALU.mult),
                     r=[sgb, pub], w=[actb])
    wdv = wd.rearrange("(fc p) d -> p fc d", p=128)
    o, ob = k.ring("ffn_o", [128, 8, 1024], F32, 1)
    for db in range(4):
        dw, dwb = cx.wload(wdv[:, :, db * 256:(db + 1) * 256], [128, 22, 256])
        for tt in range(NT):
            sl = slice(tt * 512, (tt + 1) * 512)
            for dc in range(2):
                d = db * 2 + dc
                po, pob = cx.psum()
                for f in range(22):
                    k.op("pe", lambda e: e.matmul(po, lhsT=dw[:, f, dc * 128:(dc + 1) * 128], rhs=act[:, f, sl],
                                                  start=(f == 0), stop=(f == 21)), r=[dwb, actb], w=[pob], inc=(f == 21))
                k.op("act", lambda e: e.activation(out=o[:, d, sl], in_=po, func=AF.Copy), r=[pob], w=[ob])
    for tt in range(NT):
        sl = slice(tt * 512, (tt + 1) * 512)
        post_norm_add(cx, x[:, :, sl], xb, o[:, :, sl], ob, g_out, gb, 0.5)


def post_norm_add(cx, x_ap, xb, o_ap, ob, g_ap, gb, coef):
    k = cx.k
    rs, rsb = rms_rstd(cx, o_ap, ob)
    for c in range(8):
        eng = "dve"
        tmp, tb = k.ring("pn_tmp" + eng, [128, 512], F32, 2)
        T = o_ap.shape[2]
        k.op(eng, lambda e: e.scalar_tensor_tensor(out=tmp[:, 0:T], in0=o_ap[:, c, :], scalar=g_ap[:, c:c + 1], in1=rs,
                                                   op0=ALU.mult, op1=ALU.mult), r=[ob, gb, rsb], w=[tb])
        k.op(eng, lambda e: e.scalar_tensor_tensor(out=x_ap[:, c, :], in0=tmp[:, 0:T], scalar=float(coef), in1=x_ap[:, c, :],
                                                   op0=ALU.mult, op1=ALU.add), r=[tb, xb], w=[xb])


def load_gains(cx, gains_dram):
    k = cx.k
    g, gb = k.tile("gains", [128, 12, 8], F32)
    k.dma("sp", g, gains_dram, w=[gb], sb=gb)
    return g, gb


def build_l1(nc=None, cx=None, io=None):
    nc, cx, io = _std(nc, cx, io)
    xT = io.inp("xT", [128, 8, TOK]); gains = io.inp("gains", [128, 12, 8])
    wg = io.inp("wg", [D, DFF]); wu = io.inp("wu", [D, DFF]); wd = io.inp("wd", [DFF, D])
    x1T = io.out("x1T", [128, 8, TOK], F32)
    h0T = io.out("h0T", [1024, TOK], BF16).rearrange("(kc p) t -> p kc t", p=128)
    k = cx.k
    g, gb = load_gains(cx, gains)
    outs = []
    for half in range(2):
        hs = slice(half * 1024, (half + 1) * 1024)
        x, xb = k.ring("x_res", [128, 8, 1024], F32, 1)
        k.dma("sp", x, xT[:, :, hs], w=[xb], sb=xb)
        ffn_half(cx, x, xb, 2, wg, wu, wd, g[:, 0, :], g[:, 1, :], gb)
        k.dma("sp", x1T[:, :, hs], x, r=[xb], sb=xb)
        hh, hhb = k.ring("h_bf", [128, 8, 1024], BF16, 1)
        for tt in range(2):
            sl = slice(tt * 512, (tt + 1) * 512)
            norm_bf16(cx, x[:, :, sl], xb, g[:, 2, :], gb, hh[:, :, sl], hhb)
        k.dma("sp", h0T[:, :, hs], hh, r=[hhb], sb=hhb)
        outs += [xb, hhb]
    k.wait_all("sp", outs)
    return nc


def fm(a):
    t = a.shape[0]
    return np.ascontiguousarray(a.T.reshape(8, 128, t).transpose(1, 0, 2))


def unfm(a):
    t = a.shape[2]
    return np.ascontiguousarray(a.transpose(1, 0, 2).reshape(1024, t).T)


def gains_layout(norm_gains):
    g = norm_gains.reshape(12, 8, 128).transpose(2, 0, 1)
    return np.ascontiguousarray(g)


SEQ = 8192
RW_EXPC = 0.6065306597126334
GN_EPS = 64e-5


def mm(k, ps_ap, pairs, rbufs, wbuf):
    n = len(pairs)
    for i, (l, r) in enumerate(pairs):
        k.op("pe", lambda e: e.matmul(ps_ap, lhsT=l, rhs=r, start=(i == 0), stop=(i == n - 1)),
             r=rbufs, w=[wbuf], inc=(i == n - 1))


class _Stop(Exception):
    pass


def build_l4(ntiles=16, stop=99, nc=None, cx=None, io=None):
    try:
        return _build_l4(ntiles, stop, nc, cx, io)
    except _Stop as e:
        return e.args[0]


def _build_l4(ntiles=16, stop=99, nc=None, cx=None, io=None):
    nc, cx, io = _std(nc, cx, io)
    dt_in = io.inp
    hall = dt_in("hall", [4096, TOK], BF16)
    hzero = dt_in("hzero", [1024, 1], BF16)
    W = {"r": dt_in("wr", [D, 256]), "k": dt_in("wk", [D, 256]), "v": dt_in("wv", [D, 256]),
         "w1": dt_in("w1", [D, 64]), "a1": dt_in("a1", [D, 64]), "g1": dt_in("g1", [D, 160])}
    w2d = dt_in("w2", [64, 256]); a2d = dt_in("a2", [64, 256]); g2d = dt_in("g2", [160, 256])
    mud = dt_in("mu", [128, 6, 8])
    vecd = dt_in("vecs", [128, 7, 2])
    maskd = dt_in("masks", [128, 3, 128])
    bonesd = dt_in("bones", [128, 128])
    resetd = dt_in("resetm", [128, 512])
    ygq = io.out("ygT", [1024, TOK], BF16).rearrange("(j c p) t -> j p c t", j=4, c=2, p=128)
    k = cx.k
    PS = lambda: k.ring("psr", [128, 512], F32, 4, psum=True)
    ybank = [k.ps("ybank%d" % i, [128, 512]) for i in range(2)]
    ybb = [Buf("ybank%d" % i) for i in range(2)]
    ybank2 = [k.ps("ybankb%d" % i, [128, 512]) for i in range(2)]
    ybb2 = [Buf("ybankb%d" % i) for i in range(2)]

    mu, mub = k.tile("mu", [128, 6, 8], F32); k.dma("sp", mu, mud, w=[mub], sb=mub)
    vec, vecb = k.tile("vecs", [128, 7, 2], F32); k.dma("sp", vec, vecd, w=[vecb], sb=vecb)
    msk, mskb = k.tile("masks", [128, 3, 128], F32); k.dma("sp", msk, maskd, w=[mskb], sb=mskb)
    bones, bonesb = k.tile("bones", [128, 128], F32); k.dma("sp", bones, bonesd, w=[bonesb], sb=bonesb)
    rstm, rstmb = k.tile("resetm", [128, 512], F32); k.dma("sp", rstm, resetd, w=[rstmb], sb=rstmb)
    W0, A0, KK, KA, RK, LNG, LNB = range(7)
    m4 = lambda i: msk[:, i, :].unsqueeze(1).to_broadcast([128, 4, 128])
    id4 = cx.ident.unsqueeze(1).to_broadcast([128, 4, 128])
    c_tiny = cx.const(1e-24)
    c_gneps = cx.const(GN_EPS)

    order = {"r": 0, "w1": 1, "k": 2, "v": 3, "a1": 4, "g1": 5}
    Wa, Wb, Wbuf = {}, {}, {}
    for nm, wd_ in W.items():
        n = wd_.shape[1]
        wa, wab = k.tile("wa_" + nm, [128, 8, n], BF16)
        wb, wbb = k.tile("wb_" + nm, [128, 8, n], BF16)
        Wa[nm], Wb[nm], Wbuf[nm] = wa, wb, [wab, wbb]
    import contextlib
    _outer = getattr(k, "stack", None)
    k.stack = contextlib.ExitStack()
    k.phase = getattr(k, "phase", 0)
    for nm, wd_ in W.items():
        n = wd_.shape[1]
        wa, wb = Wa[nm], Wb[nm]
        wab, wbb = Wbuf[nm]
        st, stb = k.ring("wstage", [128, 8, 256], F32, 1)
        k.dma("sp", st[:, :, 0:n], wd_.rearrange("(kc p) n -> p kc n", p=128), w=[stb], sb=stb)
        tmp, tmpb = k.ring("wstage2", [128, 8, 256], F32, 1)
        i = order[nm]
        k.op("dve", lambda e: e.tensor_tensor(out=tmp[:, :, 0:n], in0=st[:, :, 0:n],
                                              in1=mu[:, i, :].unsqueeze(2).to_broadcast([128, 8, n]), op=ALU.mult),
             r=[stb, mub], w=[tmpb])
        k.op("dve", lambda e: e.tensor_tensor(out=wa, in0=st[:, :, 0:n], in1=tmp[:, :, 0:n], op=ALU.subtract),
             r=[stb, tmpb], w=[wab])
        k.op("act", lambda e: e.activation(out=wb, in_=tmp[:, :, 0:n], func=AF.Copy), r=[tmpb], w=[wbb])
    k.barrier()
    k.stack.close()
    k.stack = _outer
    k.rings.pop("wstage"); k.rings.pop("wstage2")
    w2, w2b = k.tile("w2", [64, 256], BF16); k.dma("pool", w2, w2d, w=[w2b], sb=w2b)
    a2, a2b = k.tile("a2", [64, 256], BF16); k.dma("pool", a2, a2d, w=[a2b], sb=a2b)
    g2a, g2ab = k.tile("g2a", [128, 256], BF16); k.dma("pool", g2a, g2d[0:128, :], w=[g2ab], sb=g2ab)
    g2c, g2cb = k.tile("g2c", [32, 256], BF16); k.dma("pool", g2c, g2d[128:160, :], w=[g2cb], sb=g2cb)

    U = []
    for hd in range(4):
        pp = []
        for j in range(2):
            u, ub = k.tile("U%d_%d" % (hd, j), [128, 64], F32)
            k.op("pool", lambda e: e.memset(u, 0.0), w=[ub])
            pp.append((u, ub))
        U.append(pp)
    ucur = [0, 0, 0, 0]
    PL = []
    for pc in range(2):
        p_, pb_ = k.tile("PL%d" % pc, [128, 9], F32)
        k.op("pool", lambda e: e.memset(p_, 1.0), w=[pb_])
        PL.append((p_, pb_))
    outbufs = []
    if stop == 1:
        raise _Stop(nc)

    for ti in range(ntiles):
        t0 = ti * 512
        hb, hbb = k.ring("hb", [128, 8, 514], BF16, 2)
        dma_hall(k, hb[:, :, 1:513], hall, t0, 512, hbb)
        if t0 == 0:
            k.dma("sp", hb[:, :, 0:1], hzero.rearrange("(kc p) t -> p kc t", p=128), w=[hbb], sb=hbb, allow_slow_non_contiguous=True)
        else:
            dma_hall(k, hb[:, :, 0:1], hall, t0 - 1, 1, hbb, allow_slow_non_contiguous=True)

        def proj_pairs(nm, cols, tok=None):
            prs = []
            for kc in range(8):
                prs.append((Wa[nm][:, kc, cols], hb[:, kc, 1:513]))
                prs.append((Wb[nm][:, kc, cols], hb[:, kc, 0:512]))
            return prs

        FM = {}

        def evac(name, ps, pb, rows=128, func=AF.Copy, bias=None, dt=F32, extra=()):
            o, ob = k.ring("fm_" + name, [128, 512], dt, 1)
            kw = {}
            if bias is not None:
                kw["bias"] = bias
            k.op("act", lambda e: e.activation(out=o[0:rows, :], in_=ps[0:rows, :], func=func, **kw),
                 r=[pb] + list(extra), w=[ob])
            return o, ob

        for pc in range(2):
            cols = slice(pc * 128, (pc + 1) * 128)
            for nm in ("r", "k", "v"):
                ps, pb = PS()
                mm(k, ps, proj_pairs(nm, cols), [hbb] + Wbuf[nm], pb)
                FM[(nm, pc)] = evac("%s%d" % (nm, pc), ps, pb)
        vtok, vtokb = k.ring("vtok", [128, 4, 256], F32, 2)
        for half in range(2):
            ps, pb = PS()
            for bi in range(2):
                blk = half * 2 + bi
                prs = []
                for kc in range(8):
                    prs.append((hb[:, kc, 1 + blk * 128:1 + (blk + 1) * 128], Wa["v"][:, kc, :]))
                    prs.append((hb[:, kc, blk * 128:(blk + 1) * 128], Wb["v"][:, kc, :]))
                mm(k, ps[:, bi * 256:(bi + 1) * 256], prs, [hbb] + Wbuf["v"], pb)
            k.op("act", lambda e: e.activation(out=vtok[:, half * 2:half * 2 + 2, :],
                                               in_=ps.rearrange("p (a b) -> p a b", a=2), func=AF.Copy),
                 r=[pb], w=[vtokb])
        ps, pb = PS(); mm(k, ps[0:64, :], proj_pairs("w1", slice(0, 64)), [hbb] + Wbuf["w1"], pb)
        hw, hwb = evac("hw", ps, pb, rows=64, func=AF.Tanh, dt=BF16)
        ps, pb = PS(); mm(k, ps[0:64, :], proj_pairs("a1", slice(0, 64)), [hbb] + Wbuf["a1"], pb)
        ha, hab = evac("ha", ps, pb, rows=64, dt=BF16)
        ps, pb = PS(); mm(k, ps, proj_pairs("g1", slice(0, 128)), [hbb] + Wbuf["g1"], pb)
        hg0, hg0b = evac("hg0", ps, pb, func=AF.Sigmoid, dt=BF16)
        ps, pb = PS(); mm(k, ps[0:32, :], proj_pairs("g1", slice(128, 160)), [hbb] + Wbuf["g1"], pb)
        hg1, hg1b = evac("hg1", ps, pb, rows=32, func=AF.Sigmoid, dt=BF16)
        for pc in range(2):
            cols = slice(pc * 128, (pc + 1) * 128)
            ps, pb = PS(); mm(k, ps, [(w2[:, cols], hw[0:64, :])], [w2b, hwb], pb)
            FM[("sgw", pc)] = evac("sgw%d" % pc, ps, pb, func=AF.Sigmoid, bias=vec[:, W0, pc:pc + 1], extra=[vecb])
            ps, pb = PS(); mm(k, ps, [(a2[:, cols], ha[0:64, :])], [a2b, hab], pb)
            FM[("a", pc)] = evac("a%d" % pc, ps, pb, func=AF.Sigmoid, bias=vec[:, A0, pc:pc + 1], extra=[vecb])
            ps, pb = PS(); mm(k, ps, [(g2a[:, cols], hg0), (g2c[:, cols], hg1[0:32, :])], [g2ab, g2cb, hg0b, hg1b], pb)
            FM[("g", pc)] = evac("g%d" % pc, ps, pb)

        if stop == 2:
            raise _Stop(nc)
        def tmpt(name, dt=F32, n=2):
            return k.ring("tmp", [128, 512], F32, 9)

        PR = {}
        for pc in range(2):
            r_, rb_ = FM[("r", pc)]; k_, kb_ = FM[("k", pc)]; a_, ab_ = FM[("a", pc)]; sg_, sgb_ = FM[("sgw", pc)]
            kk0, kk0b = tmpt("kk0")
            k.op("dve", lambda e: e.tensor_scalar(out=kk0, in0=k_, scalar1=vec[:, KK, pc:pc + 1], scalar2=None, op0=ALU.mult),
                 r=[kb_, vecb], w=[kk0b])
            sq, sqb = tmpt("sq")
            k.op("pool", lambda e: e.tensor_tensor(out=sq, in0=kk0, in1=kk0, op=ALU.mult), r=[kk0b], w=[sqb])
            ps, pb = PS(); mm(k, ps, [(bones, sq)], [bonesb, sqb], pb)
            rn, rnb = tmpt("rn")
            k.op("act", lambda e: e.activation(out=rn, in_=ps, func=AF.Sqrt, bias=c_tiny, scale=1.0), r=[pb, cx.b_consts], w=[rnb])
            k.op("dve", lambda e: e.reciprocal(out=rn, in_=rn), r=[rnb], w=[rnb])
            kap, kapb = tmpt("kap")
            k.op("dve", lambda e: e.tensor_tensor(out=kap, in0=kk0, in1=rn, op=ALU.mult), r=[kk0b, rnb], w=[kapb])
            am, amb = tmpt("am")
            k.op("dve", lambda e: e.tensor_scalar(out=am, in0=a_, scalar1=-1.0, scalar2=vec[:, KA, pc:pc + 1],
                                                  op0=ALU.add, op1=ALU.mult), r=[ab_, vecb], w=[amb])
            kp, kpb = k.ring("kp%d" % pc, [128, 512], F32, 1)
            k.op("dve", lambda e: e.scalar_tensor_tensor(out=kp, in0=am, scalar=1.0, in1=k_, op0=ALU.add, op1=ALU.mult),
                 r=[amb, kb_], w=[kpb])
            logd, logdb = tmpt("logd")
            k.op("pool", lambda e: e.tensor_scalar(out=logd, in0=sg_, scalar1=-RW_EXPC, scalar2=None, op0=ALU.mult),
                 r=[sgb_], w=[logdb])
            Lc, Lcb = tmpt("Lc")
            k.op("dve", lambda e: e.tensor_tensor_scan(out=Lc, data0=rstm, data1=logd, initial=0.0, op0=ALU.mult, op1=ALU.add),
                 r=[rstmb, logdb], w=[Lcb])
            Lm, Lmb = tmpt("Lm")
            k.op("pool", lambda e: e.tensor_tensor(out=Lm, in0=Lc, in1=logd, op=ALU.subtract), r=[Lcb, logdb], w=[Lmb])
            P_, Pb_ = tmpt("P"); Pi, Pib = tmpt("Pi"); Pp, Ppb = tmpt("Pp")
            k.op("act", lambda e: e.activation(out=P_, in_=Lc, func=AF.Exp), r=[Lcb], w=[Pb_])
            k.op("act", lambda e: e.activation(out=Pi, in_=Lc, func=AF.Exp, scale=-1.0), r=[Lcb], w=[Pib])
            k.op("act", lambda e: e.activation(out=Pp, in_=Lm, func=AF.Exp), r=[Lmb], w=[Ppb])
            pl, plb = PL[pc]
            k.op("dve", lambda e: e.tensor_copy(out=pl[:, 0:1], in_=pl[:, 8:9]), r=[plb], w=[plb])
            k.op("dve", lambda e: e.tensor_copy(out=pl[:, 1:9], in_=P_.rearrange("p (c t) -> p c t", t=64)[:, :, 63]),
                 r=[Pb_, plb], w=[plb])
            rt, rtb = k.ring("rt%d" % pc, [128, 512], F32, 1)
            kt, ktb = k.ring("kt%d" % pc, [128, 512], F32, 1)
            bt, btb = k.ring("bt%d" % pc, [128, 512], F32, 1)
            kkt, kktb = k.ring("kkt%d" % pc, [128, 512], F32, 1)
            k.op("dve", lambda e: e.tensor_tensor(out=rt, in0=r_, in1=P_, op=ALU.mult), r=[rb_, Pb_], w=[rtb])
            k.op("pool", lambda e: e.tensor_tensor(out=kt, in0=kap, in1=Pp, op=ALU.mult), r=[kapb, Ppb], w=[ktb])
            ka_, kab_ = tmpt("ka")
            k.op("pool", lambda e: e.tensor_tensor(out=ka_, in0=kap, in1=a_, op=ALU.mult), r=[kapb, ab_], w=[kab_])
            k.op("dve", lambda e: e.tensor_tensor(out=bt, in0=ka_, in1=Pi, op=ALU.mult), r=[kab_, Pib], w=[btb])
            k.op("pool", lambda e: e.tensor_tensor(out=kkt, in0=kp, in1=Pi, op=ALU.mult), r=[kpb, Pib], w=[kktb])
            rts, rtsb = k.ring("rts%d" % pc, [128, 512], F32, 1)
            kts, ktsb = k.ring("kts%d" % pc, [128, 512], F32, 1)
            plbc = pl[:, 0:8].unsqueeze(2).to_broadcast([128, 8, 64])
            k.op("dve", lambda e: e.tensor_tensor(out=rts.rearrange("p (c t) -> p c t", t=64),
                                                  in0=rt.rearrange("p (c t) -> p c t", t=64), in1=plbc, op=ALU.mult),
                 r=[rtb, plb], w=[rtsb])
            k.op("dve", lambda e: e.tensor_tensor(out=kts.rearrange("p (c t) -> p c t", t=64),
                                                  in0=kt.rearrange("p (c t) -> p c t", t=64), in1=plbc, op=ALU.mult),
                 r=[ktb, plb], w=[ktsb])
            PR[pc] = dict(rt=(rt, rtb), kt=(kt, ktb), bt=(bt, btb), kkt=(kkt, kktb), rts=(rts, rtsb), kts=(kts, ktsb),
                          kp=(kp, kpb))
        if stop == 3:
            raise _Stop(nc)
        for pc in range(2):
            for nm_ in ("rt", "kt", "bt", "kkt"):
                src_, srcb_ = PR[pc][nm_]
                sh, shb = k.ring("sh_%s%d" % (nm_, pc), [128, 512], BF16, 1)
                k.op("act", lambda e: e.activation(out=sh, in_=src_, func=AF.Copy), r=[srcb_], w=[shb])
                PR[pc][nm_ + "_h"] = (sh, shb)
        btok, btokb = k.ring("btok", [128, 4, 256], F32, 1)
        ktok, ktokb = k.ring("ktok", [128, 4, 256], F32, 1)
        for (src, dst, dstb) in (("bt", btok, btokb), ("kkt", ktok, ktokb)):
            for pc in range(2):
                s_, sb_ = PR[pc][src]
                ps, pb = PS()
                for blk in range(4):
                    k.op("pe", lambda e: e.transpose(out=ps[:, blk * 128:(blk + 1) * 128], in_=s_[:, blk * 128:(blk + 1) * 128],
                                                     identity=cx.ident), r=[sb_, cx.b_ident], w=[pb], inc=(blk == 3))
                k.op("act", lambda e: e.activation(out=dst[:, :, pc * 128:(pc + 1) * 128],
                                                   in_=ps.rearrange("p (a b) -> p a b", a=4), func=AF.Copy), r=[pb], w=[dstb])

        if stop == 4:
            raise _Stop(nc)
        HD = {}
        for hd in range(4):
            pc, hp = hd // 2, hd % 2
            rows = slice(hp * 64, hp * 64 + 64)
            hcols = slice(hd * 64, hd * 64 + 64)
            pr = PR[pc]

            def intra(lname, rname, mi, nm, depth=2, odt=F32):
                l_, lb_ = pr[lname + "_h"]; r2, rb2 = pr[rname + "_h"]
                ps, pb = PS()
                for blk in range(4):
                    bs = slice(blk * 128, (blk + 1) * 128)
                    mm(k, ps[:, bs], [(l_[rows, bs], r2[rows, bs])], [lb_, rb2], pb)
                o, ob = k.ring("im_" + nm, [128, 4, 128], odt, depth)
                k.op("dve", lambda e: e.tensor_tensor(out=o, in0=ps.rearrange("p (a b) -> p a b", a=4), in1=m4(mi), op=ALU.mult),
                     r=[pb, mskb], w=[ob])
                return o, ob

            Pm, Pmb = intra("kt", "bt", 0, "P", 2, BF16)
            Qm, Qmb = intra("bt", "kt", 1, "Q", 2, BF16)
            AkT, AkTb = intra("kkt", "kt", 1, "AkT%d" % hd, 1)
            QBT, QBTb = intra("bt", "rt", 2, "QBT%d" % hd, 1)
            QKT, QKTb = intra("kkt", "rt", 2, "QKT%d" % hd, 1)
            Rm, Rmb = k.ring("im_R", [128, 4, 128], F32, 2)
            k.op("pool", lambda e: e.tensor_tensor(out=Rm, in0=id4, in1=Qm, op=ALU.subtract), r=[cx.b_ident, Qmb], w=[Rmb])
            Rh, Rhb = k.ring("im_Rh", [128, 4, 128], BF16, 2)
            k.op("act", lambda e: e.activation(out=Rh, in_=Rm, func=AF.Copy), r=[Rmb], w=[Rhb])
            for lev in range(1, 6):
                if lev < 5:
                    ps, pb = PS()
                    for blk in range(4):
                        bs = slice(blk * 128, (blk + 1) * 128)
                        mm(k, ps[:, bs], [(Pm[:, blk, :], Qm[:, blk, :])], [Pmb, Qmb], pb)
                    Qn, Qnb = k.ring("im_Q", [128, 4, 128], BF16, 2)
                    k.op("act", lambda e: e.activation(out=Qn, in_=ps.rearrange("p (a b) -> p a b", a=4), func=AF.Copy), r=[pb], w=[Qnb])
                ps, pb = PS()
                for blk in range(4):
                    bs = slice(blk * 128, (blk + 1) * 128)
                    mm(k, ps[:, bs], [(Qm[:, blk, :], Pm[:, blk, :])], [Pmb, Qmb], pb)
                Pn, Pnb = k.ring("im_P", [128, 4, 128], BF16, 2)
                k.op("act", lambda e: e.activation(out=Pn, in_=ps.rearrange("p (a b) -> p a b", a=4), func=AF.Copy), r=[pb], w=[Pnb])
                ps, pb = PS()
                for blk in range(4):
                    bs = slice(blk * 128, (blk + 1) * 128)
                    mm(k, ps[:, bs], [(Pn[:, blk, :], Rh[:, blk, :])], [Pnb, Rhb], pb)
                Rn, Rnb = k.ring("im_R", [128, 4, 128], F32, 2) if lev < 5 else k.ring("im_Rfin%d" % hd, [128, 4, 128], F32, 1)
                k.op("dve", lambda e: e.tensor_tensor(out=Rn, in0=ps.rearrange("p (a b) -> p a b", a=4), in1=Rm, op=ALU.add),
                     r=[pb, Rmb], w=[Rnb])
                Pm, Pmb = Pn, Pnb
                if lev < 5:
                    Qm, Qmb = Qn, Qnb
                    Rh, Rhb = k.ring("im_Rh", [128, 4, 128], BF16, 2)
                    k.op("act", lambda e: e.activation(out=Rh, in_=Rn, func=AF.Copy), r=[Rnb], w=[Rhb])
                Rm, Rmb = Rn, Rnb
            if stop == 5:
                raise _Stop(nc)
            HD[hd] = (AkT, AkTb, QBT, QBTb, QKT, QKTb, Rm, Rmb)
        XA = {}
        for hd in range(4):
            hcols = slice(hd * 64, hd * 64 + 64)
            AkT, AkTb = HD[hd][0], HD[hd][1]
            ps, pb = PS()
            for hf in range(2):
                trows = slice(hf * 64, hf * 64 + 64)
                for blk in range(4):
                    mm(k, ps[trows, blk * 64:(blk + 1) * 64], [(AkT[trows, blk, trows], vtok[trows, blk, hcols])], [AkTb, vtokb], pb)
            xa, xab = k.ring("XaAll%d" % hd, [128, 4, 64], F32, 1)
            k.op("act", lambda e: e.activation(out=xa, in_=ps[:, 0:256].rearrange("p (a b) -> p a b", a=4), func=AF.Copy, scale=-1.0),
                 r=[pb], w=[xab])
            XA[hd] = (xa, xab)
        for c in range(8):
            blk, hf = c // 2, c % 2
            trows = slice(hf * 64, hf * 64 + 64)
            diag = slice(hf * 64, hf * 64 + 64)
            tcol = slice(c * 64, c * 64 + 64)
            st = {}
            for hd in range(4):
                pc, hp = hd // 2, hd % 2
                rows = slice(hp * 64, hp * 64 + 64)
                uo, uob = U[hd][ucur[hd]]
                un, unb = U[hd][1 - ucur[hd]]
                ucur[hd] = 1 - ucur[hd]
                kts, ktsb = PR[pc]["kts"]
                ps1b, pb1b = PS()
                mm(k, ps1b[trows, 0:64], [(kts[rows, tcol], uo[rows, :])], [ktsb, uob], pb1b)
                st[hd] = dict(rows=rows, hcols=slice(hd * 64, hd * 64 + 64), pc=pc, uo=uo, uob=uob, un=un, unb=unb, ps1b=ps1b, pb1b=pb1b)
            for hd in range(4):
                d_ = st[hd]
                xa, xab = XA[hd]
                X, Xb = k.ring("X", [128, 64], F32, 4)
                k.op("dve", lambda e: e.tensor_tensor(out=X[trows, :], in0=xa[trows, blk, :], in1=d_["ps1b"][trows, 0:64], op=ALU.subtract),
                     r=[xab, d_["pb1b"]], w=[Xb])
                d_["X"], d_["Xb"] = X, Xb
            for hd in range(4):
                d_ = st[hd]
                Rm, Rmb = HD[hd][6], HD[hd][7]
                ps2, pb2 = PS()
                mm(k, ps2[trows, 0:64], [(Rm[trows, blk, diag], d_["X"][trows, :])], [Rmb, d_["Xb"]], pb2)
                d_["ps2"], d_["pb2"] = ps2, pb2
            for hd in range(4):
                d_ = st[hd]
                SA, SAb = k.ring("SA", [128, 64], F32, 4)
                eng = "dve" if hd % 2 == 0 else "act"
                if eng == "dve":
                    k.op("dve", lambda e: e.tensor_copy(out=SA[trows, :], in_=d_["ps2"][trows, 0:64]), r=[d_["pb2"]], w=[SAb])
                else:
                    k.op("act", lambda e: e.activation(out=SA[trows, :], in_=d_["ps2"][trows, 0:64], func=AF.Copy), r=[d_["pb2"]], w=[SAb])
                d_["SA"], d_["SAb"] = SA, SAb
            for hd in range(4):
                d_ = st[hd]
                rows, hcols = d_["rows"], d_["hcols"]
                ps3, pb3 = PS()
                mm(k, ps3[rows, 0:64], [(ktok[trows, blk, hcols], vtok[trows, blk, hcols]),
                                        (btok[trows, blk, hcols], d_["SA"][trows, :])], [ktokb, btokb, vtokb, d_["SAb"]], pb3)
                d_["ps3"], d_["pb3"] = ps3, pb3
            for hd in range(4):
                d_ = st[hd]
                pc, rows, hcols = d_["pc"], d_["rows"], d_["hcols"]
                rts, rtsb = PR[pc]["rts"]
                QBT, QBTb, QKT, QKTb = HD[hd][2], HD[hd][3], HD[hd][4], HD[hd][5]
                mm(k, ybank[pc][rows, tcol], [(d_["uo"][rows, :], rts[rows, tcol])], [d_["uob"], rtsb], ybb[pc])
                mm(k, ybank2[pc][rows, tcol], [(d_["SA"][trows, :], QBT[trows, blk, diag]),
                                               (vtok[trows, blk, hcols], QKT[trows, blk, diag])],
                   [d_["SAb"], QBTb, QKTb, vtokb], ybb2[pc])
            for hd in range(4):
                d_ = st[hd]
                pc, rows = d_["pc"], d_["rows"]
                pl, plb = PL[pc]
                k.op("dve", lambda e: e.scalar_tensor_tensor(out=d_["un"][rows, :], in0=d_["uo"][rows, :], scalar=pl[rows, c:c + 1],
                                                             in1=d_["ps3"][rows, 0:64], op0=ALU.mult, op1=ALU.add),
                     r=[d_["uob"], plb, d_["pb3"]], w=[d_["unb"]])
        if stop == 6:
            raise _Stop(nc)
        for pc in range(2):
            r_, rb_ = FM[("r", pc)]; v_, vb_ = FM[("v", pc)]; g_, gb_ = FM[("g", pc)]
            kp, kpb = PR[pc]["kp"]
            ysb, ysbb = tmpt("ysb")
            k.op("act", lambda e: e.activation(out=ysb, in_=ybank[pc], func=AF.Copy), r=[ybb[pc]], w=[ysbb])
            k.op("dve", lambda e: e.tensor_tensor(out=ysb, in0=ysb, in1=ybank2[pc], op=ALU.add), r=[ysbb, ybb2[pc]], w=[ysbb])
            ysq, ysqb = tmpt("ysq")
            k.op("act", lambda e: e.activation(out=ysq, in_=ysb, func=AF.Square), r=[ysbb], w=[ysqb])
            psm, pbm = PS(); mm(k, psm, [(bones, ysb)], [bonesb, ysbb], pbm)
            pse, pbe = PS(); mm(k, pse, [(bones, ysq)], [bonesb, ysqb], pbe)
            mean, meanb = tmpt("mean")
            k.op("act", lambda e: e.activation(out=mean, in_=psm, func=AF.Copy, scale=1.0 / 64), r=[pbm], w=[meanb])
            var, varb = tmpt("var")
            k.op("dve", lambda e: e.tensor_tensor(out=var, in0=mean, in1=mean, op=ALU.mult), r=[meanb], w=[varb])
            k.op("dve", lambda e: e.scalar_tensor_tensor(out=var, in0=pse, scalar=1.0 / 64, in1=var, op0=ALU.mult, op1=ALU.subtract),
                 r=[pbe, varb], w=[varb])
            k.op("act", lambda e: e.activation(out=var, in_=var, func=AF.Sqrt, bias=c_gneps, scale=1.0), r=[varb, cx.b_consts], w=[varb])
            k.op("dve", lambda e: e.reciprocal(out=var, in_=var), r=[varb], w=[varb])
            yn, ynb = tmpt("yn")
            k.op("dve", lambda e: e.tensor_tensor(out=yn, in0=ysb, in1=mean, op=ALU.subtract), r=[ysbb, meanb], w=[ynb])
            k.op("dve", lambda e: e.tensor_tensor(out=yn, in0=yn, in1=var, op=ALU.mult), r=[ynb, varb], w=[ynb])
            k.op("dve", lambda e: e.tensor_scalar(out=yn, in0=yn, scalar1=vec[:, LNG, pc:pc + 1], scalar2=vec[:, LNB, pc:pc + 1],
                                                  op0=ALU.mult, op1=ALU.add), r=[ynb, vecb], w=[ynb])
            rk, rkb = tmpt("rk")
            k.op("pool", lambda e: e.tensor_tensor(out=rk, in0=r_, in1=kp, op=ALU.mult), r=[rb_, kpb], w=[rkb])
            k.op("pool", lambda e: e.tensor_scalar(out=rk, in0=rk, scalar1=vec[:, RK, pc:pc + 1], scalar2=None, op0=ALU.mult),
                 r=[rkb, vecb], w=[rkb])
            psr, pbr = PS(); mm(k, psr, [(bones, rk)], [bonesb, rkb], pbr)
            bon, bonb = tmpt("bon")
            k.op("dve", lambda e: e.tensor_tensor(out=bon, in0=psr, in1=v_, op=ALU.mult), r=[pbr, vb_], w=[bonb])
            k.op("pool", lambda e: e.tensor_tensor(out=yn, in0=yn, in1=bon, op=ALU.add), r=[ynb, bonb], w=[ynb])
            yo, yob = k.ring("yo", [128, 512], BF16, 2)
            k.op("dve", lambda e: e.tensor_tensor(out=yo, in0=yn, in1=g_, op=ALU.mult), r=[ynb, gb_], w=[yob])
            k.dma("sp", ygq[ti // 4][:, pc, (ti % 4) * 512:(ti % 4) * 512 + 512], yo, r=[yob], sb=yob)
            outbufs.append(yob)
    k.wait_all("sp", list({id(b): b for b in outbufs}.values()))
    return nc


def rwkv_consts():
    t = np.arange(128)
    same = (t[:, None] // 64) == (t[None, :] // 64)
    m_sl = (same & (t[None, :] < t[:, None])).astype(np.float32)
    m_su = (same & (t[:, None] < t[None, :])).astype(np.float32)
    m_iu = (same & (t[:, None] <= t[None, :])).astype(np.float32)
    masks = np.ascontiguousarray(np.stack([m_sl, m_su, m_iu], axis=1))
    bones = same.astype(np.float32)
    resetm = np.ones((128, 512), np.float32)
    resetm[:, ::64] = 0.0
    return masks, np.ascontiguousarray(bones), resetm


def l4_inputs(h1T_b, core_hg, P):
    cs = slice(core_hg * 256, (core_hg + 1) * 256)
    masks, bones, resetm = rwkv_consts()
    import ml_dtypes
    col = lambda v: np.ascontiguousarray(v[cs].reshape(2, 128).T)
    vecs = np.stack([col(P["c_w0"]), col(P["c_a0"]), col(P["c_k_k"]), col(P["c_k_a"]), col(P["c_r_k"].reshape(-1)),
                     col(P["c_ln_g"]), col(P["c_ln_b"])], axis=1)
    mu = np.ascontiguousarray(P["c_mu"].reshape(6, 8, 128).transpose(2, 0, 1))
    return {"hall": h1T_b, "hzero": np.zeros((1024, 1), ml_dtypes.bfloat16), "wr": np.ascontiguousarray(P["c_w_r"][:, cs]), "wk": np.ascontiguousarray(P["c_w_k"][:, cs]),
            "wv": np.ascontiguousarray(P["c_w_v"][:, cs]), "w1": P["c_w1"], "a1": P["c_a1"], "g1": P["c_g1"],
            "w2": np.ascontiguousarray(P["c_w2"][:, cs]), "a2": np.ascontiguousarray(P["c_a2"][:, cs]),
            "g2": np.ascontiguousarray(P["c_g2"][:, cs]), "mu": mu, "vecs": np.ascontiguousarray(vecs.astype(np.float32)),
            "masks": masks, "bones": bones, "resetm": resetm}


NEG = -30000.0
NQB = 32


def build_l2a(nqb=NQB, ntile1=16, stop=99, nc=None, cx=None, io=None):
    try:
        return _build_l2a(nqb, ntile1, stop, nc, cx, io)
    except _Stop as e:
        return e.args[0]


def _build_l2a(nqb=NQB, ntile1=16, stop=99, nc=None, cx=None, io=None):
    nc, cx, io = _std(nc, cx, io)
    din = io.inp
    hall = din("hall", [4096, TOK], BF16)
    psel_d = din("psel", [128, 2])
    wq_d = din("wq", [D, 256]); wks_d = din("wks", [D, 64]); wkw_d = din("wkw", [D, 64])
    wkv_d = din("wkvc", [D, 128]); wv2_d = din("wv2", [D, 128]); wg_d = din("wgate", [D, 12])
    tabA_c = din("tabA_c", [64, SEQ]); tabA_s = din("tabA_s", [64, SEQ])
    tabB_c = din("tabB_c", [128, SEQ]); tabB_s = din("tabB_s", [128, SEQ])
    tabQ_c = din("tabQ_c", [64, NQB * 128]); tabQ_s = din("tabQ_s", [64, NQB * 128])
    rot_d = din("rotT", [128, 128])
    ebig_d = din("ebig", [128, SEQ], BF16)
    w1_d = din("w1kv", [128, 32, 64]); w2_d = din("w2kv", [128, 64]); pe_d = din("pekv", [128, 32, 2])
    vcx_d = din("vcx", [128, 4, 129], BF16)
    cm_d = din("cmask", [128, 9, 128], BF16)
    sm_d = din("smask", [128, 6, 128], BF16)
    fv_d = din("fv", [NQB, 128, 2, 128])
    oa = io.out("oa", [NQB * 128, 256], BF16)
    k = cx.k
    PS = lambda: k.ring("psr", [128, 512], F32, 4, psum=True)
    held = {}
    for nm in ("oc0", "oc1", "os", "ow"):
        held[nm] = (k.ps("h_" + nm, [128, 512]), Buf("h_" + nm))
    psel, pselb = k.tile("psel", [128, 2], F32)
    k.dma("sp", psel, psel_d, w=[pselb], sb=pselb)

    def cload(name, dram, shape, dt, q="sp"):
        t, b = k.tile(name, shape, dt)
        k.dma(q, t, dram, w=[b], sb=b)
        return t, b

    rot, rotb = cload("rot", rot_d, [128, 128], F32)
    ebig, ebigb = cload("ebig", ebig_d, [128, SEQ], BF16)
    cm, cmb = cload("cm", cm_d, [128, 9, 128], BF16)
    sm, smb = cload("sm", sm_d, [128, 6, 128], BF16)
    identb, identbb = k.tile("identb", [128, 128], BF16)
    k.op("act", lambda e: e.activation(out=identb, in_=cx.ident, func=AF.Copy), r=[cx.b_ident], w=[identbb])
    ones_f, ones_fb = k.tile("ones_f", [128, 128], F32)
    k.op("pool", lambda e: e.memset(ones_f, 1.0), w=[ones_fb])

    def wcast(name, dram, n, q="pool"):
        t, b = k.tile(name, [128, 8, n], BF16)
        k.dma(q, t, dram.rearrange("(kc p) n -> p kc n", p=128), w=[b], sb=b)
        return t, b

    wq, wqb = wcast("wq", wq_d, 256); wks, wksb = wcast("wks", wks_d, 64); wkw, wkwb = wcast("wkw", wkw_d, 64)
    wkv, wkvb = wcast("wkv", wkv_d, 128); wv2, wv2b = wcast("wv2", wv2_d, 128); wgt, wgtb = wcast("wgt", wg_d, 12)
    w1, w1b = k.tile("w1", [128, 32, 64], BF16); k.dma("pool", w1, w1_d, w=[w1b], sb=w1b)
    w2, w2b = k.tile("w2", [128, 64], BF16); k.dma("pool", w2, w2_d, w=[w2b], sb=w2b)
    pe, peb = k.tile("pe", [128, 32, 2], BF16); k.dma("pool", pe, pe_d, w=[peb], sb=peb)

    ksel, kselb = k.tile("ksel", [128, SEQ], BF16)
    kwin, kwinb = k.tile("kwin", [128, SEQ], BF16)
    kvc, kvcb = k.tile("kvc", [128, SEQ + 32], BF16)
    k.op("pool", lambda e: e.memset(kvc[:, SEQ:SEQ + 32], 0.0), w=[kvcb])
    vsel, vselb = k.tile("vsel", [128, 64, 96], BF16)
    vwin, vwinb = k.tile("vwin", [128, 64, 96], BF16)
    for t_, b_ in ((ksel, kselb), (kwin, kwinb)):
        k.op("pool", lambda e: e.memset(t_[64:128, :], 0.0), w=[b_])
        k.op("pool", lambda e: e.memset(t_[64:65, :], 1.0), w=[b_])
    for t_, b_ in ((vsel, vselb), (vwin, vwinb)):
        k.op("pool", lambda e: e.memset(t_[:, :, 64:65], 1.0), w=[b_])
    kmax2, kmax2b = k.tile("kmax2", [128, 1], F32)
    k.op("pool", lambda e: e.memset(kmax2, 0.0), w=[kmax2b])

    def upd_kmax(src, srcb, rows, n):
        sq, sqb = k.ring("ksq", [128, 512], F32, 2)
        k.op("pool", lambda e: e.tensor_tensor(out=sq[rows, 0:n], in0=src, in1=src, op=ALU.mult), r=[srcb], w=[sqb])
        ps, pb = PS()
        mm(k, ps[:, 0:n], [(ones_f[rows, :], sq[rows, 0:n])], [ones_fb, sqb], pb)
        k.op("act", lambda e: e.activation(out=sq[:, 0:n], in_=ps[:, 0:n], func=AF.Copy), r=[pb], w=[sqb])
        mx, mxb = k.ring("kmx", [128, 8], F32, 2)
        k.op("dve", lambda e: e.max(out=mx, in_=sq[:, 0:n]), r=[sqb], w=[mxb])
        k.op("dve", lambda e: e.tensor_tensor(out=kmax2, in0=kmax2, in1=mx[:, 0:1], op=ALU.max), r=[mxb, kmax2b], w=[kmax2b])

    def rope_store(ps, pb, rows, tc_d, ts_d, t0, n, dst, dstb, rbase=0):
        R = slice(rbase, rbase + rows)
        xk, xkb = k.ring("xk", [128, 512], F32, 2)
        k.op("act", lambda e: e.activation(out=xk[R, 0:n], in_=ps[R, 0:n], func=AF.Copy), r=[pb], w=[xkb])
        tc_, tcb = k.ring("tabc", [128, 512], F32, 2)
        ts_, tsb = k.ring("tabs", [128, 512], F32, 2)
        k.dma("sp", tc_[R, 0:n], tc_d[:, t0:t0 + n], w=[tcb], sb=tcb)
        k.dma("sp", ts_[R, 0:n], ts_d[:, t0:t0 + n], w=[tsb], sb=tsb)
        ps2, pb2 = PS()
        mm(k, ps2[R, 0:n], [(rot[R, R], xk[R, 0:n])], [rotb, xkb], pb2)
        t1, t1b = k.ring("rp1", [128, 512], F32, 2)
        t2, t2b = k.ring("rp2", [128, 512], F32, 2)
        k.op("pool", lambda e: e.tensor_tensor(out=t1[R, 0:n], in0=xk[R, 0:n], in1=tc_[R, 0:n], op=ALU.mult), r=[xkb, tcb], w=[t1b])
        k.op("dve", lambda e: e.tensor_tensor(out=t2[R, 0:n], in0=ps2[R, 0:n], in1=ts_[R, 0:n], op=ALU.mult), r=[pb2, tsb], w=[t2b])
        k.op("dve", lambda e: e.tensor_tensor(out=dst, in0=t1[R, 0:n], in1=t2[R, 0:n], op=ALU.add), r=[t1b, t2b], w=[dstb])

    if stop == 10:
        raise _Stop(nc)
    for ti in range(ntile1):
        t0 = ti * 512
        hb, hbb = k.ring("hb", [128, 8, 512], BF16, 2)
        dma_hall(k, hb, hall, t0, 512, hbb)
        for (w_, wb_, dst, dstb) in ((wks, wksb, ksel, kselb), (wkw, wkwb, kwin, kwinb)):
            ps, pb = PS()
            mm(k, ps[0:64, :], [(w_[:, kc, :], hb[:, kc, :]) for kc in range(8)], [wb_, hbb], pb)
            if stop == 11 + 100 * ti:
                raise _Stop(nc)
            rope_store(ps, pb, 64, tabA_c, tabA_s, t0, 512, dst[0:64, t0:t0 + 512], dstb)
            if stop == 12 + 100 * ti:
                raise _Stop(nc)
            upd_kmax(dst[0:64, t0:t0 + 512], dstb, slice(0, 64), 512)
            if stop == 13 + 100 * ti:
                raise _Stop(nc)
        if stop == 14 + 100 * ti:
            raise _Stop(nc)
        ps, pb = PS()
        mm(k, ps, [(wkv[:, kc, :], hb[:, kc, :]) for kc in range(8)], [wkvb, hbb], pb)
        rope_store(ps, pb, 128, tabB_c, tabB_s, t0, 512, kvc[:, t0:t0 + 512], kvcb)
        if stop == 15 + 100 * ti:
            raise _Stop(nc)
        ps, pb = PS()
        for blk in range(4):
            mm(k, ps[:, blk * 128:(blk + 1) * 128], [(hb[:, kc, blk * 128:(blk + 1) * 128], wv2[:, kc, :]) for kc in range(8)],
               [wv2b, hbb], pb)
        pv = ps.rearrange("p (a b) -> p a b", a=4)
        k.op("act", lambda e: e.activation(out=vsel[:, ti * 4:ti * 4 + 4, 0:64], in_=pv[:, :, 0:64], func=AF.Copy), r=[pb], w=[vselb])
        k.op("act", lambda e: e.activation(out=vwin[:, ti * 4:ti * 4 + 4, 0:64], in_=pv[:, :, 64:128], func=AF.Copy), r=[pb], w=[vwinb])

    if stop == 1:
        raise _Stop(nc)
    kc, kcb = k.tile("kc", [128, 512], BF16)
    k.op("pool", lambda e: e.memset(kc, 0.0), w=[kcb])
    k.op("pool", lambda e: e.memset(kc[64:65, :], 1.0), w=[kcb])
    vcx, vcxb = k.tile("vcx", [128, 4, 256], BF16)
    k.dma("sp", vcx[:, :, 64:193], vcx_d, w=[vcxb], sb=vcxb)
    kv16 = kvc.rearrange("p (n s) -> p n s", s=16)
    hid, hidb = k.tile("hid", [128, 512], BF16)
    k.op("pool", lambda e: e.memset(hid, 0.0), w=[hidb])
    for R in (slice(0, 64), slice(64, 128)):
        psb, pbb = PS()
        mm(k, psb[R, 0:2], [(w1[R, l, :], pe[R, l, :]) for l in range(32)], [w1b, peb], pbb)
        bia, biab = k.ring("cbias", [128, 1], F32, 2)
        k.op("act", lambda e: e.activation(out=bia[R, :], in_=psb[R, 0:1], func=AF.Copy), r=[pbb], w=[biab])
        ps, pb = PS()
        mm(k, ps[R, 0:512], [(w1[R, l, :], kv16[R, (l // 16):(l // 16) + 512, l % 16]) for l in range(32)], [w1b, kvcb], pb)
        k.op("act", lambda e: e.activation(out=hid[R, :], in_=ps[R, :], func=AF.Silu, bias=bia[R, :]), r=[pb, biab], w=[hidb])
    ps, pb = PS()
    mm(k, ps[0:64, :], [(w2[0:64, :], hid[0:64, :])], [w2b, hidb], pb)
    k.op("act", lambda e: e.activation(out=kc[0:64, :], in_=ps[0:64, :], func=AF.Copy), r=[pb], w=[kcb])
    upd_kmax(kc[0:64, 0:512], kcb, slice(0, 64), 512)
    ps, pb = PS()
    for ch in range(4):
        mm(k, ps[:, ch * 64:(ch + 1) * 64], [(hid[64:128, ch * 128:(ch + 1) * 128], w2[64:128, :])], [w2b, hidb], pb)
    k.op("act", lambda e: e.activation(out=vcx[:, :, 0:64], in_=ps[:, 0:256].rearrange("p (a b) -> p a b", a=4), func=AF.Copy),
         r=[pb], w=[vcxb])
    nkm, nkmb = k.tile("nkm", [128, 1], F32)
    k.op("act", lambda e: e.activation(out=nkm, in_=kmax2, func=AF.Sqrt), r=[kmax2b], w=[nkmb])
    k.op("dve", lambda e: e.tensor_scalar(out=nkm, in0=nkm, scalar1=-1.0, scalar2=None, op0=ALU.mult), r=[nkmb], w=[nkmb])

    if stop == 2:
        raise _Stop(nc)
    outb = []

    def prep(i):
        q0 = i * 128
        hqc, hqcb = k.ring("hqc", [128, 2, 8, 128], BF16, 2)
        for pp in range(2):
            dma_hall(k, hqc[:, pp], hall, (2 * i + pp) * 128, 128, hqcb)
        hqt, hqtb = k.ring("hqt", [128, 8, 128], BF16, 2)
        hqb, hqbb = k.ring("hqb", [128, 8, 128], BF16, 2)
        k.op("dve", lambda e: e.tensor_scalar(out=hqt, in0=hqc[:, 0], scalar1=psel[:, 0:1], scalar2=None, op0=ALU.mult),
             r=[hqcb, pselb], w=[hqtb])
        k.op("dve", lambda e: e.scalar_tensor_tensor(out=hqb, in0=hqc[:, 1], scalar=psel[:, 1:2], in1=hqt, op0=ALU.mult, op1=ALU.add),
             r=[hqcb, pselb, hqtb], w=[hqbb])
        qa, qab = k.ring("qa", [128, 512], BF16, 2)
        k.op("pool", lambda e: e.memset(qa[64:128, :], 0.0), w=[qab])
        qf, qfb = k.ring("qf", [128, 512], F32, 2)
        tcq, tcqb = k.ring("tcq", [128, 128], F32, 2); tsq, tsqb = k.ring("tsq", [128, 128], F32, 2)
        k.dma("sp", tcq[0:64, :], tabQ_c[:, q0:q0 + 128], w=[tcqb], sb=tcqb)
        k.dma("sp", tsq[0:64, :], tabQ_s[:, q0:q0 + 128], w=[tsqb], sb=tsqb)
        ps, pb = PS()
        for g in range(4):
            mm(k, ps[0:64, g * 128:(g + 1) * 128], [(wq[:, kc, g * 64:(g + 1) * 64], hqb[:, kc, :]) for kc in range(8)], [wqb, hqbb], pb)
        xq, xqb = k.ring("xq", [128, 512], F32, 2)
        k.op("act", lambda e: e.activation(out=xq[0:64, :], in_=ps[0:64, :], func=AF.Copy), r=[pb], w=[xqb])
        ps2, pb2 = PS()
        mm(k, ps2[0:64, :], [(rot[0:64, 0:64], xq[0:64, :])], [rotb, xqb], pb2)
        v3 = lambda a: a.rearrange("p (g t) -> p g t", g=4)
        bc4 = lambda a: a.unsqueeze(1).to_broadcast([64, 4, 128])
        k.op("pool", lambda e: e.tensor_tensor(out=v3(xq[0:64, :]), in0=v3(xq[0:64, :]), in1=bc4(tcq[0:64, :]), op=ALU.mult),
             r=[xqb, tcqb], w=[xqb])
        k.op("dve", lambda e: e.tensor_tensor(out=v3(qf[0:64, :]), in0=v3(ps2[0:64, :]), in1=bc4(tsq[0:64, :]), op=ALU.mult),
             r=[pb2, tsqb], w=[qfb])
        k.op("dve", lambda e: e.tensor_tensor(out=qf[0:64, :], in0=qf[0:64, :], in1=xq[0:64, :], op=ALU.add), r=[qfb, xqb], w=[qfb])
        k.op("act", lambda e: e.activation(out=qa[0:64, :], in_=qf[0:64, :], func=AF.Copy), r=[qfb], w=[qab])
        k.op("pool", lambda e: e.tensor_tensor(out=xq[0:64, :], in0=qf[0:64, :], in1=qf[0:64, :], op=ALU.mult), r=[qfb, xqb], w=[xqb])
        ps3, pb3 = PS()
        mm(k, ps3[64:65, :], [(ones_f[0:64, 0:1], xq[0:64, :])], [ones_fb, xqb], pb3)
        mrow, mrowb = k.ring("mrow", [128, 512], F32, 2)
        k.op("act", lambda e: e.activation(out=mrow[64:65, :], in_=ps3[64:65, :], func=AF.Sqrt), r=[pb3], w=[mrowb])
        k.op("dve", lambda e: e.tensor_scalar(out=qa[64:65, :], in0=mrow[64:65, :], scalar1=nkm[64:65, 0:1], scalar2=None, op0=ALU.mult),
             r=[mrowb, nkmb], w=[qab])
        psg, pbg = PS()
        mm(k, psg[:, 0:12], [(hqb[:, kc, :], wgt[:, kc, :]) for kc in range(8)], [wgtb, hqbb], pbg)
        gt, gtb = k.ring("gates", [128, 12], F32, 2)
        k.op("act", lambda e: e.activation(out=gt, in_=psg[:, 0:12], func=AF.Sigmoid), r=[pbg], w=[gtb])

        return qa, qab, gt, gtb

    nxt = prep(0)
    for i in range(nqb):
        q0 = i * 128
        qa, qab, gt, gtb = nxt
        if stop == 3:
            raise _Stop(nc)

        def score_tile(kaug, kaugb, ktile_cols, masks):
            ps, pb = PS()
            n = 1 + len(masks)
            k.op("pe", lambda e: e.matmul(ps, lhsT=kaug[:, ktile_cols], rhs=qa, start=True, stop=(n == 1)),
                 r=[kaugb, qab], w=[pb], inc=(n == 1))
            for mi, (ml, mlb, mr, mrb) in enumerate(masks):
                last = (mi == len(masks) - 1)
                k.op("pe", lambda e: e.matmul(ps.rearrange("p (g t) -> p g t", g=4), lhsT=ml,
                                              rhs=mr.unsqueeze(1).to_broadcast([128, 4, 128]), start=False, stop=last),
                     r=[mlb, mrb], w=[pb], inc=last)
            eT, eTb = k.ring("eT", [128, 512], BF16, 3)
            k.op("act", lambda e: e.activation(out=eT, in_=ps, func=AF.Exp), r=[pb], w=[eTb])
            return eT, eTb

        qi_e = 2 * i
        last = (8 * qi_e + 6) // 128
        oc = [held["oc0"], held["oc1"]]
        def cmp_pv(cc, eT, eTb):
            for g in range(4):
                o_, ob_ = oc[g // 2]
                k.op("pe", lambda e: e.matmul(o_[:, (g % 2) * 256:(g % 2) * 256 + 193], lhsT=eT[:, g * 128:(g + 1) * 128],
                                              rhs=vcx[:, cc, 0:193], start=(cc == 0 and g % 2 == 0), stop=(cc == last and g % 2 == 1),
                                              skip_group_check=True),
                     r=[eTb, vcxb], w=[ob_], inc=(g == 3))
        pend = None
        for cc in range(last + 1):
            masks = []
            if cc == last:
                masks.append((identb, identbb, cm[:, i % 8, :], cmb))
            elif cc == last - 1 and i % 8 == 0:
                masks.append((identb, identbb, cm[:, 8, :], cmb))
            eT, eTb = score_tile(kc, kcb, slice(cc * 128, (cc + 1) * 128), masks)
            if pend is not None:
                cmp_pv(*pend)
            pend = (cc, eT, eTb)
        cmp_pv(*pend)
        if stop == 6:
            raise _Stop(nc)
        ow_, owb_ = held["ow"]
        wt = [kt for kt in range(2 * i - 4, 2 * i + 2) if kt >= 0]
        def win_pv(kt, eT, eTb):
            for g in range(4):
                k.op("pe", lambda e: e.matmul(ow_[:, g * 65:(g + 1) * 65], lhsT=eT[:, g * 128:(g + 1) * 128], rhs=vwin[:, kt, 0:65],
                                              start=(kt == wt[0] and g == 0), stop=(kt == wt[-1] and g == 3), skip_group_check=True),
                     r=[eTb, vwinb], w=[owb_], inc=(g == 3))
        pend = None
        for kt in wt:
            cols = slice(kt * 128, (kt + 1) * 128)
            pos = kt - (2 * i - 4)
            masks = []
            mi = {0: 2, 1: 3, 4: 4, 5: 5}.get(pos)
            if mi is not None:
                masks.append((identb, identbb, sm[:, mi, :], smb))
            eT, eTb = score_tile(kwin, kwinb, cols, masks)
            if pend is not None:
                win_pv(*pend)
            pend = (kt, eT, eTb)
        win_pv(*pend)
        if stop == 4:
            raise _Stop(nc)
        zc, zcb = k.ring("zc", [128, 4], F32, 2)
        for g in range(4):
            o_, ob_ = oc[g // 2]
            k.op("dve", lambda e: e.tensor_scalar(out=zc[:, g:g + 1], in0=o_[:, (g % 2) * 256 + 64:(g % 2) * 256 + 65], scalar1=1e-30,
                                                  scalar2=None, op0=ALU.max), r=[ob_], w=[zcb])
        k.op("dve", lambda e: e.reciprocal(out=zc, in_=zc), r=[zcb], w=[zcb])
        imp, impb = k.ring("imp", [128, 128], F32, 2)
        for g in range(4):
            o_, ob_ = oc[g // 2]
            src = o_[:, (g % 2) * 256 + 65:(g % 2) * 256 + 193]
            if g == 0:
                k.op("dve", lambda e: e.tensor_scalar(out=imp, in0=src, scalar1=zc[:, 0:1], scalar2=None, op0=ALU.mult),
                     r=[ob_, zcb], w=[impb])
            else:
                k.op("dve", lambda e: e.scalar_tensor_tensor(out=imp, in0=src, scalar=zc[:, g:g + 1], in1=imp, op0=ALU.mult, op1=ALU.add),
                     r=[ob_, zcb, impb], w=[impb])
        fv, fvb = k.ring("fv", [128, 2, 128], F32, 2)
        k.dma("sp", fv, fv_d[i], w=[fvb], sb=fvb)
        k.op("dve", lambda e: e.tensor_tensor(out=imp, in0=imp, in1=fv[:, 0, :], op=ALU.mult), r=[impb, fvb], w=[impb])
        k.op("dve", lambda e: e.tensor_tensor(out=imp, in0=imp, in1=fv[:, 1, :], op=ALU.add), r=[impb, fvb], w=[impb])
        m8, m8b = k.ring("m8", [128, 16], F32, 2)
        wk_, wkb_ = k.ring("impw", [128, 128], F32, 2)
        k.op("dve", lambda e: e.max(out=m8[:, 0:8], in_=imp), r=[impb], w=[m8b])
        k.op("dve", lambda e: e.match_replace(out=wk_, in_to_replace=m8[:, 0:8], in_values=imp, imm_value=-3e38), r=[m8b, impb], w=[wkb_])
        k.op("dve", lambda e: e.max(out=m8[:, 8:16], in_=wk_), r=[wkb_], w=[m8b])
        k.op("dve", lambda e: e.tensor_scalar(out=wk_, in0=imp, scalar1=m8[:, 15:16], scalar2=None, op0=ALU.is_ge), r=[impb, m8b], w=[wkb_])
        k.op("dve", lambda e: e.tensor_scalar(out=wk_, in0=wk_, scalar1=-1.0, scalar2=-NEG, op0=ALU.add, op1=ALU.mult), r=[wkb_], w=[wkb_])
        pst, pbt = PS()
        k.op("pe", lambda e: e.transpose(out=pst[:, 0:128], in_=wk_, identity=cx.ident), r=[wkb_, cx.b_ident], w=[pbt])
        nbT, nbTb = k.ring("nbT", [128, 128], BF16, 2)
        k.op("act", lambda e: e.activation(out=nbT, in_=pst[:, 0:128], func=AF.Copy), r=[pbt], w=[nbTb])

        if stop == 5:
            raise _Stop(nc)
        os_, osb_ = held["os"]
        nkt = 2 * i + 2
        def sel_pv(kt, eT, eTb):
            for g in range(4):
                k.op("pe", lambda e: e.matmul(os_[:, g * 65:(g + 1) * 65], lhsT=eT[:, g * 128:(g + 1) * 128], rhs=vsel[:, kt, 0:65],
                                              start=(kt == 0 and g == 0), stop=(kt == nkt - 1 and g == 3), skip_group_check=True),
                     r=[eTb, vselb], w=[osb_], inc=(g == 3))
        pend = None
        for kt in range(nkt):
            cols = slice(kt * 128, (kt + 1) * 128)
            masks = [(ebig[:, cols], ebigb, nbT, nbTb)]
            if kt == 2 * i:
                masks.append((identb, identbb, sm[:, 0, :], smb))
            elif kt == 2 * i + 1:
                masks.append((identb, identbb, sm[:, 1, :], smb))
            eT, eTb = score_tile(ksel, kselb, cols, masks)
            if pend is not None:
                sel_pv(*pend)
            pend = (kt, eT, eTb)
        sel_pv(*pend)
        if i + 1 < nqb:
            nxt = prep(i + 1)
        sc, scb = k.ring("sc", [128, 12], F32, 2)
        k.op("pool", lambda e: e.memset(sc, 1.0), w=[scb])
        for g in range(4):
            k.op("dve", lambda e: e.tensor_scalar(out=sc[:, g * 3 + 1:g * 3 + 2], in0=os_[:, g * 65 + 64:g * 65 + 65], scalar1=1e-30,
                                                  scalar2=None, op0=ALU.max), r=[osb_], w=[scb])
            k.op("dve", lambda e: e.tensor_scalar(out=sc[:, g * 3 + 2:g * 3 + 3], in0=ow_[:, g * 65 + 64:g * 65 + 65], scalar1=1e-30,
                                                  scalar2=None, op0=ALU.max), r=[owb_], w=[scb])
        k.op("dve", lambda e: e.reciprocal(out=sc, in_=sc), r=[scb], w=[scb])
        for g in range(4):
            k.op("dve", lambda e: e.tensor_copy(out=sc[:, g * 3:g * 3 + 1], in_=zc[:, g:g + 1]), r=[zcb], w=[scb])
        k.op("dve", lambda e: e.tensor_tensor(out=sc, in0=sc, in1=gt, op=ALU.mult), r=[scb, gtb], w=[scb])
        ot, otb = k.ring("oat", [128, 256], F32, 2)
        for g in range(4):
            o_, ob_ = oc[g // 2]
            dst = ot[:, g * 64:(g + 1) * 64]
            k.op("dve", lambda e: e.tensor_scalar(out=dst, in0=o_[:, (g % 2) * 256:(g % 2) * 256 + 64], scalar1=sc[:, g * 3:g * 3 + 1],
                                                  scalar2=None, op0=ALU.mult), r=[ob_, scb], w=[otb])
            k.op("dve", lambda e: e.scalar_tensor_tensor(out=dst, in0=os_[:, g * 65:g * 65 + 64], scalar=sc[:, g * 3 + 1:g * 3 + 2], in1=dst,
                                                         op0=ALU.mult, op1=ALU.add), r=[osb_, scb, otb], w=[otb])
            k.op("dve", lambda e: e.scalar_tensor_tensor(out=dst, in0=ow_[:, g * 65:g * 65 + 64], scalar=sc[:, g * 3 + 2:g * 3 + 3], in1=dst,
                                                         op0=ALU.mult, op1=ALU.add), r=[owb_, scb, otb], w=[otb])
        otc, otcb = k.ring("oatc", [128, 256], BF16, 2)
        k.op("act", lambda e: e.activation(out=otc, in_=ot, func=AF.Copy), r=[otb], w=[otcb])
        k.dma("sp", oa[q0:q0 + 128, :], otc, r=[otcb], sb=otcb)
        outb.append(otcb)
    k.wait_all("sp", list({id(b): b for b in outb}.values()))
    return nc


def _bf16(a):
    import ml_dtypes
    return np.ascontiguousarray(a).astype(ml_dtypes.bfloat16)


def nsa_consts():
    inv = (10000.0 ** (-np.arange(0, 64, 2, dtype=np.float32) / 64)).astype(np.float32)
    ang = np.arange(SEQ, dtype=np.float32)[:, None] * inv[None, :]
    cos, sin = np.cos(ang).astype(np.float32), np.sin(ang).astype(np.float32)
    tA_c = np.ascontiguousarray(np.concatenate([cos, cos], 1).T)
    tA_s = np.ascontiguousarray(np.concatenate([sin, sin], 1).T)
    tB_c = np.ascontiguousarray(np.concatenate([tA_c, np.ones_like(tA_c)], 0))
    tB_s = np.ascontiguousarray(np.concatenate([tA_s, np.zeros_like(tA_s)], 0))
    rot = np.zeros((128, 128), np.float32)
    for m in range(128):
        mm_ = m % 64
        if mm_ < 32:
            rot[m + 32, m] = -1.0
        else:
            rot[m - 32, m] = 1.0
    x = np.arange(SEQ)
    ebig = (np.arange(128)[:, None] == (x[None, :] // 64)).astype(np.float32)
    c = np.arange(512)
    j = np.arange(128)
    ov = ((16 * c[:, None] < 64 * j[None, :] + 64) & (16 * c[:, None] + 31 >= 64 * j[None, :])).astype(np.float32)
    vcx = np.zeros((512, 129), np.float32)
    vcx[:, 0] = 1.0
    vcx[:, 1:] = ov
    vcx[511, :] = 0.0
    vcx = np.ascontiguousarray(vcx.reshape(4, 128, 129).transpose(1, 0, 2))
    return dict(tA_c=tA_c, tA_s=tA_s, tB_c=tB_c, tB_s=tB_s, rot=rot, ebig=_bf16(ebig), vcx=_bf16(vcx))


def nsa_core_consts(par):
    p = np.arange(128)[:, None]
    tl = np.arange(128)[None, :]
    cm = np.zeros((128, 9, 128), np.float32)
    for s in range(8):
        off = 8 * ((2 * s + par) % 16)
        cm[:, s, :] = np.where(16 * (p - off) + 31 <= tl, 0.0, NEG)
    if par == 0:
        cm[:, 8, :] = np.where((p == 127) & (tl < 15), NEG, 0.0)
    causal = np.where(p <= tl, 0.0, NEG).astype(np.float32)
    winT = np.where(p > tl, 0.0, NEG).astype(np.float32)
    ALL = np.full((128, 128), NEG, np.float32)
    Z = np.zeros((128, 128), np.float32)
    sm = [causal, ALL, winT, Z, causal, ALL] if par == 0 else [Z, causal, ALL, winT, Z, causal]
    sm = np.stack(sm, axis=1)
    fv = np.zeros((NQB, 128, 2, 128), np.float32)
    jj = np.arange(128)[None, :]
    for i in range(NQB):
        qi = 2 * i + par
        t = 128 * qi + np.arange(128)[:, None]
        cur = t // 64
        valid = jj <= cur
        f0 = (jj == 0)
        f1 = (jj == cur)
        f2 = (jj == cur - 1)
        forced = f0 | f1 | f2
        V = (valid & ~forced).astype(np.float32)
        F = np.where(valid, 0.0, -1e30).astype(np.float32)
        F = np.where(f2 & valid, 1e4 + 2.0, F)
        F = np.where(f0, 1e4 + 1.0, F)
        F = np.where(f1, 1e4, F)
        fv[i, :, 0, :] = V
        fv[i, :, 1, :] = F
    return dict(cm=_bf16(cm), sm=_bf16(sm), fv=fv)


def l2a_inputs(h0T_b, kvh, par, P, C):
    w = P["ab_w_in"]
    cat = lambda *a: np.ascontiguousarray(np.concatenate(a, axis=1))
    sl = lambda o, n: w[:, o + kvh * n: o + (kvh + 1) * n]
    qtok = np.concatenate([np.arange((2 * i + par) * 128, (2 * i + par + 1) * 128) for i in range(NQB)])
    cc = nsa_core_consts(par)
    w1 = np.concatenate([P["a_cmp_w1_k"].reshape(32, 64, 64).transpose(1, 0, 2),
                         P["a_cmp_w1_v"].reshape(32, 64, 64).transpose(1, 0, 2)], 0)
    w2 = np.concatenate([P["a_cmp_w2_k"], P["a_cmp_w2_v"]], 0)
    pe = np.concatenate([P["a_cmp_pe_k"].T, P["a_cmp_pe_v"].T], 0)
    pselv = np.zeros((128, 2), np.float32); pselv[:, par] = 1.0
    return {"hall": h0T_b, "psel": pselv,
            "wq": np.ascontiguousarray(sl(0, 256)), "wks": np.ascontiguousarray(sl(768, 64)), "wkw": np.ascontiguousarray(sl(1024, 64)),
            "wkvc": cat(sl(512, 64), sl(640, 64)), "wv2": cat(sl(896, 64), sl(1152, 64)), "wgate": np.ascontiguousarray(sl(1280, 12)),
            "tabA_c": C["tA_c"], "tabA_s": C["tA_s"], "tabB_c": C["tB_c"], "tabB_s": C["tB_s"],
            "tabQ_c": np.ascontiguousarray(C["tA_c"][:, qtok] * np.float32(0.125)),
            "tabQ_s": np.ascontiguousarray(C["tA_s"][:, qtok] * np.float32(0.125)),
            "rotT": C["rot"], "ebig": C["ebig"], "w1kv": np.ascontiguousarray(w1), "w2kv": np.ascontiguousarray(w2),
            "pekv": np.ascontiguousarray(np.stack([pe, pe], axis=2)), "vcx": C["vcx"], "cmask": cc["cm"], "smask": cc["sm"], "fv": cc["fv"]}


def build_l2b(ntiles=16, nc=None, cx=None, io=None):
    nc, cx, io = _std(nc, cx, io)
    din = io.inp
    hall = din("hall", [4096, TOK], BF16)
    wx_d = din("wx", [D, 256]); wB_d = din("wB", [D, 128]); wC_d = din("wC", [D, 128]); wdt_d = din("wdt", [D, 4])
    cw_d = din("convw", [128, 4, 4]); cb_d = din("convb", [128, 4])
    dtb_d = din("dtb", [128, 16]); alog_d = din("alog", [128, 16]); dsk_d = din("dskip", [128, 4])
    tri_d = din("tri", [128, 128]); su_d = din("su", [128, 128])
    yout = io.out("y", [SEQ, 256], BF16)
    k = cx.k
    PS = lambda: k.ring("psr", [128, 512], F32, 8, psum=True)

    def cload(name, dram, shape, dt, q="sp"):
        t, b = k.tile(name, shape, dt)
        k.dma(q, t, dram, w=[b], sb=b)
        return t, b

    def wcast(name, dram, n):
        t, b = k.tile(name, [128, 8, n], BF16)
        k.dma("pool", t, dram.rearrange("(kc p) n -> p kc n", p=128), w=[b], sb=b)
        return t, b

    wx, wxb = wcast("wx", wx_d, 256); wB, wBb = wcast("wB", wB_d, 128); wC, wCb = wcast("wC", wC_d, 128)
    wdt, wdtb = wcast("wdt", wdt_d, 4)
    cw, cwb = cload("cw", cw_d, [128, 4, 4], F32); cb, cbb = cload("cb", cb_d, [128, 4], F32)
    dtb, dtbb = cload("dtb", dtb_d, [128, 16], F32); alog, alogb = cload("alog", alog_d, [128, 16], F32)
    dsk, dskb = cload("dsk", dsk_d, [128, 4], F32)
    tri, trib = cload("tri", tri_d, [128, 128], F32); su, sub_ = cload("su", su_d, [128, 128], F32)
    ones_f, ones_fb = k.tile("ones_f", [128, 128], F32)
    k.op("pool", lambda e: e.memset(ones_f, 1.0), w=[ones_fb])
    c_one = cx.const(1.0)
    arep, arepb = k.tile("arep", [128, 16], F32)
    k.op("act", lambda e: e.activation(out=arep, in_=alog, func=AF.Exp), r=[alogb], w=[arepb])
    k.op("dve", lambda e: e.tensor_scalar(out=arep, in0=arep, scalar1=-1.0, scalar2=None, op0=ALU.mult), r=[arepb], w=[arepb])
    S, Sb = k.tile("S", [128, 256], F32)
    k.op("pool", lambda e: e.memset(S, 0.0), w=[Sb])
    xbc = []
    for m in range(4):
        t, b = k.tile("xbc%d" % m, [128, 516], F32)
        k.op("pool", lambda e: e.memset(t, 0.0), w=[b])
        xbc.append((t, b))
    outb = []
    wsel = [(wx, wxb, slice(0, 128)), (wx, wxb, slice(128, 256)), (wB, wBb, slice(0, 128)), (wC, wCb, slice(0, 128))]
    bc64 = lambda a, n: a.unsqueeze(2).to_broadcast([128, n, 64])
    for ti in range(ntiles):
        t0 = ti * 512
        hb, hbb = k.ring("hb", [128, 8, 512], BF16, 2)
        dma_hall(k, hb, hall, t0, 512, hbb)
        xc = []
        for m in range(4):
            w_, wb_, cols = wsel[m]
            ps, pb = PS()
            mm(k, ps, [(w_[:, kc, cols], hb[:, kc, :]) for kc in range(8)], [wb_, hbb], pb)
            xt, xtb = xbc[m]
            k.op("act", lambda e: e.activation(out=xt[:, 3:515], in_=ps, func=AF.Copy), r=[pb], w=[xtb])
            acc, accb = k.ring("cacc", [128, 512], F32, 2)
            k.op("dve", lambda e: e.tensor_scalar(out=acc, in0=xt[:, 0:512], scalar1=cw[:, m, 0:1], scalar2=cb[:, m:m + 1],
                                                  op0=ALU.mult, op1=ALU.add), r=[xtb, cwb, cbb], w=[accb])
            for j in range(1, 4):
                k.op("dve", lambda e: e.scalar_tensor_tensor(out=acc, in0=xt[:, j:j + 512], scalar=cw[:, m, j:j + 1], in1=acc,
                                                             op0=ALU.mult, op1=ALU.add), r=[xtb, cwb, accb], w=[accb])
            o, ob = k.ring("xc%d" % m, [128, 512], F32, 2)
            k.op("act", lambda e: e.activation(out=o, in_=acc, func=AF.Silu), r=[accb], w=[ob])
            k.op("pool", lambda e: e.tensor_copy(out=xt[:, 0:3], in_=xt[:, 512:515]), r=[xtb], w=[xtb])
            xc.append((o, ob))
        psd, pbd = PS()
        for c in range(4):
            mm(k, psd[:, c * 4:(c + 1) * 4], [(hb[:, kc, c * 128:(c + 1) * 128], wdt[:, kc, :]) for kc in range(8)], [wdtb, hbb], pbd)
        dt, dtb_ = k.ring("dt", [128, 16], F32, 2)
        k.op("dve", lambda e: e.tensor_tensor(out=dt, in0=psd[:, 0:16], in1=dtb, op=ALU.add), r=[pbd, dtbb], w=[dtb_])
        k.op("act", lambda e: e.activation(out=dt, in_=dt, func=AF.Exp), r=[dtb_], w=[dtb_])
        k.op("act", lambda e: e.activation(out=dt, in_=dt, func=AF.Ln, bias=c_one, scale=1.0), r=[dtb_, cx.b_consts], w=[dtb_])
        da, dab = k.ring("da", [128, 16], F32, 2)
        k.op("dve", lambda e: e.tensor_tensor(out=da, in0=dt, in1=arep, op=ALU.mult), r=[dtb_, arepb], w=[dab])
        psa, pba = PS(); mm(k, psa[:, 0:16], [(tri, da)], [trib, dab], pba)
        pst_, pbt_ = PS(); mm(k, pst_[:, 0:16], [(ones_f, da)], [ones_fb, dab], pbt_)
        acs, acsb = k.ring("acs", [128, 16], F32, 2)
        k.op("act", lambda e: e.activation(out=acs, in_=psa[:, 0:16], func=AF.Copy), r=[pba], w=[acsb])
        eacs, eacsb = k.ring("eacs", [128, 16], F32, 2)
        k.op("act", lambda e: e.activation(out=eacs, in_=acs, func=AF.Exp), r=[acsb], w=[eacsb])
        decs, decsb = k.ring("decs", [128, 16], F32, 2)
        k.op("dve", lambda e: e.tensor_tensor(out=decs, in0=pst_[:, 0:16], in1=acs, op=ALU.subtract), r=[pbt_, acsb], w=[decsb])
        k.op("act", lambda e: e.activation(out=decs, in_=decs, func=AF.Exp), r=[decsb], w=[decsb])
        cd, cdb = k.ring("cd", [128, 16], F32, 2)
        k.op("act", lambda e: e.activation(out=cd, in_=pst_[:, 0:16], func=AF.Exp), r=[pbt_], w=[cdb])
        xtok, xtokb = k.ring("xtok", [128, 4, 256], F32, 2)
        btok, btokb = k.ring("btok", [128, 4, 128], F32, 2)
        for half in range(2):
            ps, pb = PS()
            for ci in range(2):
                c = half * 2 + ci
                for m in range(2):
                    k.op("pe", lambda e: e.transpose(out=ps[:, ci * 256 + m * 128:ci * 256 + (m + 1) * 128],
                                                     in_=xc[m][0][:, c * 128:(c + 1) * 128], identity=cx.ident),
                         r=[xc[m][1], cx.b_ident], w=[pb], inc=True)
            k.op("act", lambda e: e.activation(out=xtok[:, half * 2:half * 2 + 2, :], in_=ps.rearrange("p (a b) -> p a b", a=2), func=AF.Copy),
                 r=[pb], w=[xtokb])
        ps, pb = PS()
        for c in range(4):
            k.op("pe", lambda e: e.transpose(out=ps[:, c * 128:(c + 1) * 128], in_=xc[2][0][:, c * 128:(c + 1) * 128], identity=cx.ident),
                 r=[xc[2][1], cx.b_ident], w=[pb], inc=True)
        k.op("act", lambda e: e.activation(out=btok, in_=ps.rearrange("p (a b) -> p a b", a=4), func=AF.Copy), r=[pb], w=[btokb])
        xd, xdb = k.ring("xd", [128, 4, 256], F32, 2)
        xdd, xddb = k.ring("xdd", [128, 4, 256], F32, 2)
        v16 = lambda a: a.rearrange("p c (h d) -> p (c h) d", d=64)
        k.op("dve", lambda e: e.tensor_tensor(out=v16(xd), in0=v16(xtok), in1=bc64(dt, 16), op=ALU.mult), r=[xtokb, dtb_], w=[xdb])
        k.op("pool", lambda e: e.tensor_tensor(out=v16(xdd), in0=v16(xd), in1=bc64(decs, 16), op=ALU.mult), r=[xdb, decsb], w=[xddb])
        Bc, Bcb = xc[2]; Cc, Ccb = xc[3]
        v4 = lambda a_: a_.rearrange("p (h d) -> p h d", d=64)

        def stage_a(c):
            cs = slice(c * 128, (c + 1) * 128)
            ps, pb = PS(); mm(k, ps[:, 0:128], [(Bc[:, cs], Cc[:, cs])], [Bcb, Ccb], pb)
            cbm, cbmb = k.ring("cbm", [128, 128], F32, 3)
            k.op("dve", lambda e: e.tensor_tensor(out=cbm, in0=ps[:, 0:128], in1=tri, op=ALU.mult), r=[pb, trib], w=[cbmb])
            pdf, pdfb = PS()
            for h in range(4):
                lh, lhb = k.ring("lh", [128, 128], F32, 4)
                if h % 2 == 0:
                    k.op("dve", lambda e: e.tensor_scalar(out=lh, in0=su, scalar1=da[:, c * 4 + h:c * 4 + h + 1], scalar2=None, op0=ALU.mult),
                         r=[sub_, dab], w=[lhb])
                else:
                    k.op("act", lambda e: e.activation(out=lh, in_=su, func=AF.Copy, scale=da[:, c * 4 + h:c * 4 + h + 1]),
                         r=[sub_, dab], w=[lhb])
                mm(k, pdf[:, h * 128:(h + 1) * 128], [(lh, tri)], [lhb, trib], pdfb)
            seg, segb = k.ring("seg", [128, 4, 128], F32, 3)
            k.op("act", lambda e: e.activation(out=seg, in_=pdf.rearrange("p (a b) -> p a b", a=4), func=AF.Exp), r=[pdfb], w=[segb])
            k.op("dve", lambda e: e.tensor_tensor(out=seg, in0=seg, in1=cbm.unsqueeze(1).to_broadcast([128, 4, 128]), op=ALU.mult),
                 r=[segb, cbmb], w=[segb])
            return seg, segb

        def stage_b(c, seg, segb):
            cs = slice(c * 128, (c + 1) * 128)
            py, pyb = PS()
            for h in range(4):
                mm(k, py[:, h * 64:(h + 1) * 64], [(seg[:, h, :], xd[:, c, h * 64:(h + 1) * 64])], [segb, xdb], pyb)
            po, pob = PS(); mm(k, po[:, 0:256], [(Cc[:, cs], S)], [Ccb, Sb], pob)
            t1, t1b = k.ring("yt1", [128, 256], F32, 2)
            k.op("dve", lambda e: e.tensor_tensor(out=v4(t1), in0=v4(po[:, 0:256]), in1=bc64(eacs[:, c * 4:(c + 1) * 4], 4), op=ALU.mult),
                 r=[pob, eacsb], w=[t1b])
            k.op("dve", lambda e: e.tensor_tensor(out=t1, in0=t1, in1=py[:, 0:256], op=ALU.add), r=[t1b, pyb], w=[t1b])
            t2, t2b = k.ring("yt2", [128, 256], F32, 2)
            k.op("pool", lambda e: e.tensor_tensor(out=v4(t2), in0=v4(xtok[:, c, :]), in1=bc64(dsk, 4), op=ALU.mult), r=[xtokb, dskb], w=[t2b])
            yo, yob = k.ring("yo", [128, 256], BF16, 2)
            k.op("pool", lambda e: e.tensor_tensor(out=yo, in0=t1, in1=t2, op=ALU.add), r=[t1b, t2b], w=[yob])
            r0 = (ti * 4 + c) * 128
            k.dma("sp", yout[r0:r0 + 128, :], yo, r=[yob], sb=yob)
            outb.append(yob)
            pss, pssb = PS(); mm(k, pss[:, 0:256], [(btok[:, c, :], xdd[:, c, :])], [btokb, xddb], pssb)
            k.op("dve", lambda e: e.tensor_tensor(out=v4(S), in0=v4(S), in1=bc64(cd[:, c * 4:(c + 1) * 4], 4), op=ALU.mult), r=[Sb, cdb], w=[Sb])
            k.op("dve", lambda e: e.tensor_tensor(out=S, in0=S, in1=pss[:, 0:256], op=ALU.add), r=[Sb, pssb], w=[Sb])

        pend = None
        for c in range(4):
            cur = (c,) + stage_a(c)
            if pend is not None:
                stage_b(*pend)
            pend = cur
        stage_b(*pend)
    k.wait_all("sp", list({id(b): b for b in outb}.values()))
    return nc


def l2b_inputs(h0T_b, hg, P):
    w = P["ab_w_in"]
    g = hg // 2
    xo = 2328
    rep = lambda v, n: np.ascontiguousarray(np.tile(np.asarray(v, np.float32)[None, :], (128, n)))
    chans = [np.arange(hg * 256, hg * 256 + 128), np.arange(hg * 256 + 128, hg * 256 + 256),
             1024 + g * 128 + np.arange(128), 1280 + g * 128 + np.arange(128)]
    cwf = P["b_conv_w"][:, 0, :]
    convw = np.stack([cwf[:, ch].T for ch in chans], axis=1)
    convb = np.stack([P["b_conv_b"][ch] for ch in chans], axis=1)
    hs = slice(hg * 4, hg * 4 + 4)
    t = np.arange(128)
    return {"hall": h0T_b, "wx": np.ascontiguousarray(w[:, xo + hg * 256: xo + (hg + 1) * 256]),
            "wB": np.ascontiguousarray(w[:, xo + 1024 + g * 128: xo + 1024 + (g + 1) * 128]),
            "wC": np.ascontiguousarray(w[:, xo + 1280 + g * 128: xo + 1280 + (g + 1) * 128]),
            "wdt": np.ascontiguousarray(w[:, 3864 + hg * 4: 3864 + hg * 4 + 4]),
            "convw": np.ascontiguousarray(convw.astype(np.float32)), "convb": np.ascontiguousarray(convb.astype(np.float32)),
            "dtb": rep(P["b_dt_bias"][hs], 4), "alog": rep(P["b_a_log"][hs], 4), "dskip": rep(P["b_d_skip"][hs], 1),
            "tri": (t[:, None] <= t[None, :]).astype(np.float32), "su": (t[:, None] > t[None, :]).astype(np.float32)}


def linear_tile(cx, in_ap, inb, W_dram, KC, M, out_fn):
    k = cx.k
    Wv = W_dram.rearrange("(kc p) m -> p kc m", p=128)
    for mb in range(M // 256):
        w, wb = cx.wload(Wv[:, :, mb * 256:(mb + 1) * 256], [128, KC, 256])
        for m2 in range(2):
            ps, pb = cx.psum()
            mm(k, ps, [(w[:, kc, m2 * 128:(m2 + 1) * 128], in_ap[:, kc, :]) for kc in range(KC)], [wb, inb], pb)
            out_fn(mb * 2 + m2, ps, pb)


def build_l3(nc=None, cx=None, io=None):
    nc, cx, io = _std(nc, cx, io)
    din = io.inp
    x1T = din("x1T", [128, 8, TOK]); h0T = din("h0T", [1024, TOK], BF16).rearrange("(kc p) t -> p kc t", p=128)
    oaall = din("oaall", [4 * 4096, 256], BF16); yall = din("yall", [4 * SEQ, 256], BF16); qsel_d = din("qsel", [128, 4])
    gains = din("gains", [128, 12, 8]); normw_d = din("normw", [128, 8])
    wz = din("wz", [D, D]); wout = din("wout", [1536, D])
    f2 = [din("f2g", [D, DFF]), din("f2u", [D, DFF]), din("f2d", [DFF, D])]
    f1 = [din("f1g", [D, DFF]), din("f1u", [D, DFF]), din("f1d", [DFF, D])]
    x4T = io.out("x4T", [128, 8, TOK], F32)
    h1T = io.out("h1T", [1024, TOK], BF16).rearrange("(kc p) t -> p kc t", p=128)
    k = cx.k
    cx.wring_n = 2
    cx.wstg_n = 1
    g, gb = load_gains(cx, gains)
    nw, nwb = k.tile("normw", [128, 8], F32); k.dma("sp", nw, normw_d, w=[nwb], sb=nwb)
    qsel, qselb = k.tile("qsel", [128, 4], F32); k.dma("sp", qsel, qsel_d, w=[qselb], sb=qselb)
    ones512, ones512b = k.tile("ones512", [128, 128], F32)
    k.op("pool", lambda e: e.memset(ones512, 1.0 / 512), w=[ones512b])
    idq, idqb = k.tile("idq", [128, 4, 128], BF16)
    for j in range(4):
        k.op("dve", lambda e: e.tensor_scalar(out=idq[:, j, :], in0=cx.ident, scalar1=qsel[:, j:j + 1], scalar2=None, op0=ALU.mult),
             r=[cx.b_ident, qselb], w=[idqb])
    c_eps5 = cx.const(1e-5)
    outs = []
    for half in range(2):
        x, xb = k.ring("x_res", [128, 8, 1024], F32, 1)
        k.dma("sp", x, x1T[:, :, half * 1024:(half + 1) * 1024], w=[xb], sb=xb)
        for tt in range(2):
            t0 = half * 1024 + tt * 512
            tq = t0 // 512
            sl = slice(tt * 512, (tt + 1) * 512)
            o, ob = k.ring("ffn_o", [128, 8, 1024], F32, 1)
            act, actb = k.ring("act_bf", [128, 22, 1024], BF16, 1)
            ys = o[:, :, 0:512]
            mo = o[:, :, 512:1024]
            mixin = act[:, 0:6, :].rearrange("p a (b t) -> p (a b) t", t=512)
            h0 = act[:, 6:10, :].rearrange("p a (b t) -> p (a b) t", t=512)
            k.dma("sp", h0, h0T[:, :, t0:t0 + 512], w=[actb], sb=actb)
            for hg in range(4):
                cand, candb = k.ring("ycand", [128, 4, 4, 256], BF16, 1)
                for j in range(4):
                    r0 = (j * 4 + hg) * TOK + t0
                    k.dma("sp", cand[:, j], yall[r0:r0 + 512, :].rearrange("(tb p) c -> p tb c", p=128), w=[candb], sb=candb)
                for hh in range(2):
                    ps, pb = cx.psum()
                    for tb in range(4):
                        mm(k, ps[:, tb * 128:(tb + 1) * 128],
                           [(cand[:, j, tb, hh * 128:(hh + 1) * 128], idq[:, j, :]) for j in range(4)], [candb, idqb], pb)
                    k.op("act", lambda e: e.activation(out=ys[:, hg * 2 + hh, :], in_=ps, func=AF.Copy), r=[pb], w=[ob])
            for kvh in range(2):
                ocand, ocandb = k.ring("ocand", [128, 2, 4, 2, 256], BF16, 1)
                for par in range(2):
                    for j in range(4):
                        r0 = ((j // 2) * 4 + kvh * 2 + par) * 2048 + (j % 2) * 1024 + (2 * tq) * 128
                        k.dma("sp", ocand[:, par, j], oaall[r0:r0 + 256, :].rearrange("(i p) c -> p i c", p=128), w=[ocandb], sb=ocandb)
                for hh in range(2):
                    ps, pb = cx.psum()
                    for u in range(4):
                        par, i2 = u % 2, u // 2
                        mm(k, ps[:, u * 128:(u + 1) * 128],
                           [(ocand[:, par, j, i2, hh * 128:(hh + 1) * 128], idq[:, j, :]) for j in range(4)], [ocandb, idqb], pb)
                    k.op("act", lambda e: e.activation(out=mixin[:, kvh * 2 + hh, :], in_=ps, func=AF.Copy), r=[pb], w=[actb])

            def z_out(mc, ps, pb):
                zs, zsb = k.ring("sg", [128, 512], F32, 3)
                k.op("act", lambda e: e.activation(out=zs, in_=ps, func=AF.Silu), r=[pb], w=[zsb])
                k.op("dve", lambda e: e.tensor_tensor(out=ys[:, mc, :], in0=ys[:, mc, :], in1=zs, op=ALU.mult), r=[zsb, ob], w=[ob])
            linear_tile(cx, h0, actb, wz, 8, D, z_out)
            sq, sqb = k.ring("sq", [128, 8, 512], F32, 1)
            k.op("act", lambda e: e.activation(out=sq, in_=ys, func=AF.Square), r=[ob], w=[sqb])
            for gi in range(2):
                ps, pb = cx.psum()
                mm(k, ps, [(ones512, sq[:, gi * 4 + c, :]) for c in range(4)], [ones512b, sqb], pb)
                rs, rsb = k.ring("rstd", [128, 512], F32, 2)
                k.op("act", lambda e: e.activation(out=rs, in_=ps, func=AF.Sqrt, bias=c_eps5, scale=1.0), r=[pb, cx.b_consts], w=[rsb])
                k.op("dve", lambda e: e.reciprocal(out=rs, in_=rs), r=[rsb], w=[rsb])
                for c in range(4):
                    ch = gi * 4 + c
                    k.op("dve", lambda e: e.scalar_tensor_tensor(out=mixin[:, 4 + ch, :], in0=ys[:, ch, :], scalar=nw[:, ch:ch + 1], in1=rs,
                                                                 op0=ALU.mult, op1=ALU.mult), r=[ob, nwb, rsb], w=[actb])

            def mix_out(mc, ps, pb):
                k.op("act", lambda e: e.activation(out=mo[:, mc, :], in_=ps, func=AF.Copy), r=[pb], w=[ob])
            linear_tile(cx, mixin, actb, wout, 12, D, mix_out)
            post_norm_add(cx, x[:, :, sl], xb, mo, ob, g[:, 3, :], gb, 1.0)
        ffn_half(cx, x, xb, 2, f2[0], f2[1], f2[2], g[:, 4, :], g[:, 5, :], gb)
        ffn_half(cx, x, xb, 2, f1[0], f1[1], f1[2], g[:, 6, :], g[:, 7, :], gb)
        k.dma("sp", x4T[:, :, half * 1024:(half + 1) * 1024], x, r=[xb], sb=xb)
        hh_, hhb = k.ring("h_bf", [128, 8, 1024], BF16, 1)
        for tt in range(2):
            sl = slice(tt * 512, (tt + 1) * 512)
            norm_bf16(cx, x[:, :, sl], xb, g[:, 8, :], gb, hh_[:, :, sl], hhb)
        k.dma("sp", h1T[:, :, half * 1024:(half + 1) * 1024], hh_, r=[hhb], sb=hhb)
        outs += [xb, hhb]
    k.wait_all("sp", outs)
    return nc


def build_l5(nc=None, cx=None, io=None):
    nc, cx, io = _std(nc, cx, io)
    din = io.inp
    x4T = din("x4T", [128, 8, TOK])
    ygall = din("ygall", [4096, TOK], BF16).rearrange("(j hg pc p) t -> j p (hg pc) t", j=4, hg=4, pc=2, p=128)
    qsel_d = din("qsel", [128, 4])
    gains = din("gains", [128, 12, 8]); wo = din("wo", [D, D])
    f2 = [din("f2g", [D, DFF]), din("f2u", [D, DFF]), din("f2d", [DFF, D])]
    outT = io.out("outT", [128, 8, TOK], F32)
    k = cx.k
    cx.wstg_n = 1
    g, gb = load_gains(cx, gains)
    qsel, qselb = k.tile("qsel", [128, 4], F32); k.dma("sp", qsel, qsel_d, w=[qselb], sb=qselb)
    outs = []
    for half in range(2):
        x, xb = k.ring("x_res", [128, 8, 1024], F32, 1)
        k.dma("sp", x, x4T[:, :, half * 1024:(half + 1) * 1024], w=[xb], sb=xb)
        for tt in range(2):
            t0 = half * 1024 + tt * 512
            sl = slice(tt * 512, (tt + 1) * 512)
            o, ob = k.ring("ffn_o", [128, 8, 1024], F32, 1)
            act, actb = k.ring("act_bf", [128, 22, 1024], BF16, 1)
            mo = o[:, :, 512:1024]
            yg = act[:, 6:10, :].rearrange("p a (b t) -> p (a b) t", t=512)
            for j in range(4):
                cand, candb = k.ring("ygcand", [128, 8, 512], BF16, 1)
                k.dma("sp", cand, ygall[j][:, :, t0:t0 + 512], w=[candb], sb=candb)
                if j == 0:
                    k.op("dve", lambda e: e.tensor_scalar(out=yg, in0=cand, scalar1=qsel[:, 0:1], scalar2=None, op0=ALU.mult),
                         r=[candb, qselb], w=[actb])
                else:
                    k.op("dve", lambda e: e.scalar_tensor_tensor(out=yg, in0=cand, scalar=qsel[:, j:j + 1], in1=yg, op0=ALU.mult, op1=ALU.add),
                         r=[candb, qselb, actb], w=[actb])

            def mix_out(mc, ps, pb):
                k.op("act", lambda e: e.activation(out=mo[:, mc, :], in_=ps, func=AF.Copy), r=[pb], w=[ob])
            linear_tile(cx, yg, actb, wo, 8, D, mix_out)
            post_norm_add(cx, x[:, :, sl], xb, mo, ob, g[:, 9, :], gb, 1.0)
        ffn_half(cx, x, xb, 2, f2[0], f2[1], f2[2], g[:, 10, :], g[:, 11, :], gb)
        k.dma("sp", outT[:, :, half * 1024:(half + 1) * 1024], x, r=[xb], sb=xb)
        outs.append(xb)
    k.wait_all("sp", outs)
    return nc


def _run(nc, in_maps):
    res = run_bass_kernel_spmd(nc, in_maps, core_ids=list(range(NCORE)))
    return res.results


def kernel_unfused(**inp):
    import ml_dtypes
    f32 = lambda a: np.ascontiguousarray(np.asarray(a, dtype=np.float32))
    I = {k_: f32(v) for k_, v in inp.items()}
    x = I["x"].reshape(16384, D)
    g = gains_layout(I["norm_gains"])
    tok = [slice(c * TOK, (c + 1) * TOK) for c in range(NCORE)]
    grp = lambda lst, b: np.ascontiguousarray(np.concatenate(lst[4 * b:4 * b + 4], axis=0))
    qsel = []
    for c in range(NCORE):
        q_ = np.zeros((128, 4), np.float32); q_[:, c % 4] = 1.0
        qsel.append(q_)
    r1 = _run(build_l1(), [{"xT": fm(x[tok[c]]), "gains": g, "wg": I["ffn1_w_gate"][0], "wu": I["ffn1_w_up"][0],
                            "wd": I["ffn1_w_down"][0]} for c in range(NCORE)])
    x1T = [np.asarray(r["x1T"]) for r in r1]
    h0loc = [np.asarray(r["h0T"]) for r in r1]
    h0all = [grp(h0loc, b) for b in range(2)]
    PA = {k_: I[k_][0] for k_ in I if k_.startswith("a_") or k_.startswith("ab_") or k_.startswith("b_")}
    C = nsa_consts()
    r2a = _run(build_l2a(), [l2a_inputs(h0all[c // 4], (c % 4) // 2, c % 2, PA, C) for c in range(NCORE)])
    r2b = _run(build_l2b(), [l2b_inputs(h0all[c // 4], c % 4, PA) for c in range(NCORE)])
    oaall = [grp([np.asarray(r["oa"]) for r in r2a], b) for b in range(2)]
    yall = [grp([np.asarray(r["y"]) for r in r2b], b) for b in range(2)]
    w_in = I["ab_w_in"][0]
    m3 = []
    for c in range(NCORE):
        m3.append({"x1T": x1T[c], "h0T": h0loc[c], "oaall": oaall[c // 4], "yall": yall[c // 4], "qsel": qsel[c], "gains": g,
                   "normw": np.ascontiguousarray(I["b_norm_w"][0].reshape(8, 128).T),
                   "wz": np.ascontiguousarray(w_in[:, 1304:2328]), "wout": I["ab_w_out"][0],
                   "f2g": I["ffn2_w_gate"][0], "f2u": I["ffn2_w_up"][0], "f2d": I["ffn2_w_down"][0],
                   "f1g": I["ffn1_w_gate"][1], "f1u": I["ffn1_w_up"][1], "f1d": I["ffn1_w_down"][1]})
    r3 = _run(build_l3(), m3)
    x4T = [np.asarray(r["x4T"]) for r in r3]
    h1all = [grp([np.asarray(r["h1T"]) for r in r3], b) for b in range(2)]
    PC = {k_: I[k_][0] for k_ in I if k_.startswith("c_")}
    r4 = _run(build_l4(), [l4_inputs(h1all[c // 4], c % 4, PC) for c in range(NCORE)])
    ygall = [grp([np.asarray(r["ygT"]) for r in r4], b) for b in range(2)]
    m5 = []
    for c in range(NCORE):
        m5.append({"x4T": x4T[c], "ygall": ygall[c // 4], "qsel": qsel[c], "gains": g,
                   "wo": I["c_w_o"][0], "f2g": I["ffn2_w_gate"][1], "f2u": I["ffn2_w_up"][1], "f2d": I["ffn2_w_down"][1]})
    r5 = _run(build_l5(), m5)
    out = np.concatenate([unfm(np.asarray(r["outT"])) for r in r5], axis=0)
    return np.ascontiguousarray(out.reshape(2, SEQ, D).astype(np.float32))


RG = [[0, 1, 2, 3], [4, 5, 6, 7]]


def build_fused(upto=99):
    nc = bass.Bass("TRN2", target_bir_lowering=False, num_devices=NCORE)
    k = K(nc)
    idram = lambda n, sh, dt: nc.dram_tensor(n, list(sh), dt, kind="Internal").ap()
    x1T = idram("i_x1T", [128, 8, TOK], F32)
    h0loc = idram("i_h0loc", [1024, TOK], BF16); h0all = idram("i_h0all", [4096, TOK], BF16)
    oaloc = idram("i_oaloc", [NQB * 128, 256], BF16); oaall = idram("i_oaall", [4 * 4096, 256], BF16)
    yloc = idram("i_yloc", [SEQ, 256], BF16); yall = idram("i_yall", [4 * SEQ, 256], BF16)
    x4T = idram("i_x4T", [128, 8, TOK], F32)
    h1loc = idram("i_h1loc", [1024, TOK], BF16); h1all = idram("i_h1all", [4096, TOK], BF16)
    ygloc = idram("i_ygloc", [1024, TOK], BF16); ygall = idram("i_ygall", [4096, TOK], BF16)
    ccsem = Sem(nc.alloc_semaphore(name="ccsem"), "cc")
    k.dall = [ccsem]; k.dused = []; k.dfree = []

    def allgather(src, dst, wait=True):
        rows, cols = src.shape
        R = (1 << 20) // (cols * mybir.dt.size(src.dtype))
        k.barrier()
        for i in range(rows // R):
            ins = nc.gpsimd.collective_compute("AllGather", ALU.bypass, replica_groups=RG,
                                               ins=[src[i * R:(i + 1) * R, :]], outs=[dst[i * 4 * R:(i + 1) * 4 * R, :]])
            ins.then_inc(ccsem.h, 1)
            ccsem.total += 1
        if wait:
            k.barrier()

    def phase(fn, pre, ext, **kw):
        k.begin_phase()
        cx = Ctx(nc, k)
        fn(nc=nc, cx=cx, io=IO(nc, pre, ext), **kw)
        k.end_phase()

    steps = [lambda: phase(build_l1, "l1_", {"x1T": x1T, "h0T": h0loc}),
             lambda: allgather(h0loc, h0all),
             lambda: phase(build_l2a, "l2a_", {"hall": h0all, "oa": oaloc}),
             lambda: allgather(oaloc, oaall, wait=False),
             lambda: phase(build_l2b, "l2b_", {"hall": h0all, "y": yloc}),
             lambda: allgather(yloc, yall),
             lambda: phase(build_l3, "l3_", {"x1T": x1T, "h0T": h0loc, "oaall": oaall, "yall": yall, "x4T": x4T, "h1T": h1loc}),
             lambda: allgather(h1loc, h1all),
             lambda: phase(build_l4, "l4_", {"hall": h1all, "ygT": ygloc}),
             lambda: allgather(ygloc, ygall),
             lambda: phase(build_l5, "l5_", {"x4T": x4T, "ygall": ygall})]
    for st in steps[:upto]:
        st()
    if upto < len(steps):
        nc.dram_tensor("l5_outT", [128, 8, TOK], F32, kind="ExternalOutput")
    return nc


def kernel(**inp):
    f32 = lambda a: np.ascontiguousarray(np.asarray(a, dtype=np.float32))
    I = {k_: f32(v) for k_, v in inp.items()}
    x = I["x"].reshape(16384, D)
    g = gains_layout(I["norm_gains"])
    PA = {k_: I[k_][0] for k_ in I if k_.startswith("a_") or k_.startswith("ab_") or k_.startswith("b_")}
    PC = {k_: I[k_][0] for k_ in I if k_.startswith("c_")}
    C = nsa_consts()
    w_in = I["ab_w_in"][0]
    in_maps = []
    for c in range(NCORE):
        m = {}
        qs = np.zeros((128, 4), np.float32); qs[:, c % 4] = 1.0
        m.update({"l1_" + k_: v for k_, v in {"xT": fm(x[c * TOK:(c + 1) * TOK]), "gains": g, "wg": I["ffn1_w_gate"][0],
                                             "wu": I["ffn1_w_up"][0], "wd": I["ffn1_w_down"][0]}.items()})
        a = l2a_inputs(None, (c % 4) // 2, c % 2, PA, C); a.pop("hall")
        m.update({"l2a_" + k_: v for k_, v in a.items()})
        b_ = l2b_inputs(None, c % 4, PA); b_.pop("hall")
        m.update({"l2b_" + k_: v for k_, v in b_.items()})
        m.update({"l3_" + k_: v for k_, v in {"qsel": qs, "gains": g,
                  "normw": np.ascontiguousarray(I["b_norm_w"][0].reshape(8, 128).T),
                  "wz": np.ascontiguousarray(w_in[:, 1304:2328]), "wout": I["ab_w_out"][0],
                  "f2g": I["ffn2_w_gate"][0], "f2u": I["ffn2_w_up"][0], "f2d": I["ffn2_w_down"][0],
                  "f1g": I["ffn1_w_gate"][1], "f1u": I["ffn1_w_up"][1], "f1d": I["ffn1_w_down"][1]}.items()})
        d4 = l4_inputs(None, c % 4, PC); d4.pop("hall")
        m.update({"l4_" + k_: v for k_, v in d4.items()})
        m.update({"l5_" + k_: v for k_, v in {"qsel": qs, "gains": g, "wo": I["c_w_o"][0], "f2g": I["ffn2_w_gate"][1],
                                             "f2u": I["ffn2_w_up"][1], "f2d": I["ffn2_w_down"][1]}.items()})
        in_maps.append(m)
    import os
    upto = int(os.environ.get('FUSED_UPTO', '99'))
    nc_ = build_fused(upto)
    if upto < 99:
        names = {a.memorylocations[0].name for a in nc_.allocations if hasattr(a, 'memorylocations') and a.memorylocations}
        in_maps = [{k_: v for k_, v in m.items() if k_ in names} for m in in_maps]
    res = _run(nc_, in_maps)
    out = np.concatenate([unfm(np.asarray(r["l5_outT"])) for r in res], axis=0)
    return np.ascontiguousarray(out.reshape(2, SEQ, D).astype(np.float32))
```

```python
import numpy as np
import concourse.bass as bass
import concourse.mybir as mybir
from concourse.bass_utils import run_bass_kernel_spmd

F32 = mybir.dt.float32
BF16 = mybir.dt.bfloat16
AF = mybir.ActivationFunctionType
ALU = mybir.AluOpType
AX = mybir.AxisListType

D = 1024
DFF = 2816
NCORE = 8
TOK = 2048
EPS = 1e-6


class Buf:
    __slots__ = ("name", "lw", "rd", "dsem", "lw_dma")

    def __init__(self, name):
        self.name = name
        self.lw = None
        self.rd = []
        self.dsem = None
        self.lw_dma = False


class Sem:
    __slots__ = ("h", "total", "name")

    def __init__(self, h, name):
        self.h = h
        self.total = 0
        self.name = name


class K:
    def __init__(self, nc):
        self.nc = nc
        self.engs = {"pe": nc.tensor, "act": nc.scalar, "dve": nc.vector,
                     "pool": nc.gpsimd, "sp": nc.sync}
        self.esem = {n: Sem(nc.alloc_semaphore(name="es_" + n), n) for n in self.engs}
        self.waited = {n: {} for n in self.engs}
        self.nsem = 0
        self.ninstr = 0
        self.nwait = 0
        self.rings = {}

    def begin_phase(self):
        import contextlib
        self.phase = getattr(self, "phase", 0) + 1
        self.stack = contextlib.ExitStack()
        self.rings = {}
        self.dfree = getattr(self, "dfree", [])

    def end_phase(self):
        self.barrier()
        self.stack.close()
        self.stack = None
        self.dfree = list(self.dused)
        self.dused = []

    def barrier(self):
        sems = list(self.esem.values()) + list(getattr(self, "dall", []))
        for eng in self.engs:
            needs = {s_: s_.total for s_ in sems if s_.total > 0}
            self._emit_waits(eng, needs)

    def sb(self, name, shape, dt):
        if getattr(self, "stack", None) is not None:
            return self.stack.enter_context(self.nc.sbuf_tensor("s%d_%s" % (self.phase, name), list(shape), dt)).ap()
        return self._sb_static(name, shape, dt)

    def _sb_static(self, name, shape, dt):
        return self.nc.alloc_sbuf_tensor("s_" + name, list(shape), dt).ap()

    def ps(self, name, shape, dt=F32):
        if getattr(self, "stack", None) is not None:
            return self.stack.enter_context(self.nc.psum_tensor("p%d_%s" % (self.phase, name), list(shape), dt)).ap()
        return self.nc.alloc_psum_tensor("p_" + name, list(shape), dt).ap()

    def tile(self, name, shape, dt):
        return self.sb(name, shape, dt), Buf(name)

    def ring(self, name, shape, dt, n, psum=False):
        if name not in self.rings:
            sl = []
            for i in range(n):
                nm = "%s_%d" % (name, i)
                ap = self.ps(nm, shape, dt) if psum else self.sb(nm, shape, dt)
                sl.append((ap, Buf(nm)))
            self.rings[name] = [sl, 0]
        r = self.rings[name]
        s = r[0][r[1] % len(r[0])]
        r[1] += 1
        return s

    def _dsem(self, b, q="sp"):
        if b.dsem is None:
            if not hasattr(self, "dall"):
                self.dall, self.dused, self.dfree = [], [], getattr(self, "dfree", [])
            if self.dfree and q != "pool":
                b.dsem = self.dfree.pop()
            else:
                b.dsem = Sem(self.nc.alloc_semaphore(name="ds%d" % self.nsem), b.name)
                self.nsem += 1
                self.dall.append(b.dsem)
            if q != "pool":
                self.dused.append(b.dsem)
        return b.dsem

    def _need(self, eng, tok, needs):
        if tok is None:
            return
        s, v = tok
        if v is None:
            v = s.total
        if eng == "pe" and s is self.esem["pe"]:
            return
        if v > needs.get(s, 0):
            needs[s] = v

    def _emit_waits(self, eng, needs):
        e = self.engs[eng]
        w = self.waited[eng]
        for s, v in needs.items():
            if w.get(s, 0) >= v:
                continue
            e.wait_ge(s.h, v)
            self.nwait += 1
            w[s] = v

    def op(self, eng, fn, r=(), w=(), inc=True):
        needs = {}
        for b in r:
            self._need(eng, b.lw, needs)
        for b in w:
            self._need(eng, b.lw, needs)
            for t in b.rd:
                self._need(eng, t, needs)
        self._emit_waits(eng, needs)
        ins = fn(self.engs[eng])
        s = self.esem[eng]
        if inc:
            s.total += 1
            ins.then_inc(s.h, 1)
            tok = (s, s.total)
        else:
            tok = (s, s.total + 1)
        for b in r:
            b.rd.append(tok)
            if len(b.rd) > 24:
                b.rd = self._compact(b.rd)
        for b in w:
            b.lw = tok
            b.rd = []
            b.lw_dma = False
        self.ninstr += 1
        return ins

    def _compact(self, toks):
        best = {}
        for s, v in toks:
            if v is None:
                best[s] = None
            elif s not in best or (best[s] is not None and v > best[s]):
                best[s] = v
        return [(s, v) for s, v in best.items()]

    def dma(self, q, out, in_, r=(), w=(), sb=None, **kw):
        s = self._dsem(sb, q)
        needs = {}
        for b in r:
            self._need(q, b.lw, needs)
        for b in w:
            if not (b.lw_dma and b.lw is not None and b.lw[0] is s and not b.rd):
                self._need(q, b.lw, needs)
            for t in b.rd:
                self._need(q, t, needs)
        self._emit_waits(q, needs)
        ins = self.engs[q].dma_start(out=out, in_=in_, **kw)
        s.total += 16
        ins.then_inc(s.h, 16)
        tok = (s, None)
        for b in r:
            b.rd.append(tok)
        for b in w:
            b.lw = tok
            b.rd = []
            b.lw_dma = True
        self.ninstr += 1
        return ins

    def wait_all(self, eng, bufs):
        needs = {}
        for b in bufs:
            self._need(eng, b.lw, needs)
            for t in b.rd:
                self._need(eng, t, needs)
        self._emit_waits(eng, needs)


class Ctx:
    def __init__(self, nc, k=None):
        self.nc = nc
        self.k = k if k is not None else K(nc)
        k = self.k
        self.ones_d, self.b_ones_d = k.tile("ones_d", [128, 128], F32)
        k.op("pool", lambda e: e.memset(self.ones_d, 1.0 / D), w=[self.b_ones_d])
        self.ident, self.b_ident = k.tile("ident", [128, 128], F32)
        k.op("pool", lambda e: e.memset(self.ident, 1.0), w=[self.b_ident])
        k.op("pool", lambda e: e.affine_select(out=self.ident, in_=self.ident, pattern=[[-1, 128]],
                                               compare_op=ALU.is_equal, fill=0.0, base=0,
                                               channel_multiplier=1), r=[self.b_ident], w=[self.b_ident])
        self.dq = 0
        self.consts, self.b_consts = k.tile("consts", [128, 16], F32)
        self.cvals = {}

    def const(self, v):
        if v not in self.cvals:
            i = len(self.cvals)
            self.cvals[v] = i
            self.k.op("pool", lambda e: e.memset(self.consts[:, i:i + 1], float(v)), w=[self.b_consts])
        i = self.cvals[v]
        return self.consts[:, i:i + 1]

    def psum(self):
        return self.k.ring("psum", [128, 512], F32, 8, psum=True)

    def wload(self, dram_ap, shape, alt=False):
        k = self.k
        ap, b = k.ring("wring", [128, 5632], BF16, getattr(self, "wring_n", 3))
        n = shape[1] * shape[2]
        v = ap[:, 0:n].rearrange("p (a b) -> p a b", a=shape[1])
        if alt and n <= 2048:
            st, stb = k.ring("wstg", [128, 2048], F32, getattr(self, "wstg_n", 2))
            sv = st[:, 0:n].rearrange("p (a b) -> p a b", a=shape[1])
            k.dma("sp", sv, dram_ap, w=[stb], sb=stb)
            k.op("dve", lambda e: e.tensor_copy(out=v, in_=sv), r=[stb], w=[b])
        else:
            k.dma("pool", v, dram_ap, w=[b], sb=b)
        return v, b


class IO:
    def __init__(self, nc, pre="", ext=None):
        self.nc, self.pre, self.ext = nc, pre, dict(ext or {})

    def inp(self, name, shape, dt=F32):
        if name in self.ext:
            return self.ext[name]
        return self.nc.dram_tensor(self.pre + name, list(shape), dt, kind="ExternalInput").ap()

    def out(self, name, shape, dt=F32):
        if name in self.ext:
            return self.ext[name]
        return self.nc.dram_tensor(self.pre + name, list(shape), dt, kind="ExternalOutput").ap()


def _std(nc, cx, io):
    if nc is None:
        nc = bass.Bass("TRN2", target_bir_lowering=False)
    if cx is None:
        cx = Ctx(nc)
    if io is None:
        io = IO(nc)
    return nc, cx, io


def hall_tile(hall, t0, n):
    r, col = t0 // TOK, t0 % TOK
    v = hall.rearrange("(i r kk p) t -> r p i kk t", i=4, r=4, kk=2, p=128)
    return v[r][:, :, :, col:col + n]


def dma_hall(k, dst, hall, t0, n, buf, **kw):
    src = hall_tile(hall, t0, n)
    for i in range(4):
        k.dma("sp", dst[:, 2 * i:2 * i + 2, :], src[:, i], w=[buf], sb=buf, **kw)


def rms_rstd(cx, x_ap, xb, eps=EPS):
    k = cx.k
    T = x_ap.shape[2]
    sq, sqb = k.ring("sq", [128, 8, 512], F32, 1)
    k.op("act", lambda e: e.activation(out=sq[:, :, 0:T], in_=x_ap, func=AF.Square), r=[xb], w=[sqb])
    ps, pb = cx.psum()
    for c in range(8):
        k.op("pe", lambda e: e.matmul(ps[:, 0:T], lhsT=cx.ones_d, rhs=sq[:, c, 0:T], start=(c == 0), stop=(c == 7)),
             r=[sqb, cx.b_ones_d], w=[pb], inc=(c == 7))
    rs, rsb = k.ring("rstd", [128, 512], F32, 2)
    c_eps = cx.const(eps)
    k.op("act", lambda e: e.activation(out=rs[:, 0:T], in_=ps[:, 0:T], func=AF.Sqrt, bias=c_eps, scale=1.0),
         r=[pb, cx.b_consts], w=[rsb])
    k.op("dve", lambda e: e.reciprocal(out=rs[:, 0:T], in_=rs[:, 0:T]), r=[rsb], w=[rsb])
    return rs[:, 0:T], rsb


def norm_bf16(cx, x_ap, xb, g_ap, gb, out_ap, outb):
    k = cx.k
    rs, rsb = rms_rstd(cx, x_ap, xb)
    for c in range(8):
        eng = "dve"
        k.op(eng, lambda e: e.scalar_tensor_tensor(out=out_ap[:, c, :], in0=x_ap[:, c, :], scalar=g_ap[:, c:c + 1],
                                                   in1=rs, op0=ALU.mult, op1=ALU.mult),
             r=[xb, gb, rsb], w=[outb])


def ffn_half(cx, x, xb, NT, wg, wu, wd, g_in, g_out, gb):
    k = cx.k
    T = NT * 512
    h, hb = k.ring("h_bf", [128, 8, 1024], BF16, 1)
    act, actb = k.ring("act_bf", [128, 22, 1024], BF16, 1)
    for tt in range(NT):
        sl = slice(tt * 512, (tt + 1) * 512)
        norm_bf16(cx, x[:, :, sl], xb, g_in, gb, h[:, :, sl], hb)
    wgv = wg.rearrange("(kc p) f -> p kc f", p=128)
    wuv = wu.rearrange("(kc p) f -> p kc f", p=128)
    for fb in range(11):
        gw, gwb = cx.wload(wgv[:, :, fb * 256:(fb + 1) * 256], [128, 8, 256])
        uw, uwb = cx.wload(wuv[:, :, fb * 256:(fb + 1) * 256], [128, 8, 256], alt=True)
        for tt in range(NT):
            sl = slice(tt * 512, (tt + 1) * 512)
            for fc in range(2):
                f = fb * 2 + fc
                pg, pgb = cx.psum()
                pu, pub = cx.psum()
                for c in range(8):
                    k.op("pe", lambda e: e.matmul(pg, lhsT=gw[:, c, fc * 128:(fc + 1) * 128], rhs=h[:, c, sl],
                                                  start=(c == 0), stop=(c == 7)), r=[gwb, hb], w=[pgb], inc=(c == 7))
                for c in range(8):
                    k.op("pe", lambda e: e.matmul(pu, lhsT=uw[:, c, fc * 128:(fc + 1) * 128], rhs=h[:, c, sl],
                                                  start=(c == 0), stop=(c == 7)), r=[uwb, hb], w=[pub], inc=(c == 7))
                sg, sgb = k.ring("sg", [128, 512], F32, 3)
                k.op("act", lambda e: e.activation(out=sg, in_=pg, func=AF.Silu), r=[pgb], w=[sgb])
                k.op("dve", lambda e: e.tensor_tensor(out=act[:, f, sl], in0=sg, in1=pu, op=ALU.mult),
                     r=[sgb, pub], w=[actb])
    wdv = wd.rearrange("(fc p) d -> p fc d", p=128)
    o, ob = k.ring("ffn_o", [128, 8, 1024], F32, 1)
    for db in range(4):
        dw, dwb = cx.wload(wdv[:, :, db * 256:(db + 1) * 256], [128, 22, 256])
        for tt in range(NT):
            sl = slice(tt * 512, (tt + 1) * 512)
            for dc in range(2):
                d = db * 2 + dc
                po, pob = cx.psum()
                for f in range(22):
                    k.op("pe", lambda e: e.matmul(po, lhsT=dw[:, f, dc * 128:(dc + 1) * 128], rhs=act[:, f, sl],
                                                  start=(f == 0), stop=(f == 21)), r=[dwb, actb], w=[pob], inc=(f == 21))
                k.op("act", lambda e: e.activation(out=o[:, d, sl], in_=po, func=AF.Copy), r=[pob], w=[ob])
    for tt in range(NT):
        sl = slice(tt * 512, (tt + 1) * 512)
        post_norm_add(cx, x[:, :, sl], xb, o[:, :, sl], ob, g_out, gb, 0.5)


def post_norm_add(cx, x_ap, xb, o_ap, ob, g_ap, gb, coef):
    k = cx.k
    rs, rsb = rms_rstd(cx, o_ap, ob)
    for c in range(8):
        eng = "dve"
        tmp, tb = k.ring("pn_tmp" + eng, [128, 512], F32, 2)
        T = o_ap.shape[2]
        k.op(eng, lambda e: e.scalar_tensor_tensor(out=tmp[:, 0:T], in0=o_ap[:, c, :], scalar=g_ap[:, c:c + 1], in1=rs,
                                                   op0=ALU.mult, op1=ALU.mult), r=[ob, gb, rsb], w=[tb])
        k.op(eng, lambda e: e.scalar_tensor_tensor(out=x_ap[:, c, :], in0=tmp[:, 0:T], scalar=float(coef), in1=x_ap[:, c, :],
                                                   op0=ALU.mult, op1=ALU.add), r=[tb, xb], w=[xb])


def load_gains(cx, gains_dram):
    k = cx.k
    g, gb = k.tile("gains", [128, 12, 8], F32)
    k.dma("sp", g, gains_dram, w=[gb], sb=gb)
    return g, gb


def build_l1(nc=None, cx=None, io=None):
    nc, cx, io = _std(nc, cx, io)
    xT = io.inp("xT", [128, 8, TOK]); gains = io.inp("gains", [128, 12, 8])
    wg = io.inp("wg", [D, DFF]); wu = io.inp("wu", [D, DFF]); wd = io.inp("wd", [DFF, D])
    x1T = io.out("x1T", [128, 8, TOK], F32)
    h0T = io.out("h0T", [1024, TOK], BF16).rearrange("(kc p) t -> p kc t", p=128)
    k = cx.k
    g, gb = load_gains(cx, gains)
    outs = []
    for half in range(2):
        hs = slice(half * 1024, (half + 1) * 1024)
        x, xb = k.ring("x_res", [128, 8, 1024], F32, 1)
        k.dma("sp", x, xT[:, :, hs], w=[xb], sb=xb)
        ffn_half(cx, x, xb, 2, wg, wu, wd, g[:, 0, :], g[:, 1, :], gb)
        k.dma("sp", x1T[:, :, hs], x, r=[xb], sb=xb)
        hh, hhb = k.ring("h_bf", [128, 8, 1024], BF16, 1)
        for tt in range(2):
            sl = slice(tt * 512, (tt + 1) * 512)
            norm_bf16(cx, x[:, :, sl], xb, g[:, 2, :], gb, hh[:, :, sl], hhb)
        k.dma("sp", h0T[:, :, hs], hh, r=[hhb], sb=hhb)
        outs += [xb, hhb]
    k.wait_all("sp", outs)
    return nc


def fm(a):
    t = a.shape[0]
    return np.ascontiguousarray(a.T.reshape(8, 128, t).transpose(1, 0, 2))


def unfm(a):
    t = a.shape[2]
    return np.ascontiguousarray(a.transpose(1, 0, 2).reshape(1024, t).T)


def gains_layout(norm_gains):
    g = norm_gains.reshape(12, 8, 128).transpose(2, 0, 1)
    return np.ascontiguousarray(g)


SEQ = 8192
RW_EXPC = 0.6065306597126334
GN_EPS = 64e-5


def mm(k, ps_ap, pairs, rbufs, wbuf):
    n = len(pairs)
    for i, (l, r) in enumerate(pairs):
        k.op("pe", lambda e: e.matmul(ps_ap, lhsT=l, rhs=r, start=(i == 0), stop=(i == n - 1)),
             r=rbufs, w=[wbuf], inc=(i == n - 1))


class _Stop(Exception):
    pass


def build_l4(ntiles=16, stop=99, nc=None, cx=None, io=None):
    try:
        return _build_l4(ntiles, stop, nc, cx, io)
    except _Stop as e:
        return e.args[0]


def _build_l4(ntiles=16, stop=99, nc=None, cx=None, io=None):
    nc, cx, io = _std(nc, cx, io)
    dt_in = io.inp
    hall = dt_in("hall", [4096, TOK], BF16)
    hzero = dt_in("hzero", [1024, 1], BF16)
    W = {"r": dt_in("wr", [D, 256]), "k": dt_in("wk", [D, 256]), "v": dt_in("wv", [D, 256]),
         "w1": dt_in("w1", [D, 64]), "a1": dt_in("a1", [D, 64]), "g1": dt_in("g1", [D, 160])}
    w2d = dt_in("w2", [64, 256]); a2d = dt_in("a2", [64, 256]); g2d = dt_in("g2", [160, 256])
    mud = dt_in("mu", [128, 6, 8])
    vecd = dt_in("vecs", [128, 7, 2])
    maskd = dt_in("masks", [128, 3, 128])
    bonesd = dt_in("bones", [128, 128])
    resetd = dt_in("resetm", [128, 512])
    ygq = io.out("ygT", [1024, TOK], BF16).rearrange("(j c p) t -> j p c t", j=4, c=2, p=128)
    k = cx.k
    PS = lambda: k.ring("psr", [128, 512], F32, 4, psum=True)
    ybank = [k.ps("ybank%d" % i, [128, 512]) for i in range(2)]
    ybb = [Buf("ybank%d" % i) for i in range(2)]
    ybank2 = [k.ps("ybankb%d" % i, [128, 512]) for i in range(2)]
    ybb2 = [Buf("ybankb%d" % i) for i in range(2)]

    mu, mub = k.tile("mu", [128, 6, 8], F32); k.dma("sp", mu, mud, w=[mub], sb=mub)
    vec, vecb = k.tile("vecs", [128, 7, 2], F32); k.dma("sp", vec, vecd, w=[vecb], sb=vecb)
    msk, mskb = k.tile("masks", [128, 3, 128], F32); k.dma("sp", msk, maskd, w=[mskb], sb=mskb)
    bones, bonesb = k.tile("bones", [128, 128], F32); k.dma("sp", bones, bonesd, w=[bonesb], sb=bonesb)
    rstm, rstmb = k.tile("resetm", [128, 512], F32); k.dma("sp", rstm, resetd, w=[rstmb], sb=rstmb)
    W0, A0, KK, KA, RK, LNG, LNB = range(7)
    m4 = lambda i: msk[:, i, :].unsqueeze(1).to_broadcast([128, 4, 128])
    id4 = cx.ident.unsqueeze(1).to_broadcast([128, 4, 128])
    c_tiny = cx.const(1e-24)
    c_gneps = cx.const(GN_EPS)

    order = {"r": 0, "w1": 1, "k": 2, "v": 3, "a1": 4, "g1": 5}
    Wa, Wb, Wbuf = {}, {}, {}
    for nm, wd_ in W.items():
        n = wd_.shape[1]
        wa, wab = k.tile("wa_" + nm, [128, 8, n], BF16)
        wb, wbb = k.tile("wb_" + nm, [128, 8, n], BF16)
        Wa[nm], Wb[nm], Wbuf[nm] = wa, wb, [wab, wbb]
    import contextlib
    _outer = getattr(k, "stack", None)
    k.stack = contextlib.ExitStack()
    k.phase = getattr(k, "phase", 0)
    for nm, wd_ in W.items():
        n = wd_.shape[1]
        wa, wb = Wa[nm], Wb[nm]
        wab, wbb = Wbuf[nm]
        st, stb = k.ring("wstage", [128, 8, 256], F32, 1)
        k.dma("sp", st[:, :, 0:n], wd_.rearrange("(kc p) n -> p kc n", p=128), w=[stb], sb=stb)
        tmp, tmpb = k.ring("wstage2", [128, 8, 256], F32, 1)
        i = order[nm]
        k.op("dve", lambda e: e.tensor_tensor(out=tmp[:, :, 0:n], in0=st[:, :, 0:n],
                                              in1=mu[:, i, :].unsqueeze(2).to_broadcast([128, 8, n]), op=ALU.mult),
             r=[stb, mub], w=[tmpb])
        k.op("dve", lambda e: e.tensor_tensor(out=wa, in0=st[:, :, 0:n], in1=tmp[:, :, 0:n], op=ALU.subtract),
             r=[stb, tmpb], w=[wab])
        k.op("act", lambda e: e.activation(out=wb, in_=tmp[:, :, 0:n], func=AF.Copy), r=[tmpb], w=[wbb])
    k.barrier()
    k.stack.close()
    k.stack = _outer
    k.rings.pop("wstage"); k.rings.pop("wstage2")
    w2, w2b = k.tile("w2", [64, 256], BF16); k.dma("pool", w2, w2d, w=[w2b], sb=w2b)
    a2, a2b = k.tile("a2", [64, 256], BF16); k.dma("pool", a2, a2d, w=[a2b], sb=a2b)
    g2a, g2ab = k.tile("g2a", [128, 256], BF16); k.dma("pool", g2a, g2d[0:128, :], w=[g2ab], sb=g2ab)
    g2c, g2cb = k.tile("g2c", [32, 256], BF16); k.dma("pool", g2c, g2d[128:160, :], w=[g2cb], sb=g2cb)

    U = []
    for hd in range(4):
        pp = []
        for j in range(2):
            u, ub = k.tile("U%d_%d" % (hd, j), [128, 64], F32)
            k.op("pool", lambda e: e.memset(u, 0.0), w=[ub])
            pp.append((u, ub))
        U.append(pp)
    ucur = [0, 0, 0, 0]
    PL = []
    for pc in range(2):
        p_, pb_ = k.tile("PL%d" % pc, [128, 9], F32)
        k.op("pool", lambda e: e.memset(p_, 1.0), w=[pb_])
        PL.append((p_, pb_))
    outbufs = []
    if stop == 1:
        raise _Stop(nc)

    for ti in range(ntiles):
        t0 = ti * 512
        hb, hbb = k.ring("hb", [128, 8, 514], BF16, 2)
        dma_hall(k, hb[:, :, 1:513], hall, t0, 512, hbb)
        if t0 == 0:
            k.dma("sp", hb[:, :, 0:1], hzero.rearrange("(kc p) t -> p kc t", p=128), w=[hbb], sb=hbb, allow_slow_non_contiguous=True)
        else:
            dma_hall(k, hb[:, :, 0:1], hall, t0 - 1, 1, hbb, allow_slow_non_contiguous=True)

        def proj_pairs(nm, cols, tok=None):
            prs = []
            for kc in range(8):
                prs.append((Wa[nm][:, kc, cols], hb[:, kc, 1:513]))
                prs.append((Wb[nm][:, kc, cols], hb[:, kc, 0:512]))
            return prs

        FM = {}

        def evac(name, ps, pb, rows=128, func=AF.Copy, bias=None, dt=F32, extra=()):
            o, ob = k.ring("fm_" + name, [128, 512], dt, 1)
            kw = {}
            if bias is not None:
                kw["bias"] = bias
            k.op("act", lambda e: e.activation(out=o[0:rows, :], in_=ps[0:rows, :], func=func, **kw),
                 r=[pb] + list(extra), w=[ob])
            return o, ob

        for pc in range(2):
            cols = slice(pc * 128, (pc + 1) * 128)
            for nm in ("r", "k", "v"):
                ps, pb = PS()
                mm(k, ps, proj_pairs(nm, cols), [hbb] + Wbuf[nm], pb)
                FM[(nm, pc)] = evac("%s%d" % (nm, pc), ps, pb)
        vtok, vtokb = k.ring("vtok", [128, 4, 256], F32, 2)
        for half in range(2):
            ps, pb = PS()
            for bi in range(2):
                blk = half * 2 + bi
                prs = []
                for kc in range(8):
                    prs.append((hb[:, kc, 1 + blk * 128:1 + (blk + 1) * 128], Wa["v"][:, kc, :]))
                    prs.append((hb[:, kc, blk * 128:(blk + 1) * 128], Wb["v"][:, kc, :]))
                mm(k, ps[:, bi * 256:(bi + 1) * 256], prs, [hbb] + Wbuf["v"], pb)
            k.op("act", lambda e: e.activation(out=vtok[:, half * 2:half * 2 + 2, :],
                                               in_=ps.rearrange("p (a b) -> p a b", a=2), func=AF.Copy),
                 r=[pb], w=[vtokb])
        ps, pb = PS(); mm(k, ps[0:64, :], proj_pairs("w1", slice(0, 64)), [hbb] + Wbuf["w1"], pb)
        hw, hwb = evac("hw", ps, pb, rows=64, func=AF.Tanh, dt=BF16)
        ps, pb = PS(); mm(k, ps[0:64, :], proj_pairs("a1", slice(0, 64)), [hbb] + Wbuf["a1"], pb)
        ha, hab = evac("ha", ps, pb, rows=64, dt=BF16)
        ps, pb = PS(); mm(k, ps, proj_pairs("g1", slice(0, 128)), [hbb] + Wbuf["g1"], pb)
        hg0, hg0b = evac("hg0", ps, pb, func=AF.Sigmoid, dt=BF16)
        ps, pb = PS(); mm(k, ps[0:32, :], proj_pairs("g1", slice(128, 160)), [hbb] + Wbuf["g1"], pb)
        hg1, hg1b = evac("hg1", ps, pb, rows=32, func=AF.Sigmoid, dt=BF16)
        for pc in range(2):
            cols = slice(pc * 128, (pc + 1) * 128)
            ps, pb = PS(); mm(k, ps, [(w2[:, cols], hw[0:64, :])], [w2b, hwb], pb)
            FM[("sgw", pc)] = evac("sgw%d" % pc, ps, pb, func=AF.Sigmoid, bias=vec[:, W0, pc:pc + 1], extra=[vecb])
            ps, pb = PS(); mm(k, ps, [(a2[:, cols], ha[0:64, :])], [a2b, hab], pb)
            FM[("a", pc)] = evac("a%d" % pc, ps, pb, func=AF.Sigmoid, bias=vec[:, A0, pc:pc + 1], extra=[vecb])
            ps, pb = PS(); mm(k, ps, [(g2a[:, cols], hg0), (g2c[:, cols], hg1[0:32, :])], [g2ab, g2cb, hg0b, hg1b], pb)
            FM[("g", pc)] = evac("g%d" % pc, ps, pb)

        if stop == 2:
            raise _Stop(nc)
        def tmpt(name, dt=F32, n=2):
            return k.ring("tmp", [128, 512], F32, 9)

        PR = {}
        for pc in range(2):
            r_, rb_ = FM[("r", pc)]; k_, kb_ = FM[("k", pc)]; a_, ab_ = FM[("a", pc)]; sg_, sgb_ = FM[("sgw", pc)]
            kk0, kk0b = tmpt("kk0")
            k.op("dve", lambda e: e.tensor_scalar(out=kk0, in0=k_, scalar1=vec[:, KK, pc:pc + 1], scalar2=None, op0=ALU.mult),
                 r=[kb_, vecb], w=[kk0b])
            sq, sqb = tmpt("sq")
            k.op("pool", lambda e: e.tensor_tensor(out=sq, in0=kk0, in1=kk0, op=ALU.mult), r=[kk0b], w=[sqb])
            ps, pb = PS(); mm(k, ps, [(bones, sq)], [bonesb, sqb], pb)
            rn, rnb = tmpt("rn")
            k.op("act", lambda e: e.activation(out=rn, in_=ps, func=AF.Sqrt, bias=c_tiny, scale=1.0), r=[pb, cx.b_consts], w=[rnb])
            k.op("dve", lambda e: e.reciprocal(out=rn, in_=rn), r=[rnb], w=[rnb])
            kap, kapb = tmpt("kap")
            k.op("dve", lambda e: e.tensor_tensor(out=kap, in0=kk0, in1=rn, op=ALU.mult), r=[kk0b, rnb], w=[kapb])
            am, amb = tmpt("am")
            k.op("dve", lambda e: e.tensor_scalar(out=am, in0=a_, scalar1=-1.0, scalar2=vec[:, KA, pc:pc + 1],
                                                  op0=ALU.add, op1=ALU.mult), r=[ab_, vecb], w=[amb])
            kp, kpb = k.ring("kp%d" % pc, [128, 512], F32, 1)
            k.op("dve", lambda e: e.scalar_tensor_tensor(out=kp, in0=am, scalar=1.0, in1=k_, op0=ALU.add, op1=ALU.mult),
                 r=[amb, kb_], w=[kpb])
            logd, logdb = tmpt("logd")
            k.op("pool", lambda e: e.tensor_scalar(out=logd, in0=sg_, scalar1=-RW_EXPC, scalar2=None, op0=ALU.mult),
                 r=[sgb_], w=[logdb])
            Lc, Lcb = tmpt("Lc")
            k.op("dve", lambda e: e.tensor_tensor_scan(out=Lc, data0=rstm, data1=logd, initial=0.0, op0=ALU.mult, op1=ALU.add),
                 r=[rstmb, logdb], w=[Lcb])
            Lm, Lmb = tmpt("Lm")
            k.op("pool", lambda e: e.tensor_tensor(out=Lm, in0=Lc, in1=logd, op=ALU.subtract), r=[Lcb, logdb], w=[Lmb])
            P_, Pb_ = tmpt("P"); Pi, Pib = tmpt("Pi"); Pp, Ppb = tmpt("Pp")
            k.op("act", lambda e: e.activation(out=P_, in_=Lc, func=AF.Exp), r=[Lcb], w=[Pb_])
            k.op("act", lambda e: e.activation(out=Pi, in_=Lc, func=AF.Exp, scale=-1.0), r=[Lcb], w=[Pib])
            k.op("act", lambda e: e.activation(out=Pp, in_=Lm, func=AF.Exp), r=[Lmb], w=[Ppb])
            pl, plb = PL[pc]
            k.op("dve", lambda e: e.tensor_copy(out=pl[:, 0:1], in_=pl[:, 8:9]), r=[plb], w=[plb])
            k.op("dve", lambda e: e.tensor_copy(out=pl[:, 1:9], in_=P_.rearrange("p (c t) -> p c t", t=64)[:, :, 63]),
                 r=[Pb_, plb], w=[plb])
            rt, rtb = k.ring("rt%d" % pc, [128, 512], F32, 1)
            kt, ktb = k.ring("kt%d" % pc, [128, 512], F32, 1)
            bt, btb = k.ring("bt%d" % pc, [128, 512], F32, 1)
            kkt, kktb = k.ring("kkt%d" % pc, [128, 512], F32, 1)
            k.op("dve", lambda e: e.tensor_tensor(out=rt, in0=r_, in1=P_, op=ALU.mult), r=[rb_, Pb_], w=[rtb])
            k.op("pool", lambda e: e.tensor_tensor(out=kt, in0=kap, in1=Pp, op=ALU.mult), r=[kapb, Ppb], w=[ktb])
            ka_, kab_ = tmpt("ka")
            k.op("pool", lambda e: e.tensor_tensor(out=ka_, in0=kap, in1=a_, op=ALU.mult), r=[kapb, ab_], w=[kab_])
            k.op("dve", lambda e: e.tensor_tensor(out=bt, in0=ka_, in1=Pi, op=ALU.mult), r=[kab_, Pib], w=[btb])
            k.op("pool", lambda e: e.tensor_tensor(out=kkt, in0=kp, in1=Pi, op=ALU.mult), r=[kpb, Pib], w=[kktb])
            rts, rtsb = k.ring("rts%d" % pc, [128, 512], F32, 1)
            kts, ktsb = k.ring("kts%d" % pc, [128, 512], F32, 1)
            plbc = pl[:, 0:8].unsqueeze(2).to_broadcast([128, 8, 64])
            k.op("dve", lambda e: e.tensor_tensor(out=rts.rearrange("p (c t) -> p c t", t=64),
                                                  in0=rt.rearrange("p (c t) -> p c t", t=64), in1=plbc, op=ALU.mult),
                 r=[rtb, plb], w=[rtsb])
            k.op("dve", lambda e: e.tensor_tensor(out=kts.rearrange("p (c t) -> p c t", t=64),
                                                  in0=kt.rearrange("p (c t) -> p c t", t=64), in1=plbc, op=ALU.mult),
                 r=[ktb, plb], w=[ktsb])
            PR[pc] = dict(rt=(rt, rtb), kt=(kt, ktb), bt=(bt, btb), kkt=(kkt, kktb), rts=(rts, rtsb), kts=(kts, ktsb),
                          kp=(kp, kpb))
        if stop == 3:
            raise _Stop(nc)
        for pc in range(2):
            for nm_ in ("rt", "kt", "bt", "kkt"):
                src_, srcb_ = PR[pc][nm_]
                sh, shb = k.ring("sh_%s%d" % (nm_, pc), [128, 512], BF16, 1)
                k.op("act", lambda e: e.activation(out=sh, in_=src_, func=AF.Copy), r=[srcb_], w=[shb])
                PR[pc][nm_ + "_h"] = (sh, shb)
        btok, btokb = k.ring("btok", [128, 4, 256], F32, 1)
        ktok, ktokb = k.ring("ktok", [128, 4, 256], F32, 1)
        for (src, dst, dstb) in (("bt", btok, btokb), ("kkt", ktok, ktokb)):
            for pc in range(2):
                s_, sb_ = PR[pc][src]
                ps, pb = PS()
                for blk in range(4):
                    k.op("pe", lambda e: e.transpose(out=ps[:, blk * 128:(blk + 1) * 128], in_=s_[:, blk * 128:(blk + 1) * 128],
                                                     identity=cx.ident), r=[sb_, cx.b_ident], w=[pb], inc=(blk == 3))
                k.op("act", lambda e: e.activation(out=dst[:, :, pc * 128:(pc + 1) * 128],
                                                   in_=ps.rearrange("p (a b) -> p a b", a=4), func=AF.Copy), r=[pb], w=[dstb])

        if stop == 4:
            raise _Stop(nc)
        HD = {}
        for hd in range(4):
            pc, hp = hd // 2, hd % 2
            rows = slice(hp * 64, hp * 64 + 64)
            hcols = slice(hd * 64, hd * 64 + 64)
            pr = PR[pc]

            def intra(lname, rname, mi, nm, depth=2, odt=F32):
                l_, lb_ = pr[lname + "_h"]; r2, rb2 = pr[rname + "_h"]
                ps, pb = PS()
                for blk in range(4):
                    bs = slice(blk * 128, (blk + 1) * 128)
                    mm(k, ps[:, bs], [(l_[rows, bs], r2[rows, bs])], [lb_, rb2], pb)
                o, ob = k.ring("im_" + nm, [128, 4, 128], odt, depth)
                k.op("dve", lambda e: e.tensor_tensor(out=o, in0=ps.rearrange("p (a b) -> p a b", a=4), in1=m4(mi), op=ALU.mult),
                     r=[pb, mskb], w=[ob])
                return o, ob

            Pm, Pmb = intra("kt", "bt", 0, "P", 2, BF16)
            Qm, Qmb = intra("bt", "kt", 1, "Q", 2, BF16)
            AkT, AkTb = intra("kkt", "kt", 1, "AkT%d" % hd, 1)
            QBT, QBTb = intra("bt", "rt", 2, "QBT%d" % hd, 1)
            QKT, QKTb = intra("kkt", "rt", 2, "QKT%d" % hd, 1)
            Rm, Rmb = k.ring("im_R", [128, 4, 128], F32, 2)
            k.op("pool", lambda e: e.tensor_tensor(out=Rm, in0=id4, in1=Qm, op=ALU.subtract), r=[cx.b_ident, Qmb], w=[Rmb])
            Rh, Rhb = k.ring("im_Rh", [128, 4, 128], BF16, 2)
            k.op("act", lambda e: e.activation(out=Rh, in_=Rm, func=AF.Copy), r=[Rmb], w=[Rhb])
            for lev in range(1, 6):
                if lev < 5:
                    ps, pb = PS()
                    for blk in range(4):
                        bs = slice(blk * 128, (blk + 1) * 128)
                        mm(k, ps[:, bs], [(Pm[:, blk, :], Qm[:, blk, :])], [Pmb, Qmb], pb)
                    Qn, Qnb = k.ring("im_Q", [128, 4, 128], BF16, 2)
                    k.op("act", lambda e: e.activation(out=Qn, in_=ps.rearrange("p (a b) -> p a b", a=4), func=AF.Copy), r=[pb], w=[Qnb])
                ps, pb = PS()
                for blk in range(4):
                    bs = slice(blk * 128, (blk + 1) * 128)
                    mm(k, ps[:, bs], [(Qm[:, blk, :], Pm[:, blk, :])], [Pmb, Qmb], pb)
                Pn, Pnb = k.ring("im_P", [128, 4, 128], BF16, 2)
                k.op("act", lambda e: e.activation(out=Pn, in_=ps.rearrange("p (a b) -> p a b", a=4), func=AF.Copy), r=[pb], w=[Pnb])
                ps, pb = PS()
                for blk in range(4):
                    bs = slice(blk * 128, (blk + 1) * 128)
                    mm(k, ps[:, bs], [(Pn[:, blk, :], Rh[:, blk, :])], [Pnb, Rhb], pb)
                Rn, Rnb = k.ring("im_R", [128, 4, 128], F32, 2) if lev < 5 else k.ring("im_Rfin%d" % hd, [128, 4, 128], F32, 1)
                k.op("dve", lambda e: e.tensor_tensor(out=Rn, in0=ps.rearrange("p (a b) -> p a b", a=4), in1=Rm, op=ALU.add),
                     r=[pb, Rmb], w=[Rnb])
                Pm, Pmb = Pn, Pnb
                if lev < 5:
                    Qm, Qmb = Qn, Qnb
                    Rh, Rhb = k.ring("im_Rh", [128, 4, 128], BF16, 2)
                    k.op("act", lambda e: e.activation(out=Rh, in_=Rn, func=AF.Copy), r=[Rnb], w=[Rhb])
                Rm, Rmb = Rn, Rnb
            if stop == 5:
                raise _Stop(nc)
            HD[hd] = (AkT, AkTb, QBT, QBTb, QKT, QKTb, Rm, Rmb)
        XA = {}
        for hd in range(4):
            hcols = slice(hd * 64, hd * 64 + 64)
            AkT, AkTb = HD[hd][0], HD[hd][1]
            ps, pb = PS()
            for hf in range(2):
                trows = slice(hf * 64, hf * 64 + 64)
                for blk in range(4):
                    mm(k, ps[trows, blk * 64:(blk + 1) * 64], [(AkT[trows, blk, trows], vtok[trows, blk, hcols])], [AkTb, vtokb], pb)
            xa, xab = k.ring("XaAll%d" % hd, [128, 4, 64], F32, 1)
            k.op("act", lambda e: e.activation(out=xa, in_=ps[:, 0:256].rearrange("p (a b) -> p a b", a=4), func=AF.Copy, scale=-1.0),
                 r=[pb], w=[xab])
            XA[hd] = (xa, xab)
        for c in range(8):
            blk, hf = c // 2, c % 2
            trows = slice(hf * 64, hf * 64 + 64)
            diag = slice(hf * 64, hf * 64 + 64)
            tcol = slice(c * 64, c * 64 + 64)
            st = {}
            for hd in range(4):
                pc, hp = hd // 2, hd % 2
                rows = slice(hp * 64, hp * 64 + 64)
                uo, uob = U[hd][ucur[hd]]
                un, unb = U[hd][1 - ucur[hd]]
                ucur[hd] = 1 - ucur[hd]
                kts, ktsb = PR[pc]["kts"]
                ps1b, pb1b = PS()
                mm(k, ps1b[trows, 0:64], [(kts[rows, tcol], uo[rows, :])], [ktsb, uob], pb1b)
                st[hd] = dict(rows=rows, hcols=slice(hd * 64, hd * 64 + 64), pc=pc, uo=uo, uob=uob, un=un, unb=unb, ps1b=ps1b, pb1b=pb1b)
            for hd in range(4):
                d_ = st[hd]
                xa, xab = XA[hd]
                X, Xb = k.ring("X", [128, 64], F32, 4)
                k.op("dve", lambda e: e.tensor_tensor(out=X[trows, :], in0=xa[trows, blk, :], in1=d_["ps1b"][trows, 0:64], op=ALU.subtract),
                     r=[xab, d_["pb1b"]], w=[Xb])
                d_["X"], d_["Xb"] = X, Xb
            for hd in range(4):
                d_ = st[hd]
                Rm, Rmb = HD[hd][6], HD[hd][7]
                ps2, pb2 = PS()
                mm(k, ps2[trows, 0:64], [(Rm[trows, blk, diag], d_["X"][trows, :])], [Rmb, d_["Xb"]], pb2)
                d_["ps2"], d_["pb2"] = ps2, pb2
            for hd in range(4):
                d_ = st[hd]
                SA, SAb = k.ring("SA", [128, 64], F32, 4)
                eng = "dve" if hd % 2 == 0 else "act"
                if eng == "dve":
                    k.op("dve", lambda e: e.tensor_copy(out=SA[trows, :], in_=d_["ps2"][trows, 0:64]), r=[d_["pb2"]], w=[SAb])
                else:
                    k.op("act", lambda e: e.activation(out=SA[trows, :], in_=d_["ps2"][trows, 0:64], func=AF.Copy), r=[d_["pb2"]], w=[SAb])
                d_["SA"], d_["SAb"] = SA, SAb
            for hd in range(4):
                d_ = st[hd]
                rows, hcols = d_["rows"], d_["hcols"]
                ps3, pb3 = PS()
                mm(k, ps3[rows, 0:64], [(ktok[trows, blk, hcols], vtok[trows, blk, hcols]),
                                        (btok[trows, blk, hcols], d_["SA"][trows, :])], [ktokb, btokb, vtokb, d_["SAb"]], pb3)
                d_["ps3"], d_["pb3"] = ps3, pb3
            for hd in range(4):
                d_ = st[hd]
                pc, rows, hcols = d_["pc"], d_["rows"], d_["hcols"]
                rts, rtsb = PR[pc]["rts"]
                QBT, QBTb, QKT, QKTb = HD[hd][2], HD[hd][3], HD[hd][4], HD[hd][5]
                mm(k, ybank[pc][rows, tcol], [(d_["uo"][rows, :], rts[rows, tcol])], [d_["uob"], rtsb], ybb[pc])
                mm(k, ybank2[pc][rows, tcol], [(d_["SA"][trows, :], QBT[trows, blk, diag]),
                                               (vtok[trows, blk, hcols], QKT[trows, blk, diag])],
                   [d_["SAb"], QBTb, QKTb, vtokb], ybb2[pc])
            for hd in range(4):
                d_ = st[hd]
                pc, rows = d_["pc"], d_["rows"]
                pl, plb = PL[pc]
                k.op("dve", lambda e: e.scalar_tensor_tensor(out=d_["un"][rows, :], in0=d_["uo"][rows, :], scalar=pl[rows, c:c + 1],
                                                             in1=d_["ps3"][rows, 0:64], op0=ALU.mult, op1=ALU.add),
                     r=[d_["uob"], plb, d_["pb3"]], w=[d_["unb"]])
        if stop == 6:
            raise _Stop(nc)
        for pc in range(2):
            r_, rb_ = FM[("r", pc)]; v_, vb_ = FM[("v", pc)]; g_, gb_ = FM[("g", pc)]
            kp, kpb = PR[pc]["kp"]
            ysb, ysbb = tmpt("ysb")
            k.op("act", lambda e: e.activation(out=ysb, in_=ybank[pc], func=AF.Copy), r=[ybb[pc]], w=[ysbb])
            k.op("dve", lambda e: e.tensor_tensor(out=ysb, in0=ysb, in1=ybank2[pc], op=ALU.add), r=[ysbb, ybb2[pc]], w=[ysbb])
            ysq, ysqb = tmpt("ysq")
            k.op("act", lambda e: e.activation(out=ysq, in_=ysb, func=AF.Square), r=[ysbb], w=[ysqb])
            psm, pbm = PS(); mm(k, psm, [(bones, ysb)], [bonesb, ysbb], pbm)
            pse, pbe = PS(); mm(k, pse, [(bones, ysq)], [bonesb, ysqb], pbe)
            mean, meanb = tmpt("mean")
            k.op("act", lambda e: e.activation(out=mean, in_=psm, func=AF.Copy, scale=1.0 / 64), r=[pbm], w=[meanb])
            var, varb = tmpt("var")
            k.op("dve", lambda e: e.tensor_tensor(out=var, in0=mean, in1=mean, op=ALU.mult), r=[meanb], w=[varb])
            k.op("dve", lambda e: e.scalar_tensor_tensor(out=var, in0=pse, scalar=1.0 / 64, in1=var, op0=ALU.mult, op1=ALU.subtract),
                 r=[pbe, varb], w=[varb])
            k.op("act", lambda e: e.activation(out=var, in_=var, func=AF.Sqrt, bias=c_gneps, scale=1.0), r=[varb, cx.b_consts], w=[varb])
            k.op("dve", lambda e: e.reciprocal(out=var, in_=var), r=[varb], w=[varb])
            yn, ynb = tmpt("yn")
            k.op("dve", lambda e: e.tensor_tensor(out=yn, in0=ysb, in1=mean, op=ALU.subtract), r=[ysbb, meanb], w=[ynb])
            k.op("dve", lambda e: e.tensor_tensor(out=yn, in0=yn, in1=var, op=ALU.mult), r=[ynb, varb], w=[ynb])
            k.op("dve", lambda e: e.tensor_scalar(out=yn, in0=yn, scalar1=vec[:, LNG, pc:pc + 1], scalar2=vec[:, LNB, pc:pc + 1],
                                                  op0=ALU.mult, op1=ALU.add), r=[ynb, vecb], w=[ynb])
            rk, rkb = tmpt("rk")
            k.op("pool", lambda e: e.tensor_tensor(out=rk, in0=r_, in1=kp, op=ALU.mult), r=[rb_, kpb], w=[rkb])
            k.op("pool", lambda e: e.tensor_scalar(out=rk, in0=rk, scalar1=vec[:, RK, pc:pc + 1], scalar2=None, op0=ALU.mult),
                 r=[rkb, vecb], w=[rkb])
            psr, pbr = PS(); mm(k, psr, [(bones, rk)], [bonesb, rkb], pbr)
            bon, bonb = tmpt("bon")
            k.op("dve", lambda e: e.tensor_tensor(out=bon, in0=psr, in1=v_, op=ALU.mult), r=[pbr, vb_], w=[bonb])
            k.op("pool", lambda e: e.tensor_tensor(out=yn, in0=yn, in1=bon, op=ALU.add), r=[ynb, bonb], w=[ynb])
            yo, yob = k.ring("yo", [128, 512], BF16, 2)
            k.op("dve", lambda e: e.tensor_tensor(out=yo, in0=yn, in1=g_, op=ALU.mult), r=[ynb, gb_], w=[yob])
            k.dma("sp", ygq[ti // 4][:, pc, (ti % 4) * 512:(ti % 4) * 512 + 512], yo, r=[yob], sb=yob)
            outbufs.append(yob)
    k.wait_all("sp", list({id(b): b for b in outbufs}.values()))
    return nc


def rwkv_consts():
    t = np.arange(128)
    same = (t[:, None] // 64) == (t[None, :] // 64)
    m_sl = (same & (t[None, :] < t[:, None])).astype(np.float32)
    m_su = (same & (t[:, None] < t[None, :])).astype(np.float32)
    m_iu = (same & (t[:, None] <= t[None, :])).astype(np.float32)
    masks = np.ascontiguousarray(np.stack([m_sl, m_su, m_iu], axis=1))
    bones = same.astype(np.float32)
    resetm = np.ones((128, 512), np.float32)
    resetm[:, ::64] = 0.0
    return masks, np.ascontiguousarray(bones), resetm


def l4_inputs(h1T_b, core_hg, P):
    cs = slice(core_hg * 256, (core_hg + 1) * 256)
    masks, bones, resetm = rwkv_consts()
    import ml_dtypes
    col = lambda v: np.ascontiguousarray(v[cs].reshape(2, 128).T)
    vecs = np.stack([col(P["c_w0"]), col(P["c_a0"]), col(P["c_k_k"]), col(P["c_k_a"]), col(P["c_r_k"].reshape(-1)),
                     col(P["c_ln_g"]), col(P["c_ln_b"])], axis=1)
    mu = np.ascontiguousarray(P["c_mu"].reshape(6, 8, 128).transpose(2, 0, 1))
    return {"hall": h1T_b, "hzero": np.zeros((1024, 1), ml_dtypes.bfloat16), "wr": np.ascontiguousarray(P["c_w_r"][:, cs]), "wk": np.ascontiguousarray(P["c_w_k"][:, cs]),
            "wv": np.ascontiguousarray(P["c_w_v"][:, cs]), "w1": P["c_w1"], "a1": P["c_a1"], "g1": P["c_g1"],
            "w2": np.ascontiguousarray(P["c_w2"][:, cs]), "a2": np.ascontiguousarray(P["c_a2"][:, cs]),
            "g2": np.ascontiguousarray(P["c_g2"][:, cs]), "mu": mu, "vecs": np.ascontiguousarray(vecs.astype(np.float32)),
            "masks": masks, "bones": bones, "resetm": resetm}


NEG = -30000.0
NQB = 32


def build_l2a(nqb=NQB, ntile1=16, stop=99, nc=None, cx=None, io=None):
    try:
        return _build_l2a(nqb, ntile1, stop, nc, cx, io)
    except _Stop as e:
        return e.args[0]


def _build_l2a(nqb=NQB, ntile1=16, stop=99, nc=None, cx=None, io=None):
    nc, cx, io = _std(nc, cx, io)
    din = io.inp
    hall = din("hall", [4096, TOK], BF16)
    psel_d = din("psel", [128, 2])
    wq_d = din("wq", [D, 256]); wks_d = din("wks", [D, 64]); wkw_d = din("wkw", [D, 64])
    wkv_d = din("wkvc", [D, 128]); wv2_d = din("wv2", [D, 128]); wg_d = din("wgate", [D, 12])
    tabA_c = din("tabA_c", [64, SEQ]); tabA_s = din("tabA_s", [64, SEQ])
    tabB_c = din("tabB_c", [128, SEQ]); tabB_s = din("tabB_s", [128, SEQ])
    tabQ_c = din("tabQ_c", [64, NQB * 128]); tabQ_s = din("tabQ_s", [64, NQB * 128])
    rot_d = din("rotT", [128, 128])
    ebig_d = din("ebig", [128, SEQ], BF16)
    w1_d = din("w1kv", [128, 32, 64]); w2_d = din("w2kv", [128, 64]); pe_d = din("pekv", [128, 32, 2])
    vcx_d = din("vcx", [128, 4, 129], BF16)
    cm_d = din("cmask", [128, 9, 128], BF16)
    sm_d = din("smask", [128, 6, 128], BF16)
    fv_d = din("fv", [NQB, 128, 2, 128])
    oa = io.out("oa", [NQB * 128, 256], BF16)
    k = cx.k
    PS = lambda: k.ring("psr", [128, 512], F32, 4, psum=True)
    held = {}
    for nm in ("oc0", "oc1", "os", "ow"):
        held[nm] = (k.ps("h_" + nm, [128, 512]), Buf("h_" + nm))
    psel, pselb = k.tile("psel", [128, 2], F32)
    k.dma("sp", psel, psel_d, w=[pselb], sb=pselb)

    def cload(name, dram, shape, dt, q="sp"):
        t, b = k.tile(name, shape, dt)
        k.dma(q, t, dram, w=[b], sb=b)
        return t, b

    rot, rotb = cload("rot", rot_d, [128, 128], F32)
    ebig, ebigb = cload("ebig", ebig_d, [128, SEQ], BF16)
    cm, cmb = cload("cm", cm_d, [128, 9, 128], BF16)
    sm, smb = cload("sm", sm_d, [128, 6, 128], BF16)
    identb, identbb = k.tile("identb", [128, 128], BF16)
    k.op("act", lambda e: e.activation(out=identb, in_=cx.ident, func=AF.Copy), r=[cx.b_ident], w=[identbb])
    ones_f, ones_fb = k.tile("ones_f", [128, 128], F32)
    k.op("pool", lambda e: e.memset(ones_f, 1.0), w=[ones_fb])

    def wcast(name, dram, n, q="pool"):
        t, b = k.tile(name, [128, 8, n], BF16)
        k.dma(q, t, dram.rearrange("(kc p) n -> p kc n", p=128), w=[b], sb=b)
        return t, b

    wq, wqb = wcast("wq", wq_d, 256); wks, wksb = wcast("wks", wks_d, 64); wkw, wkwb = wcast("wkw", wkw_d, 64)
    wkv, wkvb = wcast("wkv", wkv_d, 128); wv2, wv2b = wcast("wv2", wv2_d, 128); wgt, wgtb = wcast("wgt", wg_d, 12)
    w1, w1b = k.tile("w1", [128, 32, 64], BF16); k.dma("pool", w1, w1_d, w=[w1b], sb=w1b)
    w2, w2b = k.tile("w2", [128, 64], BF16); k.dma("pool", w2, w2_d, w=[w2b], sb=w2b)
    pe, peb = k.tile("pe", [128, 32, 2], BF16); k.dma("pool", pe, pe_d, w=[peb], sb=peb)

    ksel, kselb = k.tile("ksel", [128, SEQ], BF16)
    kwin, kwinb = k.tile("kwin", [128, SEQ], BF16)
    kvc, kvcb = k.tile("kvc", [128, SEQ + 32], BF16)
    k.op("pool", lambda e: e.memset(kvc[:, SEQ:SEQ + 32], 0.0), w=[kvcb])
    vsel, vselb = k.tile("vsel", [128, 64, 96], BF16)
    vwin, vwinb = k.tile("vwin", [128, 64, 96], BF16)
    for t_, b_ in ((ksel, kselb), (kwin, kwinb)):
        k.op("pool", lambda e: e.memset(t_[64:128, :], 0.0), w=[b_])
        k.op("pool", lambda e: e.memset(t_[64:65, :], 1.0), w=[b_])
    for t_, b_ in ((vsel, vselb), (vwin, vwinb)):
        k.op("pool", lambda e: e.memset(t_[:, :, 64:65], 1.0), w=[b_])
    kmax2, kmax2b = k.tile("kmax2", [128, 1], F32)
    k.op("pool", lambda e: e.memset(kmax2, 0.0), w=[kmax2b])

    def upd_kmax(src, srcb, rows, n):
        sq, sqb = k.ring("ksq", [128, 512], F32, 2)
        k.op("pool", lambda e: e.tensor_tensor(out=sq[rows, 0:n], in0=src, in1=src, op=ALU.mult), r=[srcb], w=[sqb])
        ps, pb = PS()
        mm(k, ps[:, 0:n], [(ones_f[rows, :], sq[rows, 0:n])], [ones_fb, sqb], pb)
        k.op("act", lambda e: e.activation(out=sq[:, 0:n], in_=ps[:, 0:n], func=AF.Copy), r=[pb], w=[sqb])
        mx, mxb = k.ring("kmx", [128, 8], F32, 2)
        k.op("dve", lambda e: e.max(out=mx, in_=sq[:, 0:n]), r=[sqb], w=[mxb])
        k.op("dve", lambda e: e.tensor_tensor(out=kmax2, in0=kmax2, in1=mx[:, 0:1], op=ALU.max), r=[mxb, kmax2b], w=[kmax2b])

    def rope_store(ps, pb, rows, tc_d, ts_d, t0, n, dst, dstb, rbase=0):
        R = slice(rbase, rbase + rows)
        xk, xkb = k.ring("xk", [128, 512], F32, 2)
        k.op("act", lambda e: e.activation(out=xk[R, 0:n], in_=ps[R, 0:n], func=AF.Copy), r=[pb], w=[xkb])
        tc_, tcb = k.ring("tabc", [128, 512], F32, 2)
        ts_, tsb = k.ring("tabs", [128, 512], F32, 2)
        k.dma("sp", tc_[R, 0:n], tc_d[:, t0:t0 + n], w=[tcb], sb=tcb)
        k.dma("sp", ts_[R, 0:n], ts_d[:, t0:t0 + n], w=[tsb], sb=tsb)
        ps2, pb2 = PS()
        mm(k, ps2[R, 0:n], [(rot[R, R], xk[R, 0:n])], [rotb, xkb], pb2)
        t1, t1b = k.ring("rp1", [128, 512], F32, 2)
        t2, t2b = k.ring("rp2", [128, 512], F32, 2)
        k.op("pool", lambda e: e.tensor_tensor(out=t1[R, 0:n], in0=xk[R, 0:n], in1=tc_[R, 0:n], op=ALU.mult), r=[xkb, tcb], w=[t1b])
        k.op("dve", lambda e: e.tensor_tensor(out=t2[R, 0:n], in0=ps2[R, 0:n], in1=ts_[R, 0:n], op=ALU.mult), r=[pb2, tsb], w=[t2b])
        k.op("dve", lambda e: e.tensor_tensor(out=dst, in0=t1[R, 0:n], in1=t2[R, 0:n], op=ALU.add), r=[t1b, t2b], w=[dstb])

    if stop == 10:
        raise _Stop(nc)
    for ti in range(ntile1):
        t0 = ti * 512
        hb, hbb = k.ring("hb", [128, 8, 512], BF16, 2)
        dma_hall(k, hb, hall, t0, 512, hbb)
        for (w_, wb_, dst, dstb) in ((wks, wksb, ksel, kselb), (wkw, wkwb, kwin, kwinb)):
            ps, pb = PS()
            mm(k, ps[0:64, :], [(w_[:, kc, :], hb[:, kc, :]) for kc in range(8)], [wb_, hbb], pb)
            if stop == 11 + 100 * ti:
                raise _Stop(nc)
            rope_store(ps, pb, 64, tabA_c, tabA_s, t0, 512, dst[0:64, t0:t0 + 512], dstb)
            if stop == 12 + 100 * ti:
                raise _Stop(nc)
            upd_kmax(dst[0:64, t0:t0 + 512], dstb, slice(0, 64), 512)
            if stop == 13 + 100 * ti:
                raise _Stop(nc)
        if stop == 14 + 100 * ti:
            raise _Stop(nc)
        ps, pb = PS()
        mm(k, ps, [(wkv[:, kc, :], hb[:, kc, :]) for kc in range(8)], [wkvb, hbb], pb)
        rope_store(ps, pb, 128, tabB_c, tabB_s, t0, 512, kvc[:, t0:t0 + 512], kvcb)
        if stop == 15 + 100 * ti:
            raise _Stop(nc)
        ps, pb = PS()
        for blk in range(4):
            mm(k, ps[:, blk * 128:(blk + 1) * 128], [(hb[:, kc, blk * 128:(blk + 1) * 128], wv2[:, kc, :]) for kc in range(8)],
               [wv2b, hbb], pb)
        pv = ps.rearrange("p (a b) -> p a b", a=4)
        k.op("act", lambda e: e.activation(out=vsel[:, ti * 4:ti * 4 + 4, 0:64], in_=pv[:, :, 0:64], func=AF.Copy), r=[pb], w=[vselb])
        k.op("act", lambda e: e.activation(out=vwin[:, ti * 4:ti * 4 + 4, 0:64], in_=pv[:, :, 64:128], func=AF.Copy), r=[pb], w=[vwinb])

    if stop == 1:
        raise _Stop(nc)
    kc, kcb = k.tile("kc", [128, 512], BF16)
    k.op("pool", lambda e: e.memset(kc, 0.0), w=[kcb])
    k.op("pool", lambda e: e.memset(kc[64:65, :], 1.0), w=[kcb])
    vcx, vcxb = k.tile("vcx", [128, 4, 256], BF16)
    k.dma("sp", vcx[:, :, 64:193], vcx_d, w=[vcxb], sb=vcxb)
    kv16 = kvc.rearrange("p (n s) -> p n s", s=16)
    hid, hidb = k.tile("hid", [128, 512], BF16)
    k.op("pool", lambda e: e.memset(hid, 0.0), w=[hidb])
    for R in (slice(0, 64), slice(64, 128)):
        psb, pbb = PS()
        mm(k, psb[R, 0:2], [(w1[R, l, :], pe[R, l, :]) for l in range(32)], [w1b, peb], pbb)
        bia, biab = k.ring("cbias", [128, 1], F32, 2)
        k.op("act", lambda e: e.activation(out=bia[R, :], in_=psb[R, 0:1], func=AF.Copy), r=[pbb], w=[biab])
        ps, pb = PS()
        mm(k, ps[R, 0:512], [(w1[R, l, :], kv16[R, (l // 16):(l // 16) + 512, l % 16]) for l in range(32)], [w1b, kvcb], pb)
        k.op("act", lambda e: e.activation(out=hid[R, :], in_=ps[R, :], func=AF.Silu, bias=bia[R, :]), r=[pb, biab], w=[hidb])
    ps, pb = PS()
    mm(k, ps[0:64, :], [(w2[0:64, :], hid[0:64, :])], [w2b, hidb], pb)
    k.op("act", lambda e: e.activation(out=kc[0:64, :], in_=ps[0:64, :], func=AF.Copy), r=[pb], w=[kcb])
    upd_kmax(kc[0:64, 0:512], kcb, slice(0, 64), 512)
    ps, pb = PS()
    for ch in range(4):
        mm(k, ps[:, ch * 64:(ch + 1) * 64], [(hid[64:128, ch * 128:(ch + 1) * 128], w2[64:128, :])], [w2b, hidb], pb)
    k.op("act", lambda e: e.activation(out=vcx[:, :, 0:64], in_=ps[:, 0:256].rearrange("p (a b) -> p a b", a=4), func=AF.Copy),
         r=[pb], w=[vcxb])
    nkm, nkmb = k.tile("nkm", [128, 1], F32)
    k.op("act", lambda e: e.activation(out=nkm, in_=kmax2, func=AF.Sqrt), r=[kmax2b], w=[nkmb])
    k.op("dve", lambda e: e.tensor_scalar(out=nkm, in0=nkm, scalar1=-1.0, scalar2=None, op0=ALU.mult), r=[nkmb], w=[nkmb])

    if stop == 2:
        raise _Stop(nc)
    outb = []

    def prep(i):
        q0 = i * 128
        hqc, hqcb = k.ring("hqc", [128, 2, 8, 128], BF16, 2)
        for pp in range(2):
            dma_hall(k, hqc[:, pp], hall, (2 * i + pp) * 128, 128, hqcb)
        hqt, hqtb = k.ring("hqt", [128, 8, 128], BF16, 2)
        hqb, hqbb = k.ring("hqb", [128, 8, 128], BF16, 2)
        k.op("dve", lambda e: e.tensor_scalar(out=hqt, in0=hqc[:, 0], scalar1=psel[:, 0:1], scalar2=None, op0=ALU.mult),
             r=[hqcb, pselb], w=[hqtb])
        k.op("dve", lambda e: e.scalar_tensor_tensor(out=hqb, in0=hqc[:, 1], scalar=psel[:, 1:2], in1=hqt, op0=ALU.mult, op1=ALU.add),
             r=[hqcb, pselb, hqtb], w=[hqbb])
        qa, qab = k.ring("qa", [128, 512], BF16, 2)
        k.op("pool", lambda e: e.memset(qa[64:128, :], 0.0), w=[qab])
        qf, qfb = k.ring("qf", [128, 512], F32, 2)
        tcq, tcqb = k.ring("tcq", [128, 128], F32, 2); tsq, tsqb = k.ring("tsq", [128, 128], F32, 2)
        k.dma("sp", tcq[0:64, :], tabQ_c[:, q0:q0 + 128], w=[tcqb], sb=tcqb)
        k.dma("sp", tsq[0:64, :], tabQ_s[:, q0:q0 + 128], w=[tsqb], sb=tsqb)
        ps, pb = PS()
        for g in range(4):
            mm(k, ps[0:64, g * 128:(g + 1) * 128], [(wq[:, kc, g * 64:(g + 1) * 64], hqb[:, kc, :]) for kc in range(8)], [wqb, hqbb], pb)
        xq, xqb = k.ring("xq", [128, 512], F32, 2)
        k.op("act", lambda e: e.activation(out=xq[0:64, :], in_=ps[0:64, :], func=AF.Copy), r=[pb], w=[xqb])
        ps2, pb2 = PS()
        mm(k, ps2[0:64, :], [(rot[0:64, 0:64], xq[0:64, :])], [rotb, xqb], pb2)
        v3 = lambda a: a.rearrange("p (g t) -> p g t", g=4)
        bc4 = lambda a: a.unsqueeze(1).to_broadcast([64, 4, 128])
        k.op("pool", lambda e: e.tensor_tensor(out=v3(xq[0:64, :]), in0=v3(xq[0:64, :]), in1=bc4(tcq[0:64, :]), op=ALU.mult),
             r=[xqb, tcqb], w=[xqb])
        k.op("dve", lambda e: e.tensor_tensor(out=v3(qf[0:64, :]), in0=v3(ps2[0:64, :]), in1=bc4(tsq[0:64, :]), op=ALU.mult),
             r=[pb2, tsqb], w=[qfb])
        k.op("dve", lambda e: e.tensor_tensor(out=qf[0:64, :], in0=qf[0:64, :], in1=xq[0:64, :], op=ALU.add), r=[qfb, xqb], w=[qfb])
        k.op("act", lambda e: e.activation(out=qa[0:64, :], in_=qf[0:64, :], func=AF.Copy), r=[qfb], w=[qab])
        k.op("pool", lambda e: e.tensor_tensor(out=xq[0:64, :], in0=qf[0:64, :], in1=qf[0:64, :], op=ALU.mult), r=[qfb, xqb], w=[xqb])
        ps3, pb3 = PS()
        mm(k, ps3[64:65, :], [(ones_f[0:64, 0:1], xq[0:64, :])], [ones_fb, xqb], pb3)
        mrow, mrowb = k.ring("mrow", [128, 512], F32, 2)
        k.op("act", lambda e: e.activation(out=mrow[64:65, :], in_=ps3[64:65, :], func=AF.Sqrt), r=[pb3], w=[mrowb])
        k.op("dve", lambda e: e.tensor_scalar(out=qa[64:65, :], in0=mrow[64:65, :], scalar1=nkm[64:65, 0:1], scalar2=None, op0=ALU.mult),
             r=[mrowb, nkmb], w=[qab])
        psg, pbg = PS()
        mm(k, psg[:, 0:12], [(hqb[:, kc, :], wgt[:, kc, :]) for kc in range(8)], [wgtb, hqbb], pbg)
        gt, gtb = k.ring("gates", [128, 12], F32, 2)
        k.op("act", lambda e: e.activation(out=gt, in_=psg[:, 0:12], func=AF.Sigmoid), r=[pbg], w=[gtb])

        return qa, qab, gt, gtb

    nxt = prep(0)
    for i in range(nqb):
        q0 = i * 128
        qa, qab, gt, gtb = nxt
        if stop == 3:
            raise _Stop(nc)

        def score_tile(kaug, kaugb, ktile_cols, masks):
            ps, pb = PS()
            n = 1 + len(masks)
            k.op("pe", lambda e: e.matmul(ps, lhsT=kaug[:, ktile_cols], rhs=qa, start=True, stop=(n == 1)),
                 r=[kaugb, qab], w=[pb], inc=(n == 1))
            for mi, (ml, mlb, mr, mrb) in enumerate(masks):
                last = (mi == len(masks) - 1)
                k.op("pe", lambda e: e.matmul(ps.rearrange("p (g t) -> p g t", g=4), lhsT=ml,
                                              rhs=mr.unsqueeze(1).to_broadcast([128, 4, 128]), start=False, stop=last),
                     r=[mlb, mrb], w=[pb], inc=last)
            eT, eTb = k.ring("eT", [128, 512], BF16, 3)
            k.op("act", lambda e: e.activation(out=eT, in_=ps, func=AF.Exp), r=[pb], w=[eTb])
            return eT, eTb

        qi_e = 2 * i
        last = (8 * qi_e + 6) // 128
        oc = [held["oc0"], held["oc1"]]
        def cmp_pv(cc, eT, eTb):
            for g in range(4):
                o_, ob_ = oc[g // 2]
                k.op("pe", lambda e: e.matmul(o_[:, (g % 2) * 256:(g % 2) * 256 + 193], lhsT=eT[:, g * 128:(g + 1) * 128],
                                              rhs=vcx[:, cc, 0:193], start=(cc == 0 and g % 2 == 0), stop=(cc == last and g % 2 == 1),
                                              skip_group_check=True),
                     r=[eTb, vcxb], w=[ob_], inc=(g == 3))
        pend = None
        for cc in range(last + 1):
            masks = []
            if cc == last:
                masks.append((identb, identbb, cm[:, i % 8, :], cmb))
            elif cc == last - 1 and i % 8 == 0:
                masks.append((identb, identbb, cm[:, 8, :], cmb))
            eT, eTb = score_tile(kc, kcb, slice(cc * 128, (cc + 1) * 128), masks)
            if pend is not None:
                cmp_pv(*pend)
            pend = (cc, eT, eTb)
        cmp_pv(*pend)
        if stop == 6:
            raise _Stop(nc)
        ow_, owb_ = held["ow"]
        wt = [kt for kt in range(2 * i - 4, 2 * i + 2) if kt >= 0]
        def win_pv(kt, eT, eTb):
            for g in range(4):
                k.op("pe", lambda e: e.matmul(ow_[:, g * 65:(g + 1) * 65], lhsT=eT[:, g * 128:(g + 1) * 128], rhs=vwin[:, kt, 0:65],
                                              start=(kt == wt[0] and g == 0), stop=(kt == wt[-1] and g == 3), skip_group_check=True),
                     r=[eTb, vwinb], w=[owb_], inc=(g == 3))
        pend = None
        for kt in wt:
            cols = slice(kt * 128, (kt + 1) * 128)
            pos = kt - (2 * i - 4)
            masks = []
            mi = {0: 2, 1: 3, 4: 4, 5: 5}.get(pos)
            if mi is not None:
                masks.append((identb, identbb, sm[:, mi, :], smb))
            eT, eTb = score_tile(kwin, kwinb, cols, masks)
            if pend is not None:
                win_pv(*pend)
            pend = (kt, eT, eTb)
        win_pv(*pend)
        if stop == 4:
            raise _Stop(nc)
        zc, zcb = k.ring("zc", [128, 4], F32, 2)
        for g in range(4):
            o_, ob_ = oc[g // 2]
            k.op("dve", lambda e: e.tensor_scalar(out=zc[:, g:g + 1], in0=o_[:, (g % 2) * 256 + 64:(g % 2) * 256 + 65], scalar1=1e-30,
                                                  scalar2=None, op0=ALU.max), r=[ob_], w=[zcb])
        k.op("dve", lambda e: e.reciprocal(out=zc, in_=zc), r=[zcb], w=[zcb])
        imp, impb = k.ring("imp", [128, 128], F32, 2)
        for g in range(4):
            o_, ob_ = oc[g // 2]
            src = o_[:, (g % 2) * 256 + 65:(g % 2) * 256 + 193]
            if g == 0:
                k.op("dve", lambda e: e.tensor_scalar(out=imp, in0=src, scalar1=zc[:, 0:1], scalar2=None, op0=ALU.mult),
                     r=[ob_, zcb], w=[impb])
            else:
                k.op("dve", lambda e: e.scalar_tensor_tensor(out=imp, in0=src, scalar=zc[:, g:g + 1], in1=imp, op0=ALU.mult, op1=ALU.add),
                     r=[ob_, zcb, impb], w=[impb])
        fv, fvb = k.ring("fv", [128, 2, 128], F32, 2)
        k.dma("sp", fv, fv_d[i], w=[fvb], sb=fvb)
        k.op("dve", lambda e: e.tensor_tensor(out=imp, in0=imp, in1=fv[:, 0, :], op=ALU.mult), r=[impb, fvb], w=[impb])
        k.op("dve", lambda e: e.tensor_tensor(out=imp, in0=imp, in1=fv[:, 1, :], op=ALU.add), r=[impb, fvb], w=[impb])
        m8, m8b = k.ring("m8", [128, 16], F32, 2)
        wk_, wkb_ = k.ring("impw", [128, 128], F32, 2)
        k.op("dve", lambda e: e.max(out=m8[:, 0:8], in_=imp), r=[impb], w=[m8b])
        k.op("dve", lambda e: e.match_replace(out=wk_, in_to_replace=m8[:, 0:8], in_values=imp, imm_value=-3e38), r=[m8b, impb], w=[wkb_])
        k.op("dve", lambda e: e.max(out=m8[:, 8:16], in_=wk_), r=[wkb_], w=[m8b])
        k.op("dve", lambda e: e.tensor_scalar(out=wk_, in0=imp, scalar1=m8[:, 15:16], scalar2=None, op0=ALU.is_ge), r=[impb, m8b], w=[wkb_])
        k.op("dve", lambda e: e.tensor_scalar(out=wk_, in0=wk_, scalar1=-1.0, scalar2=-NEG, op0=ALU.add, op1=ALU.mult), r=[wkb_], w=[wkb_])
        pst, pbt = PS()
        k.op("pe", lambda e: e.transpose(out=pst[:, 0:128], in_=wk_, identity=cx.ident), r=[wkb_, cx.b_ident], w=[pbt])
        nbT, nbTb = k.ring("nbT", [128, 128], BF16, 2)
        k.op("act", lambda e: e.activation(out=nbT, in_=pst[:, 0:128], func=AF.Copy), r=[pbt], w=[nbTb])

        if stop == 5:
            raise _Stop(nc)
        os_, osb_ = held["os"]
        nkt = 2 * i + 2
        def sel_pv(kt, eT, eTb):
            for g in range(4):
                k.op("pe", lambda e: e.matmul(os_[:, g * 65:(g + 1) * 65], lhsT=eT[:, g * 128:(g + 1) * 128], rhs=vsel[:, kt, 0:65],
                                              start=(kt == 0 and g == 0), stop=(kt == nkt - 1 and g == 3), skip_group_check=True),
                     r=[eTb, vselb], w=[osb_], inc=(g == 3))
        pend = None
        for kt in range(nkt):
            cols = slice(kt * 128, (kt + 1) * 128)
            masks = [(ebig[:, cols], ebigb, nbT, nbTb)]
            if kt == 2 * i:
                masks.append((identb, identbb, sm[:, 0, :], smb))
            elif kt == 2 * i + 1:
                masks.append((identb, identbb, sm[:, 1, :], smb))
            eT, eTb = score_tile(ksel, kselb, cols, masks)
            if pend is not None:
                sel_pv(*pend)
            pend = (kt, eT, eTb)
        sel_pv(*pend)
        if i + 1 < nqb:
            nxt = prep(i + 1)
        sc, scb = k.ring("sc", [128, 12], F32, 2)
        k.op("pool", lambda e: e.memset(sc, 1.0), w=[scb])
        for g in range(4):
            k.op("dve", lambda e: e.tensor_scalar(out=sc[:, g * 3 + 1:g * 3 + 2], in0=os_[:, g * 65 + 64:g * 65 + 65], scalar1=1e-30,
                                                  scalar2=None, op0=ALU.max), r=[osb_], w=[scb])
            k.op("dve", lambda e: e.tensor_scalar(out=sc[:, g * 3 + 2:g * 3 + 3], in0=ow_[:, g * 65 + 64:g * 65 + 65], scalar1=1e-30,
                                                  scalar2=None, op0=ALU.max), r=[owb_], w=[scb])
        k.op("dve", lambda e: e.reciprocal(out=sc, in_=sc), r=[scb], w=[scb])
        for g in range(4):
            k.op("dve", lambda e: e.tensor_copy(out=sc[:, g * 3:g * 3 + 1], in_=zc[:, g:g + 1]), r=[zcb], w=[scb])
        k.op("dve", lambda e: e.tensor_tensor(out=sc, in0=sc, in1=gt, op=ALU.mult), r=[scb, gtb], w=[scb])
        ot, otb = k.ring("oat", [128, 256], F32, 2)
        for g in range(4):
            o_, ob_ = oc[g // 2]
            dst = ot[:, g * 64:(g + 1) * 64]
            k.op("dve", lambda e: e.tensor_scalar(out=dst, in0=o_[:, (g % 2) * 256:(g % 2) * 256 + 64], scalar1=sc[:, g * 3:g * 3 + 1],
                                                  scalar2=None, op0=ALU.mult), r=[ob_, scb], w=[otb])
            k.op("dve", lambda e: e.scalar_tensor_tensor(out=dst, in0=os_[:, g * 65:g * 65 + 64], scalar=sc[:, g * 3 + 1:g * 3 + 2], in1=dst,
                                                         op0=ALU.mult, op1=ALU.add), r=[osb_, scb, otb], w=[otb])
            k.op("dve", lambda e: e.scalar_tensor_tensor(out=dst, in0=ow_[:, g * 65:g * 65 + 64], scalar=sc[:, g * 3 + 2:g * 3 + 3], in1=dst,
                                                         op0=ALU.mult, op1=ALU.add), r=[owb_, scb, otb], w=[otb])
        otc, otcb = k.ring("oatc", [128, 256], BF16, 2)
        k.op("act", lambda e: e.activation(out=otc, in_=ot, func=AF.Copy), r=[otb], w=[otcb])
        k.dma("sp", oa[q0:q0 + 128, :], otc, r=[otcb], sb=otcb)
        outb.append(otcb)
    k.wait_all("sp", list({id(b): b for b in outb}.values()))
    return nc


def _bf16(a):
    import ml_dtypes
    return np.ascontiguousarray(a).astype(ml_dtypes.bfloat16)


def nsa_consts():
    inv = (10000.0 ** (-np.arange(0, 64, 2, dtype=np.float32) / 64)).astype(np.float32)
    ang = np.arange(SEQ, dtype=np.float32)[:, None] * inv[None, :]
    cos, sin = np.cos(ang).astype(np.float32), np.sin(ang).astype(np.float32)
    tA_c = np.ascontiguousarray(np.concatenate([cos, cos], 1).T)
    tA_s = np.ascontiguousarray(np.concatenate([sin, sin], 1).T)
    tB_c = np.ascontiguousarray(np.concatenate([tA_c, np.ones_like(tA_c)], 0))
    tB_s = np.ascontiguousarray(np.concatenate([tA_s, np.zeros_like(tA_s)], 0))
    rot = np.zeros((128, 128), np.float32)
    for m in range(128):
        mm_ = m % 64
        if mm_ < 32:
            rot[m + 32, m] = -1.0
        else:
            rot[m - 32, m] = 1.0
    x = np.arange(SEQ)
    ebig = (np.arange(128)[:, None] == (x[None, :] // 64)).astype(np.float32)
    c = np.arange(512)
    j = np.arange(128)
    ov = ((16 * c[:, None] < 64 * j[None, :] + 64) & (16 * c[:, None] + 31 >= 64 * j[None, :])).astype(np.float32)
    vcx = np.zeros((512, 129), np.float32)
    vcx[:, 0] = 1.0
    vcx[:, 1:] = ov
    vcx[511, :] = 0.0
    vcx = np.ascontiguousarray(vcx.reshape(4, 128, 129).transpose(1, 0, 2))
    return dict(tA_c=tA_c, tA_s=tA_s, tB_c=tB_c, tB_s=tB_s, rot=rot, ebig=_bf16(ebig), vcx=_bf16(vcx))


def nsa_core_consts(par):
    p = np.arange(128)[:, None]
    tl = np.arange(128)[None, :]
    cm = np.zeros((128, 9, 128), np.float32)
    for s in range(8):
        off = 8 * ((2 * s + par) % 16)
        cm[:, s, :] = np.where(16 * (p - off) + 31 <= tl, 0.0, NEG)
    if par == 0:
        cm[:, 8, :] = np.where((p == 127) & (tl < 15), NEG, 0.0)
    causal = np.where(p <= tl, 0.0, NEG).astype(np.float32)
    winT = np.where(p > tl, 0.0, NEG).astype(np.float32)
    ALL = np.full((128, 128), NEG, np.float32)
    Z = np.zeros((128, 128), np.float32)
    sm = [causal, ALL, winT, Z, causal, ALL] if par == 0 else [Z, causal, ALL, winT, Z, causal]
    sm = np.stack(sm, axis=1)
    fv = np.zeros((NQB, 128, 2, 128), np.float32)
    jj = np.arange(128)[None, :]
    for i in range(NQB):
        qi = 2 * i + par
        t = 128 * qi + np.arange(128)[:, None]
        cur = t // 64
        valid = jj <= cur
        f0 = (jj == 0)
        f1 = (jj == cur)
        f2 = (jj == cur - 1)
        forced = f0 | f1 | f2
        V = (valid & ~forced).astype(np.float32)
        F = np.where(valid, 0.0, -1e30).astype(np.float32)
        F = np.where(f2 & valid, 1e4 + 2.0, F)
        F = np.where(f0, 1e4 + 1.0, F)
        F = np.where(f1, 1e4, F)
        fv[i, :, 0, :] = V
        fv[i, :, 1, :] = F
    return dict(cm=_bf16(cm), sm=_bf16(sm), fv=fv)


def l2a_inputs(h0T_b, kvh, par, P, C):
    w = P["ab_w_in"]
    cat = lambda *a: np.ascontiguousarray(np.concatenate(a, axis=1))
    sl = lambda o, n: w[:, o + kvh * n: o + (kvh + 1) * n]
    qtok = np.concatenate([np.arange((2 * i + par) * 128, (2 * i + par + 1) * 128) for i in range(NQB)])
    cc = nsa_core_consts(par)
    w1 = np.concatenate([P["a_cmp_w1_k"].reshape(32, 64, 64).transpose(1, 0, 2),
                         P["a_cmp_w1_v"].reshape(32, 64, 64).transpose(1, 0, 2)], 0)
    w2 = np.concatenate([P["a_cmp_w2_k"], P["a_cmp_w2_v"]], 0)
    pe = np.concatenate([P["a_cmp_pe_k"].T, P["a_cmp_pe_v"].T], 0)
    pselv = np.zeros((128, 2), np.float32); pselv[:, par] = 1.0
    return {"hall": h0T_b, "psel": pselv,
            "wq": np.ascontiguousarray(sl(0, 256)), "wks": np.ascontiguousarray(sl(768, 64)), "wkw": np.ascontiguousarray(sl(1024, 64)),
            "wkvc": cat(sl(512, 64), sl(640, 64)), "wv2": cat(sl(896, 64), sl(1152, 64)), "wgate": np.ascontiguousarray(sl(1280, 12)),
            "tabA_c": C["tA_c"], "tabA_s": C["tA_s"], "tabB_c": C["tB_c"], "tabB_s": C["tB_s"],
            "tabQ_c": np.ascontiguousarray(C["tA_c"][:, qtok] * np.float32(0.125)),
            "tabQ_s": np.ascontiguousarray(C["tA_s"][:, qtok] * np.float32(0.125)),
            "rotT": C["rot"], "ebig": C["ebig"], "w1kv": np.ascontiguousarray(w1), "w2kv": np.ascontiguousarray(w2),
            "pekv": np.ascontiguousarray(np.stack([pe, pe], axis=2)), "vcx": C["vcx"], "cmask": cc["cm"], "smask": cc["sm"], "fv": cc["fv"]}


def build_l2b(ntiles=16, nc=None, cx=None, io=None):
    nc, cx, io = _std(nc, cx, io)
    din = io.inp
    hall = din("hall", [4096, TOK], BF16)
    wx_d = din("wx", [D, 256]); wB_d = din("wB", [D, 128]); wC_d = din("wC", [D, 128]); wdt_d = din("wdt", [D, 4])
    cw_d = din("convw", [128, 4, 4]); cb_d = din("convb", [128, 4])
    dtb_d = din("dtb", [128, 16]); alog_d = din("alog", [128, 16]); dsk_d = din("dskip", [128, 4])
    tri_d = din("tri", [128, 128]); su_d = din("su", [128, 128])
    yout = io.out("y", [SEQ, 256], BF16)
    k = cx.k
    PS = lambda: k.ring("psr", [128, 512], F32, 8, psum=True)

    def cload(name, dram, shape, dt, q="sp"):
        t, b = k.tile(name, shape, dt)
        k.dma(q, t, dram, w=[b], sb=b)
        return t, b

    def wcast(name, dram, n):
        t, b = k.tile(name, [128, 8, n], BF16)
        k.dma("pool", t, dram.rearrange("(kc p) n -> p kc n", p=128), w=[b], sb=b)
        return t, b

    wx, wxb = wcast("wx", wx_d, 256); wB, wBb = wcast("wB", wB_d, 128); wC, wCb = wcast("wC", wC_d, 128)
    wdt, wdtb = wcast("wdt", wdt_d, 4)
    cw, cwb = cload("cw", cw_d, [128, 4, 4], F32); cb, cbb = cload("cb", cb_d, [128, 4], F32)
    dtb, dtbb = cload("dtb", dtb_d, [128, 16], F32); alog, alogb = cload("alog", alog_d, [128, 16], F32)
    dsk, dskb = cload("dsk", dsk_d, [128, 4], F32)
    tri, trib = cload("tri", tri_d, [128, 128], F32); su, sub_ = cload("su", su_d, [128, 128], F32)
    ones_f, ones_fb = k.tile("ones_f", [128, 128], F32)
    k.op("pool", lambda e: e.memset(ones_f, 1.0), w=[ones_fb])
    c_one = cx.const(1.0)
    arep, arepb = k.tile("arep", [128, 16], F32)
    k.op("act", lambda e: e.activation(out=arep, in_=alog, func=AF.Exp), r=[alogb], w=[arepb])
    k.op("dve", lambda e: e.tensor_scalar(out=arep, in0=arep, scalar1=-1.0, scalar2=None, op0=ALU.mult), r=[arepb], w=[arepb])
    S, Sb = k.tile("S", [128, 256], F32)
    k.op("pool", lambda e: e.memset(S, 0.0), w=[Sb])
    xbc = []
    for m in range(4):
        t, b = k.tile("xbc%d" % m, [128, 516], F32)
        k.op("pool", lambda e: e.memset(t, 0.0), w=[b])
        xbc.append((t, b))
    outb = []
    wsel = [(wx, wxb, slice(0, 128)), (wx, wxb, slice(128, 256)), (wB, wBb, slice(0, 128)), (wC, wCb, slice(0, 128))]
    bc64 = lambda a, n: a.unsqueeze(2).to_broadcast([128, n, 64])
    def prologue(ti):
        t0 = ti * 512
        hb, hbb = k.ring("hb", [128, 8, 512], BF16, 2)
        dma_hall(k, hb, hall, t0, 512, hbb)
        xc = []
        for m in range(4):
            w_, wb_, cols = wsel[m]
            ps, pb = PS()
            mm(k, ps, [(w_[:, kc, cols], hb[:, kc, :]) for kc in range(8)], [wb_, hbb], pb)
            xt, xtb = xbc[m]
            k.op("act", lambda e: e.activation(out=xt[:, 3:515], in_=ps, func=AF.Copy), r=[pb], w=[xtb])
            acc, accb = k.ring("cacc", [128, 512], F32, 2)
            k.op("dve", lambda e: e.tensor_scalar(out=acc, in0=xt[:, 0:512], scalar1=cw[:, m, 0:1], scalar2=cb[:, m:m + 1],
                                                  op0=ALU.mult, op1=ALU.add), r=[xtb, cwb, cbb], w=[accb])
            for j in range(1, 4):
                k.op("dve", lambda e: e.scalar_tensor_tensor(out=acc, in0=xt[:, j:j + 512], scalar=cw[:, m, j:j + 1], in1=acc,
                                                             op0=ALU.mult, op1=ALU.add), r=[xtb, cwb, accb], w=[accb])
            o, ob = k.ring("xc%d" % m, [128, 512], F32, 2)
            k.op("act", lambda e: e.activation(out=o, in_=acc, func=AF.Silu), r=[accb], w=[ob])
            k.op("pool", lambda e: e.tensor_copy(out=xt[:, 0:3], in_=xt[:, 512:515]), r=[xtb], w=[xtb])
            xc.append((o, ob))
        psd, pbd = PS()
        for c in range(4):
            mm(k, psd[:, c * 4:(c + 1) * 4], [(hb[:, kc, c * 128:(c + 1) * 128], wdt[:, kc, :]) for kc in range(8)], [wdtb, hbb], pbd)
        dt, dtb_ = k.ring("dt", [128, 16], F32, 2)
        k.op("dve", lambda e: e.tensor_tensor(out=dt, in0=psd[:, 0:16], in1=dtb, op=ALU.add), r=[pbd, dtbb], w=[dtb_])
        k.op("act", lambda e: e.activation(out=dt, in_=dt, func=AF.Exp), r=[dtb_], w=[dtb_])
        k.op("act", lambda e: e.activation(out=dt, in_=dt, func=AF.Ln, bias=c_one, scale=1.0), r=[dtb_, cx.b_consts], w=[dtb_])
        da, dab = k.ring("da", [128, 16], F32, 2)
        k.op("dve", lambda e: e.tensor_tensor(out=da, in0=dt, in1=arep, op=ALU.mult), r=[dtb_, arepb], w=[dab])
        psa, pba = PS(); mm(k, psa[:, 0:16], [(tri, da)], [trib, dab], pba)
        pst_, pbt_ = PS(); mm(k, pst_[:, 0:16], [(ones_f, da)], [ones_fb, dab], pbt_)
        acs, acsb = k.ring("acs", [128, 16], F32, 2)
        k.op("act", lambda e: e.activation(out=acs, in_=psa[:, 0:16], func=AF.Copy), r=[pba], w=[acsb])
        eacs, eacsb = k.ring("eacs", [128, 16], F32, 2)
        k.op("act", lambda e: e.activation(out=eacs, in_=acs, func=AF.Exp), r=[acsb], w=[eacsb])
        decs, decsb = k.ring("decs", [128, 16], F32, 2)
        k.op("dve", lambda e: e.tensor_tensor(out=decs, in0=pst_[:, 0:16], in1=acs, op=ALU.subtract), r=[pbt_, acsb], w=[decsb])
        k.op("act", lambda e: e.activation(out=decs, in_=decs, func=AF.Exp), r=[decsb], w=[decsb])
        cd, cdb = k.ring("cd", [128, 16], F32, 2)
        k.op("act", lambda e: e.activation(out=cd, in_=pst_[:, 0:16], func=AF.Exp), r=[pbt_], w=[cdb])
        xtok, xtokb = k.ring("xtok", [128, 4, 256], F32, 2)
        btok, btokb = k.ring("btok", [128, 4, 128], F32, 2)
        for half in range(2):
            ps, pb = PS()
            for ci in range(2):
                c = half * 2 + ci
                for m in range(2):
                    k.op("pe", lambda e: e.transpose(out=ps[:, ci * 256 + m * 128:ci * 256 + (m + 1) * 128],
                                                     in_=xc[m][0][:, c * 128:(c + 1) * 128], identity=cx.ident),
                         r=[xc[m][1], cx.b_ident], w=[pb], inc=True)
            k.op("act", lambda e: e.activation(out=xtok[:, half * 2:half * 2 + 2, :], in_=ps.rearrange("p (a b) -> p a b", a=2), func=AF.Copy),
                 r=[pb], w=[xtokb])
        ps, pb = PS()
        for c in range(4):
            k.op("pe", lambda e: e.transpose(out=ps[:, c * 128:(c + 1) * 128], in_=xc[2][0][:, c * 128:(c + 1) * 128], identity=cx.ident),
                 r=[xc[2][1], cx.b_ident], w=[pb], inc=True)
        k.op("act", lambda e: e.activation(out=btok, in_=ps.rearrange("p (a b) -> p a b", a=4), func=AF.Copy), r=[pb], w=[btokb])
        xd, xdb = k.ring("xd", [128, 4, 256], F32, 2)
        xdd, xddb = k.ring("xdd", [128, 4, 256], F32, 2)
        v16 = lambda a: a.rearrange("p c (h d) -> p (c h) d", d=64)
        k.op("dve", lambda e: e.tensor_tensor(out=v16(xd), in0=v16(xtok), in1=bc64(dt, 16), op=ALU.mult), r=[xtokb, dtb_], w=[xdb])
        k.op("pool", lambda e: e.tensor_tensor(out=v16(xdd), in0=v16(xd), in1=bc64(decs, 16), op=ALU.mult), r=[xdb, decsb], w=[xddb])
        return dict(xc=xc, xd=xd, xdb=xdb, xdd=xdd, xddb=xddb, xtok=xtok, xtokb=xtokb, btok=btok, btokb=btokb,
                    da=da, dab=dab, eacs=eacs, eacsb=eacsb, cd=cd, cdb=cdb)

    def chunks(ti, L):
        xc, xd, xdb, xdd, xddb = L['xc'], L['xd'], L['xdb'], L['xdd'], L['xddb']
        xtok, xtokb, btok, btokb = L['xtok'], L['xtokb'], L['btok'], L['btokb']
        da, dab, eacs, eacsb, cd, cdb = L['da'], L['dab'], L['eacs'], L['eacsb'], L['cd'], L['cdb']
        Bc, Bcb = xc[2]; Cc, Ccb = xc[3]
        v4 = lambda a_: a_.rearrange("p (h d) -> p h d", d=64)

        def stage_a(c):
            cs = slice(c * 128, (c + 1) * 128)
            ps, pb = PS(); mm(k, ps[:, 0:128], [(Bc[:, cs], Cc[:, cs])], [Bcb, Ccb], pb)
            cbm, cbmb = k.ring("cbm", [128, 128], F32, 3)
            k.op("dve", lambda e: e.tensor_tensor(out=cbm, in0=ps[:, 0:128], in1=tri, op=ALU.mult), r=[pb, trib], w=[cbmb])
            pdf, pdfb = PS()
            for h in range(4):
                lh, lhb = k.ring("lh", [128, 128], F32, 4)
                if h % 2 == 0:
                    k.op("dve", lambda e: e.tensor_scalar(out=lh, in0=su, scalar1=da[:, c * 4 + h:c * 4 + h + 1], scalar2=None, op0=ALU.mult),
                         r=[sub_, dab], w=[lhb])
                else:
                    k.op("act", lambda e: e.activation(out=lh, in_=su, func=AF.Copy, scale=da[:, c * 4 + h:c * 4 + h + 1]),
                         r=[sub_, dab], w=[lhb])
                mm(k, pdf[:, h * 128:(h + 1) * 128], [(lh, tri)], [lhb, trib], pdfb)
            seg, segb = k.ring("seg", [128, 4, 128], F32, 3)
            k.op("act", lambda e: e.activation(out=seg, in_=pdf.rearrange("p (a b) -> p a b", a=4), func=AF.Exp), r=[pdfb], w=[segb])
            k.op("dve", lambda e: e.tensor_tensor(out=seg, in0=seg, in1=cbm.unsqueeze(1).to_broadcast([128, 4, 128]), op=ALU.mult),
                 r=[segb, cbmb], w=[segb])
            return seg, segb

        def stage_b(c, seg, segb):
            cs = slice(c * 128, (c + 1) * 128)
            py, pyb = PS()
            for h in range(4):
                mm(k, py[:, h * 64:(h + 1) * 64], [(seg[:, h, :], xd[:, c, h * 64:(h + 1) * 64])], [segb, xdb], pyb)
            po, pob = PS(); mm(k, po[:, 0:256], [(Cc[:, cs], S)], [Ccb, Sb], pob)
            t1, t1b = k.ring("yt1", [128, 256], F32, 2)
            k.op("dve", lambda e: e.tensor_tensor(out=v4(t1), in0=v4(po[:, 0:256]), in1=bc64(eacs[:, c * 4:(c + 1) * 4], 4), op=ALU.mult),
                 r=[pob, eacsb], w=[t1b])
            k.op("dve", lambda e: e.tensor_tensor(out=t1, in0=t1, in1=py[:, 0:256], op=ALU.add), r=[t1b, pyb], w=[t1b])
            t2, t2b = k.ring("yt2", [128, 256], F32, 2)
            k.op("pool", lambda e: e.tensor_tensor(out=v4(t2), in0=v4(xtok[:, c, :]), in1=bc64(dsk, 4), op=ALU.mult), r=[xtokb, dskb], w=[t2b])
            yo, yob = k.ring("yo", [128, 256], BF16, 2)
            k.op("pool", lambda e: e.tensor_tensor(out=yo, in0=t1, in1=t2, op=ALU.add), r=[t1b, t2b], w=[yob])
            r0 = (ti * 4 + c) * 128
            k.dma("sp", yout[r0:r0 + 128, :], yo, r=[yob], sb=yob)
            outb.append(yob)
            pss, pssb = PS(); mm(k, pss[:, 0:256], [(btok[:, c, :], xdd[:, c, :])], [btokb, xddb], pssb)
            k.op("dve", lambda e: e.tensor_tensor(out=v4(S), in0=v4(S), in1=bc64(cd[:, c * 4:(c + 1) * 4], 4), op=ALU.mult), r=[Sb, cdb], w=[Sb])
            k.op("dve", lambda e: e.tensor_tensor(out=S, in0=S, in1=pss[:, 0:256], op=ALU.add), r=[Sb, pssb], w=[Sb])

        pend = None
        for c in range(4):
            cur = (c,) + stage_a(c)
            if pend is not None:
                stage_b(*pend)
            pend = cur
        stage_b(*pend)

    cur = prologue(0)
    for ti in range(ntiles):
        nxt = prologue(ti + 1) if ti + 1 < ntiles else None
        chunks(ti, cur)
        cur = nxt
    k.wait_all("sp", list({id(b): b for b in outb}.values()))
    return nc


def l2b_inputs(h0T_b, hg, P):
    w = P["ab_w_in"]
    g = hg // 2
    xo = 2328
    rep = lambda v, n: np.ascontiguousarray(np.tile(np.asarray(v, np.float32)[None, :], (128, n)))
    chans = [np.arange(hg * 256, hg * 256 + 128), np.arange(hg * 256 + 128, hg * 256 + 256),
             1024 + g * 128 + np.arange(128), 1280 + g * 128 + np.arange(128)]
    cwf = P["b_conv_w"][:, 0, :]
    convw = np.stack([cwf[:, ch].T for ch in chans], axis=1)
    convb = np.stack([P["b_conv_b"][ch] for ch in chans], axis=1)
    hs = slice(hg * 4, hg * 4 + 4)
    t = np.arange(128)
    return {"hall": h0T_b, "wx": np.ascontiguousarray(w[:, xo + hg * 256: xo + (hg + 1) * 256]),
            "wB": np.ascontiguousarray(w[:, xo + 1024 + g * 128: xo + 1024 + (g + 1) * 128]),
            "wC": np.ascontiguousarray(w[:, xo + 1280 + g * 128: xo + 1280 + (g + 1) * 128]),
            "wdt": np.ascontiguousarray(w[:, 3864 + hg * 4: 3864 + hg * 4 + 4]),
            "convw": np.ascontiguousarray(convw.astype(np.float32)), "convb": np.ascontiguousarray(convb.astype(np.float32)),
            "dtb": rep(P["b_dt_bias"][hs], 4), "alog": rep(P["b_a_log"][hs], 4), "dskip": rep(P["b_d_skip"][hs], 1),
            "tri": (t[:, None] <= t[None, :]).astype(np.float32), "su": (t[:, None] > t[None, :]).astype(np.float32)}


def linear_tile(cx, in_ap, inb, W_dram, KC, M, out_fn):
    k = cx.k
    Wv = W_dram.rearrange("(kc p) m -> p kc m", p=128)
    for mb in range(M // 256):
        w, wb = cx.wload(Wv[:, :, mb * 256:(mb + 1) * 256], [128, KC, 256])
        for m2 in range(2):
            ps, pb = cx.psum()
            mm(k, ps, [(w[:, kc, m2 * 128:(m2 + 1) * 128], in_ap[:, kc, :]) for kc in range(KC)], [wb, inb], pb)
            out_fn(mb * 2 + m2, ps, pb)


def build_l3(nc=None, cx=None, io=None):
    nc, cx, io = _std(nc, cx, io)
    din = io.inp
    x1T = din("x1T", [128, 8, TOK]); h0T = din("h0T", [1024, TOK], BF16).rearrange("(kc p) t -> p kc t", p=128)
    oaall = din("oaall", [4 * 4096, 256], BF16); yall = din("yall", [4 * SEQ, 256], BF16); qsel_d = din("qsel", [128, 4])
    gains = din("gains", [128, 12, 8]); normw_d = din("normw", [128, 8])
    wz = din("wz", [D, D]); wout = din("wout", [1536, D])
    f2 = [din("f2g", [D, DFF]), din("f2u", [D, DFF]), din("f2d", [DFF, D])]
    f1 = [din("f1g", [D, DFF]), din("f1u", [D, DFF]), din("f1d", [DFF, D])]
    x4T = io.out("x4T", [128, 8, TOK], F32)
    h1T = io.out("h1T", [1024, TOK], BF16).rearrange("(kc p) t -> p kc t", p=128)
    k = cx.k
    cx.wring_n = 2
    cx.wstg_n = 1
    g, gb = load_gains(cx, gains)
    nw, nwb = k.tile("normw", [128, 8], F32); k.dma("sp", nw, normw_d, w=[nwb], sb=nwb)
    qsel, qselb = k.tile("qsel", [128, 4], F32); k.dma("sp", qsel, qsel_d, w=[qselb], sb=qselb)
    ones512, ones512b = k.tile("ones512", [128, 128], F32)
    k.op("pool", lambda e: e.memset(ones512, 1.0 / 512), w=[ones512b])
    idq, idqb = k.tile("idq", [128, 4, 128], BF16)
    for j in range(4):
        k.op("dve", lambda e: e.tensor_scalar(out=idq[:, j, :], in0=cx.ident, scalar1=qsel[:, j:j + 1], scalar2=None, op0=ALU.mult),
             r=[cx.b_ident, qselb], w=[idqb])
    c_eps5 = cx.const(1e-5)
    outs = []
    for half in range(2):
        x, xb = k.ring("x_res", [128, 8, 1024], F32, 1)
        k.dma("sp", x, x1T[:, :, half * 1024:(half + 1) * 1024], w=[xb], sb=xb)
        for tt in range(2):
            t0 = half * 1024 + tt * 512
            tq = t0 // 512
            sl = slice(tt * 512, (tt + 1) * 512)
            o, ob = k.ring("ffn_o", [128, 8, 1024], F32, 1)
            act, actb = k.ring("act_bf", [128, 22, 1024], BF16, 1)
            ys = o[:, :, 0:512]
            mo = o[:, :, 512:1024]
            mixin = act[:, 0:6, :].rearrange("p a (b t) -> p (a b) t", t=512)
            h0 = act[:, 6:10, :].rearrange("p a (b t) -> p (a b) t", t=512)
            k.dma("sp", h0, h0T[:, :, t0:t0 + 512], w=[actb], sb=actb)
            for hg in range(4):
                cand, candb = k.ring("ycand", [128, 4, 4, 256], BF16, 1)
                for j in range(4):
                    r0 = (j * 4 + hg) * TOK + t0
                    k.dma("sp", cand[:, j], yall[r0:r0 + 512, :].rearrange("(tb p) c -> p tb c", p=128), w=[candb], sb=candb)
                for hh in range(2):
                    ps, pb = cx.psum()
                    for tb in range(4):
                        mm(k, ps[:, tb * 128:(tb + 1) * 128],
                           [(cand[:, j, tb, hh * 128:(hh + 1) * 128], idq[:, j, :]) for j in range(4)], [candb, idqb], pb)
                    k.op("act", lambda e: e.activation(out=ys[:, hg * 2 + hh, :], in_=ps, func=AF.Copy), r=[pb], w=[ob])
            for kvh in range(2):
                ocand, ocandb = k.ring("ocand", [128, 2, 4, 2, 256], BF16, 1)
                for par in range(2):
                    for j in range(4):
                        r0 = ((j // 2) * 4 + kvh * 2 + par) * 2048 + (j % 2) * 1024 + (2 * tq) * 128
                        k.dma("sp", ocand[:, par, j], oaall[r0:r0 + 256, :].rearrange("(i p) c -> p i c", p=128), w=[ocandb], sb=ocandb)
                for hh in range(2):
                    ps, pb = cx.psum()
                    for u in range(4):
                        par, i2 = u % 2, u // 2
                        mm(k, ps[:, u * 128:(u + 1) * 128],
                           [(ocand[:, par, j, i2, hh * 128:(hh + 1) * 128], idq[:, j, :]) for j in range(4)], [ocandb, idqb], pb)
                    k.op("act", lambda e: e.activation(out=mixin[:, kvh * 2 + hh, :], in_=ps, func=AF.Copy), r=[pb], w=[actb])

            def z_out(mc, ps, pb):
                zs, zsb = k.ring("sg", [128, 512], F32, 3)
                k.op("act", lambda e: e.activation(out=zs, in_=ps, func=AF.Silu), r=[pb], w=[zsb])
                k.op("dve", lambda e: e.tensor_tensor(out=ys[:, mc, :], in0=ys[:, mc, :], in1=zs, op=ALU.mult), r=[zsb, ob], w=[ob])
            linear_tile(cx, h0, actb, wz, 8, D, z_out)
            sq, sqb = k.ring("sq", [128, 8, 512], F32, 1)
            k.op("act", lambda e: e.activation(out=sq, in_=ys, func=AF.Square), r=[ob], w=[sqb])
            for gi in range(2):
                ps, pb = cx.psum()
                mm(k, ps, [(ones512, sq[:, gi * 4 + c, :]) for c in range(4)], [ones512b, sqb], pb)
                rs, rsb = k.ring("rstd", [128, 512], F32, 2)
                k.op("act", lambda e: e.activation(out=rs, in_=ps, func=AF.Sqrt, bias=c_eps5, scale=1.0), r=[pb, cx.b_consts], w=[rsb])
                k.op("dve", lambda e: e.reciprocal(out=rs, in_=rs), r=[rsb], w=[rsb])
                for c in range(4):
                    ch = gi * 4 + c
                    k.op("dve", lambda e: e.scalar_tensor_tensor(out=mixin[:, 4 + ch, :], in0=ys[:, ch, :], scalar=nw[:, ch:ch + 1], in1=rs,
                                                                 op0=ALU.mult, op1=ALU.mult), r=[ob, nwb, rsb], w=[actb])

            def mix_out(mc, ps, pb):
                k.op("act", lambda e: e.activation(out=mo[:, mc, :], in_=ps, func=AF.Copy), r=[pb], w=[ob])
            linear_tile(cx, mixin, actb, wout, 12, D, mix_out)
            post_norm_add(cx, x[:, :, sl], xb, mo, ob, g[:, 3, :], gb, 1.0)
        ffn_half(cx, x, xb, 2, f2[0], f2[1], f2[2], g[:, 4, :], g[:, 5, :], gb)
        ffn_half(cx, x, xb, 2, f1[0], f1[1], f1[2], g[:, 6, :], g[:, 7, :], gb)
        k.dma("sp", x4T[:, :, half * 1024:(half + 1) * 1024], x, r=[xb], sb=xb)
        hh_, hhb = k.ring("h_bf", [128, 8, 1024], BF16, 1)
        for tt in range(2):
            sl = slice(tt * 512, (tt + 1) * 512)
            norm_bf16(cx, x[:, :, sl], xb, g[:, 8, :], gb, hh_[:, :, sl], hhb)
        k.dma("sp", h1T[:, :, half * 1024:(half + 1) * 1024], hh_, r=[hhb], sb=hhb)
        outs += [xb, hhb]
    k.wait_all("sp", outs)
    return nc


def build_l5(nc=None, cx=None, io=None):
    nc, cx, io = _std(nc, cx, io)
    din = io.inp
    x4T = din("x4T", [128, 8, TOK])
    ygall = din("ygall", [4096, TOK], BF16).rearrange("(j hg pc p) t -> j p (hg pc) t", j=4, hg=4, pc=2, p=128)
    qsel_d = din("qsel", [128, 4])
    gains = din("gains", [128, 12, 8]); wo = din("wo", [D, D])
    f2 = [din("f2g", [D, DFF]), din("f2u", [D, DFF]), din("f2d", [DFF, D])]
    outT = io.out("outT", [128, 8, TOK], F32)
    k = cx.k
    cx.wstg_n = 1
    g, gb = load_gains(cx, gains)
    qsel, qselb = k.tile("qsel", [128, 4], F32); k.dma("sp", qsel, qsel_d, w=[qselb], sb=qselb)
    outs = []
    for half in range(2):
        x, xb = k.ring("x_res", [128, 8, 1024], F32, 1)
        k.dma("sp", x, x4T[:, :, half * 1024:(half + 1) * 1024], w=[xb], sb=xb)
        for tt in range(2):
            t0 = half * 1024 + tt * 512
            sl = slice(tt * 512, (tt + 1) * 512)
            o, ob = k.ring("ffn_o", [128, 8, 1024], F32, 1)
            act, actb = k.ring("act_bf", [128, 22, 1024], BF16, 1)
            mo = o[:, :, 512:1024]
            yg = act[:, 6:10, :].rearrange("p a (b t) -> p (a b) t", t=512)
            for j in range(4):
                cand, candb = k.ring("ygcand", [128, 8, 512], BF16, 1)
                k.dma("sp", cand, ygall[j][:, :, t0:t0 + 512], w=[candb], sb=candb)
                if j == 0:
                    k.op("dve", lambda e: e.tensor_scalar(out=yg, in0=cand, scalar1=qsel[:, 0:1], scalar2=None, op0=ALU.mult),
                         r=[candb, qselb], w=[actb])
                else:
                    k.op("dve", lambda e: e.scalar_tensor_tensor(out=yg, in0=cand, scalar=qsel[:, j:j + 1], in1=yg, op0=ALU.mult, op1=ALU.add),
                         r=[candb, qselb, actb], w=[actb])

            def mix_out(mc, ps, pb):
                k.op("act", lambda e: e.activation(out=mo[:, mc, :], in_=ps, func=AF.Copy), r=[pb], w=[ob])
            linear_tile(cx, yg, actb, wo, 8, D, mix_out)
            post_norm_add(cx, x[:, :, sl], xb, mo, ob, g[:, 9, :], gb, 1.0)
        ffn_half(cx, x, xb, 2, f2[0], f2[1], f2[2], g[:, 10, :], g[:, 11, :], gb)
        k.dma("sp", outT[:, :, half * 1024:(half + 1) * 1024], x, r=[xb], sb=xb)
        outs.append(xb)
    k.wait_all("sp", outs)
    return nc


def _run(nc, in_maps):
    res = run_bass_kernel_spmd(nc, in_maps, core_ids=list(range(NCORE)))
    return res.results


def kernel_unfused(**inp):
    import ml_dtypes
    f32 = lambda a: np.ascontiguousarray(np.asarray(a, dtype=np.float32))
    I = {k_: f32(v) for k_, v in inp.items()}
    x = I["x"].reshape(16384, D)
    g = gains_layout(I["norm_gains"])
    tok = [slice(c * TOK, (c + 1) * TOK) for c in range(NCORE)]
    grp = lambda lst, b: np.ascontiguousarray(np.concatenate(lst[4 * b:4 * b + 4], axis=0))
    qsel = []
    for c in range(NCORE):
        q_ = np.zeros((128, 4), np.float32); q_[:, c % 4] = 1.0
        qsel.append(q_)
    r1 = _run(build_l1(), [{"xT": fm(x[tok[c]]), "gains": g, "wg": I["ffn1_w_gate"][0], "wu": I["ffn1_w_up"][0],
                            "wd": I["ffn1_w_down"][0]} for c in range(NCORE)])
    x1T = [np.asarray(r["x1T"]) for r in r1]
    h0loc = [np.asarray(r["h0T"]) for r in r1]
    h0all = [grp(h0loc, b) for b in range(2)]
    PA = {k_: I[k_][0] for k_ in I if k_.startswith("a_") or k_.startswith("ab_") or k_.startswith("b_")}
    C = nsa_consts()
    r2a = _run(build_l2a(), [l2a_inputs(h0all[c // 4], (c % 4) // 2, c % 2, PA, C) for c in range(NCORE)])
    r2b = _run(build_l2b(), [l2b_inputs(h0all[c // 4], c % 4, PA) for c in range(NCORE)])
    oaall = [grp([np.asarray(r["oa"]) for r in r2a], b) for b in range(2)]
    yall = [grp([np.asarray(r["y"]) for r in r2b], b) for b in range(2)]
    w_in = I["ab_w_in"][0]
    m3 = []
    for c in range(NCORE):
        m3.append({"x1T": x1T[c], "h0T": h0loc[c], "oaall": oaall[c // 4], "yall": yall[c // 4], "qsel": qsel[c], "gains": g,
                   "normw": np.ascontiguousarray(I["b_norm_w"][0].reshape(8, 128).T),
                   "wz": np.ascontiguousarray(w_in[:, 1304:2328]), "wout": I["ab_w_out"][0],
                   "f2g": I["ffn2_w_gate"][0], "f2u": I["ffn2_w_up"][0], "f2d": I["ffn2_w_down"][0],
                   "f1g": I["ffn1_w_gate"][1], "f1u": I["ffn1_w_up"][1], "f1d": I["ffn1_w_down"][1]})
    r3 = _run(build_l3(), m3)
    x4T = [np.asarray(r["x4T"]) for r in r3]
    h1all = [grp([np.asarray(r["h1T"]) for r in r3], b) for b in range(2)]
    PC = {k_: I[k_][0] for k_ in I if k_.startswith("c_")}
    r4 = _run(build_l4(), [l4_inputs(h1all[c // 4], c % 4, PC) for c in range(NCORE)])
    ygall = [grp([np.asarray(r["ygT"]) for r in r4], b) for b in range(2)]
    m5 = []
    for c in range(NCORE):
        m5.append({"x4T": x4T[c], "ygall": ygall[c // 4], "qsel": qsel[c], "gains": g,
                   "wo": I["c_w_o"][0], "f2g": I["ffn2_w_gate"][1], "f2u": I["ffn2_w_up"][1], "f2d": I["ffn2_w_down"][1]})
    r5 = _run(build_l5(), m5)
    out = np.concatenate([unfm(np.asarray(r["outT"])) for r in r5], axis=0)
    return np.ascontiguousarray(out.reshape(2, SEQ, D).astype(np.float32))


RG = [[0, 1, 2, 3], [4, 5, 6, 7]]


def build_fused(upto=99):
    nc = bass.Bass("TRN2", target_bir_lowering=False, num_devices=NCORE)
    k = K(nc)
    idram = lambda n, sh, dt: nc.dram_tensor(n, list(sh), dt, kind="Internal").ap()
    x1T = idram("i_x1T", [128, 8, TOK], F32)
    h0loc = idram("i_h0loc", [1024, TOK], BF16); h0all = idram("i_h0all", [4096, TOK], BF16)
    oaloc = idram("i_oaloc", [NQB * 128, 256], BF16); oaall = idram("i_oaall", [4 * 4096, 256], BF16)
    yloc = idram("i_yloc", [SEQ, 256], BF16); yall = idram("i_yall", [4 * SEQ, 256], BF16)
    x4T = idram("i_x4T", [128, 8, TOK], F32)
    h1loc = idram("i_h1loc", [1024, TOK], BF16); h1all = idram("i_h1all", [4096, TOK], BF16)
    ygloc = idram("i_ygloc", [1024, TOK], BF16); ygall = idram("i_ygall", [4096, TOK], BF16)
    ccsem = Sem(nc.alloc_semaphore(name="ccsem"), "cc")
    k.dall = [ccsem]; k.dused = []; k.dfree = []

    def allgather(src, dst, wait=True):
        rows, cols = src.shape
        R = (1 << 20) // (cols * mybir.dt.size(src.dtype))
        k.barrier()
        for i in range(rows // R):
            ins = nc.gpsimd.collective_compute("AllGather", ALU.bypass, replica_groups=RG,
                                               ins=[src[i * R:(i + 1) * R, :]], outs=[dst[i * 4 * R:(i + 1) * 4 * R, :]])
            ins.then_inc(ccsem.h, 1)
            ccsem.total += 1
        if wait:
            k.barrier()

    def phase(fn, pre, ext, **kw):
        k.begin_phase()
        cx = Ctx(nc, k)
        fn(nc=nc, cx=cx, io=IO(nc, pre, ext), **kw)
        k.end_phase()

    steps = [lambda: phase(build_l1, "l1_", {"x1T": x1T, "h0T": h0loc}),
             lambda: allgather(h0loc, h0all),
             lambda: phase(build_l2a, "l2a_", {"hall": h0all, "oa": oaloc}),
             lambda: allgather(oaloc, oaall, wait=False),
             lambda: phase(build_l2b, "l2b_", {"hall": h0all, "y": yloc}),
             lambda: allgather(yloc, yall),
             lambda: phase(build_l3, "l3_", {"x1T": x1T, "h0T": h0loc, "oaall": oaall, "yall": yall, "x4T": x4T, "h1T": h1loc}),
             lambda: allgather(h1loc, h1all),
             lambda: phase(build_l4, "l4_", {"hall": h1all, "ygT": ygloc}),
             lambda: allgather(ygloc, ygall),
             lambda: phase(build_l5, "l5_", {"x4T": x4T, "ygall": ygall})]
    for st in steps[:upto]:
        st()
    if upto < len(steps):
        nc.dram_tensor("l5_outT", [128, 8, TOK], F32, kind="ExternalOutput")
    return nc


def kernel(**inp):
    f32 = lambda a: np.ascontiguousarray(np.asarray(a, dtype=np.float32))
    I = {k_: f32(v) for k_, v in inp.items()}
    x = I["x"].reshape(16384, D)
    g = gains_layout(I["norm_gains"])
    PA = {k_: I[k_][0] for k_ in I if k_.startswith("a_") or k_.startswith("ab_") or k_.startswith("b_")}
    PC = {k_: I[k_][0] for k_ in I if k_.startswith("c_")}
    C = nsa_consts()
    w_in = I["ab_w_in"][0]
    in_maps = []
    for c in range(NCORE):
        m = {}
        qs = np.zeros((128, 4), np.float32); qs[:, c % 4] = 1.0
        m.update({"l1_" + k_: v for k_, v in {"xT": fm(x[c * TOK:(c + 1) * TOK]), "gains": g, "wg": I["ffn1_w_gate"][0],
                                             "wu": I["ffn1_w_up"][0], "wd": I["ffn1_w_down"][0]}.items()})
        a = l2a_inputs(None, (c % 4) // 2, c % 2, PA, C); a.pop("hall")
        m.update({"l2a_" + k_: v for k_, v in a.items()})
        b_ = l2b_inputs(None, c % 4, PA); b_.pop("hall")
        m.update({"l2b_" + k_: v for k_, v in b_.items()})
        m.update({"l3_" + k_: v for k_, v in {"qsel": qs, "gains": g,
                  "normw": np.ascontiguousarray(I["b_norm_w"][0].reshape(8, 128).T),
                  "wz": np.ascontiguousarray(w_in[:, 1304:2328]), "wout": I["ab_w_out"][0],
                  "f2g": I["ffn2_w_gate"][0], "f2u": I["ffn2_w_up"][0], "f2d": I["ffn2_w_down"][0],
                  "f1g": I["ffn1_w_gate"][1], "f1u": I["ffn1_w_up"][1], "f1d": I["ffn1_w_down"][1]}.items()})
        d4 = l4_inputs(None, c % 4, PC); d4.pop("hall")
        m.update({"l4_" + k_: v for k_, v in d4.items()})
        m.update({"l5_" + k_: v for k_, v in {"qsel": qs, "gains": g, "wo": I["c_w_o"][0], "f2g": I["ffn2_w_gate"][1],
                                             "f2u": I["ffn2_w_up"][1], "f2d": I["ffn2_w_down"][1]}.items()})
        in_maps.append(m)
    import os
    upto = int(os.environ.get('FUSED_UPTO', '99'))
    nc_ = build_fused(upto)
    if upto < 99:
        names = {a.memorylocations[0].name for a in nc_.allocations if hasattr(a, 'memorylocations') and a.memorylocations}
        in_maps = [{k_: v for k_, v in m.items() if k_ in names} for m in in_maps]
    res = _run(nc_, in_maps)
    out = np.concatenate([unfm(np.asarray(r["l5_outT"])) for r in res], axis=0)
    return np.ascontiguousarray(out.reshape(2, SEQ, D).astype(np.float32))
```

```python
import numpy as np
import concourse.bass as bass
import concourse.mybir as mybir
from concourse.bass_utils import run_bass_kernel_spmd

F32 = mybir.dt.float32
BF16 = mybir.dt.bfloat16
AF = mybir.ActivationFunctionType
ALU = mybir.AluOpType
AX = mybir.AxisListType

D = 1024
DFF = 2816
NCORE = 8
TOK = 2048
EPS = 1e-6


class Buf:
    __slots__ = ("name", "lw", "rd", "dsem", "lw_dma")

    def __init__(self, name):
        self.name = name
        self.lw = None
        self.rd = []
        self.dsem = None
        self.lw_dma = False


class Sem:
    __slots__ = ("h", "total", "name")

    def __init__(self, h, name):
        self.h = h
        self.total = 0
        self.name = name


class K:
    def __init__(self, nc):
        self.nc = nc
        self.engs = {"pe": nc.tensor, "act": nc.scalar, "dve": nc.vector,
                     "pool": nc.gpsimd, "sp": nc.sync}
        self.esem = {n: Sem(nc.alloc_semaphore(name="es_" + n), n) for n in self.engs}
        self.waited = {n: {} for n in self.engs}
        self.nsem = 0
        self.ninstr = 0
        self.nwait = 0
        self.rings = {}

    def begin_phase(self):
        import contextlib
        self.phase = getattr(self, "phase", 0) + 1
        self.stack = contextlib.ExitStack()
        self.rings = {}
        self.dfree = getattr(self, "dfree", [])

    def end_phase(self):
        self.barrier()
        self.stack.close()
        self.stack = None
        self.dfree = list(self.dused)
        self.dused = []

    def barrier(self):
        sems = list(self.esem.values()) + list(getattr(self, "dall", []))
        for eng in self.engs:
            needs = {s_: s_.total for s_ in sems if s_.total > 0}
            self._emit_waits(eng, needs)

    def sb(self, name, shape, dt):
        if getattr(self, "stack", None) is not None:
            return self.stack.enter_context(self.nc.sbuf_tensor("s%d_%s" % (self.phase, name), list(shape), dt)).ap()
        return self._sb_static(name, shape, dt)

    def _sb_static(self, name, shape, dt):
        return self.nc.alloc_sbuf_tensor("s_" + name, list(shape), dt).ap()

    def ps(self, name, shape, dt=F32):
        if getattr(self, "stack", None) is not None:
            return self.stack.enter_context(self.nc.psum_tensor("p%d_%s" % (self.phase, name), list(shape), dt)).ap()
        return self.nc.alloc_psum_tensor("p_" + name, list(shape), dt).ap()

    def tile(self, name, shape, dt):
        return self.sb(name, shape, dt), Buf(name)

    def ring(self, name, shape, dt, n, psum=False):
        if name not in self.rings:
            sl = []
            for i in range(n):
                nm = "%s_%d" % (name, i)
                ap = self.ps(nm, shape, dt) if psum else self.sb(nm, shape, dt)
                sl.append((ap, Buf(nm)))
            self.rings[name] = [sl, 0]
        r = self.rings[name]
        s = r[0][r[1] % len(r[0])]
        r[1] += 1
        return s

    def _dsem(self, b, q="sp"):
        if b.dsem is None:
            if not hasattr(self, "dall"):
                self.dall, self.dused, self.dfree = [], [], getattr(self, "dfree", [])
            if self.dfree and q != "pool":
                b.dsem = self.dfree.pop()
            else:
                b.dsem = Sem(self.nc.alloc_semaphore(name="ds%d" % self.nsem), b.name)
                self.nsem += 1
                self.dall.append(b.dsem)
            if q != "pool":
                self.dused.append(b.dsem)
        return b.dsem

    def _need(self, eng, tok, needs):
        if tok is None:
            return
        s, v = tok
        if v is None:
            v = s.total
        if eng == "pe" and s is self.esem["pe"]:
            return
        if v > needs.get(s, 0):
            needs[s] = v

    def _emit_waits(self, eng, needs):
        e = self.engs[eng]
        w = self.waited[eng]
        for s, v in needs.items():
            if w.get(s, 0) >= v:
                continue
            e.wait_ge(s.h, v)
            self.nwait += 1
            w[s] = v

    def op(self, eng, fn, r=(), w=(), inc=True):
        needs = {}
        for b in r:
            self._need(eng, b.lw, needs)
        for b in w:
            self._need(eng, b.lw, needs)
            for t in b.rd:
                self._need(eng, t, needs)
        self._emit_waits(eng, needs)
        ins = fn(self.engs[eng])
        s = self.esem[eng]
        if inc:
            s.total += 1
            ins.then_inc(s.h, 1)
            tok = (s, s.total)
        else:
            tok = (s, s.total + 1)
        for b in r:
            b.rd.append(tok)
            if len(b.rd) > 24:
                b.rd = self._compact(b.rd)
        for b in w:
            b.lw = tok
            b.rd = []
            b.lw_dma = False
        self.ninstr += 1
        return ins

    def _compact(self, toks):
        best = {}
        for s, v in toks:
            if v is None:
                best[s] = None
            elif s not in best or (best[s] is not None and v > best[s]):
                best[s] = v
        return [(s, v) for s, v in best.items()]

    def dma(self, q, out, in_, r=(), w=(), sb=None, **kw):
        s = self._dsem(sb, q)
        needs = {}
        for b in r:
            self._need(q, b.lw, needs)
        for b in w:
            if not (b.lw_dma and b.lw is not None and b.lw[0] is s and not b.rd):
                self._need(q, b.lw, needs)
            for t in b.rd:
                self._need(q, t, needs)
        self._emit_waits(q, needs)
        ins = self.engs[q].dma_start(out=out, in_=in_, **kw)
        s.total += 16
        ins.then_inc(s.h, 16)
        tok = (s, None)
        for b in r:
            b.rd.append(tok)
        for b in w:
            b.lw = tok
            b.rd = []
            b.lw_dma = True
        self.ninstr += 1
        return ins

    def wait_all(self, eng, bufs):
        needs = {}
        for b in bufs:
            self._need(eng, b.lw, needs)
            for t in b.rd:
                self._need(eng, t, needs)
        self._emit_waits(eng, needs)


class Ctx:
    def __init__(self, nc, k=None):
        self.nc = nc
        self.k = k if k is not None else K(nc)
        k = self.k
        self.ones_d, self.b_ones_d = k.tile("ones_d", [128, 128], BF16)
        k.op("pool", lambda e: e.memset(self.ones_d, 1.0 / D), w=[self.b_ones_d])
        self.ident, self.b_ident = k.tile("ident", [128, 128], F32)
        k.op("pool", lambda e: e.memset(self.ident, 1.0), w=[self.b_ident])
        k.op("pool", lambda e: e.affine_select(out=self.ident, in_=self.ident, pattern=[[-1, 128]],
                                               compare_op=ALU.is_equal, fill=0.0, base=0,
                                               channel_multiplier=1), r=[self.b_ident], w=[self.b_ident])
        self.dq = 0
        self.consts, self.b_consts = k.tile("consts", [128, 16], F32)
        self.cvals = {}

    def const(self, v):
        if v not in self.cvals:
            i = len(self.cvals)
            self.cvals[v] = i
            self.k.op("pool", lambda e: e.memset(self.consts[:, i:i + 1], float(v)), w=[self.b_consts])
        i = self.cvals[v]
        return self.consts[:, i:i + 1]

    def psum(self):
        return self.k.ring("psum", [128, 512], F32, 8, psum=True)

    def wload(self, dram_ap, shape, alt=False):
        k = self.k
        ap, b = k.ring("wring", [128, 5632], BF16, getattr(self, "wring_n", 3))
        n = shape[1] * shape[2]
        v = ap[:, 0:n].rearrange("p (a b) -> p a b", a=shape[1])
        if alt and n <= 2048:
            st, stb = k.ring("wstg", [128, 2048], F32, getattr(self, "wstg_n", 2))
            sv = st[:, 0:n].rearrange("p (a b) -> p a b", a=shape[1])
            k.dma("sp", sv, dram_ap, w=[stb], sb=stb)
            k.op("dve", lambda e: e.tensor_copy(out=v, in_=sv), r=[stb], w=[b])
        else:
            k.dma("pool", v, dram_ap, w=[b], sb=b)
        return v, b


class IO:
    def __init__(self, nc, pre="", ext=None):
        self.nc, self.pre, self.ext = nc, pre, dict(ext or {})

    def inp(self, name, shape, dt=F32):
        if name in self.ext:
            return self.ext[name]
        return self.nc.dram_tensor(self.pre + name, list(shape), dt, kind="ExternalInput").ap()

    def out(self, name, shape, dt=F32):
        if name in self.ext:
            return self.ext[name]
        return self.nc.dram_tensor(self.pre + name, list(shape), dt, kind="ExternalOutput").ap()


def _std(nc, cx, io):
    if nc is None:
        nc = bass.Bass("TRN2", target_bir_lowering=False)
    if cx is None:
        cx = Ctx(nc)
    if io is None:
        io = IO(nc)
    return nc, cx, io


def hall_tile(hall, t0, n):
    r, col = t0 // TOK, t0 % TOK
    v = hall.rearrange("(i r kk p) t -> r p i kk t", i=4, r=4, kk=2, p=128)
    return v[r][:, :, :, col:col + n]


def dma_hall(k, dst, hall, t0, n, buf, **kw):
    src = hall_tile(hall, t0, n)
    for i in range(4):
        k.dma("sp", dst[:, 2 * i:2 * i + 2, :], src[:, i], w=[buf], sb=buf, **kw)


def rms_rstd(cx, x_ap, xb, eps=EPS):
    k = cx.k
    T = x_ap.shape[2]
    sq, sqb = k.ring("sq", [128, 8, 512], BF16, 1)
    k.op("act", lambda e: e.activation(out=sq[:, :, 0:T], in_=x_ap, func=AF.Square), r=[xb], w=[sqb])
    ps, pb = cx.psum()
    for c in range(8):
        k.op("pe", lambda e: e.matmul(ps[:, 0:T], lhsT=cx.ones_d, rhs=sq[:, c, 0:T], start=(c == 0), stop=(c == 7)),
             r=[sqb, cx.b_ones_d], w=[pb], inc=(c == 7))
    rs, rsb = k.ring("rstd", [128, 512], F32, 2)
    c_eps = cx.const(eps)
    k.op("act", lambda e: e.activation(out=rs[:, 0:T], in_=ps[:, 0:T], func=AF.Sqrt, bias=c_eps, scale=1.0),
         r=[pb, cx.b_consts], w=[rsb])
    k.op("dve", lambda e: e.reciprocal(out=rs[:, 0:T], in_=rs[:, 0:T]), r=[rsb], w=[rsb])
    return rs[:, 0:T], rsb


def norm_bf16(cx, x_ap, xb, g_ap, gb, out_ap, outb):
    k = cx.k
    rs, rsb = rms_rstd(cx, x_ap, xb)
    for c in range(8):
        eng = "dve"
        k.op(eng, lambda e: e.scalar_tensor_tensor(out=out_ap[:, c, :], in0=x_ap[:, c, :], scalar=g_ap[:, c:c + 1],
                                                   in1=rs, op0=ALU.mult, op1=ALU.mult),
             r=[xb, gb, rsb], w=[outb])


def ffn_half(cx, x, xb, NT, wg, wu, wd, g_in, g_out, gb):
    k = cx.k
    T = NT * 512
    h, hb = k.ring("h_bf", [128, 8, 1024], BF16, 1)
    act, actb = k.ring("act_bf", [128, 22, 1024], BF16, 1)
    for tt in range(NT):
        sl = slice(tt * 512, (tt + 1) * 512)
        norm_bf16(cx, x[:, :, sl], xb, g_in, gb, h[:, :, sl], hb)
    wgv = wg.rearrange("(kc p) f -> p kc f", p=128)
    wuv = wu.rearrange("(kc p) f -> p kc f", p=128)
    for fb in range(11):
        gw, gwb = cx.wload(wgv[:, :, fb * 256:(fb + 1) * 256], [128, 8, 256])
        uw, uwb = cx.wload(wuv[:, :, fb * 256:(fb + 1) * 256], [128, 8, 256], alt=True)
        for tt in range(NT):
            sl = slice(tt * 512, (tt + 1) * 512)
            for fc in range(2):
                f = fb * 2 + fc
                pg, pgb = cx.psum()
                pu, pub = cx.psum()
                for c in range(8):
                    k.op("pe", lambda e: e.matmul(pg, lhsT=gw[:, c, fc * 128:(fc + 1) * 128], rhs=h[:, c, sl],
                                                  start=(c == 0), stop=(c == 7)), r=[gwb, hb], w=[pgb], inc=(c == 7))
                for c in range(8):
                    k.op("pe", lambda e: e.matmul(pu, lhsT=uw[:, c, fc * 128:(fc + 1) * 128], rhs=h[:, c, sl],
                                                  start=(c == 0), stop=(c == 7)), r=[uwb, hb], w=[pub], inc=(c == 7))
                sg, sgb = k.ring("sg", [128, 512], F32, 3)
                k.op("act", lambda e: e.activation(out=sg, in_=pg, func=AF.Silu), r=[pgb], w=[sgb])
                k.op("dve", lambda e: e.tensor_tensor(out=act[:, f, sl], in0=sg, in1=pu, op=ALU.mult),
                     r=[sgb, pub], w=[actb])
    wdv = wd.rearrange("(fc p) d -> p fc d", p=128)
    o, ob = k.ring("ffn_o", [128, 8, 1024], F32, 1)
    for db in range(4):
        dw, dwb = cx.wload(wdv[:, :, db * 256:(db + 1) * 256], [128, 22, 256])
        for tt in range(NT):
            sl = slice(tt * 512, (tt + 1) * 512)
            for dc in range(2):
                d = db * 2 + dc
                po, pob = cx.psum()
                for f in range(22):
                    k.op("pe", lambda e: e.matmul(po, lhsT=dw[:, f, dc * 128:(dc + 1) * 128], rhs=act[:, f, sl],
                                                  start=(f == 0), stop=(f == 21)), r=[dwb, actb], w=[pob], inc=(f == 21))
                k.op("act", lambda e: e.activation(out=o[:, d, sl], in_=po, func=AF.Copy), r=[pob], w=[ob])
    for tt in range(NT):
        sl = slice(tt * 512, (tt + 1) * 512)
        post_norm_add(cx, x[:, :, sl], xb, o[:, :, sl], ob, g_out, gb, 0.5)


def post_norm_add(cx, x_ap, xb, o_ap, ob, g_ap, gb, coef):
    k = cx.k
    rs, rsb = rms_rstd(cx, o_ap, ob)
    for c in range(8):
        eng = "dve"
        tmp, tb = k.ring("pn_tmp" + eng, [128, 512], F32, 2)
        T = o_ap.shape[2]
        k.op(eng, lambda e: e.scalar_tensor_tensor(out=tmp[:, 0:T], in0=o_ap[:, c, :], scalar=g_ap[:, c:c + 1], in1=rs,
                                                   op0=ALU.mult, op1=ALU.mult), r=[ob, gb, rsb], w=[tb])
        k.op(eng, lambda e: e.scalar_tensor_tensor(out=x_ap[:, c, :], in0=tmp[:, 0:T], scalar=float(coef), in1=x_ap[:, c, :],
                                                   op0=ALU.mult, op1=ALU.add), r=[tb, xb], w=[xb])


def load_gains(cx, gains_dram):
    k = cx.k
    g, gb = k.tile("gains", [128, 12, 8], F32)
    k.dma("sp", g, gains_dram, w=[gb], sb=gb)
    return g, gb


def build_l1(nc=None, cx=None, io=None):
    nc, cx, io = _std(nc, cx, io)
    xT = io.inp("xT", [128, 8, TOK]); gains = io.inp("gains", [128, 12, 8])
    wg = io.inp("wg", [D, DFF]); wu = io.inp("wu", [D, DFF]); wd = io.inp("wd", [DFF, D])
    x1T = io.out("x1T", [128, 8, TOK], F32)
    h0T = io.out("h0T", [1024, TOK], BF16).rearrange("(kc p) t -> p kc t", p=128)
    k = cx.k
    g, gb = load_gains(cx, gains)
    outs = []
    for half in range(2):
        hs = slice(half * 1024, (half + 1) * 1024)
        x, xb = k.ring("x_res", [128, 8, 1024], F32, 1)
        k.dma("sp", x, xT[:, :, hs], w=[xb], sb=xb)
        ffn_half(cx, x, xb, 2, wg, wu, wd, g[:, 0, :], g[:, 1, :], gb)
        k.dma("sp", x1T[:, :, hs], x, r=[xb], sb=xb)
        hh, hhb = k.ring("h_bf", [128, 8, 1024], BF16, 1)
        for tt in range(2):
            sl = slice(tt * 512, (tt + 1) * 512)
            norm_bf16(cx, x[:, :, sl], xb, g[:, 2, :], gb, hh[:, :, sl], hhb)
        k.dma("sp", h0T[:, :, hs], hh, r=[hhb], sb=hhb)
        outs += [xb, hhb]
    k.wait_all("sp", outs)
    return nc


def fm(a):
    t = a.shape[0]
    return np.ascontiguousarray(a.T.reshape(8, 128, t).transpose(1, 0, 2))


def unfm(a):
    t = a.shape[2]
    return np.ascontiguousarray(a.transpose(1, 0, 2).reshape(1024, t).T)


def gains_layout(norm_gains):
    g = norm_gains.reshape(12, 8, 128).transpose(2, 0, 1)
    return np.ascontiguousarray(g)


SEQ = 8192
RW_EXPC = 0.6065306597126334
GN_EPS = 64e-5


def mm(k, ps_ap, pairs, rbufs, wbuf):
    n = len(pairs)
    for i, (l, r) in enumerate(pairs):
        k.op("pe", lambda e: e.matmul(ps_ap, lhsT=l, rhs=r, start=(i == 0), stop=(i == n - 1)),
             r=rbufs, w=[wbuf], inc=(i == n - 1))


class _Stop(Exception):
    pass


def build_l4(ntiles=16, stop=99, nc=None, cx=None, io=None):
    try:
        return _build_l4(ntiles, stop, nc, cx, io)
    except _Stop as e:
        return e.args[0]


def _build_l4(ntiles=16, stop=99, nc=None, cx=None, io=None):
    nc, cx, io = _std(nc, cx, io)
    dt_in = io.inp
    hall = dt_in("hall", [4096, TOK], BF16)
    hzero = dt_in("hzero", [1024, 1], BF16)
    W = {"r": dt_in("wr", [D, 256]), "k": dt_in("wk", [D, 256]), "v": dt_in("wv", [D, 256]),
         "w1": dt_in("w1", [D, 64]), "a1": dt_in("a1", [D, 64]), "g1": dt_in("g1", [D, 160])}
    w2d = dt_in("w2", [64, 256]); a2d = dt_in("a2", [64, 256]); g2d = dt_in("g2", [160, 256])
    mud = dt_in("mu", [128, 6, 8])
    vecd = dt_in("vecs", [128, 7, 2])
    maskd = dt_in("masks", [128, 3, 128])
    bonesd = dt_in("bones", [128, 128])
    resetd = dt_in("resetm", [128, 512])
    ygq = io.out("ygT", [1024, TOK], BF16).rearrange("(j c p) t -> j p c t", j=4, c=2, p=128)
    k = cx.k
    PS = lambda: k.ring("psr", [128, 512], F32, 4, psum=True)
    ybank = [k.ps("ybank%d" % i, [128, 512]) for i in range(2)]
    ybb = [Buf("ybank%d" % i) for i in range(2)]
    ybank2 = [k.ps("ybankb%d" % i, [128, 512]) for i in range(2)]
    ybb2 = [Buf("ybankb%d" % i) for i in range(2)]

    mu, mub = k.tile("mu", [128, 6, 8], F32); k.dma("sp", mu, mud, w=[mub], sb=mub)
    vec, vecb = k.tile("vecs", [128, 7, 2], F32); k.dma("sp", vec, vecd, w=[vecb], sb=vecb)
    msk, mskb = k.tile("masks", [128, 3, 128], F32); k.dma("sp", msk, maskd, w=[mskb], sb=mskb)
    bones, bonesb = k.tile("bones", [128, 128], F32); k.dma("sp", bones, bonesd, w=[bonesb], sb=bonesb)
    rstm, rstmb = k.tile("resetm", [128, 512], F32); k.dma("sp", rstm, resetd, w=[rstmb], sb=rstmb)
    W0, A0, KK, KA, RK, LNG, LNB = range(7)
    m4 = lambda i: msk[:, i, :].unsqueeze(1).to_broadcast([128, 4, 128])
    id4 = cx.ident.unsqueeze(1).to_broadcast([128, 4, 128])
    c_tiny = cx.const(1e-24)
    c_gneps = cx.const(GN_EPS)

    order = {"r": 0, "w1": 1, "k": 2, "v": 3, "a1": 4, "g1": 5}
    Wa, Wb, Wbuf = {}, {}, {}
    for nm, wd_ in W.items():
        n = wd_.shape[1]
        wa, wab = k.tile("wa_" + nm, [128, 8, n], BF16)
        wb, wbb = k.tile("wb_" + nm, [128, 8, n], BF16)
        Wa[nm], Wb[nm], Wbuf[nm] = wa, wb, [wab, wbb]
    import contextlib
    _outer = getattr(k, "stack", None)
    k.stack = contextlib.ExitStack()
    k.phase = getattr(k, "phase", 0)
    for nm, wd_ in W.items():
        n = wd_.shape[1]
        wa, wb = Wa[nm], Wb[nm]
        wab, wbb = Wbuf[nm]
        st, stb = k.ring("wstage", [128, 8, 256], F32, 1)
        k.dma("sp", st[:, :, 0:n], wd_.rearrange("(kc p) n -> p kc n", p=128), w=[stb], sb=stb)
        tmp, tmpb = k.ring("wstage2", [128, 8, 256], F32, 1)
        i = order[nm]
        k.op("dve", lambda e: e.tensor_tensor(out=tmp[:, :, 0:n], in0=st[:, :, 0:n],
                                              in1=mu[:, i, :].unsqueeze(2).to_broadcast([128, 8, n]), op=ALU.mult),
             r=[stb, mub], w=[tmpb])
        k.op("dve", lambda e: e.tensor_tensor(out=wa, in0=st[:, :, 0:n], in1=tmp[:, :, 0:n], op=ALU.subtract),
             r=[stb, tmpb], w=[wab])
        k.op("act", lambda e: e.activation(out=wb, in_=tmp[:, :, 0:n], func=AF.Copy), r=[tmpb], w=[wbb])
    k.barrier()
    k.stack.close()
    k.stack = _outer
    k.rings.pop("wstage"); k.rings.pop("wstage2")
    w2, w2b = k.tile("w2", [64, 256], BF16); k.dma("pool", w2, w2d, w=[w2b], sb=w2b)
    a2, a2b = k.tile("a2", [64, 256], BF16); k.dma("pool", a2, a2d, w=[a2b], sb=a2b)
    g2a, g2ab = k.tile("g2a", [128, 256], BF16); k.dma("pool", g2a, g2d[0:128, :], w=[g2ab], sb=g2ab)
    g2c, g2cb = k.tile("g2c", [32, 256], BF16); k.dma("pool", g2c, g2d[128:160, :], w=[g2cb], sb=g2cb)

    U = []
    for hd in range(4):
        pp = []
        for j in range(2):
            u, ub = k.tile("U%d_%d" % (hd, j), [128, 64], F32)
            k.op("pool", lambda e: e.memset(u, 0.0), w=[ub])
            pp.append((u, ub))
        U.append(pp)
    ucur = [0, 0, 0, 0]
    PL = []
    for pc in range(2):
        p_, pb_ = k.tile("PL%d" % pc, [128, 9], F32)
        k.op("pool", lambda e: e.memset(p_, 1.0), w=[pb_])
        PL.append((p_, pb_))
    outbufs = []
    if stop == 1:
        raise _Stop(nc)

    for ti in range(ntiles):
        t0 = ti * 512
        hb, hbb = k.ring("hb", [128, 8, 514], BF16, 2)
        dma_hall(k, hb[:, :, 1:513], hall, t0, 512, hbb)
        if t0 == 0:
            k.dma("sp", hb[:, :, 0:1], hzero.rearrange("(kc p) t -> p kc t", p=128), w=[hbb], sb=hbb, allow_slow_non_contiguous=True)
        else:
            dma_hall(k, hb[:, :, 0:1], hall, t0 - 1, 1, hbb, allow_slow_non_contiguous=True)

        def proj_pairs(nm, cols, tok=None):
            prs = []
            for kc in range(8):
                prs.append((Wa[nm][:, kc, cols], hb[:, kc, 1:513]))
                prs.append((Wb[nm][:, kc, cols], hb[:, kc, 0:512]))
            return prs

        FM = {}

        def evac(name, ps, pb, rows=128, func=AF.Copy, bias=None, dt=F32, extra=()):
            o, ob = k.ring("fm_" + name, [128, 512], dt, 1)
            kw = {}
            if bias is not None:
                kw["bias"] = bias
            k.op("act", lambda e: e.activation(out=o[0:rows, :], in_=ps[0:rows, :], func=func, **kw),
                 r=[pb] + list(extra), w=[ob])
            return o, ob

        for pc in range(2):
            cols = slice(pc * 128, (pc + 1) * 128)
            for nm in ("r", "k", "v"):
                ps, pb = PS()
                mm(k, ps, proj_pairs(nm, cols), [hbb] + Wbuf[nm], pb)
                FM[(nm, pc)] = evac("%s%d" % (nm, pc), ps, pb)
        vtok, vtokb = k.ring("vtok", [128, 4, 256], F32, 2)
        for half in range(2):
            ps, pb = PS()
            for bi in range(2):
                blk = half * 2 + bi
                prs = []
                for kc in range(8):
                    prs.append((hb[:, kc, 1 + blk * 128:1 + (blk + 1) * 128], Wa["v"][:, kc, :]))
                    prs.append((hb[:, kc, blk * 128:(blk + 1) * 128], Wb["v"][:, kc, :]))
                mm(k, ps[:, bi * 256:(bi + 1) * 256], prs, [hbb] + Wbuf["v"], pb)
            k.op("act", lambda e: e.activation(out=vtok[:, half * 2:half * 2 + 2, :],
                                               in_=ps.rearrange("p (a b) -> p a b", a=2), func=AF.Copy),
                 r=[pb], w=[vtokb])
        ps, pb = PS(); mm(k, ps[0:64, :], proj_pairs("w1", slice(0, 64)), [hbb] + Wbuf["w1"], pb)
        hw, hwb = evac("hw", ps, pb, rows=64, func=AF.Tanh, dt=BF16)
        ps, pb = PS(); mm(k, ps[0:64, :], proj_pairs("a1", slice(0, 64)), [hbb] + Wbuf["a1"], pb)
        ha, hab = evac("ha", ps, pb, rows=64, dt=BF16)
        ps, pb = PS(); mm(k, ps, proj_pairs("g1", slice(0, 128)), [hbb] + Wbuf["g1"], pb)
        hg0, hg0b = evac("hg0", ps, pb, func=AF.Sigmoid, dt=BF16)
        ps, pb = PS(); mm(k, ps[0:32, :], proj_pairs("g1", slice(128, 160)), [hbb] + Wbuf["g1"], pb)
        hg1, hg1b = evac("hg1", ps, pb, rows=32, func=AF.Sigmoid, dt=BF16)
        for pc in range(2):
            cols = slice(pc * 128, (pc + 1) * 128)
            ps, pb = PS(); mm(k, ps, [(w2[:, cols], hw[0:64, :])], [w2b, hwb], pb)
            FM[("sgw", pc)] = evac("sgw%d" % pc, ps, pb, func=AF.Sigmoid, bias=vec[:, W0, pc:pc + 1], extra=[vecb])
            ps, pb = PS(); mm(k, ps, [(a2[:, cols], ha[0:64, :])], [a2b, hab], pb)
            FM[("a", pc)] = evac("a%d" % pc, ps, pb, func=AF.Sigmoid, bias=vec[:, A0, pc:pc + 1], extra=[vecb])
            ps, pb = PS(); mm(k, ps, [(g2a[:, cols], hg0), (g2c[:, cols], hg1[0:32, :])], [g2ab, g2cb, hg0b, hg1b], pb)
            FM[("g", pc)] = evac("g%d" % pc, ps, pb)

        if stop == 2:
            raise _Stop(nc)
        def tmpt(name, dt=F32, n=2):
            return k.ring("tmp", [128, 512], F32, 9)

        PR = {}
        for pc in range(2):
            r_, rb_ = FM[("r", pc)]; k_, kb_ = FM[("k", pc)]; a_, ab_ = FM[("a", pc)]; sg_, sgb_ = FM[("sgw", pc)]
            kk0, kk0b = tmpt("kk0")
            k.op("dve", lambda e: e.tensor_scalar(out=kk0, in0=k_, scalar1=vec[:, KK, pc:pc + 1], scalar2=None, op0=ALU.mult),
                 r=[kb_, vecb], w=[kk0b])
            sq, sqb = tmpt("sq")
            k.op("pool", lambda e: e.tensor_tensor(out=sq, in0=kk0, in1=kk0, op=ALU.mult), r=[kk0b], w=[sqb])
            ps, pb = PS(); mm(k, ps, [(bones, sq)], [bonesb, sqb], pb)
            rn, rnb = tmpt("rn")
            k.op("act", lambda e: e.activation(out=rn, in_=ps, func=AF.Sqrt, bias=c_tiny, scale=1.0), r=[pb, cx.b_consts], w=[rnb])
            k.op("dve", lambda e: e.reciprocal(out=rn, in_=rn), r=[rnb], w=[rnb])
            kap, kapb = tmpt("kap")
            k.op("dve", lambda e: e.tensor_tensor(out=kap, in0=kk0, in1=rn, op=ALU.mult), r=[kk0b, rnb], w=[kapb])
            am, amb = tmpt("am")
            k.op("dve", lambda e: e.tensor_scalar(out=am, in0=a_, scalar1=-1.0, scalar2=vec[:, KA, pc:pc + 1],
                                                  op0=ALU.add, op1=ALU.mult), r=[ab_, vecb], w=[amb])
            kp, kpb = k.ring("kp%d" % pc, [128, 512], F32, 1)
            k.op("dve", lambda e: e.scalar_tensor_tensor(out=kp, in0=am, scalar=1.0, in1=k_, op0=ALU.add, op1=ALU.mult),
                 r=[amb, kb_], w=[kpb])
            logd, logdb = tmpt("logd")
            k.op("pool", lambda e: e.tensor_scalar(out=logd, in0=sg_, scalar1=-RW_EXPC, scalar2=None, op0=ALU.mult),
                 r=[sgb_], w=[logdb])
            Lc, Lcb = tmpt("Lc")
            k.op("dve", lambda e: e.tensor_tensor_scan(out=Lc, data0=rstm, data1=logd, initial=0.0, op0=ALU.mult, op1=ALU.add),
                 r=[rstmb, logdb], w=[Lcb])
            Lm, Lmb = tmpt("Lm")
            k.op("pool", lambda e: e.tensor_tensor(out=Lm, in0=Lc, in1=logd, op=ALU.subtract), r=[Lcb, logdb], w=[Lmb])
            P_, Pb_ = tmpt("P"); Pi, Pib = tmpt("Pi"); Pp, Ppb = tmpt("Pp")
            k.op("act", lambda e: e.activation(out=P_, in_=Lc, func=AF.Exp), r=[Lcb], w=[Pb_])
            k.op("act", lambda e: e.activation(out=Pi, in_=Lc, func=AF.Exp, scale=-1.0), r=[Lcb], w=[Pib])
            k.op("act", lambda e: e.activation(out=Pp, in_=Lm, func=AF.Exp), r=[Lmb], w=[Ppb])
            pl, plb = PL[pc]
            k.op("dve", lambda e: e.tensor_copy(out=pl[:, 0:1], in_=pl[:, 8:9]), r=[plb], w=[plb])
            k.op("dve", lambda e: e.tensor_copy(out=pl[:, 1:9], in_=P_.rearrange("p (c t) -> p c t", t=64)[:, :, 63]),
                 r=[Pb_, plb], w=[plb])
            rt, rtb = k.ring("rt%d" % pc, [128, 512], F32, 1)
            kt, ktb = k.ring("kt%d" % pc, [128, 512], F32, 1)
            bt, btb = k.ring("bt%d" % pc, [128, 512], F32, 1)
            kkt, kktb = k.ring("kkt%d" % pc, [128, 512], F32, 1)
            k.op("dve", lambda e: e.tensor_tensor(out=rt, in0=r_, in1=P_, op=ALU.mult), r=[rb_, Pb_], w=[rtb])
            k.op("pool", lambda e: e.tensor_tensor(out=kt, in0=kap, in1=Pp, op=ALU.mult), r=[kapb, Ppb], w=[ktb])
            ka_, kab_ = tmpt("ka")
            k.op("pool", lambda e: e.tensor_tensor(out=ka_, in0=kap, in1=a_, op=ALU.mult), r=[kapb, ab_], w=[kab_])
            k.op("dve", lambda e: e.tensor_tensor(out=bt, in0=ka_, in1=Pi, op=ALU.mult), r=[kab_, Pib], w=[btb])
            k.op("pool", lambda e: e.tensor_tensor(out=kkt, in0=kp, in1=Pi, op=ALU.mult), r=[kpb, Pib], w=[kktb])
            rts, rtsb = k.ring("rts%d" % pc, [128, 512], F32, 1)
            kts, ktsb = k.ring("kts%d" % pc, [128, 512], F32, 1)
            plbc = pl[:, 0:8].unsqueeze(2).to_broadcast([128, 8, 64])
            k.op("dve", lambda e: e.tensor_tensor(out=rts.rearrange("p (c t) -> p c t", t=64),
                                                  in0=rt.rearrange("p (c t) -> p c t", t=64), in1=plbc, op=ALU.mult),
                 r=[rtb, plb], w=[rtsb])
            k.op("dve", lambda e: e.tensor_tensor(out=kts.rearrange("p (c t) -> p c t", t=64),
                                                  in0=kt.rearrange("p (c t) -> p c t", t=64), in1=plbc, op=ALU.mult),
                 r=[ktb, plb], w=[ktsb])
            PR[pc] = dict(rt=(rt, rtb), kt=(kt, ktb), bt=(bt, btb), kkt=(kkt, kktb), rts=(rts, rtsb), kts=(kts, ktsb),
                          kp=(kp, kpb))
        if stop == 3:
            raise _Stop(nc)
        for pc in range(2):
            for nm_ in ("rt", "kt", "bt", "kkt"):
                src_, srcb_ = PR[pc][nm_]
                sh, shb = k.ring("sh_%s%d" % (nm_, pc), [128, 512], BF16, 1)
                k.op("act", lambda e: e.activation(out=sh, in_=src_, func=AF.Copy), r=[srcb_], w=[shb])
                PR[pc][nm_ + "_h"] = (sh, shb)
        btok, btokb = k.ring("btok", [128, 4, 256], F32, 1)
        ktok, ktokb = k.ring("ktok", [128, 4, 256], F32, 1)
        for (src, dst, dstb) in (("bt", btok, btokb), ("kkt", ktok, ktokb)):
            for pc in range(2):
                s_, sb_ = PR[pc][src]
                ps, pb = PS()
                for blk in range(4):
                    k.op("pe", lambda e: e.transpose(out=ps[:, blk * 128:(blk + 1) * 128], in_=s_[:, blk * 128:(blk + 1) * 128],
                                                     identity=cx.ident), r=[sb_, cx.b_ident], w=[pb], inc=(blk == 3))
                k.op("act", lambda e: e.activation(out=dst[:, :, pc * 128:(pc + 1) * 128],
                                                   in_=ps.rearrange("p (a b) -> p a b", a=4), func=AF.Copy), r=[pb], w=[dstb])

        if stop == 4:
            raise _Stop(nc)
        HD = {}
        for hd in range(4):
            pc, hp = hd // 2, hd % 2
            rows = slice(hp * 64, hp * 64 + 64)
            hcols = slice(hd * 64, hd * 64 + 64)
            pr = PR[pc]

            def intra(lname, rname, mi, nm, depth=2, odt=F32):
                l_, lb_ = pr[lname + "_h"]; r2, rb2 = pr[rname + "_h"]
                ps, pb = PS()
                for blk in range(4):
                    bs = slice(blk * 128, (blk + 1) * 128)
                    mm(k, ps[:, bs], [(l_[rows, bs], r2[rows, bs])], [lb_, rb2], pb)
                o, ob = k.ring("im_" + nm, [128, 4, 128], odt, depth)
                k.op("dve", lambda e: e.tensor_tensor(out=o, in0=ps.rearrange("p (a b) -> p a b", a=4), in1=m4(mi), op=ALU.mult),
                     r=[pb, mskb], w=[ob])
                return o, ob

            Pm, Pmb = intra("kt", "bt", 0, "P", 2, BF16)
            Qm, Qmb = intra("bt", "kt", 1, "Q", 2, BF16)
            AkT, AkTb = intra("kkt", "kt", 1, "AkT%d" % hd, 1)
            QBT, QBTb = intra("bt", "rt", 2, "QBT%d" % hd, 1)
            QKT, QKTb = intra("kkt", "rt", 2, "QKT%d" % hd, 1)
            Rm, Rmb = k.ring("im_R", [128, 4, 128], F32, 2)
            k.op("pool", lambda e: e.tensor_tensor(out=Rm, in0=id4, in1=Qm, op=ALU.subtract), r=[cx.b_ident, Qmb], w=[Rmb])
            Rh, Rhb = k.ring("im_Rh", [128, 4, 128], BF16, 2)
            k.op("act", lambda e: e.activation(out=Rh, in_=Rm, func=AF.Copy), r=[Rmb], w=[Rhb])
            for lev in range(1, 6):
                if lev < 5:
                    ps, pb = PS()
                    for blk in range(4):
                        bs = slice(blk * 128, (blk + 1) * 128)
                        mm(k, ps[:, bs], [(Pm[:, blk, :], Qm[:, blk, :])], [Pmb, Qmb], pb)
                    Qn, Qnb = k.ring("im_Q", [128, 4, 128], BF16, 2)
                    k.op("act", lambda e: e.activation(out=Qn, in_=ps.rearrange("p (a b) -> p a b", a=4), func=AF.Copy), r=[pb], w=[Qnb])
                ps, pb = PS()
                for blk in range(4):
                    bs = slice(blk * 128, (blk + 1) * 128)
                    mm(k, ps[:, bs], [(Qm[:, blk, :], Pm[:, blk, :])], [Pmb, Qmb], pb)
                Pn, Pnb = k.ring("im_P", [128, 4, 128], BF16, 2)
                k.op("act", lambda e: e.activation(out=Pn, in_=ps.rearrange("p (a b) -> p a b", a=4), func=AF.Copy), r=[pb], w=[Pnb])
                ps, pb = PS()
                for blk in range(4):
                    bs = slice(blk * 128, (blk + 1) * 128)
                    mm(k, ps[:, bs], [(Pn[:, blk, :], Rh[:, blk, :])], [Pnb, Rhb], pb)
                Rn, Rnb = k.ring("im_R", [128, 4, 128], F32, 2) if lev < 5 else k.ring("im_Rfin%d" % hd, [128, 4, 128], F32, 1)
                k.op("dve", lambda e: e.tensor_tensor(out=Rn, in0=ps.rearrange("p (a b) -> p a b", a=4), in1=Rm, op=ALU.add),
                     r=[pb, Rmb], w=[Rnb])
                Pm, Pmb = Pn, Pnb
                if lev < 5:
                    Qm, Qmb = Qn, Qnb
                    Rh, Rhb = k.ring("im_Rh", [128, 4, 128], BF16, 2)
                    k.op("act", lambda e: e.activation(out=Rh, in_=Rn, func=AF.Copy), r=[Rnb], w=[Rhb])
                Rm, Rmb = Rn, Rnb
            if stop == 5:
                raise _Stop(nc)
            HD[hd] = (AkT, AkTb, QBT, QBTb, QKT, QKTb, Rm, Rmb)
        XA = {}
        for hd in range(4):
            hcols = slice(hd * 64, hd * 64 + 64)
            AkT, AkTb = HD[hd][0], HD[hd][1]
            ps, pb = PS()
            for hf in range(2):
                trows = slice(hf * 64, hf * 64 + 64)
                for blk in range(4):
                    mm(k, ps[trows, blk * 64:(blk + 1) * 64], [(AkT[trows, blk, trows], vtok[trows, blk, hcols])], [AkTb, vtokb], pb)
            xa, xab = k.ring("XaAll%d" % hd, [128, 4, 64], F32, 1)
            k.op("act", lambda e: e.activation(out=xa, in_=ps[:, 0:256].rearrange("p (a b) -> p a b", a=4), func=AF.Copy, scale=-1.0),
                 r=[pb], w=[xab])
            XA[hd] = (xa, xab)
        for c in range(8):
            blk, hf = c // 2, c % 2
            trows = slice(hf * 64, hf * 64 + 64)
            diag = slice(hf * 64, hf * 64 + 64)
            tcol = slice(c * 64, c * 64 + 64)
            st = {}
            for hd in range(4):
                pc, hp = hd // 2, hd % 2
                rows = slice(hp * 64, hp * 64 + 64)
                uo, uob = U[hd][ucur[hd]]
                un, unb = U[hd][1 - ucur[hd]]
                ucur[hd] = 1 - ucur[hd]
                kts, ktsb = PR[pc]["kts"]
                ps1b, pb1b = PS()
                mm(k, ps1b[trows, 0:64], [(kts[rows, tcol], uo[rows, :])], [ktsb, uob], pb1b)
                st[hd] = dict(rows=rows, hcols=slice(hd * 64, hd * 64 + 64), pc=pc, uo=uo, uob=uob, un=un, unb=unb, ps1b=ps1b, pb1b=pb1b)
            for hd in range(4):
                d_ = st[hd]
                xa, xab = XA[hd]
                X, Xb = k.ring("X", [128, 64], F32, 4)
                k.op("dve", lambda e: e.tensor_tensor(out=X[trows, :], in0=xa[trows, blk, :], in1=d_["ps1b"][trows, 0:64], op=ALU.subtract),
                     r=[xab, d_["pb1b"]], w=[Xb])
                d_["X"], d_["Xb"] = X, Xb
            for hd in range(4):
                d_ = st[hd]
                Rm, Rmb = HD[hd][6], HD[hd][7]
                ps2, pb2 = PS()
                mm(k, ps2[trows, 0:64], [(Rm[trows, blk, diag], d_["X"][trows, :])], [Rmb, d_["Xb"]], pb2)
                d_["ps2"], d_["pb2"] = ps2, pb2
            for hd in range(4):
                d_ = st[hd]
                SA, SAb = k.ring("SA", [128, 64], F32, 4)
                eng = "dve" if hd % 2 == 0 else "act"
                if eng == "dve":
                    k.op("dve", lambda e: e.tensor_copy(out=SA[trows, :], in_=d_["ps2"][trows, 0:64]), r=[d_["pb2"]], w=[SAb])
                else:
                    k.op("act", lambda e: e.activation(out=SA[trows, :], in_=d_["ps2"][trows, 0:64], func=AF.Copy), r=[d_["pb2"]], w=[SAb])
                d_["SA"], d_["SAb"] = SA, SAb
            for hd in range(4):
                d_ = st[hd]
                rows, hcols = d_["rows"], d_["hcols"]
                ps3, pb3 = PS()
                mm(k, ps3[rows, 0:64], [(ktok[trows, blk, hcols], vtok[trows, blk, hcols]),
                                        (btok[trows, blk, hcols], d_["SA"][trows, :])], [ktokb, btokb, vtokb, d_["SAb"]], pb3)
                d_["ps3"], d_["pb3"] = ps3, pb3
            for hd in range(4):
                d_ = st[hd]
                pc, rows, hcols = d_["pc"], d_["rows"], d_["hcols"]
                rts, rtsb = PR[pc]["rts"]
                QBT, QBTb, QKT, QKTb = HD[hd][2], HD[hd][3], HD[hd][4], HD[hd][5]
                mm(k, ybank[pc][rows, tcol], [(d_["uo"][rows, :], rts[rows, tcol])], [d_["uob"], rtsb], ybb[pc])
                mm(k, ybank2[pc][rows, tcol], [(d_["SA"][trows, :], QBT[trows, blk, diag]),
                                               (vtok[trows, blk, hcols], QKT[trows, blk, diag])],
                   [d_["SAb"], QBTb, QKTb, vtokb], ybb2[pc])
            for hd in range(4):
                d_ = st[hd]
                pc, rows = d_["pc"], d_["rows"]
                pl, plb = PL[pc]
                k.op("dve", lambda e: e.scalar_tensor_tensor(out=d_["un"][rows, :], in0=d_["uo"][rows, :], scalar=pl[rows, c:c + 1],
                                                             in1=d_["ps3"][rows, 0:64], op0=ALU.mult, op1=ALU.add),
                     r=[d_["uob"], plb, d_["pb3"]], w=[d_["unb"]])
        if stop == 6:
            raise _Stop(nc)
        for pc in range(2):
            r_, rb_ = FM[("r", pc)]; v_, vb_ = FM[("v", pc)]; g_, gb_ = FM[("g", pc)]
            kp, kpb = PR[pc]["kp"]
            ysb, ysbb = tmpt("ysb")
            k.op("act", lambda e: e.activation(out=ysb, in_=ybank[pc], func=AF.Copy), r=[ybb[pc]], w=[ysbb])
            k.op("dve", lambda e: e.tensor_tensor(out=ysb, in0=ysb, in1=ybank2[pc], op=ALU.add), r=[ysbb, ybb2[pc]], w=[ysbb])
            ysq, ysqb = tmpt("ysq")
            k.op("act", lambda e: e.activation(out=ysq, in_=ysb, func=AF.Square), r=[ysbb], w=[ysqb])
            psm, pbm = PS(); mm(k, psm, [(bones, ysb)], [bonesb, ysbb], pbm)
            pse, pbe = PS(); mm(k, pse, [(bones, ysq)], [bonesb, ysqb], pbe)
            mean, meanb = tmpt("mean")
            k.op("act", lambda e: e.activation(out=mean, in_=psm, func=AF.Copy, scale=1.0 / 64), r=[pbm], w=[meanb])
            var, varb = tmpt("var")
            k.op("dve", lambda e: e.tensor_tensor(out=var, in0=mean, in1=mean, op=ALU.mult), r=[meanb], w=[varb])
            k.op("dve", lambda e: e.scalar_tensor_tensor(out=var, in0=pse, scalar=1.0 / 64, in1=var, op0=ALU.mult, op1=ALU.subtract),
                 r=[pbe, varb], w=[varb])
            k.op("act", lambda e: e.activation(out=var, in_=var, func=AF.Sqrt, bias=c_gneps, scale=1.0), r=[varb, cx.b_consts], w=[varb])
            k.op("dve", lambda e: e.reciprocal(out=var, in_=var), r=[varb], w=[varb])
            yn, ynb = tmpt("yn")
            k.op("dve", lambda e: e.tensor_tensor(out=yn, in0=ysb, in1=mean, op=ALU.subtract), r=[ysbb, meanb], w=[ynb])
            k.op("dve", lambda e: e.tensor_tensor(out=yn, in0=yn, in1=var, op=ALU.mult), r=[ynb, varb], w=[ynb])
            k.op("dve", lambda e: e.tensor_scalar(out=yn, in0=yn, scalar1=vec[:, LNG, pc:pc + 1], scalar2=vec[:, LNB, pc:pc + 1],
                                                  op0=ALU.mult, op1=ALU.add), r=[ynb, vecb], w=[ynb])
            rk, rkb = tmpt("rk")
            k.op("pool", lambda e: e.tensor_tensor(out=rk, in0=r_, in1=kp, op=ALU.mult), r=[rb_, kpb], w=[rkb])
            k.op("pool", lambda e: e.tensor_scalar(out=rk, in0=rk, scalar1=vec[:, RK, pc:pc + 1], scalar2=None, op0=ALU.mult),
                 r=[rkb, vecb], w=[rkb])
            psr, pbr = PS(); mm(k, psr, [(bones, rk)], [bonesb, rkb], pbr)
            bon, bonb = tmpt("bon")
            k.op("dve", lambda e: e.tensor_tensor(out=bon, in0=psr, in1=v_, op=ALU.mult), r=[pbr, vb_], w=[bonb])
            k.op("pool", lambda e: e.tensor_tensor(out=yn, in0=yn, in1=bon, op=ALU.add), r=[ynb, bonb], w=[ynb])
            yo, yob = k.ring("yo", [128, 512], BF16, 2)
            k.op("dve", lambda e: e.tensor_tensor(out=yo, in0=yn, in1=g_, op=ALU.mult), r=[ynb, gb_], w=[yob])
            k.dma("sp", ygq[ti // 4][:, pc, (ti % 4) * 512:(ti % 4) * 512 + 512], yo, r=[yob], sb=yob)
            outbufs.append(yob)
    k.wait_all("sp", list({id(b): b for b in outbufs}.values()))
    return nc


def rwkv_consts():
    t = np.arange(128)
    same = (t[:, None] // 64) == (t[None, :] // 64)
    m_sl = (same & (t[None, :] < t[:, None])).astype(np.float32)
    m_su = (same & (t[:, None] < t[None, :])).astype(np.float32)
    m_iu = (same & (t[:, None] <= t[None, :])).astype(np.float32)
    masks = np.ascontiguousarray(np.stack([m_sl, m_su, m_iu], axis=1))
    bones = same.astype(np.float32)
    resetm = np.ones((128, 512), np.float32)
    resetm[:, ::64] = 0.0
    return masks, np.ascontiguousarray(bones), resetm


def l4_inputs(h1T_b, core_hg, P):
    cs = slice(core_hg * 256, (core_hg + 1) * 256)
    masks, bones, resetm = rwkv_consts()
    import ml_dtypes
    col = lambda v: np.ascontiguousarray(v[cs].reshape(2, 128).T)
    vecs = np.stack([col(P["c_w0"]), col(P["c_a0"]), col(P["c_k_k"]), col(P["c_k_a"]), col(P["c_r_k"].reshape(-1)),
                     col(P["c_ln_g"]), col(P["c_ln_b"])], axis=1)
    mu = np.ascontiguousarray(P["c_mu"].reshape(6, 8, 128).transpose(2, 0, 1))
    return {"hall": h1T_b, "hzero": np.zeros((1024, 1), ml_dtypes.bfloat16), "wr": np.ascontiguousarray(P["c_w_r"][:, cs]), "wk": np.ascontiguousarray(P["c_w_k"][:, cs]),
            "wv": np.ascontiguousarray(P["c_w_v"][:, cs]), "w1": P["c_w1"], "a1": P["c_a1"], "g1": P["c_g1"],
            "w2": np.ascontiguousarray(P["c_w2"][:, cs]), "a2": np.ascontiguousarray(P["c_a2"][:, cs]),
            "g2": np.ascontiguousarray(P["c_g2"][:, cs]), "mu": mu, "vecs": np.ascontiguousarray(vecs.astype(np.float32)),
            "masks": masks, "bones": bones, "resetm": resetm}


NEG = -30000.0
NQB = 32


def build_l2a(nqb=NQB, ntile1=16, stop=99, nc=None, cx=None, io=None):
    try:
        return _build_l2a(nqb, ntile1, stop, nc, cx, io)
    except _Stop as e:
        return e.args[0]


def _build_l2a(nqb=NQB, ntile1=16, stop=99, nc=None, cx=None, io=None):
    nc, cx, io = _std(nc, cx, io)
    din = io.inp
    hall = din("hall", [4096, TOK], BF16)
    psel_d = din("psel", [128, 2])
    wq_d = din("wq", [D, 256]); wks_d = din("wks", [D, 64]); wkw_d = din("wkw", [D, 64])
    wkv_d = din("wkvc", [D, 128]); wv2_d = din("wv2", [D, 128]); wg_d = din("wgate", [D, 12])
    tabA_c = din("tabA_c", [64, SEQ]); tabA_s = din("tabA_s", [64, SEQ])
    tabB_c = din("tabB_c", [128, SEQ]); tabB_s = din("tabB_s", [128, SEQ])
    tabQ_c = din("tabQ_c", [64, NQB * 128]); tabQ_s = din("tabQ_s", [64, NQB * 128])
    rot_d = din("rotT", [128, 128])
    ebig_d = din("ebig", [128, SEQ], BF16)
    w1_d = din("w1kv", [128, 32, 64]); w2_d = din("w2kv", [128, 64]); pe_d = din("pekv", [128, 32, 2])
    vcx_d = din("vcx", [128, 4, 129], BF16)
    cm_d = din("cmask", [128, 9, 128], BF16)
    sm_d = din("smask", [128, 6, 128], BF16)
    fv_d = din("fv", [NQB, 128, 2, 128])
    oa = io.out("oa", [NQB * 128, 256], BF16)
    k = cx.k
    PS = lambda: k.ring("psr", [128, 512], F32, 4, psum=True)
    held = {}
    for nm in ("oc0", "oc1", "os", "ow"):
        held[nm] = (k.ps("h_" + nm, [128, 512]), Buf("h_" + nm))
    psel, pselb = k.tile("psel", [128, 2], F32)
    k.dma("sp", psel, psel_d, w=[pselb], sb=pselb)

    def cload(name, dram, shape, dt, q="sp"):
        t, b = k.tile(name, shape, dt)
        k.dma(q, t, dram, w=[b], sb=b)
        return t, b

    rot, rotb = cload("rot", rot_d, [128, 128], F32)
    ebig, ebigb = cload("ebig", ebig_d, [128, SEQ], BF16)
    cm, cmb = cload("cm", cm_d, [128, 9, 128], BF16)
    sm, smb = cload("sm", sm_d, [128, 6, 128], BF16)
    identb, identbb = k.tile("identb", [128, 128], BF16)
    k.op("act", lambda e: e.activation(out=identb, in_=cx.ident, func=AF.Copy), r=[cx.b_ident], w=[identbb])
    ones_f, ones_fb = k.tile("ones_f", [128, 128], F32)
    k.op("pool", lambda e: e.memset(ones_f, 1.0), w=[ones_fb])

    def wcast(name, dram, n, q="pool"):
        t, b = k.tile(name, [128, 8, n], BF16)
        k.dma(q, t, dram.rearrange("(kc p) n -> p kc n", p=128), w=[b], sb=b)
        return t, b

    wq, wqb = wcast("wq", wq_d, 256); wks, wksb = wcast("wks", wks_d, 64); wkw, wkwb = wcast("wkw", wkw_d, 64)
    wkv, wkvb = wcast("wkv", wkv_d, 128); wv2, wv2b = wcast("wv2", wv2_d, 128); wgt, wgtb = wcast("wgt", wg_d, 12)
    w1, w1b = k.tile("w1", [128, 32, 64], BF16); k.dma("pool", w1, w1_d, w=[w1b], sb=w1b)
    w2, w2b = k.tile("w2", [128, 64], BF16); k.dma("pool", w2, w2_d, w=[w2b], sb=w2b)
    pe, peb = k.tile("pe", [128, 32, 2], BF16); k.dma("pool", pe, pe_d, w=[peb], sb=peb)

    ksel, kselb = k.tile("ksel", [128, SEQ], BF16)
    kwin, kwinb = k.tile("kwin", [128, SEQ], BF16)
    kvc, kvcb = k.tile("kvc", [128, SEQ + 32], BF16)
    k.op("pool", lambda e: e.memset(kvc[:, SEQ:SEQ + 32], 0.0), w=[kvcb])
    vsel, vselb = k.tile("vsel", [128, 64, 96], BF16)
    vwin, vwinb = k.tile("vwin", [128, 64, 96], BF16)
    for t_, b_ in ((ksel, kselb), (kwin, kwinb)):
        k.op("pool", lambda e: e.memset(t_[64:128, :], 0.0), w=[b_])
        k.op("pool", lambda e: e.memset(t_[64:65, :], 1.0), w=[b_])
    for t_, b_ in ((vsel, vselb), (vwin, vwinb)):
        k.op("pool", lambda e: e.memset(t_[:, :, 64:65], 1.0), w=[b_])
    kmax2, kmax2b = k.tile("kmax2", [128, 1], F32)
    k.op("pool", lambda e: e.memset(kmax2, 0.0), w=[kmax2b])

    def upd_kmax(src, srcb, rows, n):
        sq, sqb = k.ring("ksq", [128, 512], F32, 2)
        k.op("pool", lambda e: e.tensor_tensor(out=sq[rows, 0:n], in0=src, in1=src, op=ALU.mult), r=[srcb], w=[sqb])
        ps, pb = PS()
        mm(k, ps[:, 0:n], [(ones_f[rows, :], sq[rows, 0:n])], [ones_fb, sqb], pb)
        k.op("act", lambda e: e.activation(out=sq[:, 0:n], in_=ps[:, 0:n], func=AF.Copy), r=[pb], w=[sqb])
        mx, mxb = k.ring("kmx", [128, 8], F32, 2)
        k.op("dve", lambda e: e.max(out=mx, in_=sq[:, 0:n]), r=[sqb], w=[mxb])
        k.op("dve", lambda e: e.tensor_tensor(out=kmax2, in0=kmax2, in1=mx[:, 0:1], op=ALU.max), r=[mxb, kmax2b], w=[kmax2b])

    def rope_store(ps, pb, rows, tc_d, ts_d, t0, n, dst, dstb, rbase=0):
        R = slice(rbase, rbase + rows)
        xk, xkb = k.ring("xk", [128, 512], F32, 2)
        k.op("act", lambda e: e.activation(out=xk[R, 0:n], in_=ps[R, 0:n], func=AF.Copy), r=[pb], w=[xkb])
        tc_, tcb = k.ring("tabc", [128, 512], F32, 2)
        ts_, tsb = k.ring("tabs", [128, 512], F32, 2)
        k.dma("sp", tc_[R, 0:n], tc_d[:, t0:t0 + n], w=[tcb], sb=tcb)
        k.dma("sp", ts_[R, 0:n], ts_d[:, t0:t0 + n], w=[tsb], sb=tsb)
        ps2, pb2 = PS()
        mm(k, ps2[R, 0:n], [(rot[R, R], xk[R, 0:n])], [rotb, xkb], pb2)
        t1, t1b = k.ring("rp1", [128, 512], F32, 2)
        t2, t2b = k.ring("rp2", [128, 512], F32, 2)
        k.op("pool", lambda e: e.tensor_tensor(out=t1[R, 0:n], in0=xk[R, 0:n], in1=tc_[R, 0:n], op=ALU.mult), r=[xkb, tcb], w=[t1b])
        k.op("dve", lambda e: e.tensor_tensor(out=t2[R, 0:n], in0=ps2[R, 0:n], in1=ts_[R, 0:n], op=ALU.mult), r=[pb2, tsb], w=[t2b])
        k.op("dve", lambda e: e.tensor_tensor(out=dst, in0=t1[R, 0:n], in1=t2[R, 0:n], op=ALU.add), r=[t1b, t2b], w=[dstb])

    if stop == 10:
        raise _Stop(nc)
    for ti in range(ntile1):
        t0 = ti * 512
        hb, hbb = k.ring("hb", [128, 8, 512], BF16, 2)
        dma_hall(k, hb, hall, t0, 512, hbb)
        for (w_, wb_, dst, dstb) in ((wks, wksb, ksel, kselb), (wkw, wkwb, kwin, kwinb)):
            ps, pb = PS()
            mm(k, ps[0:64, :], [(w_[:, kc, :], hb[:, kc, :]) for kc in range(8)], [wb_, hbb], pb)
            if stop == 11 + 100 * ti:
                raise _Stop(nc)
            rope_store(ps, pb, 64, tabA_c, tabA_s, t0, 512, dst[0:64, t0:t0 + 512], dstb)
            if stop == 12 + 100 * ti:
                raise _Stop(nc)
            upd_kmax(dst[0:64, t0:t0 + 512], dstb, slice(0, 64), 512)
            if stop == 13 + 100 * ti:
                raise _Stop(nc)
        if stop == 14 + 100 * ti:
            raise _Stop(nc)
        ps, pb = PS()
        mm(k, ps, [(wkv[:, kc, :], hb[:, kc, :]) for kc in range(8)], [wkvb, hbb], pb)
        rope_store(ps, pb, 128, tabB_c, tabB_s, t0, 512, kvc[:, t0:t0 + 512], kvcb)
        if stop == 15 + 100 * ti:
            raise _Stop(nc)
        ps, pb = PS()
        for blk in range(4):
            mm(k, ps[:, blk * 128:(blk + 1) * 128], [(hb[:, kc, blk * 128:(blk + 1) * 128], wv2[:, kc, :]) for kc in range(8)],
               [wv2b, hbb], pb)
        pv = ps.rearrange("p (a b) -> p a b", a=4)
        k.op("act", lambda e: e.activation(out=vsel[:, ti * 4:ti * 4 + 4, 0:64], in_=pv[:, :, 0:64], func=AF.Copy), r=[pb], w=[vselb])
        k.op("act", lambda e: e.activation(out=vwin[:, ti * 4:ti * 4 + 4, 0:64], in_=pv[:, :, 64:128], func=AF.Copy), r=[pb], w=[vwinb])

    if stop == 1:
        raise _Stop(nc)
    kc, kcb = k.tile("kc", [128, 512], BF16)
    k.op("pool", lambda e: e.memset(kc, 0.0), w=[kcb])
    k.op("pool", lambda e: e.memset(kc[64:65, :], 1.0), w=[kcb])
    vcx, vcxb = k.tile("vcx", [128, 4, 256], BF16)
    k.dma("sp", vcx[:, :, 64:193], vcx_d, w=[vcxb], sb=vcxb)
    kv16 = kvc.rearrange("p (n s) -> p n s", s=16)
    hid, hidb = k.tile("hid", [128, 512], BF16)
    k.op("pool", lambda e: e.memset(hid, 0.0), w=[hidb])
    for R in (slice(0, 64), slice(64, 128)):
        psb, pbb = PS()
        mm(k, psb[R, 0:2], [(w1[R, l, :], pe[R, l, :]) for l in range(32)], [w1b, peb], pbb)
        bia, biab = k.ring("cbias", [128, 1], F32, 2)
        k.op("act", lambda e: e.activation(out=bia[R, :], in_=psb[R, 0:1], func=AF.Copy), r=[pbb], w=[biab])
        ps, pb = PS()
        mm(k, ps[R, 0:512], [(w1[R, l, :], kv16[R, (l // 16):(l // 16) + 512, l % 16]) for l in range(32)], [w1b, kvcb], pb)
        k.op("act", lambda e: e.activation(out=hid[R, :], in_=ps[R, :], func=AF.Silu, bias=bia[R, :]), r=[pb, biab], w=[hidb])
    ps, pb = PS()
    mm(k, ps[0:64, :], [(w2[0:64, :], hid[0:64, :])], [w2b, hidb], pb)
    k.op("act", lambda e: e.activation(out=kc[0:64, :], in_=ps[0:64, :], func=AF.Copy), r=[pb], w=[kcb])
    upd_kmax(kc[0:64, 0:512], kcb, slice(0, 64), 512)
    ps, pb = PS()
    for ch in range(4):
        mm(k, ps[:, ch * 64:(ch + 1) * 64], [(hid[64:128, ch * 128:(ch + 1) * 128], w2[64:128, :])], [w2b, hidb], pb)
    k.op("act", lambda e: e.activation(out=vcx[:, :, 0:64], in_=ps[:, 0:256].rearrange("p (a b) -> p a b", a=4), func=AF.Copy),
         r=[pb], w=[vcxb])
    nkm, nkmb = k.tile("nkm", [128, 1], F32)
    k.op("act", lambda e: e.activation(out=nkm, in_=kmax2, func=AF.Sqrt), r=[kmax2b], w=[nkmb])
    k.op("dve", lambda e: e.tensor_scalar(out=nkm, in0=nkm, scalar1=-1.0, scalar2=None, op0=ALU.mult), r=[nkmb], w=[nkmb])

    if stop == 2:
        raise _Stop(nc)
    outb = []

    def prep(i):
        q0 = i * 128
        hqc, hqcb = k.ring("hqc", [128, 2, 8, 128], BF16, 2)
        for pp in range(2):
            dma_hall(k, hqc[:, pp], hall, (2 * i + pp) * 128, 128, hqcb)
        hqt, hqtb = k.ring("hqt", [128, 8, 128], BF16, 2)
        hqb, hqbb = k.ring("hqb", [128, 8, 128], BF16, 2)
        k.op("dve", lambda e: e.tensor_scalar(out=hqt, in0=hqc[:, 0], scalar1=psel[:, 0:1], scalar2=None, op0=ALU.mult),
             r=[hqcb, pselb], w=[hqtb])
        k.op("dve", lambda e: e.scalar_tensor_tensor(out=hqb, in0=hqc[:, 1], scalar=psel[:, 1:2], in1=hqt, op0=ALU.mult, op1=ALU.add),
             r=[hqcb, pselb, hqtb], w=[hqbb])
        qa, qab = k.ring("qa", [128, 512], BF16, 2)
        k.op("pool", lambda e: e.memset(qa[64:128, :], 0.0), w=[qab])
        qf, qfb = k.ring("qf", [128, 512], F32, 2)
        tcq, tcqb = k.ring("tcq", [128, 128], F32, 2); tsq, tsqb = k.ring("tsq", [128, 128], F32, 2)
        k.dma("sp", tcq[0:64, :], tabQ_c[:, q0:q0 + 128], w=[tcqb], sb=tcqb)
        k.dma("sp", tsq[0:64, :], tabQ_s[:, q0:q0 + 128], w=[tsqb], sb=tsqb)
        ps, pb = PS()
        for g in range(4):
            mm(k, ps[0:64, g * 128:(g + 1) * 128], [(wq[:, kc, g * 64:(g + 1) * 64], hqb[:, kc, :]) for kc in range(8)], [wqb, hqbb], pb)
        xq, xqb = k.ring("xq", [128, 512], F32, 2)
        k.op("act", lambda e: e.activation(out=xq[0:64, :], in_=ps[0:64, :], func=AF.Copy), r=[pb], w=[xqb])
        ps2, pb2 = PS()
        mm(k, ps2[0:64, :], [(rot[0:64, 0:64], xq[0:64, :])], [rotb, xqb], pb2)
        v3 = lambda a: a.rearrange("p (g t) -> p g t", g=4)
        bc4 = lambda a: a.unsqueeze(1).to_broadcast([64, 4, 128])
        k.op("pool", lambda e: e.tensor_tensor(out=v3(xq[0:64, :]), in0=v3(xq[0:64, :]), in1=bc4(tcq[0:64, :]), op=ALU.mult),
             r=[xqb, tcqb], w=[xqb])
        k.op("dve", lambda e: e.tensor_tensor(out=v3(qf[0:64, :]), in0=v3(ps2[0:64, :]), in1=bc4(tsq[0:64, :]), op=ALU.mult),
             r=[pb2, tsqb], w=[qfb])
        k.op("dve", lambda e: e.tensor_tensor(out=qf[0:64, :], in0=qf[0:64, :], in1=xq[0:64, :], op=ALU.add), r=[qfb, xqb], w=[qfb])
        k.op("act", lambda e: e.activation(out=qa[0:64, :], in_=qf[0:64, :], func=AF.Copy), r=[qfb], w=[qab])
        k.op("pool", lambda e: e.tensor_tensor(out=xq[0:64, :], in0=qf[0:64, :], in1=qf[0:64, :], op=ALU.mult), r=[qfb, xqb], w=[xqb])
        ps3, pb3 = PS()
        mm(k, ps3[64:65, :], [(ones_f[0:64, 0:1], xq[0:64, :])], [ones_fb, xqb], pb3)
        mrow, mrowb = k.ring("mrow", [128, 512], F32, 2)
        k.op("act", lambda e: e.activation(out=mrow[64:65, :], in_=ps3[64:65, :], func=AF.Sqrt), r=[pb3], w=[mrowb])
        k.op("dve", lambda e: e.tensor_scalar(out=qa[64:65, :], in0=mrow[64:65, :], scalar1=nkm[64:65, 0:1], scalar2=None, op0=ALU.mult),
             r=[mrowb, nkmb], w=[qab])
        psg, pbg = PS()
        mm(k, psg[:, 0:12], [(hqb[:, kc, :], wgt[:, kc, :]) for kc in range(8)], [wgtb, hqbb], pbg)
        gt, gtb = k.ring("gates", [128, 12], F32, 2)
        k.op("act", lambda e: e.activation(out=gt, in_=psg[:, 0:12], func=AF.Sigmoid), r=[pbg], w=[gtb])

        return qa, qab, gt, gtb

    nxt = prep(0)
    for i in range(nqb):
        q0 = i * 128
        qa, qab, gt, gtb = nxt
        if stop == 3:
            raise _Stop(nc)

        def score_tile(kaug, kaugb, ktile_cols, masks):
            ps, pb = PS()
            n = 1 + len(masks)
            k.op("pe", lambda e: e.matmul(ps, lhsT=kaug[:, ktile_cols], rhs=qa, start=True, stop=(n == 1)),
                 r=[kaugb, qab], w=[pb], inc=(n == 1))
            for mi, (ml, mlb, mr, mrb) in enumerate(masks):
                last = (mi == len(masks) - 1)
                k.op("pe", lambda e: e.matmul(ps.rearrange("p (g t) -> p g t", g=4), lhsT=ml,
                                              rhs=mr.unsqueeze(1).to_broadcast([128, 4, 128]), start=False, stop=last),
                     r=[mlb, mrb], w=[pb], inc=last)
            eT, eTb = k.ring("eT", [128, 512], BF16, 3)
            k.op("act", lambda e: e.activation(out=eT, in_=ps, func=AF.Exp), r=[pb], w=[eTb])
            return eT, eTb

        qi_e = 2 * i
        last = (8 * qi_e + 6) // 128
        oc = [held["oc0"], held["oc1"]]
        def cmp_pv(cc, eT, eTb):
            for g in range(4):
                o_, ob_ = oc[g // 2]
                k.op("pe", lambda e: e.matmul(o_[:, (g % 2) * 256:(g % 2) * 256 + 193], lhsT=eT[:, g * 128:(g + 1) * 128],
                                              rhs=vcx[:, cc, 0:193], start=(cc == 0 and g % 2 == 0), stop=(cc == last and g % 2 == 1),
                                              skip_group_check=True),
                     r=[eTb, vcxb], w=[ob_], inc=(g == 3))
        pend = None
        for cc in range(last + 1):
            masks = []
            if cc == last:
                masks.append((identb, identbb, cm[:, i % 8, :], cmb))
            elif cc == last - 1 and i % 8 == 0:
                masks.append((identb, identbb, cm[:, 8, :], cmb))
            eT, eTb = score_tile(kc, kcb, slice(cc * 128, (cc + 1) * 128), masks)
            if pend is not None:
                cmp_pv(*pend)
            pend = (cc, eT, eTb)
        cmp_pv(*pend)
        if stop == 6:
            raise _Stop(nc)
        ow_, owb_ = held["ow"]
        wt = [kt for kt in range(2 * i - 4, 2 * i + 2) if kt >= 0]
        def win_pv(kt, eT, eTb):
            for g in range(4):
                k.op("pe", lambda e: e.matmul(ow_[:, g * 65:(g + 1) * 65], lhsT=eT[:, g * 128:(g + 1) * 128], rhs=vwin[:, kt, 0:65],
                                              start=(kt == wt[0] and g == 0), stop=(kt == wt[-1] and g == 3), skip_group_check=True),
                     r=[eTb, vwinb], w=[owb_], inc=(g == 3))
        pend = None
        for kt in wt:
            cols = slice(kt * 128, (kt + 1) * 128)
            pos = kt - (2 * i - 4)
            masks = []
            mi = {0: 2, 1: 3, 4: 4, 5: 5}.get(pos)
            if mi is not None:
                masks.append((identb, identbb, sm[:, mi, :], smb))
            eT, eTb = score_tile(kwin, kwinb, cols, masks)
            if pend is not None:
                win_pv(*pend)
            pend = (kt, eT, eTb)
        win_pv(*pend)
        if stop == 4:
            raise _Stop(nc)
        zc, zcb = k.ring("zc", [128, 4], F32, 2)
        for g in range(4):
            o_, ob_ = oc[g // 2]
            k.op("dve", lambda e: e.tensor_scalar(out=zc[:, g:g + 1], in0=o_[:, (g % 2) * 256 + 64:(g % 2) * 256 + 65], scalar1=1e-30,
                                                  scalar2=None, op0=ALU.max), r=[ob_], w=[zcb])
        k.op("dve", lambda e: e.reciprocal(out=zc, in_=zc), r=[zcb], w=[zcb])
        imp, impb = k.ring("imp", [128, 128], F32, 2)
        for g in range(4):
            o_, ob_ = oc[g // 2]
            src = o_[:, (g % 2) * 256 + 65:(g % 2) * 256 + 193]
            if g == 0:
                k.op("dve", lambda e: e.tensor_scalar(out=imp, in0=src, scalar1=zc[:, 0:1], scalar2=None, op0=ALU.mult),
                     r=[ob_, zcb], w=[impb])
            else:
                k.op("dve", lambda e: e.scalar_tensor_tensor(out=imp, in0=src, scalar=zc[:, g:g + 1], in1=imp, op0=ALU.mult, op1=ALU.add),
                     r=[ob_, zcb, impb], w=[impb])
        fv, fvb = k.ring("fv", [128, 2, 128], F32, 2)
        k.dma("sp", fv, fv_d[i], w=[fvb], sb=fvb)
        k.op("dve", lambda e: e.tensor_tensor(out=imp, in0=imp, in1=fv[:, 0, :], op=ALU.mult), r=[impb, fvb], w=[impb])
        k.op("dve", lambda e: e.tensor_tensor(out=imp, in0=imp, in1=fv[:, 1, :], op=ALU.add), r=[impb, fvb], w=[impb])
        m8, m8b = k.ring("m8", [128, 16], F32, 2)
        wk_, wkb_ = k.ring("impw", [128, 128], F32, 2)
        k.op("dve", lambda e: e.max(out=m8[:, 0:8], in_=imp), r=[impb], w=[m8b])
        k.op("dve", lambda e: e.match_replace(out=wk_, in_to_replace=m8[:, 0:8], in_values=imp, imm_value=-3e38), r=[m8b, impb], w=[wkb_])
        k.op("dve", lambda e: e.max(out=m8[:, 8:16], in_=wk_), r=[wkb_], w=[m8b])
        k.op("dve", lambda e: e.tensor_scalar(out=wk_, in0=imp, scalar1=m8[:, 15:16], scalar2=None, op0=ALU.is_ge), r=[impb, m8b], w=[wkb_])
        k.op("dve", lambda e: e.tensor_scalar(out=wk_, in0=wk_, scalar1=-1.0, scalar2=-NEG, op0=ALU.add, op1=ALU.mult), r=[wkb_], w=[wkb_])
        pst, pbt = PS()
        k.op("pe", lambda e: e.transpose(out=pst[:, 0:128], in_=wk_, identity=cx.ident), r=[wkb_, cx.b_ident], w=[pbt])
        nbT, nbTb = k.ring("nbT", [128, 128], BF16, 2)
        k.op("act", lambda e: e.activation(out=nbT, in_=pst[:, 0:128], func=AF.Copy), r=[pbt], w=[nbTb])

        if stop == 5:
            raise _Stop(nc)
        os_, osb_ = held["os"]
        nkt = 2 * i + 2
        def sel_pv(kt, eT, eTb):
            for g in range(4):
                k.op("pe", lambda e: e.matmul(os_[:, g * 65:(g + 1) * 65], lhsT=eT[:, g * 128:(g + 1) * 128], rhs=vsel[:, kt, 0:65],
                                              start=(kt == 0 and g == 0), stop=(kt == nkt - 1 and g == 3), skip_group_check=True),
                     r=[eTb, vselb], w=[osb_], inc=(g == 3))
        pend = None
        for kt in range(nkt):
            cols = slice(kt * 128, (kt + 1) * 128)
            masks = [(ebig[:, cols], ebigb, nbT, nbTb)]
            if kt == 2 * i:
                masks.append((identb, identbb, sm[:, 0, :], smb))
            elif kt == 2 * i + 1:
                masks.append((identb, identbb, sm[:, 1, :], smb))
            eT, eTb = score_tile(ksel, kselb, cols, masks)
            if pend is not None:
                sel_pv(*pend)
            pend = (kt, eT, eTb)
        sel_pv(*pend)
        if i + 1 < nqb:
            nxt = prep(i + 1)
        sc, scb = k.ring("sc", [128, 12], F32, 2)
        k.op("pool", lambda e: e.memset(sc, 1.0), w=[scb])
        for g in range(4):
            k.op("dve", lambda e: e.tensor_scalar(out=sc[:, g * 3 + 1:g * 3 + 2], in0=os_[:, g * 65 + 64:g * 65 + 65], scalar1=1e-30,
                                                  scalar2=None, op0=ALU.max), r=[osb_], w=[scb])
            k.op("dve", lambda e: e.tensor_scalar(out=sc[:, g * 3 + 2:g * 3 + 3], in0=ow_[:, g * 65 + 64:g * 65 + 65], scalar1=1e-30,
                                                  scalar2=None, op0=ALU.max), r=[owb_], w=[scb])
        k.op("dve", lambda e: e.reciprocal(out=sc, in_=sc), r=[scb], w=[scb])
        for g in range(4):
            k.op("dve", lambda e: e.tensor_copy(out=sc[:, g * 3:g * 3 + 1], in_=zc[:, g:g + 1]), r=[zcb], w=[scb])
        k.op("dve", lambda e: e.tensor_tensor(out=sc, in0=sc, in1=gt, op=ALU.mult), r=[scb, gtb], w=[scb])
        ot, otb = k.ring("oat", [128, 256], F32, 2)
        for g in range(4):
            o_, ob_ = oc[g // 2]
            dst = ot[:, g * 64:(g + 1) * 64]
            k.op("dve", lambda e: e.tensor_scalar(out=dst, in0=o_[:, (g % 2) * 256:(g % 2) * 256 + 64], scalar1=sc[:, g * 3:g * 3 + 1],
                                                  scalar2=None, op0=ALU.mult), r=[ob_, scb], w=[otb])
            k.op("dve", lambda e: e.scalar_tensor_tensor(out=dst, in0=os_[:, g * 65:g * 65 + 64], scalar=sc[:, g * 3 + 1:g * 3 + 2], in1=dst,
                                                         op0=ALU.mult, op1=ALU.add), r=[osb_, scb, otb], w=[otb])
            k.op("dve", lambda e: e.scalar_tensor_tensor(out=dst, in0=ow_[:, g * 65:g * 65 + 64], scalar=sc[:, g * 3 + 2:g * 3 + 3], in1=dst,
                                                         op0=ALU.mult, op1=ALU.add), r=[owb_, scb, otb], w=[otb])
        otc, otcb = k.ring("oatc", [128, 256], BF16, 2)
        k.op("act", lambda e: e.activation(out=otc, in_=ot, func=AF.Copy), r=[otb], w=[otcb])
        k.dma("sp", oa[q0:q0 + 128, :], otc, r=[otcb], sb=otcb)
        outb.append(otcb)
    k.wait_all("sp", list({id(b): b for b in outb}.values()))
    return nc


def _bf16(a):
    import ml_dtypes
    return np.ascontiguousarray(a).astype(ml_dtypes.bfloat16)


def nsa_consts():
    inv = (10000.0 ** (-np.arange(0, 64, 2, dtype=np.float32) / 64)).astype(np.float32)
    ang = np.arange(SEQ, dtype=np.float32)[:, None] * inv[None, :]
    cos, sin = np.cos(ang).astype(np.float32), np.sin(ang).astype(np.float32)
    tA_c = np.ascontiguousarray(np.concatenate([cos, cos], 1).T)
    tA_s = np.ascontiguousarray(np.concatenate([sin, sin], 1).T)
    tB_c = np.ascontiguousarray(np.concatenate([tA_c, np.ones_like(tA_c)], 0))
    tB_s = np.ascontiguousarray(np.concatenate([tA_s, np.zeros_like(tA_s)], 0))
    rot = np.zeros((128, 128), np.float32)
    for m in range(128):
        mm_ = m % 64
        if mm_ < 32:
            rot[m + 32, m] = -1.0
        else:
            rot[m - 32, m] = 1.0
    x = np.arange(SEQ)
    ebig = (np.arange(128)[:, None] == (x[None, :] // 64)).astype(np.float32)
    c = np.arange(512)
    j = np.arange(128)
    ov = ((16 * c[:, None] < 64 * j[None, :] + 64) & (16 * c[:, None] + 31 >= 64 * j[None, :])).astype(np.float32)
    vcx = np.zeros((512, 129), np.float32)
    vcx[:, 0] = 1.0
    vcx[:, 1:] = ov
    vcx[511, :] = 0.0
    vcx = np.ascontiguousarray(vcx.reshape(4, 128, 129).transpose(1, 0, 2))
    return dict(tA_c=tA_c, tA_s=tA_s, tB_c=tB_c, tB_s=tB_s, rot=rot, ebig=_bf16(ebig), vcx=_bf16(vcx))


def nsa_core_consts(par):
    p = np.arange(128)[:, None]
    tl = np.arange(128)[None, :]
    cm = np.zeros((128, 9, 128), np.float32)
    for s in range(8):
        off = 8 * ((2 * s + par) % 16)
        cm[:, s, :] = np.where(16 * (p - off) + 31 <= tl, 0.0, NEG)
    if par == 0:
        cm[:, 8, :] = np.where((p == 127) & (tl < 15), NEG, 0.0)
    causal = np.where(p <= tl, 0.0, NEG).astype(np.float32)
    winT = np.where(p > tl, 0.0, NEG).astype(np.float32)
    ALL = np.full((128, 128), NEG, np.float32)
    Z = np.zeros((128, 128), np.float32)
    sm = [causal, ALL, winT, Z, causal, ALL] if par == 0 else [Z, causal, ALL, winT, Z, causal]
    sm = np.stack(sm, axis=1)
    fv = np.zeros((NQB, 128, 2, 128), np.float32)
    jj = np.arange(128)[None, :]
    for i in range(NQB):
        qi = 2 * i + par
        t = 128 * qi + np.arange(128)[:, None]
        cur = t // 64
        valid = jj <= cur
        f0 = (jj == 0)
        f1 = (jj == cur)
        f2 = (jj == cur - 1)
        forced = f0 | f1 | f2
        V = (valid & ~forced).astype(np.float32)
        F = np.where(valid, 0.0, -1e30).astype(np.float32)
        F = np.where(f2 & valid, 1e4 + 2.0, F)
        F = np.where(f0, 1e4 + 1.0, F)
        F = np.where(f1, 1e4, F)
        fv[i, :, 0, :] = V
        fv[i, :, 1, :] = F
    return dict(cm=_bf16(cm), sm=_bf16(sm), fv=fv)


def l2a_inputs(h0T_b, kvh, par, P, C):
    w = P["ab_w_in"]
    cat = lambda *a: np.ascontiguousarray(np.concatenate(a, axis=1))
    sl = lambda o, n: w[:, o + kvh * n: o + (kvh + 1) * n]
    qtok = np.concatenate([np.arange((2 * i + par) * 128, (2 * i + par + 1) * 128) for i in range(NQB)])
    cc = nsa_core_consts(par)
    w1 = np.concatenate([P["a_cmp_w1_k"].reshape(32, 64, 64).transpose(1, 0, 2),
                         P["a_cmp_w1_v"].reshape(32, 64, 64).transpose(1, 0, 2)], 0)
    w2 = np.concatenate([P["a_cmp_w2_k"], P["a_cmp_w2_v"]], 0)
    pe = np.concatenate([P["a_cmp_pe_k"].T, P["a_cmp_pe_v"].T], 0)
    pselv = np.zeros((128, 2), np.float32); pselv[:, par] = 1.0
    return {"hall": h0T_b, "psel": pselv,
            "wq": np.ascontiguousarray(sl(0, 256)), "wks": np.ascontiguousarray(sl(768, 64)), "wkw": np.ascontiguousarray(sl(1024, 64)),
            "wkvc": cat(sl(512, 64), sl(640, 64)), "wv2": cat(sl(896, 64), sl(1152, 64)), "wgate": np.ascontiguousarray(sl(1280, 12)),
            "tabA_c": C["tA_c"], "tabA_s": C["tA_s"], "tabB_c": C["tB_c"], "tabB_s": C["tB_s"],
            "tabQ_c": np.ascontiguousarray(C["tA_c"][:, qtok] * np.float32(0.125)),
            "tabQ_s": np.ascontiguousarray(C["tA_s"][:, qtok] * np.float32(0.125)),
            "rotT": C["rot"], "ebig": C["ebig"], "w1kv": np.ascontiguousarray(w1), "w2kv": np.ascontiguousarray(w2),
            "pekv": np.ascontiguousarray(np.stack([pe, pe], axis=2)), "vcx": C["vcx"], "cmask": cc["cm"], "smask": cc["sm"], "fv": cc["fv"]}


def build_l2b(ntiles=16, nc=None, cx=None, io=None):
    nc, cx, io = _std(nc, cx, io)
    din = io.inp
    hall = din("hall", [4096, TOK], BF16)
    wx_d = din("wx", [D, 256]); wB_d = din("wB", [D, 128]); wC_d = din("wC", [D, 128]); wdt_d = din("wdt", [D, 4])
    cw_d = din("convw", [128, 4, 4]); cb_d = din("convb", [128, 4])
    dtb_d = din("dtb", [128, 16]); alog_d = din("alog", [128, 16]); dsk_d = din("dskip", [128, 4])
    tri_d = din("tri", [128, 128]); su_d = din("su", [128, 128])
    yout = io.out("y", [SEQ, 256], BF16)
    k = cx.k
    PS = lambda: k.ring("psr", [128, 512], F32, 8, psum=True)

    def cload(name, dram, shape, dt, q="sp"):
        t, b = k.tile(name, shape, dt)
        k.dma(q, t, dram, w=[b], sb=b)
        return t, b

    def wcast(name, dram, n):
        t, b = k.tile(name, [128, 8, n], BF16)
        k.dma("pool", t, dram.rearrange("(kc p) n -> p kc n", p=128), w=[b], sb=b)
        return t, b

    wx, wxb = wcast("wx", wx_d, 256); wB, wBb = wcast("wB", wB_d, 128); wC, wCb = wcast("wC", wC_d, 128)
    wdt, wdtb = wcast("wdt", wdt_d, 4)
    cw, cwb = cload("cw", cw_d, [128, 4, 4], F32); cb, cbb = cload("cb", cb_d, [128, 4], F32)
    dtb, dtbb = cload("dtb", dtb_d, [128, 16], F32); alog, alogb = cload("alog", alog_d, [128, 16], F32)
    dsk, dskb = cload("dsk", dsk_d, [128, 4], F32)
    tri, trib = cload("tri", tri_d, [128, 128], F32); su, sub_ = cload("su", su_d, [128, 128], F32)
    ones_f, ones_fb = k.tile("ones_f", [128, 128], F32)
    k.op("pool", lambda e: e.memset(ones_f, 1.0), w=[ones_fb])
    c_one = cx.const(1.0)
    arep, arepb = k.tile("arep", [128, 16], F32)
    k.op("act", lambda e: e.activation(out=arep, in_=alog, func=AF.Exp), r=[alogb], w=[arepb])
    k.op("dve", lambda e: e.tensor_scalar(out=arep, in0=arep, scalar1=-1.0, scalar2=None, op0=ALU.mult), r=[arepb], w=[arepb])
    S, Sb = k.tile("S", [128, 256], F32)
    k.op("pool", lambda e: e.memset(S, 0.0), w=[Sb])
    xbc = []
    for m in range(4):
        t, b = k.tile("xbc%d" % m, [128, 516], F32)
        k.op("pool", lambda e: e.memset(t, 0.0), w=[b])
        xbc.append((t, b))
    outb = []
    wsel = [(wx, wxb, slice(0, 128)), (wx, wxb, slice(128, 256)), (wB, wBb, slice(0, 128)), (wC, wCb, slice(0, 128))]
    bc64 = lambda a, n: a.unsqueeze(2).to_broadcast([128, n, 64])
    def prologue(ti):
        t0 = ti * 512
        hb, hbb = k.ring("hb", [128, 8, 512], BF16, 2)
        dma_hall(k, hb, hall, t0, 512, hbb)
        xc = []
        for m in range(4):
            w_, wb_, cols = wsel[m]
            ps, pb = PS()
            mm(k, ps, [(w_[:, kc, cols], hb[:, kc, :]) for kc in range(8)], [wb_, hbb], pb)
            xt, xtb = xbc[m]
            k.op("act", lambda e: e.activation(out=xt[:, 3:515], in_=ps, func=AF.Copy), r=[pb], w=[xtb])
            acc, accb = k.ring("cacc", [128, 512], F32, 2)
            k.op("dve", lambda e: e.tensor_scalar(out=acc, in0=xt[:, 0:512], scalar1=cw[:, m, 0:1], scalar2=cb[:, m:m + 1],
                                                  op0=ALU.mult, op1=ALU.add), r=[xtb, cwb, cbb], w=[accb])
            for j in range(1, 4):
                k.op("dve", lambda e: e.scalar_tensor_tensor(out=acc, in0=xt[:, j:j + 512], scalar=cw[:, m, j:j + 1], in1=acc,
                                                             op0=ALU.mult, op1=ALU.add), r=[xtb, cwb, accb], w=[accb])
            o, ob = k.ring("xc%d" % m, [128, 512], F32, 2)
            k.op("act", lambda e: e.activation(out=o, in_=acc, func=AF.Silu), r=[accb], w=[ob])
            k.op("pool", lambda e: e.tensor_copy(out=xt[:, 0:3], in_=xt[:, 512:515]), r=[xtb], w=[xtb])
            xc.append((o, ob))
        psd, pbd = PS()
        for c in range(4):
            mm(k, psd[:, c * 4:(c + 1) * 4], [(hb[:, kc, c * 128:(c + 1) * 128], wdt[:, kc, :]) for kc in range(8)], [wdtb, hbb], pbd)
        dt, dtb_ = k.ring("dt", [128, 16], F32, 2)
        k.op("dve", lambda e: e.tensor_tensor(out=dt, in0=psd[:, 0:16], in1=dtb, op=ALU.add), r=[pbd, dtbb], w=[dtb_])
        k.op("act", lambda e: e.activation(out=dt, in_=dt, func=AF.Exp), r=[dtb_], w=[dtb_])
        k.op("act", lambda e: e.activation(out=dt, in_=dt, func=AF.Ln, bias=c_one, scale=1.0), r=[dtb_, cx.b_consts], w=[dtb_])
        da, dab = k.ring("da", [128, 16], F32, 2)
        k.op("dve", lambda e: e.tensor_tensor(out=da, in0=dt, in1=arep, op=ALU.mult), r=[dtb_, arepb], w=[dab])
        psa, pba = PS(); mm(k, psa[:, 0:16], [(tri, da)], [trib, dab], pba)
        pst_, pbt_ = PS(); mm(k, pst_[:, 0:16], [(ones_f, da)], [ones_fb, dab], pbt_)
        acs, acsb = k.ring("acs", [128, 16], F32, 2)
        k.op("act", lambda e: e.activation(out=acs, in_=psa[:, 0:16], func=AF.Copy), r=[pba], w=[acsb])
        eacs, eacsb = k.ring("eacs", [128, 16], F32, 2)
        k.op("act", lambda e: e.activation(out=eacs, in_=acs, func=AF.Exp), r=[acsb], w=[eacsb])
        decs, decsb = k.ring("decs", [128, 16], F32, 2)
        k.op("dve", lambda e: e.tensor_tensor(out=decs, in0=pst_[:, 0:16], in1=acs, op=ALU.subtract), r=[pbt_, acsb], w=[decsb])
        k.op("act", lambda e: e.activation(out=decs, in_=decs, func=AF.Exp), r=[decsb], w=[decsb])
        cd, cdb = k.ring("cd", [128, 16], F32, 2)
        k.op("act", lambda e: e.activation(out=cd, in_=pst_[:, 0:16], func=AF.Exp), r=[pbt_], w=[cdb])
        xtok, xtokb = k.ring("xtok", [128, 4, 256], F32, 2)
        btok, btokb = k.ring("btok", [128, 4, 128], F32, 2)
        for half in range(2):
            ps, pb = PS()
            for ci in range(2):
                c = half * 2 + ci
                for m in range(2):
                    k.op("pe", lambda e: e.transpose(out=ps[:, ci * 256 + m * 128:ci * 256 + (m + 1) * 128],
                                                     in_=xc[m][0][:, c * 128:(c + 1) * 128], identity=cx.ident),
                         r=[xc[m][1], cx.b_ident], w=[pb], inc=True)
            k.op("act", lambda e: e.activation(out=xtok[:, half * 2:half * 2 + 2, :], in_=ps.rearrange("p (a b) -> p a b", a=2), func=AF.Copy),
                 r=[pb], w=[xtokb])
        ps, pb = PS()
        for c in range(4):
            k.op("pe", lambda e: e.transpose(out=ps[:, c * 128:(c + 1) * 128], in_=xc[2][0][:, c * 128:(c + 1) * 128], identity=cx.ident),
                 r=[xc[2][1], cx.b_ident], w=[pb], inc=True)
        k.op("act", lambda e: e.activation(out=btok, in_=ps.rearrange("p (a b) -> p a b", a=4), func=AF.Copy), r=[pb], w=[btokb])
        xd, xdb = k.ring("xd", [128, 4, 256], F32, 2)
        xdd, xddb = k.ring("xdd", [128, 4, 256], F32, 2)
        v16 = lambda a: a.rearrange("p c (h d) -> p (c h) d", d=64)
        k.op("dve", lambda e: e.tensor_tensor(out=v16(xd), in0=v16(xtok), in1=bc64(dt, 16), op=ALU.mult), r=[xtokb, dtb_], w=[xdb])
        k.op("pool", lambda e: e.tensor_tensor(out=v16(xdd), in0=v16(xd), in1=bc64(decs, 16), op=ALU.mult), r=[xdb, decsb], w=[xddb])
        return dict(xc=xc, xd=xd, xdb=xdb, xdd=xdd, xddb=xddb, xtok=xtok, xtokb=xtokb, btok=btok, btokb=btokb,
                    da=da, dab=dab, eacs=eacs, eacsb=eacsb, cd=cd, cdb=cdb)

    def chunks(ti, L):
        xc, xd, xdb, xdd, xddb = L['xc'], L['xd'], L['xdb'], L['xdd'], L['xddb']
        xtok, xtokb, btok, btokb = L['xtok'], L['xtokb'], L['btok'], L['btokb']
        da, dab, eacs, eacsb, cd, cdb = L['da'], L['dab'], L['eacs'], L['eacsb'], L['cd'], L['cdb']
        Bc, Bcb = xc[2]; Cc, Ccb = xc[3]
        v4 = lambda a_: a_.rearrange("p (h d) -> p h d", d=64)

        def stage_a(c):
            cs = slice(c * 128, (c + 1) * 128)
            ps, pb = PS(); mm(k, ps[:, 0:128], [(Bc[:, cs], Cc[:, cs])], [Bcb, Ccb], pb)
            cbm, cbmb = k.ring("cbm", [128, 128], F32, 3)
            k.op("dve", lambda e: e.tensor_tensor(out=cbm, in0=ps[:, 0:128], in1=tri, op=ALU.mult), r=[pb, trib], w=[cbmb])
            pdf, pdfb = PS()
            for h in range(4):
                lh, lhb = k.ring("lh", [128, 128], F32, 4)
                if h % 2 == 0:
                    k.op("dve", lambda e: e.tensor_scalar(out=lh, in0=su, scalar1=da[:, c * 4 + h:c * 4 + h + 1], scalar2=None, op0=ALU.mult),
                         r=[sub_, dab], w=[lhb])
                else:
                    k.op("act", lambda e: e.activation(out=lh, in_=su, func=AF.Copy, scale=da[:, c * 4 + h:c * 4 + h + 1]),
                         r=[sub_, dab], w=[lhb])
                mm(k, pdf[:, h * 128:(h + 1) * 128], [(lh, tri)], [lhb, trib], pdfb)
            seg, segb = k.ring("seg", [128, 4, 128], F32, 3)
            k.op("act", lambda e: e.activation(out=seg, in_=pdf.rearrange("p (a b) -> p a b", a=4), func=AF.Exp), r=[pdfb], w=[segb])
            k.op("dve", lambda e: e.tensor_tensor(out=seg, in0=seg, in1=cbm.unsqueeze(1).to_broadcast([128, 4, 128]), op=ALU.mult),
                 r=[segb, cbmb], w=[segb])
            return seg, segb

        def stage_b(c, seg, segb):
            cs = slice(c * 128, (c + 1) * 128)
            py, pyb = PS()
            for h in range(4):
                mm(k, py[:, h * 64:(h + 1) * 64], [(seg[:, h, :], xd[:, c, h * 64:(h + 1) * 64])], [segb, xdb], pyb)
            po, pob = PS(); mm(k, po[:, 0:256], [(Cc[:, cs], S)], [Ccb, Sb], pob)
            t1, t1b = k.ring("yt1", [128, 256], F32, 2)
            k.op("dve", lambda e: e.tensor_tensor(out=v4(t1), in0=v4(po[:, 0:256]), in1=bc64(eacs[:, c * 4:(c + 1) * 4], 4), op=ALU.mult),
                 r=[pob, eacsb], w=[t1b])
            k.op("dve", lambda e: e.tensor_tensor(out=t1, in0=t1, in1=py[:, 0:256], op=ALU.add), r=[t1b, pyb], w=[t1b])
            t2, t2b = k.ring("yt2", [128, 256], F32, 2)
            k.op("pool", lambda e: e.tensor_tensor(out=v4(t2), in0=v4(xtok[:, c, :]), in1=bc64(dsk, 4), op=ALU.mult), r=[xtokb, dskb], w=[t2b])
            yo, yob = k.ring("yo", [128, 256], BF16, 2)
            k.op("pool", lambda e: e.tensor_tensor(out=yo, in0=t1, in1=t2, op=ALU.add), r=[t1b, t2b], w=[yob])
            r0 = (ti * 4 + c) * 128
            k.dma("sp", yout[r0:r0 + 128, :], yo, r=[yob], sb=yob)
            outb.append(yob)
            pss, pssb = PS(); mm(k, pss[:, 0:256], [(btok[:, c, :], xdd[:, c, :])], [btokb, xddb], pssb)
            k.op("dve", lambda e: e.tensor_tensor(out=v4(S), in0=v4(S), in1=bc64(cd[:, c * 4:(c + 1) * 4], 4), op=ALU.mult), r=[Sb, cdb], w=[Sb])
            k.op("dve", lambda e: e.tensor_tensor(out=S, in0=S, in1=pss[:, 0:256], op=ALU.add), r=[Sb, pssb], w=[Sb])

        pend = None
        for c in range(4):
            cur = (c,) + stage_a(c)
            if pend is not None:
                stage_b(*pend)
            pend = cur
        stage_b(*pend)

    cur = prologue(0)
    for ti in range(ntiles):
        nxt = prologue(ti + 1) if ti + 1 < ntiles else None
        chunks(ti, cur)
        cur = nxt
    k.wait_all("sp", list({id(b): b for b in outb}.values()))
    return nc


def l2b_inputs(h0T_b, hg, P):
    w = P["ab_w_in"]
    g = hg // 2
    xo = 2328
    rep = lambda v, n: np.ascontiguousarray(np.tile(np.asarray(v, np.float32)[None, :], (128, n)))
    chans = [np.arange(hg * 256, hg * 256 + 128), np.arange(hg * 256 + 128, hg * 256 + 256),
             1024 + g * 128 + np.arange(128), 1280 + g * 128 + np.arange(128)]
    cwf = P["b_conv_w"][:, 0, :]
    convw = np.stack([cwf[:, ch].T for ch in chans], axis=1)
    convb = np.stack([P["b_conv_b"][ch] for ch in chans], axis=1)
    hs = slice(hg * 4, hg * 4 + 4)
    t = np.arange(128)
    return {"hall": h0T_b, "wx": np.ascontiguousarray(w[:, xo + hg * 256: xo + (hg + 1) * 256]),
            "wB": np.ascontiguousarray(w[:, xo + 1024 + g * 128: xo + 1024 + (g + 1) * 128]),
            "wC": np.ascontiguousarray(w[:, xo + 1280 + g * 128: xo + 1280 + (g + 1) * 128]),
            "wdt": np.ascontiguousarray(w[:, 3864 + hg * 4: 3864 + hg * 4 + 4]),
            "convw": np.ascontiguousarray(convw.astype(np.float32)), "convb": np.ascontiguousarray(convb.astype(np.float32)),
            "dtb": rep(P["b_dt_bias"][hs], 4), "alog": rep(P["b_a_log"][hs], 4), "dskip": rep(P["b_d_skip"][hs], 1),
            "tri": (t[:, None] <= t[None, :]).astype(np.float32), "su": (t[:, None] > t[None, :]).astype(np.float32)}


def linear_tile(cx, in_ap, inb, W_dram, KC, M, out_fn):
    k = cx.k
    Wv = W_dram.rearrange("(kc p) m -> p kc m", p=128)
    for mb in range(M // 256):
        w, wb = cx.wload(Wv[:, :, mb * 256:(mb + 1) * 256], [128, KC, 256])
        for m2 in range(2):
            ps, pb = cx.psum()
            mm(k, ps, [(w[:, kc, m2 * 128:(m2 + 1) * 128], in_ap[:, kc, :]) for kc in range(KC)], [wb, inb], pb)
            out_fn(mb * 2 + m2, ps, pb)


def build_l3(nc=None, cx=None, io=None):
    nc, cx, io = _std(nc, cx, io)
    din = io.inp
    x1T = din("x1T", [128, 8, TOK]); h0T = din("h0T", [1024, TOK], BF16).rearrange("(kc p) t -> p kc t", p=128)
    oaall = din("oaall", [4 * 4096, 256], BF16); yall = din("yall", [4 * SEQ, 256], BF16); qsel_d = din("qsel", [128, 4])
    gains = din("gains", [128, 12, 8]); normw_d = din("normw", [128, 8])
    wz = din("wz", [D, D]); wout = din("wout", [1536, D])
    f2 = [din("f2g", [D, DFF]), din("f2u", [D, DFF]), din("f2d", [DFF, D])]
    f1 = [din("f1g", [D, DFF]), din("f1u", [D, DFF]), din("f1d", [DFF, D])]
    x4T = io.out("x4T", [128, 8, TOK], F32)
    h1T = io.out("h1T", [1024, TOK], BF16).rearrange("(kc p) t -> p kc t", p=128)
    k = cx.k
    cx.wring_n = 2
    cx.wstg_n = 1
    g, gb = load_gains(cx, gains)
    nw, nwb = k.tile("normw", [128, 8], F32); k.dma("sp", nw, normw_d, w=[nwb], sb=nwb)
    qsel, qselb = k.tile("qsel", [128, 4], F32); k.dma("sp", qsel, qsel_d, w=[qselb], sb=qselb)
    ones512, ones512b = k.tile("ones512", [128, 128], BF16)
    k.op("pool", lambda e: e.memset(ones512, 1.0 / 512), w=[ones512b])
    idq, idqb = k.tile("idq", [128, 4, 128], BF16)
    for j in range(4):
        k.op("dve", lambda e: e.tensor_scalar(out=idq[:, j, :], in0=cx.ident, scalar1=qsel[:, j:j + 1], scalar2=None, op0=ALU.mult),
             r=[cx.b_ident, qselb], w=[idqb])
    c_eps5 = cx.const(1e-5)
    outs = []
    for half in range(2):
        x, xb = k.ring("x_res", [128, 8, 1024], F32, 1)
        k.dma("sp", x, x1T[:, :, half * 1024:(half + 1) * 1024], w=[xb], sb=xb)
        for tt in range(2):
            t0 = half * 1024 + tt * 512
            tq = t0 // 512
            sl = slice(tt * 512, (tt + 1) * 512)
            o, ob = k.ring("ffn_o", [128, 8, 1024], F32, 1)
            act, actb = k.ring("act_bf", [128, 22, 1024], BF16, 1)
            ys = o[:, :, 0:512]
            mo = o[:, :, 512:1024]
            mixin = act[:, 0:6, :].rearrange("p a (b t) -> p (a b) t", t=512)
            h0 = act[:, 6:10, :].rearrange("p a (b t) -> p (a b) t", t=512)
            k.dma("sp", h0, h0T[:, :, t0:t0 + 512], w=[actb], sb=actb)
            for hg in range(4):
                cand, candb = k.ring("ycand", [128, 4, 4, 256], BF16, 1)
                for j in range(4):
                    r0 = (j * 4 + hg) * TOK + t0
                    k.dma("sp", cand[:, j], yall[r0:r0 + 512, :].rearrange("(tb p) c -> p tb c", p=128), w=[candb], sb=candb)
                for hh in range(2):
                    ps, pb = cx.psum()
                    for tb in range(4):
                        mm(k, ps[:, tb * 128:(tb + 1) * 128],
                           [(cand[:, j, tb, hh * 128:(hh + 1) * 128], idq[:, j, :]) for j in range(4)], [candb, idqb], pb)
                    k.op("act", lambda e: e.activation(out=ys[:, hg * 2 + hh, :], in_=ps, func=AF.Copy), r=[pb], w=[ob])
            for kvh in range(2):
                ocand, ocandb = k.ring("ocand", [128, 2, 4, 2, 256], BF16, 1)
                for par in range(2):
                    for j in range(4):
                        r0 = ((j // 2) * 4 + kvh * 2 + par) * 2048 + (j % 2) * 1024 + (2 * tq) * 128
                        k.dma("sp", ocand[:, par, j], oaall[r0:r0 + 256, :].rearrange("(i p) c -> p i c", p=128), w=[ocandb], sb=ocandb)
                for hh in range(2):
                    ps, pb = cx.psum()
                    for u in range(4):
                        par, i2 = u % 2, u // 2
                        mm(k, ps[:, u * 128:(u + 1) * 128],
                           [(ocand[:, par, j, i2, hh * 128:(hh + 1) * 128], idq[:, j, :]) for j in range(4)], [ocandb, idqb], pb)
                    k.op("act", lambda e: e.activation(out=mixin[:, kvh * 2 + hh, :], in_=ps, func=AF.Copy), r=[pb], w=[actb])

            def z_out(mc, ps, pb):
                zs, zsb = k.ring("sg", [128, 512], F32, 3)
                k.op("act", lambda e: e.activation(out=zs, in_=ps, func=AF.Silu), r=[pb], w=[zsb])
                k.op("dve", lambda e: e.tensor_tensor(out=ys[:, mc, :], in0=ys[:, mc, :], in1=zs, op=ALU.mult), r=[zsb, ob], w=[ob])
            linear_tile(cx, h0, actb, wz, 8, D, z_out)
            sq, sqb = k.ring("sq", [128, 8, 512], BF16, 1)
            k.op("act", lambda e: e.activation(out=sq, in_=ys, func=AF.Square), r=[ob], w=[sqb])
            for gi in range(2):
                ps, pb = cx.psum()
                mm(k, ps, [(ones512, sq[:, gi * 4 + c, :]) for c in range(4)], [ones512b, sqb], pb)
                rs, rsb = k.ring("rstd", [128, 512], F32, 2)
                k.op("act", lambda e: e.activation(out=rs, in_=ps, func=AF.Sqrt, bias=c_eps5, scale=1.0), r=[pb, cx.b_consts], w=[rsb])
                k.op("dve", lambda e: e.reciprocal(out=rs, in_=rs), r=[rsb], w=[rsb])
                for c in range(4):
                    ch = gi * 4 + c
                    k.op("dve", lambda e: e.scalar_tensor_tensor(out=mixin[:, 4 + ch, :], in0=ys[:, ch, :], scalar=nw[:, ch:ch + 1], in1=rs,
                                                                 op0=ALU.mult, op1=ALU.mult), r=[ob, nwb, rsb], w=[actb])

            def mix_out(mc, ps, pb):
                k.op("act", lambda e: e.activation(out=mo[:, mc, :], in_=ps, func=AF.Copy), r=[pb], w=[ob])
            linear_tile(cx, mixin, actb, wout, 12, D, mix_out)
            post_norm_add(cx, x[:, :, sl], xb, mo, ob, g[:, 3, :], gb, 1.0)
        ffn_half(cx, x, xb, 2, f2[0], f2[1], f2[2], g[:, 4, :], g[:, 5, :], gb)
        ffn_half(cx, x, xb, 2, f1[0], f1[1], f1[2], g[:, 6, :], g[:, 7, :], gb)
        k.dma("sp", x4T[:, :, half * 1024:(half + 1) * 1024], x, r=[xb], sb=xb)
        hh_, hhb = k.ring("h_bf", [128, 8, 1024], BF16, 1)
        for tt in range(2):
            sl = slice(tt * 512, (tt + 1) * 512)
            norm_bf16(cx, x[:, :, sl], xb, g[:, 8, :], gb, hh_[:, :, sl], hhb)
        k.dma("sp", h1T[:, :, half * 1024:(half + 1) * 1024], hh_, r=[hhb], sb=hhb)
        outs += [xb, hhb]
    k.wait_all("sp", outs)
    return nc


def build_l5(nc=None, cx=None, io=None):
    nc, cx, io = _std(nc, cx, io)
    din = io.inp
    x4T = din("x4T", [128, 8, TOK])
    ygall = din("ygall", [4096, TOK], BF16).rearrange("(j hg pc p) t -> j p (hg pc) t", j=4, hg=4, pc=2, p=128)
    qsel_d = din("qsel", [128, 4])
    gains = din("gains", [128, 12, 8]); wo = din("wo", [D, D])
    f2 = [din("f2g", [D, DFF]), din("f2u", [D, DFF]), din("f2d", [DFF, D])]
    outT = io.out("outT", [128, 8, TOK], F32)
    k = cx.k
    cx.wstg_n = 1
    g, gb = load_gains(cx, gains)
    qsel, qselb = k.tile("qsel", [128, 4], F32); k.dma("sp", qsel, qsel_d, w=[qselb], sb=qselb)
    outs = []
    for half in range(2):
        x, xb = k.ring("x_res", [128, 8, 1024], F32, 1)
        k.dma("sp", x, x4T[:, :, half * 1024:(half + 1) * 1024], w=[xb], sb=xb)
        for tt in range(2):
            t0 = half * 1024 + tt * 512
            sl = slice(tt * 512, (tt + 1) * 512)
            o, ob = k.ring("ffn_o", [128, 8, 1024], F32, 1)
            act, actb = k.ring("act_bf", [128, 22, 1024], BF16, 1)
            mo = o[:, :, 512:1024]
            yg = act[:, 6:10, :].rearrange("p a (b t) -> p (a b) t", t=512)
            for j in range(4):
                cand, candb = k.ring("ygcand", [128, 8, 512], BF16, 1)
                k.dma("sp", cand, ygall[j][:, :, t0:t0 + 512], w=[candb], sb=candb)
                if j == 0:
                    k.op("dve", lambda e: e.tensor_scalar(out=yg, in0=cand, scalar1=qsel[:, 0:1], scalar2=None, op0=ALU.mult),
                         r=[candb, qselb], w=[actb])
                else:
                    k.op("dve", lambda e: e.scalar_tensor_tensor(out=yg, in0=cand, scalar=qsel[:, j:j + 1], in1=yg, op0=ALU.mult, op1=ALU.add),
                         r=[candb, qselb, actb], w=[actb])

            def mix_out(mc, ps, pb):
                k.op("act", lambda e: e.activation(out=mo[:, mc, :], in_=ps, func=AF.Copy), r=[pb], w=[ob])
            linear_tile(cx, yg, actb, wo, 8, D, mix_out)
            post_norm_add(cx, x[:, :, sl], xb, mo, ob, g[:, 9, :], gb, 1.0)
        ffn_half(cx, x, xb, 2, f2[0], f2[1], f2[2], g[:, 10, :], g[:, 11, :], gb)
        k.dma("sp", outT[:, :, half * 1024:(half + 1) * 1024], x, r=[xb], sb=xb)
        outs.append(xb)
    k.wait_all("sp", outs)
    return nc


def _run(nc, in_maps):
    res = run_bass_kernel_spmd(nc, in_maps, core_ids=list(range(NCORE)))
    return res.results


def kernel_unfused(**inp):
    import ml_dtypes
    f32 = lambda a: np.ascontiguousarray(np.asarray(a, dtype=np.float32))
    I = {k_: f32(v) for k_, v in inp.items()}
    x = I["x"].reshape(16384, D)
    g = gains_layout(I["norm_gains"])
    tok = [slice(c * TOK, (c + 1) * TOK) for c in range(NCORE)]
    grp = lambda lst, b: np.ascontiguousarray(np.concatenate(lst[4 * b:4 * b + 4], axis=0))
    qsel = []
    for c in range(NCORE):
        q_ = np.zeros((128, 4), np.float32); q_[:, c % 4] = 1.0
        qsel.append(q_)
    r1 = _run(build_l1(), [{"xT": fm(x[tok[c]]), "gains": g, "wg": I["ffn1_w_gate"][0], "wu": I["ffn1_w_up"][0],
                            "wd": I["ffn1_w_down"][0]} for c in range(NCORE)])
    x1T = [np.asarray(r["x1T"]) for r in r1]
    h0loc = [np.asarray(r["h0T"]) for r in r1]
    h0all = [grp(h0loc, b) for b in range(2)]
    PA = {k_: I[k_][0] for k_ in I if k_.startswith("a_") or k_.startswith("ab_") or k_.startswith("b_")}
    C = nsa_consts()
    r2a = _run(build_l2a(), [l2a_inputs(h0all[c // 4], (c % 4) // 2, c % 2, PA, C) for c in range(NCORE)])
    r2b = _run(build_l2b(), [l2b_inputs(h0all[c // 4], c % 4, PA) for c in range(NCORE)])
    oaall = [grp([np.asarray(r["oa"]) for r in r2a], b) for b in range(2)]
    yall = [grp([np.asarray(r["y"]) for r in r2b], b) for b in range(2)]
    w_in = I["ab_w_in"][0]
    m3 = []
    for c in range(NCORE):
        m3.append({"x1T": x1T[c], "h0T": h0loc[c], "oaall": oaall[c // 4], "yall": yall[c // 4], "qsel": qsel[c], "gains": g,
                   "normw": np.ascontiguousarray(I["b_norm_w"][0].reshape(8, 128).T),
                   "wz": np.ascontiguousarray(w_in[:, 1304:2328]), "wout": I["ab_w_out"][0],
                   "f2g": I["ffn2_w_gate"][0], "f2u": I["ffn2_w_up"][0], "f2d": I["ffn2_w_down"][0],
                   "f1g": I["ffn1_w_gate"][1], "f1u": I["ffn1_w_up"][1], "f1d": I["ffn1_w_down"][1]})
    r3 = _run(build_l3(), m3)
    x4T = [np.asarray(r["x4T"]) for r in r3]
    h1all = [grp([np.asarray(r["h1T"]) for r in r3], b) for b in range(2)]
    PC = {k_: I[k_][0] for k_ in I if k_.startswith("c_")}
    r4 = _run(build_l4(), [l4_inputs(h1all[c // 4], c % 4, PC) for c in range(NCORE)])
    ygall = [grp([np.asarray(r["ygT"]) for r in r4], b) for b in range(2)]
    m5 = []
    for c in range(NCORE):
        m5.append({"x4T": x4T[c], "ygall": ygall[c // 4], "qsel": qsel[c], "gains": g,
                   "wo": I["c_w_o"][0], "f2g": I["ffn2_w_gate"][1], "f2u": I["ffn2_w_up"][1], "f2d": I["ffn2_w_down"][1]})
    r5 = _run(build_l5(), m5)
    out = np.concatenate([unfm(np.asarray(r["outT"])) for r in r5], axis=0)
    return np.ascontiguousarray(out.reshape(2, SEQ, D).astype(np.float32))


RG = [[0, 1, 2, 3], [4, 5, 6, 7]]


def build_fused(upto=99):
    nc = bass.Bass("TRN2", target_bir_lowering=False, num_devices=NCORE)
    k = K(nc)
    idram = lambda n, sh, dt: nc.dram_tensor(n, list(sh), dt, kind="Internal").ap()
    x1T = idram("i_x1T", [128, 8, TOK], F32)
    h0loc = idram("i_h0loc", [1024, TOK], BF16); h0all = idram("i_h0all", [4096, TOK], BF16)
    oaloc = idram("i_oaloc", [NQB * 128, 256], BF16); oaall = idram("i_oaall", [4 * 4096, 256], BF16)
    yloc = idram("i_yloc", [SEQ, 256], BF16); yall = idram("i_yall", [4 * SEQ, 256], BF16)
    x4T = idram("i_x4T", [128, 8, TOK], F32)
    h1loc = idram("i_h1loc", [1024, TOK], BF16); h1all = idram("i_h1all", [4096, TOK], BF16)
    ygloc = idram("i_ygloc", [1024, TOK], BF16); ygall = idram("i_ygall", [4096, TOK], BF16)
    ccsem = Sem(nc.alloc_semaphore(name="ccsem"), "cc")
    k.dall = [ccsem]; k.dused = []; k.dfree = []

    def allgather(src, dst, wait=True):
        rows, cols = src.shape
        R = (1 << 20) // (cols * mybir.dt.size(src.dtype))
        k.barrier()
        for i in range(rows // R):
            ins = nc.gpsimd.collective_compute("AllGather", ALU.bypass, replica_groups=RG,
                                               ins=[src[i * R:(i + 1) * R, :]], outs=[dst[i * 4 * R:(i + 1) * 4 * R, :]])
            ins.then_inc(ccsem.h, 1)
            ccsem.total += 1
        if wait:
            k.barrier()

    def phase(fn, pre, ext, **kw):
        k.begin_phase()
        cx = Ctx(nc, k)
        fn(nc=nc, cx=cx, io=IO(nc, pre, ext), **kw)
        k.end_phase()

    steps = [lambda: phase(build_l1, "l1_", {"x1T": x1T, "h0T": h0loc}),
             lambda: allgather(h0loc, h0all),
             lambda: phase(build_l2a, "l2a_", {"hall": h0all, "oa": oaloc}),
             lambda: allgather(oaloc, oaall, wait=False),
             lambda: phase(build_l2b, "l2b_", {"hall": h0all, "y": yloc}),
             lambda: allgather(yloc, yall),
             lambda: phase(build_l3, "l3_", {"x1T": x1T, "h0T": h0loc, "oaall": oaall, "yall": yall, "x4T": x4T, "h1T": h1loc}),
             lambda: allgather(h1loc, h1all),
             lambda: phase(build_l4, "l4_", {"hall": h1all, "ygT": ygloc}),
             lambda: allgather(ygloc, ygall),
             lambda: phase(build_l5, "l5_", {"x4T": x4T, "ygall": ygall})]
    for st in steps[:upto]:
        st()
    if upto < len(steps):
        nc.dram_tensor("l5_outT", [128, 8, TOK], F32, kind="ExternalOutput")
    return nc


def kernel(**inp):
    f32 = lambda a: np.ascontiguousarray(np.asarray(a, dtype=np.float32))
    I = {k_: f32(v) for k_, v in inp.items()}
    x = I["x"].reshape(16384, D)
    g = gains_layout(I["norm_gains"])
    PA = {k_: I[k_][0] for k_ in I if k_.startswith("a_") or k_.startswith("ab_") or k_.startswith("b_")}
    PC = {k_: I[k_][0] for k_ in I if k_.startswith("c_")}
    C = nsa_consts()
    w_in = I["ab_w_in"][0]
    in_maps = []
    for c in range(NCORE):
        m = {}
        qs = np.zeros((128, 4), np.float32); qs[:, c % 4] = 1.0
        m.update({"l1_" + k_: v for k_, v in {"xT": fm(x[c * TOK:(c + 1) * TOK]), "gains": g, "wg": I["ffn1_w_gate"][0],
                                             "wu": I["ffn1_w_up"][0], "wd": I["ffn1_w_down"][0]}.items()})
        a = l2a_inputs(None, (c % 4) // 2, c % 2, PA, C); a.pop("hall")
        m.update({"l2a_" + k_: v for k_, v in a.items()})
        b_ = l2b_inputs(None, c % 4, PA); b_.pop("hall")
        m.update({"l2b_" + k_: v for k_, v in b_.items()})
        m.update({"l3_" + k_: v for k_, v in {"qsel": qs, "gains": g,
                  "normw": np.ascontiguousarray(I["b_norm_w"][0].reshape(8, 128).T),
                  "wz": np.ascontiguousarray(w_in[:, 1304:2328]), "wout": I["ab_w_out"][0],
                  "f2g": I["ffn2_w_gate"][0], "f2u": I["ffn2_w_up"][0], "f2d": I["ffn2_w_down"][0],
                  "f1g": I["ffn1_w_gate"][1], "f1u": I["ffn1_w_up"][1], "f1d": I["ffn1_w_down"][1]}.items()})
        d4 = l4_inputs(None, c % 4, PC); d4.pop("hall")
        m.update({"l4_" + k_: v for k_, v in d4.items()})
        m.update({"l5_" + k_: v for k_, v in {"qsel": qs, "gains": g, "wo": I["c_w_o"][0], "f2g": I["ffn2_w_gate"][1],
                                             "f2u": I["ffn2_w_up"][1], "f2d": I["ffn2_w_down"][1]}.items()})
        in_maps.append(m)
    import os
    upto = int(os.environ.get('FUSED_UPTO', '99'))
    nc_ = build_fused(upto)
    if upto < 99:
        names = {a.memorylocations[0].name for a in nc_.allocations if hasattr(a, 'memorylocations') and a.memorylocations}
        in_maps = [{k_: v for k_, v in m.items() if k_ in names} for m in in_maps]
    res = _run(nc_, in_maps)
    out = np.concatenate([unfm(np.asarray(r["l5_outT"])) for r in res], axis=0)
    return np.ascontiguousarray(out.reshape(2, SEQ, D).astype(np.float32))
```

```python
import numpy as np
import concourse.bass as bass
import concourse.mybir as mybir
from concourse.bass_utils import run_bass_kernel_spmd

F32 = mybir.dt.float32
BF16 = mybir.dt.bfloat16
AF = mybir.ActivationFunctionType
ALU = mybir.AluOpType
AX = mybir.AxisListType

D = 1024
DFF = 2816
NCORE = 8
TOK = 2048
EPS = 1e-6


class Buf:
    __slots__ = ("name", "lw", "rd", "dsem", "lw_dma")

    def __init__(self, name):
        self.name = name
        self.lw = None
        self.rd = []
        self.dsem = None
        self.lw_dma = False


class Sem:
    __slots__ = ("h", "total", "name")

    def __init__(self, h, name):
        self.h = h
        self.total = 0
        self.name = name


class K:
    def __init__(self, nc):
        self.nc = nc
        self.engs = {"pe": nc.tensor, "act": nc.scalar, "dve": nc.vector,
                     "pool": nc.gpsimd, "sp": nc.sync}
        self.esem = {n: Sem(nc.alloc_semaphore(name="es_" + n), n) for n in self.engs}
        self.waited = {n: {} for n in self.engs}
        self.nsem = 0
        self.ninstr = 0
        self.nwait = 0
        self.rings = {}

    def begin_phase(self):
        import contextlib
        self.phase = getattr(self, "phase", 0) + 1
        self.stack = contextlib.ExitStack()
        self.rings = {}
        self.dfree = getattr(self, "dfree", [])

    def end_phase(self):
        self.barrier()
        self.stack.close()
        self.stack = None
        self.dfree = list(self.dused)
        self.dused = []

    def barrier(self):
        sems = list(self.esem.values()) + list(getattr(self, "dall", []))
        for eng in self.engs:
            needs = {s_: s_.total for s_ in sems if s_.total > 0}
            self._emit_waits(eng, needs)

    def sb(self, name, shape, dt):
        if getattr(self, "stack", None) is not None:
            return self.stack.enter_context(self.nc.sbuf_tensor("s%d_%s" % (self.phase, name), list(shape), dt)).ap()
        return self._sb_static(name, shape, dt)

    def _sb_static(self, name, shape, dt):
        return self.nc.alloc_sbuf_tensor("s_" + name, list(shape), dt).ap()

    def ps(self, name, shape, dt=F32):
        if getattr(self, "stack", None) is not None:
            return self.stack.enter_context(self.nc.psum_tensor("p%d_%s" % (self.phase, name), list(shape), dt)).ap()
        return self.nc.alloc_psum_tensor("p_" + name, list(shape), dt).ap()

    def tile(self, name, shape, dt):
        return self.sb(name, shape, dt), Buf(name)

    def ring(self, name, shape, dt, n, psum=False):
        if name not in self.rings:
            sl = []
            for i in range(n):
                nm = "%s_%d" % (name, i)
                ap = self.ps(nm, shape, dt) if psum else self.sb(nm, shape, dt)
                sl.append((ap, Buf(nm)))
            self.rings[name] = [sl, 0]
        r = self.rings[name]
        s = r[0][r[1] % len(r[0])]
        r[1] += 1
        return s

    def _dsem(self, b, q="sp"):
        if b.dsem is None:
            if not hasattr(self, "dall"):
                self.dall, self.dused, self.dfree = [], [], getattr(self, "dfree", [])
            if self.dfree and q != "pool":
                b.dsem = self.dfree.pop()
            else:
                b.dsem = Sem(self.nc.alloc_semaphore(name="ds%d" % self.nsem), b.name)
                self.nsem += 1
                self.dall.append(b.dsem)
            if q != "pool":
                self.dused.append(b.dsem)
        return b.dsem

    def _need(self, eng, tok, needs):
        if tok is None:
            return
        s, v = tok
        if v is None:
            v = s.total
        if eng == "pe" and s is self.esem["pe"]:
            return
        if v > needs.get(s, 0):
            needs[s] = v

    def _emit_waits(self, eng, needs):
        e = self.engs[eng]
        w = self.waited[eng]
        for s, v in needs.items():
            if w.get(s, 0) >= v:
                continue
            e.wait_ge(s.h, v)
            self.nwait += 1
            w[s] = v

    def op(self, eng, fn, r=(), w=(), inc=True):
        needs = {}
        for b in r:
            self._need(eng, b.lw, needs)
        for b in w:
            self._need(eng, b.lw, needs)
            for t in b.rd:
                self._need(eng, t, needs)
        self._emit_waits(eng, needs)
        ins = fn(self.engs[eng])
        s = self.esem[eng]
        if inc:
            s.total += 1
            ins.then_inc(s.h, 1)
            tok = (s, s.total)
        else:
            tok = (s, s.total + 1)
        for b in r:
            b.rd.append(tok)
            if len(b.rd) > 24:
                b.rd = self._compact(b.rd)
        for b in w:
            b.lw = tok
            b.rd = []
            b.lw_dma = False
        self.ninstr += 1
        return ins

    def _compact(self, toks):
        best = {}
        for s, v in toks:
            if v is None:
                best[s] = None
            elif s not in best or (best[s] is not None and v > best[s]):
                best[s] = v
        return [(s, v) for s, v in best.items()]

    def dma(self, q, out, in_, r=(), w=(), sb=None, **kw):
        s = self._dsem(sb, q)
        needs = {}
        for b in r:
            self._need(q, b.lw, needs)
        for b in w:
            if not (b.lw_dma and b.lw is not None and b.lw[0] is s and not b.rd):
                self._need(q, b.lw, needs)
            for t in b.rd:
                self._need(q, t, needs)
        self._emit_waits(q, needs)
        ins = self.engs[q].dma_start(out=out, in_=in_, **kw)
        s.total += 16
        ins.then_inc(s.h, 16)
        tok = (s, None)
        for b in r:
            b.rd.append(tok)
        for b in w:
            b.lw = tok
            b.rd = []
            b.lw_dma = True
        self.ninstr += 1
        return ins

    def wait_all(self, eng, bufs):
        needs = {}
        for b in bufs:
            self._need(eng, b.lw, needs)
            for t in b.rd:
                self._need(eng, t, needs)
        self._emit_waits(eng, needs)


class Ctx:
    def __init__(self, nc, k=None):
        self.nc = nc
        self.k = k if k is not None else K(nc)
        k = self.k
        self.ones_d, self.b_ones_d = k.tile("ones_d", [128, 128], BF16)
        k.op("pool", lambda e: e.memset(self.ones_d, 1.0 / D), w=[self.b_ones_d])
        self.ident, self.b_ident = k.tile("ident", [128, 128], F32)
        k.op("pool", lambda e: e.memset(self.ident, 1.0), w=[self.b_ident])
        k.op("pool", lambda e: e.affine_select(out=self.ident, in_=self.ident, pattern=[[-1, 128]],
                                               compare_op=ALU.is_equal, fill=0.0, base=0,
                                               channel_multiplier=1), r=[self.b_ident], w=[self.b_ident])
        self.dq = 0
        self.consts, self.b_consts = k.tile("consts", [128, 16], F32)
        self.cvals = {}

    def const(self, v):
        if v not in self.cvals:
            i = len(self.cvals)
            self.cvals[v] = i
            self.k.op("pool", lambda e: e.memset(self.consts[:, i:i + 1], float(v)), w=[self.b_consts])
        i = self.cvals[v]
        return self.consts[:, i:i + 1]

    def psum(self):
        return self.k.ring("psum", [128, 512], F32, 8, psum=True)

    def wload(self, dram_ap, shape, alt=False):
        k = self.k
        ap, b = k.ring("wring", [128, 5632], BF16, getattr(self, "wring_n", 3))
        n = shape[1] * shape[2]
        v = ap[:, 0:n].rearrange("p (a b) -> p a b", a=shape[1])
        if alt and n <= 2048:
            st, stb = k.ring("wstg", [128, 2048], F32, getattr(self, "wstg_n", 2))
            sv = st[:, 0:n].rearrange("p (a b) -> p a b", a=shape[1])
            k.dma("sp", sv, dram_ap, w=[stb], sb=stb)
            k.op("dve", lambda e: e.tensor_copy(out=v, in_=sv), r=[stb], w=[b])
        else:
            k.dma("pool", v, dram_ap, w=[b], sb=b)
        return v, b


class IO:
    def __init__(self, nc, pre="", ext=None):
        self.nc, self.pre, self.ext = nc, pre, dict(ext or {})

    def inp(self, name, shape, dt=F32):
        if name in self.ext:
            return self.ext[name]
        return self.nc.dram_tensor(self.pre + name, list(shape), dt, kind="ExternalInput").ap()

    def out(self, name, shape, dt=F32):
        if name in self.ext:
            return self.ext[name]
        return self.nc.dram_tensor(self.pre + name, list(shape), dt, kind="ExternalOutput").ap()


def _std(nc, cx, io):
    if nc is None:
        nc = bass.Bass("TRN2", target_bir_lowering=False)
    if cx is None:
        cx = Ctx(nc)
    if io is None:
        io = IO(nc)
    return nc, cx, io


def hall_tile(hall, t0, n):
    r, col = t0 // TOK, t0 % TOK
    v = hall.rearrange("(i r kk p) t -> r p i kk t", i=4, r=4, kk=2, p=128)
    return v[r][:, :, :, col:col + n]


def dma_hall(k, dst, hall, t0, n, buf, **kw):
    src = hall_tile(hall, t0, n)
    for i in range(4):
        k.dma("sp", dst[:, 2 * i:2 * i + 2, :], src[:, i], w=[buf], sb=buf, **kw)


def rms_rstd(cx, x_ap, xb, eps=EPS):
    k = cx.k
    T = x_ap.shape[2]
    sq, sqb = k.ring("sq", [128, 8, 512], BF16, 1)
    k.op("act", lambda e: e.activation(out=sq[:, :, 0:T], in_=x_ap, func=AF.Square), r=[xb], w=[sqb])
    ps, pb = cx.psum()
    for c in range(8):
        k.op("pe", lambda e: e.matmul(ps[:, 0:T], lhsT=cx.ones_d, rhs=sq[:, c, 0:T], start=(c == 0), stop=(c == 7)),
             r=[sqb, cx.b_ones_d], w=[pb], inc=(c == 7))
    rs, rsb = k.ring("rstd", [128, 512], F32, 2)
    c_eps = cx.const(eps)
    k.op("act", lambda e: e.activation(out=rs[:, 0:T], in_=ps[:, 0:T], func=AF.Sqrt, bias=c_eps, scale=1.0),
         r=[pb, cx.b_consts], w=[rsb])
    k.op("dve", lambda e: e.reciprocal(out=rs[:, 0:T], in_=rs[:, 0:T]), r=[rsb], w=[rsb])
    return rs[:, 0:T], rsb


def norm_bf16(cx, x_ap, xb, g_ap, gb, out_ap, outb):
    k = cx.k
    rs, rsb = rms_rstd(cx, x_ap, xb)
    for c in range(8):
        eng = "dve"
        k.op(eng, lambda e: e.scalar_tensor_tensor(out=out_ap[:, c, :], in0=x_ap[:, c, :], scalar=g_ap[:, c:c + 1],
                                                   in1=rs, op0=ALU.mult, op1=ALU.mult),
             r=[xb, gb, rsb], w=[outb])


def ffn_half(cx, x, xb, NT, wg, wu, wd, g_in, g_out, gb):
    k = cx.k
    T = NT * 512
    h, hb = k.ring("h_bf", [128, 8, 1024], BF16, 1)
    act, actb = k.ring("act_bf", [128, 22, 1024], BF16, 1)
    for tt in range(NT):
        sl = slice(tt * 512, (tt + 1) * 512)
        norm_bf16(cx, x[:, :, sl], xb, g_in, gb, h[:, :, sl], hb)
    wgv = wg.rearrange("(kc p) f -> p kc f", p=128)
    wuv = wu.rearrange("(kc p) f -> p kc f", p=128)
    for fb in range(11):
        gw, gwb = cx.wload(wgv[:, :, fb * 256:(fb + 1) * 256], [128, 8, 256])
        uw, uwb = cx.wload(wuv[:, :, fb * 256:(fb + 1) * 256], [128, 8, 256], alt=True)
        for tt in range(NT):
            sl = slice(tt * 512, (tt + 1) * 512)
            for fc in range(2):
                f = fb * 2 + fc
                pg, pgb = cx.psum()
                pu, pub = cx.psum()
                for c in range(8):
                    k.op("pe", lambda e: e.matmul(pg, lhsT=gw[:, c, fc * 128:(fc + 1) * 128], rhs=h[:, c, sl],
                                                  start=(c == 0), stop=(c == 7)), r=[gwb, hb], w=[pgb], inc=(c == 7))
                for c in range(8):
                    k.op("pe", lambda e: e.matmul(pu, lhsT=uw[:, c, fc * 128:(fc + 1) * 128], rhs=h[:, c, sl],
                                                  start=(c == 0), stop=(c == 7)), r=[uwb, hb], w=[pub], inc=(c == 7))
                sg, sgb = k.ring("sg", [128, 512], F32, 3)
                k.op("act", lambda e: e.activation(out=sg, in_=pg, func=AF.Silu), r=[pgb], w=[sgb])
                k.op("dve", lambda e: e.tensor_tensor(out=act[:, f, sl], in0=sg, in1=pu, op=ALU.mult),
                     r=[sgb, pub], w=[actb])
    wdv = wd.rearrange("(fc p) d -> p fc d", p=128)
    o, ob = k.ring("ffn_o", [128, 8, 1024], F32, 1)
    for db in range(4):
        dw, dwb = cx.wload(wdv[:, :, db * 256:(db + 1) * 256], [128, 22, 256])
        for tt in range(NT):
            sl = slice(tt * 512, (tt + 1) * 512)
            for dc in range(2):
                d = db * 2 + dc
                po, pob = cx.psum()
                for f in range(22):
                    k.op("pe", lambda e: e.matmul(po, lhsT=dw[:, f, dc * 128:(dc + 1) * 128], rhs=act[:, f, sl],
                                                  start=(f == 0), stop=(f == 21)), r=[dwb, actb], w=[pob], inc=(f == 21))
                k.op("act", lambda e: e.activation(out=o[:, d, sl], in_=po, func=AF.Copy), r=[pob], w=[ob])
    for tt in range(NT):
        sl = slice(tt * 512, (tt + 1) * 512)
        post_norm_add(cx, x[:, :, sl], xb, o[:, :, sl], ob, g_out, gb, 0.5)


def post_norm_add(cx, x_ap, xb, o_ap, ob, g_ap, gb, coef):
    k = cx.k
    rs, rsb = rms_rstd(cx, o_ap, ob)
    for c in range(8):
        eng = "dve"
        tmp, tb = k.ring("pn_tmp" + eng, [128, 512], F32, 2)
        T = o_ap.shape[2]
        k.op(eng, lambda e: e.scalar_tensor_tensor(out=tmp[:, 0:T], in0=o_ap[:, c, :], scalar=g_ap[:, c:c + 1], in1=rs,
                                                   op0=ALU.mult, op1=ALU.mult), r=[ob, gb, rsb], w=[tb])
        k.op(eng, lambda e: e.scalar_tensor_tensor(out=x_ap[:, c, :], in0=tmp[:, 0:T], scalar=float(coef), in1=x_ap[:, c, :],
                                                   op0=ALU.mult, op1=ALU.add), r=[tb, xb], w=[xb])


def load_gains(cx, gains_dram):
    k = cx.k
    g, gb = k.tile("gains", [128, 12, 8], F32)
    k.dma("sp", g, gains_dram, w=[gb], sb=gb)
    return g, gb


def build_l1(nc=None, cx=None, io=None):
    nc, cx, io = _std(nc, cx, io)
    xT = io.inp("xT", [128, 8, TOK]); gains = io.inp("gains", [128, 12, 8])
    wg = io.inp("wg", [D, DFF]); wu = io.inp("wu", [D, DFF]); wd = io.inp("wd", [DFF, D])
    x1T = io.out("x1T", [128, 8, TOK], F32)
    h0T = io.out("h0T", [1024, TOK], BF16).rearrange("(kc p) t -> p kc t", p=128)
    k = cx.k
    g, gb = load_gains(cx, gains)
    outs = []
    for half in range(2):
        hs = slice(half * 1024, (half + 1) * 1024)
        x, xb = k.ring("x_res", [128, 8, 1024], F32, 1)
        k.dma("sp", x, xT[:, :, hs], w=[xb], sb=xb)
        ffn_half(cx, x, xb, 2, wg, wu, wd, g[:, 0, :], g[:, 1, :], gb)
        k.dma("sp", x1T[:, :, hs], x, r=[xb], sb=xb)
        hh, hhb = k.ring("h_bf", [128, 8, 1024], BF16, 1)
        for tt in range(2):
            sl = slice(tt * 512, (tt + 1) * 512)
            norm_bf16(cx, x[:, :, sl], xb, g[:, 2, :], gb, hh[:, :, sl], hhb)
        k.dma("sp", h0T[:, :, hs], hh, r=[hhb], sb=hhb)
        outs += [xb, hhb]
    k.wait_all("sp", outs)
    return nc


def fm(a):
    t = a.shape[0]
    return np.ascontiguousarray(a.T.reshape(8, 128, t).transpose(1, 0, 2))


def unfm(a):
    t = a.shape[2]
    return np.ascontiguousarray(a.transpose(1, 0, 2).reshape(1024, t).T)


def gains_layout(norm_gains):
    g = norm_gains.reshape(12, 8, 128).transpose(2, 0, 1)
    return np.ascontiguousarray(g)


SEQ = 8192
RW_EXPC = 0.6065306597126334
GN_EPS = 64e-5


def mm(k, ps_ap, pairs, rbufs, wbuf):
    n = len(pairs)
    for i, (l, r) in enumerate(pairs):
        k.op("pe", lambda e: e.matmul(ps_ap, lhsT=l, rhs=r, start=(i == 0), stop=(i == n - 1)),
             r=rbufs, w=[wbuf], inc=(i == n - 1))


class _Stop(Exception):
    pass


def build_l4(ntiles=16, stop=99, nc=None, cx=None, io=None):
    try:
        return _build_l4(ntiles, stop, nc, cx, io)
    except _Stop as e:
        return e.args[0]


def _build_l4(ntiles=16, stop=99, nc=None, cx=None, io=None):
    nc, cx, io = _std(nc, cx, io)
    dt_in = io.inp
    hall = dt_in("hall", [4096, TOK], BF16)
    hzero = dt_in("hzero", [1024, 1], BF16)
    W = {"r": dt_in("wr", [D, 256]), "k": dt_in("wk", [D, 256]), "v": dt_in("wv", [D, 256]),
         "w1": dt_in("w1", [D, 64]), "a1": dt_in("a1", [D, 64]), "g1": dt_in("g1", [D, 160])}
    w2d = dt_in("w2", [64, 256]); a2d = dt_in("a2", [64, 256]); g2d = dt_in("g2", [160, 256])
    mud = dt_in("mu", [128, 6, 8])
    vecd = dt_in("vecs", [128, 7, 2])
    maskd = dt_in("masks", [128, 3, 128])
    bonesd = dt_in("bones", [128, 128])
    resetd = dt_in("resetm", [128, 512])
    ygq = io.out("ygT", [1024, TOK], BF16).rearrange("(j c p) t -> j p c t", j=4, c=2, p=128)
    k = cx.k
    PS = lambda: k.ring("psr", [128, 512], F32, 4, psum=True)
    ybank = [k.ps("ybank%d" % i, [128, 512]) for i in range(2)]
    ybb = [Buf("ybank%d" % i) for i in range(2)]
    ybank2 = [k.ps("ybankb%d" % i, [128, 512]) for i in range(2)]
    ybb2 = [Buf("ybankb%d" % i) for i in range(2)]

    mu, mub = k.tile("mu", [128, 6, 8], F32); k.dma("sp", mu, mud, w=[mub], sb=mub)
    vec, vecb = k.tile("vecs", [128, 7, 2], F32); k.dma("sp", vec, vecd, w=[vecb], sb=vecb)
    msk, mskb = k.tile("masks", [128, 3, 128], F32); k.dma("sp", msk, maskd, w=[mskb], sb=mskb)
    bones, bonesb = k.tile("bones", [128, 128], F32); k.dma("sp", bones, bonesd, w=[bonesb], sb=bonesb)
    rstm, rstmb = k.tile("resetm", [128, 512], F32); k.dma("sp", rstm, resetd, w=[rstmb], sb=rstmb)
    W0, A0, KK, KA, RK, LNG, LNB = range(7)
    m4 = lambda i: msk[:, i, :].unsqueeze(1).to_broadcast([128, 4, 128])
    id4 = cx.ident.unsqueeze(1).to_broadcast([128, 4, 128])
    c_tiny = cx.const(1e-24)
    c_gneps = cx.const(GN_EPS)

    order = {"r": 0, "w1": 1, "k": 2, "v": 3, "a1": 4, "g1": 5}
    Wa, Wb, Wbuf = {}, {}, {}
    for nm, wd_ in W.items():
        n = wd_.shape[1]
        wa, wab = k.tile("wa_" + nm, [128, 8, n], BF16)
        wb, wbb = k.tile("wb_" + nm, [128, 8, n], BF16)
        Wa[nm], Wb[nm], Wbuf[nm] = wa, wb, [wab, wbb]
    import contextlib
    _outer = getattr(k, "stack", None)
    k.stack = contextlib.ExitStack()
    k.phase = getattr(k, "phase", 0)
    for nm, wd_ in W.items():
        n = wd_.shape[1]
        wa, wb = Wa[nm], Wb[nm]
        wab, wbb = Wbuf[nm]
        st, stb = k.ring("wstage", [128, 8, 256], F32, 1)
        k.dma("sp", st[:, :, 0:n], wd_.rearrange("(kc p) n -> p kc n", p=128), w=[stb], sb=stb)
        tmp, tmpb = k.ring("wstage2", [128, 8, 256], F32, 1)
        i = order[nm]
        k.op("dve", lambda e: e.tensor_tensor(out=tmp[:, :, 0:n], in0=st[:, :, 0:n],
                                              in1=mu[:, i, :].unsqueeze(2).to_broadcast([128, 8, n]), op=ALU.mult),
             r=[stb, mub], w=[tmpb])
        k.op("dve", lambda e: e.tensor_tensor(out=wa, in0=st[:, :, 0:n], in1=tmp[:, :, 0:n], op=ALU.subtract),
             r=[stb, tmpb], w=[wab])
        k.op("act", lambda e: e.activation(out=wb, in_=tmp[:, :, 0:n], func=AF.Copy), r=[tmpb], w=[wbb])
    k.barrier()
    k.stack.close()
    k.stack = _outer
    k.rings.pop("wstage"); k.rings.pop("wstage2")
    w2, w2b = k.tile("w2", [64, 256], BF16); k.dma("pool", w2, w2d, w=[w2b], sb=w2b)
    a2, a2b = k.tile("a2", [64, 256], BF16); k.dma("pool", a2, a2d, w=[a2b], sb=a2b)
    g2a, g2ab = k.tile("g2a", [128, 256], BF16); k.dma("pool", g2a, g2d[0:128, :], w=[g2ab], sb=g2ab)
    g2c, g2cb = k.tile("g2c", [32, 256], BF16); k.dma("pool", g2c, g2d[128:160, :], w=[g2cb], sb=g2cb)

    U = []
    for hd in range(4):
        pp = []
        for j in range(2):
            u, ub = k.tile("U%d_%d" % (hd, j), [128, 64], F32)
            k.op("pool", lambda e: e.memset(u, 0.0), w=[ub])
            pp.append((u, ub))
        U.append(pp)
    ucur = [0, 0, 0, 0]
    PL = []
    for pc in range(2):
        p_, pb_ = k.tile("PL%d" % pc, [128, 9], F32)
        k.op("pool", lambda e: e.memset(p_, 1.0), w=[pb_])
        PL.append((p_, pb_))
    outbufs = []
    if stop == 1:
        raise _Stop(nc)

    for ti in range(ntiles):
        t0 = ti * 512
        hb, hbb = k.ring("hb", [128, 8, 514], BF16, 2)
        dma_hall(k, hb[:, :, 1:513], hall, t0, 512, hbb)
        if t0 == 0:
            k.dma("sp", hb[:, :, 0:1], hzero.rearrange("(kc p) t -> p kc t", p=128), w=[hbb], sb=hbb, allow_slow_non_contiguous=True)
        else:
            dma_hall(k, hb[:, :, 0:1], hall, t0 - 1, 1, hbb, allow_slow_non_contiguous=True)

        def proj_pairs(nm, cols, tok=None):
            prs = []
            for kc in range(8):
                prs.append((Wa[nm][:, kc, cols], hb[:, kc, 1:513]))
                prs.append((Wb[nm][:, kc, cols], hb[:, kc, 0:512]))
            return prs

        FM = {}

        def evac(name, ps, pb, rows=128, func=AF.Copy, bias=None, dt=F32, extra=()):
            o, ob = k.ring("fm_" + name, [128, 512], dt, 1)
            kw = {}
            if bias is not None:
                kw["bias"] = bias
            k.op("act", lambda e: e.activation(out=o[0:rows, :], in_=ps[0:rows, :], func=func, **kw),
                 r=[pb] + list(extra), w=[ob])
            return o, ob

        for pc in range(2):
            cols = slice(pc * 128, (pc + 1) * 128)
            for nm in ("r", "k", "v"):
                ps, pb = PS()
                mm(k, ps, proj_pairs(nm, cols), [hbb] + Wbuf[nm], pb)
                FM[(nm, pc)] = evac("%s%d" % (nm, pc), ps, pb)
        vtok, vtokb = k.ring("vtok", [128, 4, 256], F32, 2)
        for half in range(2):
            ps, pb = PS()
            for bi in range(2):
                blk = half * 2 + bi
                prs = []
                for kc in range(8):
                    prs.append((hb[:, kc, 1 + blk * 128:1 + (blk + 1) * 128], Wa["v"][:, kc, :]))
                    prs.append((hb[:, kc, blk * 128:(blk + 1) * 128], Wb["v"][:, kc, :]))
                mm(k, ps[:, bi * 256:(bi + 1) * 256], prs, [hbb] + Wbuf["v"], pb)
            k.op("act", lambda e: e.activation(out=vtok[:, half * 2:half * 2 + 2, :],
                                               in_=ps.rearrange("p (a b) -> p a b", a=2), func=AF.Copy),
                 r=[pb], w=[vtokb])
        ps, pb = PS(); mm(k, ps[0:64, :], proj_pairs("w1", slice(0, 64)), [hbb] + Wbuf["w1"], pb)
        hw, hwb = evac("hw", ps, pb, rows=64, func=AF.Tanh, dt=BF16)
        ps, pb = PS(); mm(k, ps[0:64, :], proj_pairs("a1", slice(0, 64)), [hbb] + Wbuf["a1"], pb)
        ha, hab = evac("ha", ps, pb, rows=64, dt=BF16)
        ps, pb = PS(); mm(k, ps, proj_pairs("g1", slice(0, 128)), [hbb] + Wbuf["g1"], pb)
        hg0, hg0b = evac("hg0", ps, pb, func=AF.Sigmoid, dt=BF16)
        ps, pb = PS(); mm(k, ps[0:32, :], proj_pairs("g1", slice(128, 160)), [hbb] + Wbuf["g1"], pb)
        hg1, hg1b = evac("hg1", ps, pb, rows=32, func=AF.Sigmoid, dt=BF16)
        for pc in range(2):
            cols = slice(pc * 128, (pc + 1) * 128)
            ps, pb = PS(); mm(k, ps, [(w2[:, cols], hw[0:64, :])], [w2b, hwb], pb)
            FM[("sgw", pc)] = evac("sgw%d" % pc, ps, pb, func=AF.Sigmoid, bias=vec[:, W0, pc:pc + 1], extra=[vecb])
            ps, pb = PS(); mm(k, ps, [(a2[:, cols], ha[0:64, :])], [a2b, hab], pb)
            FM[("a", pc)] = evac("a%d" % pc, ps, pb, func=AF.Sigmoid, bias=vec[:, A0, pc:pc + 1], extra=[vecb])
            ps, pb = PS(); mm(k, ps, [(g2a[:, cols], hg0), (g2c[:, cols], hg1[0:32, :])], [g2ab, g2cb, hg0b, hg1b], pb)
            FM[("g", pc)] = evac("g%d" % pc, ps, pb)

        if stop == 2:
            raise _Stop(nc)
        def tmpt(name, dt=F32, n=2):
            return k.ring("tmp", [128, 512], F32, 9)

        PR = {}
        for pc in range(2):
            r_, rb_ = FM[("r", pc)]; k_, kb_ = FM[("k", pc)]; a_, ab_ = FM[("a", pc)]; sg_, sgb_ = FM[("sgw", pc)]
            kk0, kk0b = tmpt("kk0")
            k.op("dve", lambda e: e.tensor_scalar(out=kk0, in0=k_, scalar1=vec[:, KK, pc:pc + 1], scalar2=None, op0=ALU.mult),
                 r=[kb_, vecb], w=[kk0b])
            sq, sqb = tmpt("sq")
            k.op("pool", lambda e: e.tensor_tensor(out=sq, in0=kk0, in1=kk0, op=ALU.mult), r=[kk0b], w=[sqb])
            ps, pb = PS(); mm(k, ps, [(bones, sq)], [bonesb, sqb], pb)
            rn, rnb = tmpt("rn")
            k.op("act", lambda e: e.activation(out=rn, in_=ps, func=AF.Sqrt, bias=c_tiny, scale=1.0), r=[pb, cx.b_consts], w=[rnb])
            k.op("dve", lambda e: e.reciprocal(out=rn, in_=rn), r=[rnb], w=[rnb])
            kap, kapb = tmpt("kap")
            k.op("dve", lambda e: e.tensor_tensor(out=kap, in0=kk0, in1=rn, op=ALU.mult), r=[kk0b, rnb], w=[kapb])
            am, amb = tmpt("am")
            k.op("dve", lambda e: e.tensor_scalar(out=am, in0=a_, scalar1=-1.0, scalar2=vec[:, KA, pc:pc + 1],
                                                  op0=ALU.add, op1=ALU.mult), r=[ab_, vecb], w=[amb])
            kp, kpb = k.ring("kp%d" % pc, [128, 512], F32, 1)
            k.op("dve", lambda e: e.scalar_tensor_tensor(out=kp, in0=am, scalar=1.0, in1=k_, op0=ALU.add, op1=ALU.mult),
                 r=[amb, kb_], w=[kpb])
            logd, logdb = tmpt("logd")
            k.op("pool", lambda e: e.tensor_scalar(out=logd, in0=sg_, scalar1=-RW_EXPC, scalar2=None, op0=ALU.mult),
                 r=[sgb_], w=[logdb])
            Lc, Lcb = tmpt("Lc")
            k.op("dve", lambda e: e.tensor_tensor_scan(out=Lc, data0=rstm, data1=logd, initial=0.0, op0=ALU.mult, op1=ALU.add),
                 r=[rstmb, logdb], w=[Lcb])
            Lm, Lmb = tmpt("Lm")
            k.op("pool", lambda e: e.tensor_tensor(out=Lm, in0=Lc, in1=logd, op=ALU.subtract), r=[Lcb, logdb], w=[Lmb])
            P_, Pb_ = tmpt("P"); Pi, Pib = tmpt("Pi"); Pp, Ppb = tmpt("Pp")
            k.op("act", lambda e: e.activation(out=P_, in_=Lc, func=AF.Exp), r=[Lcb], w=[Pb_])
            k.op("act", lambda e: e.activation(out=Pi, in_=Lc, func=AF.Exp, scale=-1.0), r=[Lcb], w=[Pib])
            k.op("act", lambda e: e.activation(out=Pp, in_=Lm, func=AF.Exp), r=[Lmb], w=[Ppb])
            pl, plb = PL[pc]
            k.op("dve", lambda e: e.tensor_copy(out=pl[:, 0:1], in_=pl[:, 8:9]), r=[plb], w=[plb])
            k.op("dve", lambda e: e.tensor_copy(out=pl[:, 1:9], in_=P_.rearrange("p (c t) -> p c t", t=64)[:, :, 63]),
                 r=[Pb_, plb], w=[plb])
            rt, rtb = k.ring("rt%d" % pc, [128, 512], F32, 1)
            kt, ktb = k.ring("kt%d" % pc, [128, 512], F32, 1)
            bt, btb = k.ring("bt%d" % pc, [128, 512], F32, 1)
            kkt, kktb = k.ring("kkt%d" % pc, [128, 512], F32, 1)
            k.op("dve", lambda e: e.tensor_tensor(out=rt, in0=r_, in1=P_, op=ALU.mult), r=[rb_, Pb_], w=[rtb])
            k.op("pool", lambda e: e.tensor_tensor(out=kt, in0=kap, in1=Pp, op=ALU.mult), r=[kapb, Ppb], w=[ktb])
            ka_, kab_ = tmpt("ka")
            k.op("pool", lambda e: e.tensor_tensor(out=ka_, in0=kap, in1=a_, op=ALU.mult), r=[kapb, ab_], w=[kab_])
            k.op("dve", lambda e: e.tensor_tensor(out=bt, in0=ka_, in1=Pi, op=ALU.mult), r=[kab_, Pib], w=[btb])
            k.op("pool", lambda e: e.tensor_tensor(out=kkt, in0=kp, in1=Pi, op=ALU.mult), r=[kpb, Pib], w=[kktb])
            rts, rtsb = k.ring("rts%d" % pc, [128, 512], F32, 1)
            kts, ktsb = k.ring("kts%d" % pc, [128, 512], F32, 1)
            plbc = pl[:, 0:8].unsqueeze(2).to_broadcast([128, 8, 64])
            k.op("dve", lambda e: e.tensor_tensor(out=rts.rearrange("p (c t) -> p c t", t=64),
                                                  in0=rt.rearrange("p (c t) -> p c t", t=64), in1=plbc, op=ALU.mult),
                 r=[rtb, plb], w=[rtsb])
            k.op("dve", lambda e: e.tensor_tensor(out=kts.rearrange("p (c t) -> p c t", t=64),
                                                  in0=kt.rearrange("p (c t) -> p c t", t=64), in1=plbc, op=ALU.mult),
                 r=[ktb, plb], w=[ktsb])
            PR[pc] = dict(rt=(rt, rtb), kt=(kt, ktb), bt=(bt, btb), kkt=(kkt, kktb), rts=(rts, rtsb), kts=(kts, ktsb),
                          kp=(kp, kpb))
        if stop == 3:
            raise _Stop(nc)
        for pc in range(2):
            for nm_ in ("rt", "kt", "bt", "kkt"):
                src_, srcb_ = PR[pc][nm_]
                sh, shb = k.ring("sh_%s%d" % (nm_, pc), [128, 512], BF16, 1)
                k.op("act", lambda e: e.activation(out=sh, in_=src_, func=AF.Copy), r=[srcb_], w=[shb])
                PR[pc][nm_ + "_h"] = (sh, shb)
        btok, btokb = k.ring("btok", [128, 4, 256], F32, 1)
        ktok, ktokb = k.ring("ktok", [128, 4, 256], F32, 1)
        for (src, dst, dstb) in (("bt", btok, btokb), ("kkt", ktok, ktokb)):
            for pc in range(2):
                s_, sb_ = PR[pc][src]
                ps, pb = PS()
                for blk in range(4):
                    k.op("pe", lambda e: e.transpose(out=ps[:, blk * 128:(blk + 1) * 128], in_=s_[:, blk * 128:(blk + 1) * 128],
                                                     identity=cx.ident), r=[sb_, cx.b_ident], w=[pb], inc=(blk == 3))
                k.op("act", lambda e: e.activation(out=dst[:, :, pc * 128:(pc + 1) * 128],
                                                   in_=ps.rearrange("p (a b) -> p a b", a=4), func=AF.Copy), r=[pb], w=[dstb])

        if stop == 4:
            raise _Stop(nc)
        HD = {}
        for hd in range(4):
            pc, hp = hd // 2, hd % 2
            rows = slice(hp * 64, hp * 64 + 64)
            hcols = slice(hd * 64, hd * 64 + 64)
            pr = PR[pc]

            def intra(lname, rname, mi, nm, depth=2, odt=F32):
                l_, lb_ = pr[lname + "_h"]; r2, rb2 = pr[rname + "_h"]
                ps, pb = PS()
                for blk in range(4):
                    bs = slice(blk * 128, (blk + 1) * 128)
                    mm(k, ps[:, bs], [(l_[rows, bs], r2[rows, bs])], [lb_, rb2], pb)
                o, ob = k.ring("im_" + nm, [128, 4, 128], odt, depth)
                k.op("dve", lambda e: e.tensor_tensor(out=o, in0=ps.rearrange("p (a b) -> p a b", a=4), in1=m4(mi), op=ALU.mult),
                     r=[pb, mskb], w=[ob])
                return o, ob

            Pm, Pmb = intra("kt", "bt", 0, "P", 2, BF16)
            Qm, Qmb = intra("bt", "kt", 1, "Q", 2, BF16)
            AkT, AkTb = intra("kkt", "kt", 1, "AkT%d" % hd, 1)
            QBT, QBTb = intra("bt", "rt", 2, "QBT%d" % hd, 1)
            QKT, QKTb = intra("kkt", "rt", 2, "QKT%d" % hd, 1)
            Rm, Rmb = k.ring("im_R", [128, 4, 128], F32, 2)
            k.op("pool", lambda e: e.tensor_tensor(out=Rm, in0=id4, in1=Qm, op=ALU.subtract), r=[cx.b_ident, Qmb], w=[Rmb])
            Rh, Rhb = k.ring("im_Rh", [128, 4, 128], BF16, 2)
            k.op("act", lambda e: e.activation(out=Rh, in_=Rm, func=AF.Copy), r=[Rmb], w=[Rhb])
            for lev in range(1, 6):
                if lev < 5:
                    ps, pb = PS()
                    for blk in range(4):
                        bs = slice(blk * 128, (blk + 1) * 128)
                        mm(k, ps[:, bs], [(Pm[:, blk, :], Qm[:, blk, :])], [Pmb, Qmb], pb)
                    Qn, Qnb = k.ring("im_Q", [128, 4, 128], BF16, 2)
                    k.op("act", lambda e: e.activation(out=Qn, in_=ps.rearrange("p (a b) -> p a b", a=4), func=AF.Copy), r=[pb], w=[Qnb])
                ps, pb = PS()
                for blk in range(4):
                    bs = slice(blk * 128, (blk + 1) * 128)
                    mm(k, ps[:, bs], [(Qm[:, blk, :], Pm[:, blk, :])], [Pmb, Qmb], pb)
                Pn, Pnb = k.ring("im_P", [128, 4, 128], BF16, 2)
                k.op("act", lambda e: e.activation(out=Pn, in_=ps.rearrange("p (a b) -> p a b", a=4), func=AF.Copy), r=[pb], w=[Pnb])
                ps, pb = PS()
                for blk in range(4):
                    bs = slice(blk * 128, (blk + 1) * 128)
                    mm(k, ps[:, bs], [(Pn[:, blk, :], Rh[:, blk, :])], [Pnb, Rhb], pb)
                Rn, Rnb = k.ring("im_R", [128, 4, 128], F32, 2) if lev < 5 else k.ring("im_Rfin%d" % hd, [128, 4, 128], F32, 1)
                k.op("dve", lambda e: e.tensor_tensor(out=Rn, in0=ps.rearrange("p (a b) -> p a b", a=4), in1=Rm, op=ALU.add),
                     r=[pb, Rmb], w=[Rnb])
                Pm, Pmb = Pn, Pnb
                if lev < 5:
                    Qm, Qmb = Qn, Qnb
                    Rh, Rhb = k.ring("im_Rh", [128, 4, 128], BF16, 2)
                    k.op("act", lambda e: e.activation(out=Rh, in_=Rn, func=AF.Copy), r=[Rnb], w=[Rhb])
                Rm, Rmb = Rn, Rnb
            if stop == 5:
                raise _Stop(nc)
            HD[hd] = (AkT, AkTb, QBT, QBTb, QKT, QKTb, Rm, Rmb)
        XA = {}
        for hd in range(4):
            hcols = slice(hd * 64, hd * 64 + 64)
            AkT, AkTb = HD[hd][0], HD[hd][1]
            ps, pb = PS()
            for hf in range(2):
                trows = slice(hf * 64, hf * 64 + 64)
                for blk in range(4):
                    mm(k, ps[trows, blk * 64:(blk + 1) * 64], [(AkT[trows, blk, trows], vtok[trows, blk, hcols])], [AkTb, vtokb], pb)
            xa, xab = k.ring("XaAll%d" % hd, [128, 4, 64], F32, 1)
            k.op("act", lambda e: e.activation(out=xa, in_=ps[:, 0:256].rearrange("p (a b) -> p a b", a=4), func=AF.Copy, scale=-1.0),
                 r=[pb], w=[xab])
            XA[hd] = (xa, xab)
        for c in range(8):
            blk, hf = c // 2, c % 2
            trows = slice(hf * 64, hf * 64 + 64)
            diag = slice(hf * 64, hf * 64 + 64)
            tcol = slice(c * 64, c * 64 + 64)
            st = {}
            for hd in range(4):
                pc, hp = hd // 2, hd % 2
                rows = slice(hp * 64, hp * 64 + 64)
                uo, uob = U[hd][ucur[hd]]
                un, unb = U[hd][1 - ucur[hd]]
                ucur[hd] = 1 - ucur[hd]
                kts, ktsb = PR[pc]["kts"]
                ps1b, pb1b = PS()
                mm(k, ps1b[trows, 0:64], [(kts[rows, tcol], uo[rows, :])], [ktsb, uob], pb1b)
                st[hd] = dict(rows=rows, hcols=slice(hd * 64, hd * 64 + 64), pc=pc, uo=uo, uob=uob, un=un, unb=unb, ps1b=ps1b, pb1b=pb1b)
            for hd in range(4):
                d_ = st[hd]
                xa, xab = XA[hd]
                X, Xb = k.ring("X", [128, 64], F32, 4)
                k.op("dve", lambda e: e.tensor_tensor(out=X[trows, :], in0=xa[trows, blk, :], in1=d_["ps1b"][trows, 0:64], op=ALU.subtract),
                     r=[xab, d_["pb1b"]], w=[Xb])
                d_["X"], d_["Xb"] = X, Xb
            for hd in range(4):
                d_ = st[hd]
                Rm, Rmb = HD[hd][6], HD[hd][7]
                ps2, pb2 = PS()
                mm(k, ps2[trows, 0:64], [(Rm[trows, blk, diag], d_["X"][trows, :])], [Rmb, d_["Xb"]], pb2)
                d_["ps2"], d_["pb2"] = ps2, pb2
            for hd in range(4):
                d_ = st[hd]
                SA, SAb = k.ring("SA", [128, 64], F32, 4)
                eng = "dve" if hd % 2 == 0 else "act"
                if eng == "dve":
                    k.op("dve", lambda e: e.tensor_copy(out=SA[trows, :], in_=d_["ps2"][trows, 0:64]), r=[d_["pb2"]], w=[SAb])
                else:
                    k.op("act", lambda e: e.activation(out=SA[trows, :], in_=d_["ps2"][trows, 0:64], func=AF.Copy), r=[d_["pb2"]], w=[SAb])
                d_["SA"], d_["SAb"] = SA, SAb
            for hd in range(4):
                d_ = st[hd]
                rows, hcols = d_["rows"], d_["hcols"]
                ps3, pb3 = PS()
                mm(k, ps3[rows, 0:64], [(ktok[trows, blk, hcols], vtok[trows, blk, hcols]),
                                        (btok[trows, blk, hcols], d_["SA"][trows, :])], [ktokb, btokb, vtokb, d_["SAb"]], pb3)
                d_["ps3"], d_["pb3"] = ps3, pb3
            for hd in range(4):
                d_ = st[hd]
                pc, rows, hcols = d_["pc"], d_["rows"], d_["hcols"]
                rts, rtsb = PR[pc]["rts"]
                QBT, QBTb, QKT, QKTb = HD[hd][2], HD[hd][3], HD[hd][4], HD[hd][5]
                mm(k, ybank[pc][rows, tcol], [(d_["uo"][rows, :], rts[rows, tcol])], [d_["uob"], rtsb], ybb[pc])
                mm(k, ybank2[pc][rows, tcol], [(d_["SA"][trows, :], QBT[trows, blk, diag]),
                                               (vtok[trows, blk, hcols], QKT[trows, blk, diag])],
                   [d_["SAb"], QBTb, QKTb, vtokb], ybb2[pc])
            for hd in range(4):
                d_ = st[hd]
                pc, rows = d_["pc"], d_["rows"]
                pl, plb = PL[pc]
                k.op("dve", lambda e: e.scalar_tensor_tensor(out=d_["un"][rows, :], in0=d_["uo"][rows, :], scalar=pl[rows, c:c + 1],
                                                             in1=d_["ps3"][rows, 0:64], op0=ALU.mult, op1=ALU.add),
                     r=[d_["uob"], plb, d_["pb3"]], w=[d_["unb"]])
        if stop == 6:
            raise _Stop(nc)
        for pc in range(2):
            r_, rb_ = FM[("r", pc)]; v_, vb_ = FM[("v", pc)]; g_, gb_ = FM[("g", pc)]
            kp, kpb = PR[pc]["kp"]
            ysb, ysbb = tmpt("ysb")
            k.op("act", lambda e: e.activation(out=ysb, in_=ybank[pc], func=AF.Copy), r=[ybb[pc]], w=[ysbb])
            k.op("dve", lambda e: e.tensor_tensor(out=ysb, in0=ysb, in1=ybank2[pc], op=ALU.add), r=[ysbb, ybb2[pc]], w=[ysbb])
            ysq, ysqb = tmpt("ysq")
            k.op("act", lambda e: e.activation(out=ysq, in_=ysb, func=AF.Square), r=[ysbb], w=[ysqb])
            psm, pbm = PS(); mm(k, psm, [(bones, ysb)], [bonesb, ysbb], pbm)
            pse, pbe = PS(); mm(k, pse, [(bones, ysq)], [bonesb, ysqb], pbe)
            mean, meanb = tmpt("mean")
            k.op("act", lambda e: e.activation(out=mean, in_=psm, func=AF.Copy, scale=1.0 / 64), r=[pbm], w=[meanb])
            var, varb = tmpt("var")
            k.op("dve", lambda e: e.tensor_tensor(out=var, in0=mean, in1=mean, op=ALU.mult), r=[meanb], w=[varb])
            k.op("dve", lambda e: e.scalar_tensor_tensor(out=var, in0=pse, scalar=1.0 / 64, in1=var, op0=ALU.mult, op1=ALU.subtract),
                 r=[pbe, varb], w=[varb])
            k.op("act", lambda e: e.activation(out=var, in_=var, func=AF.Sqrt, bias=c_gneps, scale=1.0), r=[varb, cx.b_consts], w=[varb])
            k.op("dve", lambda e: e.reciprocal(out=var, in_=var), r=[varb], w=[varb])
            yn, ynb = tmpt("yn")
            k.op("dve", lambda e: e.tensor_tensor(out=yn, in0=ysb, in1=mean, op=ALU.subtract), r=[ysbb, meanb], w=[ynb])
            k.op("dve", lambda e: e.tensor_tensor(out=yn, in0=yn, in1=var, op=ALU.mult), r=[ynb, varb], w=[ynb])
            k.op("dve", lambda e: e.tensor_scalar(out=yn, in0=yn, scalar1=vec[:, LNG, pc:pc + 1], scalar2=vec[:, LNB, pc:pc + 1],
                                                  op0=ALU.mult, op1=ALU.add), r=[ynb, vecb], w=[ynb])
            rk, rkb = tmpt("rk")
            k.op("pool", lambda e: e.tensor_tensor(out=rk, in0=r_, in1=kp, op=ALU.mult), r=[rb_, kpb], w=[rkb])
            k.op("pool", lambda e: e.tensor_scalar(out=rk, in0=rk, scalar1=vec[:, RK, pc:pc + 1], scalar2=None, op0=ALU.mult),
                 r=[rkb, vecb], w=[rkb])
            psr, pbr = PS(); mm(k, psr, [(bones, rk)], [bonesb, rkb], pbr)
            bon, bonb = tmpt("bon")
            k.op("dve", lambda e: e.tensor_tensor(out=bon, in0=psr, in1=v_, op=ALU.mult), r=[pbr, vb_], w=[bonb])
            k.op("pool", lambda e: e.tensor_tensor(out=yn, in0=yn, in1=bon, op=ALU.add), r=[ynb, bonb], w=[ynb])
            yo, yob = k.ring("yo", [128, 512], BF16, 2)
            k.op("dve", lambda e: e.tensor_tensor(out=yo, in0=yn, in1=g_, op=ALU.mult), r=[ynb, gb_], w=[yob])
            k.dma("sp", ygq[ti // 4][:, pc, (ti % 4) * 512:(ti % 4) * 512 + 512], yo, r=[yob], sb=yob)
            outbufs.append(yob)
    k.wait_all("sp", list({id(b): b for b in outbufs}.values()))
    return nc


def rwkv_consts():
    t = np.arange(128)
    same = (t[:, None] // 64) == (t[None, :] // 64)
    m_sl = (same & (t[None, :] < t[:, None])).astype(np.float32)
    m_su = (same & (t[:, None] < t[None, :])).astype(np.float32)
    m_iu = (same & (t[:, None] <= t[None, :])).astype(np.float32)
    masks = np.ascontiguousarray(np.stack([m_sl, m_su, m_iu], axis=1))
    bones = same.astype(np.float32)
    resetm = np.ones((128, 512), np.float32)
    resetm[:, ::64] = 0.0
    return masks, np.ascontiguousarray(bones), resetm


def l4_inputs(h1T_b, core_hg, P):
    cs = slice(core_hg * 256, (core_hg + 1) * 256)
    masks, bones, resetm = rwkv_consts()
    import ml_dtypes
    col = lambda v: np.ascontiguousarray(v[cs].reshape(2, 128).T)
    vecs = np.stack([col(P["c_w0"]), col(P["c_a0"]), col(P["c_k_k"]), col(P["c_k_a"]), col(P["c_r_k"].reshape(-1)),
                     col(P["c_ln_g"]), col(P["c_ln_b"])], axis=1)
    mu = np.ascontiguousarray(P["c_mu"].reshape(6, 8, 128).transpose(2, 0, 1))
    return {"hall": h1T_b, "hzero": np.zeros((1024, 1), ml_dtypes.bfloat16), "wr": np.ascontiguousarray(P["c_w_r"][:, cs]), "wk": np.ascontiguousarray(P["c_w_k"][:, cs]),
            "wv": np.ascontiguousarray(P["c_w_v"][:, cs]), "w1": P["c_w1"], "a1": P["c_a1"], "g1": P["c_g1"],
            "w2": np.ascontiguousarray(P["c_w2"][:, cs]), "a2": np.ascontiguousarray(P["c_a2"][:, cs]),
            "g2": np.ascontiguousarray(P["c_g2"][:, cs]), "mu": mu, "vecs": np.ascontiguousarray(vecs.astype(np.float32)),
            "masks": masks, "bones": bones, "resetm": resetm}


NEG = -30000.0
NQB = 32


def build_l2a(nqb=NQB, ntile1=16, stop=99, nc=None, cx=None, io=None):
    try:
        return _build_l2a(nqb, ntile1, stop, nc, cx, io)
    except _Stop as e:
        return e.args[0]


def _build_l2a(nqb=NQB, ntile1=16, stop=99, nc=None, cx=None, io=None):
    nc, cx, io = _std(nc, cx, io)
    din = io.inp
    hall = din("hall", [4096, TOK], BF16)
    psel_d = din("psel", [128, 2])
    wq_d = din("wq", [D, 256]); wks_d = din("wks", [D, 64]); wkw_d = din("wkw", [D, 64])
    wkv_d = din("wkvc", [D, 128]); wv2_d = din("wv2", [D, 128]); wg_d = din("wgate", [D, 12])
    tabA_c = din("tabA_c", [64, SEQ]); tabA_s = din("tabA_s", [64, SEQ])
    tabB_c = din("tabB_c", [128, SEQ]); tabB_s = din("tabB_s", [128, SEQ])
    tabQ_c = din("tabQ_c", [64, NQB * 128]); tabQ_s = din("tabQ_s", [64, NQB * 128])
    rot_d = din("rotT", [128, 128])
    ebig_d = din("ebig", [128, SEQ], BF16)
    w1_d = din("w1kv", [128, 32, 64]); w2_d = din("w2kv", [128, 64]); pe_d = din("pekv", [128, 32, 2])
    vcx_d = din("vcx", [128, 4, 129], BF16)
    cm_d = din("cmask", [128, 9, 128], BF16)
    sm_d = din("smask", [128, 6, 128], BF16)
    fv_d = din("fv", [NQB, 128, 2, 128])
    oa = io.out("oa", [NQB * 128, 256], BF16)
    k = cx.k
    PS = lambda: k.ring("psr", [128, 512], F32, 4, psum=True)
    held = {}
    for nm in ("oc0", "oc1", "os", "ow"):
        held[nm] = (k.ps("h_" + nm, [128, 512]), Buf("h_" + nm))
    psel, pselb = k.tile("psel", [128, 2], F32)
    k.dma("sp", psel, psel_d, w=[pselb], sb=pselb)

    def cload(name, dram, shape, dt, q="sp"):
        t, b = k.tile(name, shape, dt)
        k.dma(q, t, dram, w=[b], sb=b)
        return t, b

    rot, rotb = cload("rot", rot_d, [128, 128], F32)
    ebig, ebigb = cload("ebig", ebig_d, [128, SEQ], BF16)
    cm, cmb = cload("cm", cm_d, [128, 9, 128], BF16)
    sm, smb = cload("sm", sm_d, [128, 6, 128], BF16)
    identb, identbb = k.tile("identb", [128, 128], BF16)
    k.op("act", lambda e: e.activation(out=identb, in_=cx.ident, func=AF.Copy), r=[cx.b_ident], w=[identbb])
    ones_f, ones_fb = k.tile("ones_f", [128, 128], F32)
    k.op("pool", lambda e: e.memset(ones_f, 1.0), w=[ones_fb])
    roth, rothb = k.tile("roth", [128, 128], BF16)
    k.op("act", lambda e: e.activation(out=roth, in_=rot, func=AF.Copy), r=[rotb], w=[rothb])

    def wcast(name, dram, n, q="pool"):
        t, b = k.tile(name, [128, 8, n], BF16)
        k.dma(q, t, dram.rearrange("(kc p) n -> p kc n", p=128), w=[b], sb=b)
        return t, b

    wq, wqb = wcast("wq", wq_d, 256); wks, wksb = wcast("wks", wks_d, 64); wkw, wkwb = wcast("wkw", wkw_d, 64)
    wkv, wkvb = wcast("wkv", wkv_d, 128); wv2, wv2b = wcast("wv2", wv2_d, 128); wgt, wgtb = wcast("wgt", wg_d, 12)
    w1, w1b = k.tile("w1", [128, 32, 64], BF16); k.dma("pool", w1, w1_d, w=[w1b], sb=w1b)
    w2, w2b = k.tile("w2", [128, 64], BF16); k.dma("pool", w2, w2_d, w=[w2b], sb=w2b)
    pe, peb = k.tile("pe", [128, 32, 2], BF16); k.dma("pool", pe, pe_d, w=[peb], sb=peb)

    ksel, kselb = k.tile("ksel", [128, SEQ], BF16)
    kwin, kwinb = k.tile("kwin", [128, SEQ], BF16)
    kvc, kvcb = k.tile("kvc", [128, SEQ + 32], BF16)
    k.op("pool", lambda e: e.memset(kvc[:, SEQ:SEQ + 32], 0.0), w=[kvcb])
    vsel, vselb = k.tile("vsel", [128, 64, 96], BF16)
    vwin, vwinb = k.tile("vwin", [128, 64, 96], BF16)
    for t_, b_ in ((ksel, kselb), (kwin, kwinb)):
        k.op("pool", lambda e: e.memset(t_[64:128, :], 0.0), w=[b_])
        k.op("pool", lambda e: e.memset(t_[64:65, :], 1.0), w=[b_])
    for t_, b_ in ((vsel, vselb), (vwin, vwinb)):
        k.op("pool", lambda e: e.memset(t_[:, :, 64:65], 1.0), w=[b_])
    kmax2, kmax2b = k.tile("kmax2", [128, 1], F32)
    k.op("pool", lambda e: e.memset(kmax2, 0.0), w=[kmax2b])

    def upd_kmax(src, srcb, rows, n):
        sq, sqb = k.ring("ksq", [128, 512], F32, 2)
        k.op("pool", lambda e: e.tensor_tensor(out=sq[rows, 0:n], in0=src, in1=src, op=ALU.mult), r=[srcb], w=[sqb])
        ps, pb = PS()
        mm(k, ps[:, 0:n], [(ones_f[rows, :], sq[rows, 0:n])], [ones_fb, sqb], pb)
        k.op("act", lambda e: e.activation(out=sq[:, 0:n], in_=ps[:, 0:n], func=AF.Copy), r=[pb], w=[sqb])
        mx, mxb = k.ring("kmx", [128, 8], F32, 2)
        k.op("dve", lambda e: e.max(out=mx, in_=sq[:, 0:n]), r=[sqb], w=[mxb])
        k.op("dve", lambda e: e.tensor_tensor(out=kmax2, in0=kmax2, in1=mx[:, 0:1], op=ALU.max), r=[mxb, kmax2b], w=[kmax2b])

    def rope_store(ps, pb, rows, tc_d, ts_d, t0, n, dst, dstb, rbase=0):
        R = slice(rbase, rbase + rows)
        xk, xkb = k.ring("xk", [128, 512], F32, 2)
        k.op("act", lambda e: e.activation(out=xk[R, 0:n], in_=ps[R, 0:n], func=AF.Copy), r=[pb], w=[xkb])
        tc_, tcb = k.ring("tabc", [128, 512], F32, 2)
        ts_, tsb = k.ring("tabs", [128, 512], F32, 2)
        k.dma("sp", tc_[R, 0:n], tc_d[:, t0:t0 + n], w=[tcb], sb=tcb)
        k.dma("sp", ts_[R, 0:n], ts_d[:, t0:t0 + n], w=[tsb], sb=tsb)
        xkh, xkhb = k.ring("xkh", [128, 512], BF16, 2)
        k.op("act", lambda e: e.activation(out=xkh[R, 0:n], in_=ps[R, 0:n], func=AF.Copy), r=[pb], w=[xkhb])
        ps2, pb2 = PS()
        mm(k, ps2[R, 0:n], [(roth[R, R], xkh[R, 0:n])], [rothb, xkhb], pb2)
        t1, t1b = k.ring("rp1", [128, 512], F32, 2)
        t2, t2b = k.ring("rp2", [128, 512], F32, 2)
        k.op("pool", lambda e: e.tensor_tensor(out=t1[R, 0:n], in0=xk[R, 0:n], in1=tc_[R, 0:n], op=ALU.mult), r=[xkb, tcb], w=[t1b])
        k.op("dve", lambda e: e.tensor_tensor(out=t2[R, 0:n], in0=ps2[R, 0:n], in1=ts_[R, 0:n], op=ALU.mult), r=[pb2, tsb], w=[t2b])
        k.op("dve", lambda e: e.tensor_tensor(out=dst, in0=t1[R, 0:n], in1=t2[R, 0:n], op=ALU.add), r=[t1b, t2b], w=[dstb])

    if stop == 10:
        raise _Stop(nc)
    for ti in range(ntile1):
        t0 = ti * 512
        hb, hbb = k.ring("hb", [128, 8, 512], BF16, 2)
        dma_hall(k, hb, hall, t0, 512, hbb)
        for (w_, wb_, dst, dstb) in ((wks, wksb, ksel, kselb), (wkw, wkwb, kwin, kwinb)):
            ps, pb = PS()
            mm(k, ps[0:64, :], [(w_[:, kc, :], hb[:, kc, :]) for kc in range(8)], [wb_, hbb], pb)
            if stop == 11 + 100 * ti:
                raise _Stop(nc)
            rope_store(ps, pb, 64, tabA_c, tabA_s, t0, 512, dst[0:64, t0:t0 + 512], dstb)
            if stop == 12 + 100 * ti:
                raise _Stop(nc)
            upd_kmax(dst[0:64, t0:t0 + 512], dstb, slice(0, 64), 512)
            if stop == 13 + 100 * ti:
                raise _Stop(nc)
        if stop == 14 + 100 * ti:
            raise _Stop(nc)
        ps, pb = PS()
        mm(k, ps, [(wkv[:, kc, :], hb[:, kc, :]) for kc in range(8)], [wkvb, hbb], pb)
        rope_store(ps, pb, 128, tabB_c, tabB_s, t0, 512, kvc[:, t0:t0 + 512], kvcb)
        if stop == 15 + 100 * ti:
            raise _Stop(nc)
        ps, pb = PS()
        for blk in range(4):
            mm(k, ps[:, blk * 128:(blk + 1) * 128], [(hb[:, kc, blk * 128:(blk + 1) * 128], wv2[:, kc, :]) for kc in range(8)],
               [wv2b, hbb], pb)
        pv = ps.rearrange("p (a b) -> p a b", a=4)
        k.op("act", lambda e: e.activation(out=vsel[:, ti * 4:ti * 4 + 4, 0:64], in_=pv[:, :, 0:64], func=AF.Copy), r=[pb], w=[vselb])
        k.op("act", lambda e: e.activation(out=vwin[:, ti * 4:ti * 4 + 4, 0:64], in_=pv[:, :, 64:128], func=AF.Copy), r=[pb], w=[vwinb])

    if stop == 1:
        raise _Stop(nc)
    kc, kcb = k.tile("kc", [128, 512], BF16)
    k.op("pool", lambda e: e.memset(kc, 0.0), w=[kcb])
    k.op("pool", lambda e: e.memset(kc[64:65, :], 1.0), w=[kcb])
    vcx, vcxb = k.tile("vcx", [128, 4, 256], BF16)
    k.dma("sp", vcx[:, :, 64:193], vcx_d, w=[vcxb], sb=vcxb)
    kv16 = kvc.rearrange("p (n s) -> p n s", s=16)
    hid, hidb = k.tile("hid", [128, 512], BF16)
    k.op("pool", lambda e: e.memset(hid, 0.0), w=[hidb])
    for R in (slice(0, 64), slice(64, 128)):
        psb, pbb = PS()
        mm(k, psb[R, 0:2], [(w1[R, l, :], pe[R, l, :]) for l in range(32)], [w1b, peb], pbb)
        bia, biab = k.ring("cbias", [128, 1], F32, 2)
        k.op("act", lambda e: e.activation(out=bia[R, :], in_=psb[R, 0:1], func=AF.Copy), r=[pbb], w=[biab])
        ps, pb = PS()
        mm(k, ps[R, 0:512], [(w1[R, l, :], kv16[R, (l // 16):(l // 16) + 512, l % 16]) for l in range(32)], [w1b, kvcb], pb)
        k.op("act", lambda e: e.activation(out=hid[R, :], in_=ps[R, :], func=AF.Silu, bias=bia[R, :]), r=[pb, biab], w=[hidb])
    ps, pb = PS()
    mm(k, ps[0:64, :], [(w2[0:64, :], hid[0:64, :])], [w2b, hidb], pb)
    k.op("act", lambda e: e.activation(out=kc[0:64, :], in_=ps[0:64, :], func=AF.Copy), r=[pb], w=[kcb])
    upd_kmax(kc[0:64, 0:512], kcb, slice(0, 64), 512)
    ps, pb = PS()
    for ch in range(4):
        mm(k, ps[:, ch * 64:(ch + 1) * 64], [(hid[64:128, ch * 128:(ch + 1) * 128], w2[64:128, :])], [w2b, hidb], pb)
    k.op("act", lambda e: e.activation(out=vcx[:, :, 0:64], in_=ps[:, 0:256].rearrange("p (a b) -> p a b", a=4), func=AF.Copy),
         r=[pb], w=[vcxb])
    nkm, nkmb = k.tile("nkm", [128, 1], F32)
    k.op("act", lambda e: e.activation(out=nkm, in_=kmax2, func=AF.Sqrt), r=[kmax2b], w=[nkmb])
    k.op("dve", lambda e: e.tensor_scalar(out=nkm, in0=nkm, scalar1=-1.0, scalar2=None, op0=ALU.mult), r=[nkmb], w=[nkmb])

    if stop == 2:
        raise _Stop(nc)
    outb = []

    def prep(i):
        q0 = i * 128
        hqc, hqcb = k.ring("hqc", [128, 2, 8, 128], BF16, 2)
        for pp in range(2):
            dma_hall(k, hqc[:, pp], hall, (2 * i + pp) * 128, 128, hqcb)
        hqt, hqtb = k.ring("hqt", [128, 8, 128], BF16, 2)
        hqb, hqbb = k.ring("hqb", [128, 8, 128], BF16, 2)
        k.op("dve", lambda e: e.tensor_scalar(out=hqt, in0=hqc[:, 0], scalar1=psel[:, 0:1], scalar2=None, op0=ALU.mult),
             r=[hqcb, pselb], w=[hqtb])
        k.op("dve", lambda e: e.scalar_tensor_tensor(out=hqb, in0=hqc[:, 1], scalar=psel[:, 1:2], in1=hqt, op0=ALU.mult, op1=ALU.add),
             r=[hqcb, pselb, hqtb], w=[hqbb])
        qa, qab = k.ring("qa", [128, 512], BF16, 2)
        k.op("pool", lambda e: e.memset(qa[64:128, :], 0.0), w=[qab])
        qf, qfb = k.ring("qf", [128, 512], F32, 2)
        tcq, tcqb = k.ring("tcq", [128, 128], F32, 2); tsq, tsqb = k.ring("tsq", [128, 128], F32, 2)
        k.dma("sp", tcq[0:64, :], tabQ_c[:, q0:q0 + 128], w=[tcqb], sb=tcqb)
        k.dma("sp", tsq[0:64, :], tabQ_s[:, q0:q0 + 128], w=[tsqb], sb=tsqb)
        ps, pb = PS()
        for g in range(4):
            mm(k, ps[0:64, g * 128:(g + 1) * 128], [(wq[:, kc, g * 64:(g + 1) * 64], hqb[:, kc, :]) for kc in range(8)], [wqb, hqbb], pb)
        xq, xqb = k.ring("xq", [128, 512], F32, 2)
        k.op("act", lambda e: e.activation(out=xq[0:64, :], in_=ps[0:64, :], func=AF.Copy), r=[pb], w=[xqb])
        xqh, xqhb = k.ring("xqh", [128, 512], BF16, 2)
        k.op("act", lambda e: e.activation(out=xqh[0:64, :], in_=ps[0:64, :], func=AF.Copy), r=[pb], w=[xqhb])
        ps2, pb2 = PS()
        mm(k, ps2[0:64, :], [(roth[0:64, 0:64], xqh[0:64, :])], [rothb, xqhb], pb2)
        v3 = lambda a: a.rearrange("p (g t) -> p g t", g=4)
        bc4 = lambda a: a.unsqueeze(1).to_broadcast([64, 4, 128])
        k.op("pool", lambda e: e.tensor_tensor(out=v3(xq[0:64, :]), in0=v3(xq[0:64, :]), in1=bc4(tcq[0:64, :]), op=ALU.mult),
             r=[xqb, tcqb], w=[xqb])
        k.op("dve", lambda e: e.tensor_tensor(out=v3(qf[0:64, :]), in0=v3(ps2[0:64, :]), in1=bc4(tsq[0:64, :]), op=ALU.mult),
             r=[pb2, tsqb], w=[qfb])
        k.op("dve", lambda e: e.tensor_tensor(out=qf[0:64, :], in0=qf[0:64, :], in1=xq[0:64, :], op=ALU.add), r=[qfb, xqb], w=[qfb])
        k.op("act", lambda e: e.activation(out=qa[0:64, :], in_=qf[0:64, :], func=AF.Copy), r=[qfb], w=[qab])
        k.op("pool", lambda e: e.tensor_tensor(out=xq[0:64, :], in0=qf[0:64, :], in1=qf[0:64, :], op=ALU.mult), r=[qfb, xqb], w=[xqb])
        ps3, pb3 = PS()
        mm(k, ps3[64:65, :], [(ones_f[0:64, 0:1], xq[0:64, :])], [ones_fb, xqb], pb3)
        mrow, mrowb = k.ring("mrow", [128, 512], F32, 2)
        k.op("act", lambda e: e.activation(out=mrow[64:65, :], in_=ps3[64:65, :], func=AF.Sqrt), r=[pb3], w=[mrowb])
        k.op("dve", lambda e: e.tensor_scalar(out=qa[64:65, :], in0=mrow[64:65, :], scalar1=nkm[64:65, 0:1], scalar2=None, op0=ALU.mult),
             r=[mrowb, nkmb], w=[qab])
        psg, pbg = PS()
        mm(k, psg[:, 0:12], [(hqb[:, kc, :], wgt[:, kc, :]) for kc in range(8)], [wgtb, hqbb], pbg)
        gt, gtb = k.ring("gates", [128, 12], F32, 2)
        k.op("act", lambda e: e.activation(out=gt, in_=psg[:, 0:12], func=AF.Sigmoid), r=[pbg], w=[gtb])

        return qa, qab, gt, gtb

    nxt = prep(0)
    for i in range(nqb):
        q0 = i * 128
        qa, qab, gt, gtb = nxt
        if stop == 3:
            raise _Stop(nc)

        def score_tile(kaug, kaugb, ktile_cols, masks):
            ps, pb = PS()
            n = 1 + len(masks)
            k.op("pe", lambda e: e.matmul(ps, lhsT=kaug[:, ktile_cols], rhs=qa, start=True, stop=(n == 1)),
                 r=[kaugb, qab], w=[pb], inc=(n == 1))
            for mi, (ml, mlb, mr, mrb) in enumerate(masks):
                last = (mi == len(masks) - 1)
                k.op("pe", lambda e: e.matmul(ps.rearrange("p (g t) -> p g t", g=4), lhsT=ml,
                                              rhs=mr.unsqueeze(1).to_broadcast([128, 4, 128]), start=False, stop=last),
                     r=[mlb, mrb], w=[pb], inc=last)
            eT, eTb = k.ring("eT", [128, 512], BF16, 3)
            k.op("act", lambda e: e.activation(out=eT, in_=ps, func=AF.Exp), r=[pb], w=[eTb])
            return eT, eTb

        qi_e = 2 * i
        last = (8 * qi_e + 6) // 128
        oc = [held["oc0"], held["oc1"]]
        def cmp_pv(cc, eT, eTb):
            for g in range(4):
                o_, ob_ = oc[g // 2]
                k.op("pe", lambda e: e.matmul(o_[:, (g % 2) * 256:(g % 2) * 256 + 193], lhsT=eT[:, g * 128:(g + 1) * 128],
                                              rhs=vcx[:, cc, 0:193], start=(cc == 0 and g % 2 == 0), stop=(cc == last and g % 2 == 1),
                                              skip_group_check=True),
                     r=[eTb, vcxb], w=[ob_], inc=(g == 3))
        pend = None
        for cc in range(last + 1):
            masks = []
            if cc == last:
                masks.append((identb, identbb, cm[:, i % 8, :], cmb))
            elif cc == last - 1 and i % 8 == 0:
                masks.append((identb, identbb, cm[:, 8, :], cmb))
            eT, eTb = score_tile(kc, kcb, slice(cc * 128, (cc + 1) * 128), masks)
            if pend is not None:
                cmp_pv(*pend)
            pend = (cc, eT, eTb)
        cmp_pv(*pend)
        if stop == 6:
            raise _Stop(nc)
        ow_, owb_ = held["ow"]
        wt = [kt for kt in range(2 * i - 4, 2 * i + 2) if kt >= 0]
        def win_pv(kt, eT, eTb):
            for g in range(4):
                k.op("pe", lambda e: e.matmul(ow_[:, g * 65:(g + 1) * 65], lhsT=eT[:, g * 128:(g + 1) * 128], rhs=vwin[:, kt, 0:65],
                                              start=(kt == wt[0] and g == 0), stop=(kt == wt[-1] and g == 3), skip_group_check=True),
                     r=[eTb, vwinb], w=[owb_], inc=(g == 3))
        pend = None
        for kt in wt:
            cols = slice(kt * 128, (kt + 1) * 128)
            pos = kt - (2 * i - 4)
            masks = []
            mi = {0: 2, 1: 3, 4: 4, 5: 5}.get(pos)
            if mi is not None:
                masks.append((identb, identbb, sm[:, mi, :], smb))
            eT, eTb = score_tile(kwin, kwinb, cols, masks)
            if pend is not None:
                win_pv(*pend)
            pend = (kt, eT, eTb)
        win_pv(*pend)
        if stop == 4:
            raise _Stop(nc)
        zc, zcb = k.ring("zc", [128, 4], F32, 2)
        for g in range(4):
            o_, ob_ = oc[g // 2]
            k.op("dve", lambda e: e.tensor_scalar(out=zc[:, g:g + 1], in0=o_[:, (g % 2) * 256 + 64:(g % 2) * 256 + 65], scalar1=1e-30,
                                                  scalar2=None, op0=ALU.max), r=[ob_], w=[zcb])
        k.op("dve", lambda e: e.reciprocal(out=zc, in_=zc), r=[zcb], w=[zcb])
        imp, impb = k.ring("imp", [128, 128], F32, 2)
        for g in range(4):
            o_, ob_ = oc[g // 2]
            src = o_[:, (g % 2) * 256 + 65:(g % 2) * 256 + 193]
            if g == 0:
                k.op("dve", lambda e: e.tensor_scalar(out=imp, in0=src, scalar1=zc[:, 0:1], scalar2=None, op0=ALU.mult),
                     r=[ob_, zcb], w=[impb])
            else:
                k.op("dve", lambda e: e.scalar_tensor_tensor(out=imp, in0=src, scalar=zc[:, g:g + 1], in1=imp, op0=ALU.mult, op1=ALU.add),
                     r=[ob_, zcb, impb], w=[impb])
        fv, fvb = k.ring("fv", [128, 2, 128], F32, 2)
        k.dma("sp", fv, fv_d[i], w=[fvb], sb=fvb)
        k.op("dve", lambda e: e.tensor_tensor(out=imp, in0=imp, in1=fv[:, 0, :], op=ALU.mult), r=[impb, fvb], w=[impb])
        k.op("dve", lambda e: e.tensor_tensor(out=imp, in0=imp, in1=fv[:, 1, :], op=ALU.add), r=[impb, fvb], w=[impb])
        m8, m8b = k.ring("m8", [128, 16], F32, 2)
        wk_, wkb_ = k.ring("impw", [128, 128], F32, 2)
        k.op("dve", lambda e: e.max(out=m8[:, 0:8], in_=imp), r=[impb], w=[m8b])
        k.op("dve", lambda e: e.match_replace(out=wk_, in_to_replace=m8[:, 0:8], in_values=imp, imm_value=-3e38), r=[m8b, impb], w=[wkb_])
        k.op("dve", lambda e: e.max(out=m8[:, 8:16], in_=wk_), r=[wkb_], w=[m8b])
        k.op("dve", lambda e: e.tensor_scalar(out=wk_, in0=imp, scalar1=m8[:, 15:16], scalar2=None, op0=ALU.is_ge), r=[impb, m8b], w=[wkb_])
        k.op("dve", lambda e: e.tensor_scalar(out=wk_, in0=wk_, scalar1=-1.0, scalar2=-NEG, op0=ALU.add, op1=ALU.mult), r=[wkb_], w=[wkb_])
        pst, pbt = PS()
        k.op("pe", lambda e: e.transpose(out=pst[:, 0:128], in_=wk_, identity=cx.ident), r=[wkb_, cx.b_ident], w=[pbt])
        nbT, nbTb = k.ring("nbT", [128, 128], BF16, 2)
        k.op("act", lambda e: e.activation(out=nbT, in_=pst[:, 0:128], func=AF.Copy), r=[pbt], w=[nbTb])

        if stop == 5:
            raise _Stop(nc)
        os_, osb_ = held["os"]
        nkt = 2 * i + 2
        def sel_pv(kt, eT, eTb):
            for g in range(4):
                k.op("pe", lambda e: e.matmul(os_[:, g * 65:(g + 1) * 65], lhsT=eT[:, g * 128:(g + 1) * 128], rhs=vsel[:, kt, 0:65],
                                              start=(kt == 0 and g == 0), stop=(kt == nkt - 1 and g == 3), skip_group_check=True),
                     r=[eTb, vselb], w=[osb_], inc=(g == 3))
        pend = None
        for kt in range(nkt):
            cols = slice(kt * 128, (kt + 1) * 128)
            masks = [(ebig[:, cols], ebigb, nbT, nbTb)]
            if kt == 2 * i:
                masks.append((identb, identbb, sm[:, 0, :], smb))
            elif kt == 2 * i + 1:
                masks.append((identb, identbb, sm[:, 1, :], smb))
            eT, eTb = score_tile(ksel, kselb, cols, masks)
            if pend is not None:
                sel_pv(*pend)
            pend = (kt, eT, eTb)
        sel_pv(*pend)
        if i + 1 < nqb:
            nxt = prep(i + 1)
        sc, scb = k.ring("sc", [128, 12], F32, 2)
        k.op("pool", lambda e: e.memset(sc, 1.0), w=[scb])
        for g in range(4):
            k.op("dve", lambda e: e.tensor_scalar(out=sc[:, g * 3 + 1:g * 3 + 2], in0=os_[:, g * 65 + 64:g * 65 + 65], scalar1=1e-30,
                                                  scalar2=None, op0=ALU.max), r=[osb_], w=[scb])
            k.op("dve", lambda e: e.tensor_scalar(out=sc[:, g * 3 + 2:g * 3 + 3], in0=ow_[:, g * 65 + 64:g * 65 + 65], scalar1=1e-30,
                                                  scalar2=None, op0=ALU.max), r=[owb_], w=[scb])
        k.op("dve", lambda e: e.reciprocal(out=sc, in_=sc), r=[scb], w=[scb])
        for g in range(4):
            k.op("dve", lambda e: e.tensor_copy(out=sc[:, g * 3:g * 3 + 1], in_=zc[:, g:g + 1]), r=[zcb], w=[scb])
        k.op("dve", lambda e: e.tensor_tensor(out=sc, in0=sc, in1=gt, op=ALU.mult), r=[scb, gtb], w=[scb])
        ot, otb = k.ring("oat", [128, 256], F32, 2)
        for g in range(4):
            o_, ob_ = oc[g // 2]
            dst = ot[:, g * 64:(g + 1) * 64]
            k.op("dve", lambda e: e.tensor_scalar(out=dst, in0=o_[:, (g % 2) * 256:(g % 2) * 256 + 64], scalar1=sc[:, g * 3:g * 3 + 1],
                                                  scalar2=None, op0=ALU.mult), r=[ob_, scb], w=[otb])
            k.op("dve", lambda e: e.scalar_tensor_tensor(out=dst, in0=os_[:, g * 65:g * 65 + 64], scalar=sc[:, g * 3 + 1:g * 3 + 2], in1=dst,
                                                         op0=ALU.mult, op1=ALU.add), r=[osb_, scb, otb], w=[otb])
            k.op("dve", lambda e: e.scalar_tensor_tensor(out=dst, in0=ow_[:, g * 65:g * 65 + 64], scalar=sc[:, g * 3 + 2:g * 3 + 3], in1=dst,
                                                         op0=ALU.mult, op1=ALU.add), r=[owb_, scb, otb], w=[otb])
        otc, otcb = k.ring("oatc", [128, 256], BF16, 2)
        k.op("act", lambda e: e.activation(out=otc, in_=ot, func=AF.Copy), r=[otb], w=[otcb])
        k.dma("sp", oa[q0:q0 + 128, :], otc, r=[otcb], sb=otcb)
        outb.append(otcb)
    k.wait_all("sp", list({id(b): b for b in outb}.values()))
    return nc


def _bf16(a):
    import ml_dtypes
    return np.ascontiguousarray(a).astype(ml_dtypes.bfloat16)


def nsa_consts():
    inv = (10000.0 ** (-np.arange(0, 64, 2, dtype=np.float32) / 64)).astype(np.float32)
    ang = np.arange(SEQ, dtype=np.float32)[:, None] * inv[None, :]
    cos, sin = np.cos(ang).astype(np.float32), np.sin(ang).astype(np.float32)
    tA_c = np.ascontiguousarray(np.concatenate([cos, cos], 1).T)
    tA_s = np.ascontiguousarray(np.concatenate([sin, sin], 1).T)
    tB_c = np.ascontiguousarray(np.concatenate([tA_c, np.ones_like(tA_c)], 0))
    tB_s = np.ascontiguousarray(np.concatenate([tA_s, np.zeros_like(tA_s)], 0))
    rot = np.zeros((128, 128), np.float32)
    for m in range(128):
        mm_ = m % 64
        if mm_ < 32:
            rot[m + 32, m] = -1.0
        else:
            rot[m - 32, m] = 1.0
    x = np.arange(SEQ)
    ebig = (np.arange(128)[:, None] == (x[None, :] // 64)).astype(np.float32)
    c = np.arange(512)
    j = np.arange(128)
    ov = ((16 * c[:, None] < 64 * j[None, :] + 64) & (16 * c[:, None] + 31 >= 64 * j[None, :])).astype(np.float32)
    vcx = np.zeros((512, 129), np.float32)
    vcx[:, 0] = 1.0
    vcx[:, 1:] = ov
    vcx[511, :] = 0.0
    vcx = np.ascontiguousarray(vcx.reshape(4, 128, 129).transpose(1, 0, 2))
    return dict(tA_c=tA_c, tA_s=tA_s, tB_c=tB_c, tB_s=tB_s, rot=rot, ebig=_bf16(ebig), vcx=_bf16(vcx))


def nsa_core_consts(par):
    p = np.arange(128)[:, None]
    tl = np.arange(128)[None, :]
    cm = np.zeros((128, 9, 128), np.float32)
    for s in range(8):
        off = 8 * ((2 * s + par) % 16)
        cm[:, s, :] = np.where(16 * (p - off) + 31 <= tl, 0.0, NEG)
    if par == 0:
        cm[:, 8, :] = np.where((p == 127) & (tl < 15), NEG, 0.0)
    causal = np.where(p <= tl, 0.0, NEG).astype(np.float32)
    winT = np.where(p > tl, 0.0, NEG).astype(np.float32)
    ALL = np.full((128, 128), NEG, np.float32)
    Z = np.zeros((128, 128), np.float32)
    sm = [causal, ALL, winT, Z, causal, ALL] if par == 0 else [Z, causal, ALL, winT, Z, causal]
    sm = np.stack(sm, axis=1)
    fv = np.zeros((NQB, 128, 2, 128), np.float32)
    jj = np.arange(128)[None, :]
    for i in range(NQB):
        qi = 2 * i + par
        t = 128 * qi + np.arange(128)[:, None]
        cur = t // 64
        valid = jj <= cur
        f0 = (jj == 0)
        f1 = (jj == cur)
        f2 = (jj == cur - 1)
        forced = f0 | f1 | f2
        V = (valid & ~forced).astype(np.float32)
        F = np.where(valid, 0.0, -1e30).astype(np.float32)
        F = np.where(f2 & valid, 1e4 + 2.0, F)
        F = np.where(f0, 1e4 + 1.0, F)
        F = np.where(f1, 1e4, F)
        fv[i, :, 0, :] = V
        fv[i, :, 1, :] = F
    return dict(cm=_bf16(cm), sm=_bf16(sm), fv=fv)


def l2a_inputs(h0T_b, kvh, par, P, C):
    w = P["ab_w_in"]
    cat = lambda *a: np.ascontiguousarray(np.concatenate(a, axis=1))
    sl = lambda o, n: w[:, o + kvh * n: o + (kvh + 1) * n]
    qtok = np.concatenate([np.arange((2 * i + par) * 128, (2 * i + par + 1) * 128) for i in range(NQB)])
    cc = nsa_core_consts(par)
    w1 = np.concatenate([P["a_cmp_w1_k"].reshape(32, 64, 64).transpose(1, 0, 2),
                         P["a_cmp_w1_v"].reshape(32, 64, 64).transpose(1, 0, 2)], 0)
    w2 = np.concatenate([P["a_cmp_w2_k"], P["a_cmp_w2_v"]], 0)
    pe = np.concatenate([P["a_cmp_pe_k"].T, P["a_cmp_pe_v"].T], 0)
    pselv = np.zeros((128, 2), np.float32); pselv[:, par] = 1.0
    return {"hall": h0T_b, "psel": pselv,
            "wq": np.ascontiguousarray(sl(0, 256)), "wks": np.ascontiguousarray(sl(768, 64)), "wkw": np.ascontiguousarray(sl(1024, 64)),
            "wkvc": cat(sl(512, 64), sl(640, 64)), "wv2": cat(sl(896, 64), sl(1152, 64)), "wgate": np.ascontiguousarray(sl(1280, 12)),
            "tabA_c": C["tA_c"], "tabA_s": C["tA_s"], "tabB_c": C["tB_c"], "tabB_s": C["tB_s"],
            "tabQ_c": np.ascontiguousarray(C["tA_c"][:, qtok] * np.float32(0.125)),
            "tabQ_s": np.ascontiguousarray(C["tA_s"][:, qtok] * np.float32(0.125)),
            "rotT": C["rot"], "ebig": C["ebig"], "w1kv": np.ascontiguousarray(w1), "w2kv": np.ascontiguousarray(w2),
            "pekv": np.ascontiguousarray(np.stack([pe, pe], axis=2)), "vcx": C["vcx"], "cmask": cc["cm"], "smask": cc["sm"], "fv": cc["fv"]}


def build_l2b(ntiles=16, nc=None, cx=None, io=None):
    nc, cx, io = _std(nc, cx, io)
    din = io.inp
    hall = din("hall", [4096, TOK], BF16)
    wx_d = din("wx", [D, 256]); wB_d = din("wB", [D, 128]); wC_d = din("wC", [D, 128]); wdt_d = din("wdt", [D, 4])
    cw_d = din("convw", [128, 4, 4]); cb_d = din("convb", [128, 4])
    dtb_d = din("dtb", [128, 16]); alog_d = din("alog", [128, 16]); dsk_d = din("dskip", [128, 4])
    tri_d = din("tri", [128, 128]); su_d = din("su", [128, 128])
    yout = io.out("y", [SEQ, 256], BF16)
    k = cx.k
    PS = lambda: k.ring("psr", [128, 512], F32, 8, psum=True)

    def cload(name, dram, shape, dt, q="sp"):
        t, b = k.tile(name, shape, dt)
        k.dma(q, t, dram, w=[b], sb=b)
        return t, b

    def wcast(name, dram, n):
        t, b = k.tile(name, [128, 8, n], BF16)
        k.dma("pool", t, dram.rearrange("(kc p) n -> p kc n", p=128), w=[b], sb=b)
        return t, b

    wx, wxb = wcast("wx", wx_d, 256); wB, wBb = wcast("wB", wB_d, 128); wC, wCb = wcast("wC", wC_d, 128)
    wdt, wdtb = wcast("wdt", wdt_d, 4)
    cw, cwb = cload("cw", cw_d, [128, 4, 4], F32); cb, cbb = cload("cb", cb_d, [128, 4], F32)
    dtb, dtbb = cload("dtb", dtb_d, [128, 16], F32); alog, alogb = cload("alog", alog_d, [128, 16], F32)
    dsk, dskb = cload("dsk", dsk_d, [128, 4], F32)
    tri, trib = cload("tri", tri_d, [128, 128], F32); su, sub_ = cload("su", su_d, [128, 128], F32)
    ones_f, ones_fb = k.tile("ones_f", [128, 128], F32)
    k.op("pool", lambda e: e.memset(ones_f, 1.0), w=[ones_fb])
    c_one = cx.const(1.0)
    arep, arepb = k.tile("arep", [128, 16], F32)
    k.op("act", lambda e: e.activation(out=arep, in_=alog, func=AF.Exp), r=[alogb], w=[arepb])
    k.op("dve", lambda e: e.tensor_scalar(out=arep, in0=arep, scalar1=-1.0, scalar2=None, op0=ALU.mult), r=[arepb], w=[arepb])
    S, Sb = k.tile("S", [128, 256], F32)
    k.op("pool", lambda e: e.memset(S, 0.0), w=[Sb])
    xbc = []
    for m in range(4):
        t, b = k.tile("xbc%d" % m, [128, 516], F32)
        k.op("pool", lambda e: e.memset(t, 0.0), w=[b])
        xbc.append((t, b))
    outb = []
    wsel = [(wx, wxb, slice(0, 128)), (wx, wxb, slice(128, 256)), (wB, wBb, slice(0, 128)), (wC, wCb, slice(0, 128))]
    bc64 = lambda a, n: a.unsqueeze(2).to_broadcast([128, n, 64])
    def prologue(ti):
        t0 = ti * 512
        hb, hbb = k.ring("hb", [128, 8, 512], BF16, 2)
        dma_hall(k, hb, hall, t0, 512, hbb)
        xc = []
        for m in range(4):
            w_, wb_, cols = wsel[m]
            ps, pb = PS()
            mm(k, ps, [(w_[:, kc, cols], hb[:, kc, :]) for kc in range(8)], [wb_, hbb], pb)
            xt, xtb = xbc[m]
            k.op("act", lambda e: e.activation(out=xt[:, 3:515], in_=ps, func=AF.Copy), r=[pb], w=[xtb])
            acc, accb = k.ring("cacc", [128, 512], F32, 2)
            k.op("dve", lambda e: e.tensor_scalar(out=acc, in0=xt[:, 0:512], scalar1=cw[:, m, 0:1], scalar2=cb[:, m:m + 1],
                                                  op0=ALU.mult, op1=ALU.add), r=[xtb, cwb, cbb], w=[accb])
            for j in range(1, 4):
                k.op("dve", lambda e: e.scalar_tensor_tensor(out=acc, in0=xt[:, j:j + 512], scalar=cw[:, m, j:j + 1], in1=acc,
                                                             op0=ALU.mult, op1=ALU.add), r=[xtb, cwb, accb], w=[accb])
            o, ob = k.ring("xc%d" % m, [128, 512], F32, 2)
            k.op("act", lambda e: e.activation(out=o, in_=acc, func=AF.Silu), r=[accb], w=[ob])
            k.op("pool", lambda e: e.tensor_copy(out=xt[:, 0:3], in_=xt[:, 512:515]), r=[xtb], w=[xtb])
            xc.append((o, ob))
        psd, pbd = PS()
        for c in range(4):
            mm(k, psd[:, c * 4:(c + 1) * 4], [(hb[:, kc, c * 128:(c + 1) * 128], wdt[:, kc, :]) for kc in range(8)], [wdtb, hbb], pbd)
        dt, dtb_ = k.ring("dt", [128, 16], F32, 2)
        k.op("dve", lambda e: e.tensor_tensor(out=dt, in0=psd[:, 0:16], in1=dtb, op=ALU.add), r=[pbd, dtbb], w=[dtb_])
        k.op("act", lambda e: e.activation(out=dt, in_=dt, func=AF.Exp), r=[dtb_], w=[dtb_])
        k.op("act", lambda e: e.activation(out=dt, in_=dt, func=AF.Ln, bias=c_one, scale=1.0), r=[dtb_, cx.b_consts], w=[dtb_])
        da, dab = k.ring("da", [128, 16], F32, 2)
        k.op("dve", lambda e: e.tensor_tensor(out=da, in0=dt, in1=arep, op=ALU.mult), r=[dtb_, arepb], w=[dab])
        psa, pba = PS(); mm(k, psa[:, 0:16], [(tri, da)], [trib, dab], pba)
        pst_, pbt_ = PS(); mm(k, pst_[:, 0:16], [(ones_f, da)], [ones_fb, dab], pbt_)
        acs, acsb = k.ring("acs", [128, 16], F32, 2)
        k.op("act", lambda e: e.activation(out=acs, in_=psa[:, 0:16], func=AF.Copy), r=[pba], w=[acsb])
        eacs, eacsb = k.ring("eacs", [128, 16], F32, 2)
        k.op("act", lambda e: e.activation(out=eacs, in_=acs, func=AF.Exp), r=[acsb], w=[eacsb])
        decs, decsb = k.ring("decs", [128, 16], F32, 2)
        k.op("dve", lambda e: e.tensor_tensor(out=decs, in0=pst_[:, 0:16], in1=acs, op=ALU.subtract), r=[pbt_, acsb], w=[decsb])
        k.op("act", lambda e: e.activation(out=decs, in_=decs, func=AF.Exp), r=[decsb], w=[decsb])
        cd, cdb = k.ring("cd", [128, 16], F32, 2)
        k.op("act", lambda e: e.activation(out=cd, in_=pst_[:, 0:16], func=AF.Exp), r=[pbt_], w=[cdb])
        xtok, xtokb = k.ring("xtok", [128, 4, 256], F32, 2)
        btok, btokb = k.ring("btok", [128, 4, 128], F32, 2)
        for half in range(2):
            ps, pb = PS()
            for ci in range(2):
                c = half * 2 + ci
                for m in range(2):
                    k.op("pe", lambda e: e.transpose(out=ps[:, ci * 256 + m * 128:ci * 256 + (m + 1) * 128],
                                                     in_=xc[m][0][:, c * 128:(c + 1) * 128], identity=cx.ident),
                         r=[xc[m][1], cx.b_ident], w=[pb], inc=True)
            k.op("act", lambda e: e.activation(out=xtok[:, half * 2:half * 2 + 2, :], in_=ps.rearrange("p (a b) -> p a b", a=2), func=AF.Copy),
                 r=[pb], w=[xtokb])
        ps, pb = PS()
        for c in range(4):
            k.op("pe", lambda e: e.transpose(out=ps[:, c * 128:(c + 1) * 128], in_=xc[2][0][:, c * 128:(c + 1) * 128], identity=cx.ident),
                 r=[xc[2][1], cx.b_ident], w=[pb], inc=True)
        k.op("act", lambda e: e.activation(out=btok, in_=ps.rearrange("p (a b) -> p a b", a=4), func=AF.Copy), r=[pb], w=[btokb])
        xd, xdb = k.ring("xd", [128, 4, 256], F32, 2)
        xdd, xddb = k.ring("xdd", [128, 4, 256], F32, 2)
        v16 = lambda a: a.rearrange("p c (h d) -> p (c h) d", d=64)
        k.op("dve", lambda e: e.tensor_tensor(out=v16(xd), in0=v16(xtok), in1=bc64(dt, 16), op=ALU.mult), r=[xtokb, dtb_], w=[xdb])
        k.op("pool", lambda e: e.tensor_tensor(out=v16(xdd), in0=v16(xd), in1=bc64(decs, 16), op=ALU.mult), r=[xdb, decsb], w=[xddb])
        return dict(xc=xc, xd=xd, xdb=xdb, xdd=xdd, xddb=xddb, xtok=xtok, xtokb=xtokb, btok=btok, btokb=btokb,
                    da=da, dab=dab, eacs=eacs, eacsb=eacsb, cd=cd, cdb=cdb)

    def chunks(ti, L):
        xc, xd, xdb, xdd, xddb = L['xc'], L['xd'], L['xdb'], L['xdd'], L['xddb']
        xtok, xtokb, btok, btokb = L['xtok'], L['xtokb'], L['btok'], L['btokb']
        da, dab, eacs, eacsb, cd, cdb = L['da'], L['dab'], L['eacs'], L['eacsb'], L['cd'], L['cdb']
        Bc, Bcb = xc[2]; Cc, Ccb = xc[3]
        v4 = lambda a_: a_.rearrange("p (h d) -> p h d", d=64)

        def stage_a(c):
            cs = slice(c * 128, (c + 1) * 128)
            ps, pb = PS(); mm(k, ps[:, 0:128], [(Bc[:, cs], Cc[:, cs])], [Bcb, Ccb], pb)
            cbm, cbmb = k.ring("cbm", [128, 128], F32, 3)
            k.op("dve", lambda e: e.tensor_tensor(out=cbm, in0=ps[:, 0:128], in1=tri, op=ALU.mult), r=[pb, trib], w=[cbmb])
            pdf, pdfb = PS()
            for h in range(4):
                lh, lhb = k.ring("lh", [128, 128], F32, 4)
                if h % 2 == 0:
                    k.op("dve", lambda e: e.tensor_scalar(out=lh, in0=su, scalar1=da[:, c * 4 + h:c * 4 + h + 1], scalar2=None, op0=ALU.mult),
                         r=[sub_, dab], w=[lhb])
                else:
                    k.op("act", lambda e: e.activation(out=lh, in_=su, func=AF.Copy, scale=da[:, c * 4 + h:c * 4 + h + 1]),
                         r=[sub_, dab], w=[lhb])
                mm(k, pdf[:, h * 128:(h + 1) * 128], [(lh, tri)], [lhb, trib], pdfb)
            seg, segb = k.ring("seg", [128, 4, 128], F32, 3)
            k.op("act", lambda e: e.activation(out=seg, in_=pdf.rearrange("p (a b) -> p a b", a=4), func=AF.Exp), r=[pdfb], w=[segb])
            k.op("dve", lambda e: e.tensor_tensor(out=seg, in0=seg, in1=cbm.unsqueeze(1).to_broadcast([128, 4, 128]), op=ALU.mult),
                 r=[segb, cbmb], w=[segb])
            return seg, segb

        def stage_b(c, seg, segb):
            cs = slice(c * 128, (c + 1) * 128)
            py, pyb = PS()
            for h in range(4):
                mm(k, py[:, h * 64:(h + 1) * 64], [(seg[:, h, :], xd[:, c, h * 64:(h + 1) * 64])], [segb, xdb], pyb)
            po, pob = PS(); mm(k, po[:, 0:256], [(Cc[:, cs], S)], [Ccb, Sb], pob)
            t1, t1b = k.ring("yt1", [128, 256], F32, 2)
            k.op("dve", lambda e: e.tensor_tensor(out=v4(t1), in0=v4(po[:, 0:256]), in1=bc64(eacs[:, c * 4:(c + 1) * 4], 4), op=ALU.mult),
                 r=[pob, eacsb], w=[t1b])
            k.op("dve", lambda e: e.tensor_tensor(out=t1, in0=t1, in1=py[:, 0:256], op=ALU.add), r=[t1b, pyb], w=[t1b])
            t2, t2b = k.ring("yt2", [128, 256], F32, 2)
            k.op("pool", lambda e: e.tensor_tensor(out=v4(t2), in0=v4(xtok[:, c, :]), in1=bc64(dsk, 4), op=ALU.mult), r=[xtokb, dskb], w=[t2b])
            yo, yob = k.ring("yo", [128, 256], BF16, 2)
            k.op("pool", lambda e: e.tensor_tensor(out=yo, in0=t1, in1=t2, op=ALU.add), r=[t1b, t2b], w=[yob])
            r0 = (ti * 4 + c) * 128
            k.dma("sp", yout[r0:r0 + 128, :], yo, r=[yob], sb=yob)
            outb.append(yob)
            pss, pssb = PS(); mm(k, pss[:, 0:256], [(btok[:, c, :], xdd[:, c, :])], [btokb, xddb], pssb)
            k.op("dve", lambda e: e.tensor_tensor(out=v4(S), in0=v4(S), in1=bc64(cd[:, c * 4:(c + 1) * 4], 4), op=ALU.mult), r=[Sb, cdb], w=[Sb])
            k.op("dve", lambda e: e.tensor_tensor(out=S, in0=S, in1=pss[:, 0:256], op=ALU.add), r=[Sb, pssb], w=[Sb])

        pend = None
        for c in range(4):
            cur = (c,) + stage_a(c)
            if pend is not None:
                stage_b(*pend)
            pend = cur
        stage_b(*pend)

    cur = prologue(0)
    for ti in range(ntiles):
        nxt = prologue(ti + 1) if ti + 1 < ntiles else None
        chunks(ti, cur)
        cur = nxt
    k.wait_all("sp", list({id(b): b for b in outb}.values()))
    return nc


def l2b_inputs(h0T_b, hg, P):
    w = P["ab_w_in"]
    g = hg // 2
    xo = 2328
    rep = lambda v, n: np.ascontiguousarray(np.tile(np.asarray(v, np.float32)[None, :], (128, n)))
    chans = [np.arange(hg * 256, hg * 256 + 128), np.arange(hg * 256 + 128, hg * 256 + 256),
             1024 + g * 128 + np.arange(128), 1280 + g * 128 + np.arange(128)]
    cwf = P["b_conv_w"][:, 0, :]
    convw = np.stack([cwf[:, ch].T for ch in chans], axis=1)
    convb = np.stack([P["b_conv_b"][ch] for ch in chans], axis=1)
    hs = slice(hg * 4, hg * 4 + 4)
    t = np.arange(128)
    return {"hall": h0T_b, "wx": np.ascontiguousarray(w[:, xo + hg * 256: xo + (hg + 1) * 256]),
            "wB": np.ascontiguousarray(w[:, xo + 1024 + g * 128: xo + 1024 + (g + 1) * 128]),
            "wC": np.ascontiguousarray(w[:, xo + 1280 + g * 128: xo + 1280 + (g + 1) * 128]),
            "wdt": np.ascontiguousarray(w[:, 3864 + hg * 4: 3864 + hg * 4 + 4]),
            "convw": np.ascontiguousarray(convw.astype(np.float32)), "convb": np.ascontiguousarray(convb.astype(np.float32)),
            "dtb": rep(P["b_dt_bias"][hs], 4), "alog": rep(P["b_a_log"][hs], 4), "dskip": rep(P["b_d_skip"][hs], 1),
            "tri": (t[:, None] <= t[None, :]).astype(np.float32), "su": (t[:, None] > t[None, :]).astype(np.float32)}


def linear_tile(cx, in_ap, inb, W_dram, KC, M, out_fn):
    k = cx.k
    Wv = W_dram.rearrange("(kc p) m -> p kc m", p=128)
    for mb in range(M // 256):
        w, wb = cx.wload(Wv[:, :, mb * 256:(mb + 1) * 256], [128, KC, 256])
        for m2 in range(2):
            ps, pb = cx.psum()
            mm(k, ps, [(w[:, kc, m2 * 128:(m2 + 1) * 128], in_ap[:, kc, :]) for kc in range(KC)], [wb, inb], pb)
            out_fn(mb * 2 + m2, ps, pb)


def build_l3(nc=None, cx=None, io=None):
    nc, cx, io = _std(nc, cx, io)
    din = io.inp
    x1T = din("x1T", [128, 8, TOK]); h0T = din("h0T", [1024, TOK], BF16).rearrange("(kc p) t -> p kc t", p=128)
    oaall = din("oaall", [4 * 4096, 256], BF16); yall = din("yall", [4 * SEQ, 256], BF16); qsel_d = din("qsel", [128, 4])
    gains = din("gains", [128, 12, 8]); normw_d = din("normw", [128, 8])
    wz = din("wz", [D, D]); wout = din("wout", [1536, D])
    f2 = [din("f2g", [D, DFF]), din("f2u", [D, DFF]), din("f2d", [DFF, D])]
    f1 = [din("f1g", [D, DFF]), din("f1u", [D, DFF]), din("f1d", [DFF, D])]
    x4T = io.out("x4T", [128, 8, TOK], F32)
    h1T = io.out("h1T", [1024, TOK], BF16).rearrange("(kc p) t -> p kc t", p=128)
    k = cx.k
    cx.wring_n = 2
    cx.wstg_n = 1
    g, gb = load_gains(cx, gains)
    nw, nwb = k.tile("normw", [128, 8], F32); k.dma("sp", nw, normw_d, w=[nwb], sb=nwb)
    qsel, qselb = k.tile("qsel", [128, 4], F32); k.dma("sp", qsel, qsel_d, w=[qselb], sb=qselb)
    ones512, ones512b = k.tile("ones512", [128, 128], BF16)
    k.op("pool", lambda e: e.memset(ones512, 1.0 / 512), w=[ones512b])
    idq, idqb = k.tile("idq", [128, 4, 128], BF16)
    for j in range(4):
        k.op("dve", lambda e: e.tensor_scalar(out=idq[:, j, :], in0=cx.ident, scalar1=qsel[:, j:j + 1], scalar2=None, op0=ALU.mult),
             r=[cx.b_ident, qselb], w=[idqb])
    c_eps5 = cx.const(1e-5)
    outs = []
    for half in range(2):
        x, xb = k.ring("x_res", [128, 8, 1024], F32, 1)
        k.dma("sp", x, x1T[:, :, half * 1024:(half + 1) * 1024], w=[xb], sb=xb)
        for tt in range(2):
            t0 = half * 1024 + tt * 512
            tq = t0 // 512
            sl = slice(tt * 512, (tt + 1) * 512)
            o, ob = k.ring("ffn_o", [128, 8, 1024], F32, 1)
            act, actb = k.ring("act_bf", [128, 22, 1024], BF16, 1)
            ys = o[:, :, 0:512]
            mo = o[:, :, 512:1024]
            mixin = act[:, 0:6, :].rearrange("p a (b t) -> p (a b) t", t=512)
            h0 = act[:, 6:10, :].rearrange("p a (b t) -> p (a b) t", t=512)
            k.dma("sp", h0, h0T[:, :, t0:t0 + 512], w=[actb], sb=actb)
            for hg in range(4):
                cand, candb = k.ring("ycand", [128, 4, 4, 256], BF16, 1)
                for j in range(4):
                    r0 = (j * 4 + hg) * TOK + t0
                    k.dma("sp", cand[:, j], yall[r0:r0 + 512, :].rearrange("(tb p) c -> p tb c", p=128), w=[candb], sb=candb)
                for hh in range(2):
                    ps, pb = cx.psum()
                    for tb in range(4):
                        mm(k, ps[:, tb * 128:(tb + 1) * 128],
                           [(cand[:, j, tb, hh * 128:(hh + 1) * 128], idq[:, j, :]) for j in range(4)], [candb, idqb], pb)
                    k.op("act", lambda e: e.activation(out=ys[:, hg * 2 + hh, :], in_=ps, func=AF.Copy), r=[pb], w=[ob])
            for kvh in range(2):
                ocand, ocandb = k.ring("ocand", [128, 2, 4, 2, 256], BF16, 1)
                for par in range(2):
                    for j in range(4):
                        r0 = ((j // 2) * 4 + kvh * 2 + par) * 2048 + (j % 2) * 1024 + (2 * tq) * 128
                        k.dma("sp", ocand[:, par, j], oaall[r0:r0 + 256, :].rearrange("(i p) c -> p i c", p=128), w=[ocandb], sb=ocandb)
                for hh in range(2):
                    ps, pb = cx.psum()
                    for u in range(4):
                        par, i2 = u % 2, u // 2
                        mm(k, ps[:, u * 128:(u + 1) * 128],
                           [(ocand[:, par, j, i2, hh * 128:(hh + 1) * 128], idq[:, j, :]) for j in range(4)], [ocandb, idqb], pb)
                    k.op("act", lambda e: e.activation(out=mixin[:, kvh * 2 + hh, :], in_=ps, func=AF.Copy), r=[pb], w=[actb])

            def z_out(mc, ps, pb):
                zs, zsb = k.ring("sg", [128, 512], F32, 3)
                k.op("act", lambda e: e.activation(out=zs, in_=ps, func=AF.Silu), r=[pb], w=[zsb])
                k.op("dve", lambda e: e.tensor_tensor(out=ys[:, mc, :], in0=ys[:, mc, :], in1=zs, op=ALU.mult), r=[zsb, ob], w=[ob])
            linear_tile(cx, h0, actb, wz, 8, D, z_out)
            sq, sqb = k.ring("sq", [128, 8, 512], BF16, 1)
            k.op("act", lambda e: e.activation(out=sq, in_=ys, func=AF.Square), r=[ob], w=[sqb])
            for gi in range(2):
                ps, pb = cx.psum()
                mm(k, ps, [(ones512, sq[:, gi * 4 + c, :]) for c in range(4)], [ones512b, sqb], pb)
                rs, rsb = k.ring("rstd", [128, 512], F32, 2)
                k.op("act", lambda e: e.activation(out=rs, in_=ps, func=AF.Sqrt, bias=c_eps5, scale=1.0), r=[pb, cx.b_consts], w=[rsb])
                k.op("dve", lambda e: e.reciprocal(out=rs, in_=rs), r=[rsb], w=[rsb])
                for c in range(4):
                    ch = gi * 4 + c
                    k.op("dve", lambda e: e.scalar_tensor_tensor(out=mixin[:, 4 + ch, :], in0=ys[:, ch, :], scalar=nw[:, ch:ch + 1], in1=rs,
                                                                 op0=ALU.mult, op1=ALU.mult), r=[ob, nwb, rsb], w=[actb])

            def mix_out(mc, ps, pb):
                k.op("act", lambda e: e.activation(out=mo[:, mc, :], in_=ps, func=AF.Copy), r=[pb], w=[ob])
            linear_tile(cx, mixin, actb, wout, 12, D, mix_out)
            post_norm_add(cx, x[:, :, sl], xb, mo, ob, g[:, 3, :], gb, 1.0)
        ffn_half(cx, x, xb, 2, f2[0], f2[1], f2[2], g[:, 4, :], g[:, 5, :], gb)
        ffn_half(cx, x, xb, 2, f1[0], f1[1], f1[2], g[:, 6, :], g[:, 7, :], gb)
        k.dma("sp", x4T[:, :, half * 1024:(half + 1) * 1024], x, r=[xb], sb=xb)
        hh_, hhb = k.ring("h_bf", [128, 8, 1024], BF16, 1)
        for tt in range(2):
            sl = slice(tt * 512, (tt + 1) * 512)
            norm_bf16(cx, x[:, :, sl], xb, g[:, 8, :], gb, hh_[:, :, sl], hhb)
        k.dma("sp", h1T[:, :, half * 1024:(half + 1) * 1024], hh_, r=[hhb], sb=hhb)
        outs += [xb, hhb]
    k.wait_all("sp", outs)
    return nc


def build_l5(nc=None, cx=None, io=None):
    nc, cx, io = _std(nc, cx, io)
    din = io.inp
    x4T = din("x4T", [128, 8, TOK])
    ygall = din("ygall", [4096, TOK], BF16).rearrange("(j hg pc p) t -> j p (hg pc) t", j=4, hg=4, pc=2, p=128)
    qsel_d = din("qsel", [128, 4])
    gains = din("gains", [128, 12, 8]); wo = din("wo", [D, D])
    f2 = [din("f2g", [D, DFF]), din("f2u", [D, DFF]), din("f2d", [DFF, D])]
    outT = io.out("outT", [128, 8, TOK], F32)
    k = cx.k
    cx.wstg_n = 1
    g, gb = load_gains(cx, gains)
    qsel, qselb = k.tile("qsel", [128, 4], F32); k.dma("sp", qsel, qsel_d, w=[qselb], sb=qselb)
    outs = []
    for half in range(2):
        x, xb = k.ring("x_res", [128, 8, 1024], F32, 1)
        k.dma("sp", x, x4T[:, :, half * 1024:(half + 1) * 1024], w=[xb], sb=xb)
        for tt in range(2):
            t0 = half * 1024 + tt * 512
            sl = slice(tt * 512, (tt + 1) * 512)
            o, ob = k.ring("ffn_o", [128, 8, 1024], F32, 1)
            act, actb = k.ring("act_bf", [128, 22, 1024], BF16, 1)
            mo = o[:, :, 512:1024]
            yg = act[:, 6:10, :].rearrange("p a (b t) -> p (a b) t", t=512)
            for j in range(4):
                cand, candb = k.ring("ygcand", [128, 8, 512], BF16, 1)
                k.dma("sp", cand, ygall[j][:, :, t0:t0 + 512], w=[candb], sb=candb)
                if j == 0:
                    k.op("dve", lambda e: e.tensor_scalar(out=yg, in0=cand, scalar1=qsel[:, 0:1], scalar2=None, op0=ALU.mult),
                         r=[candb, qselb], w=[actb])
                else:
                    k.op("dve", lambda e: e.scalar_tensor_tensor(out=yg, in0=cand, scalar=qsel[:, j:j + 1], in1=yg, op0=ALU.mult, op1=ALU.add),
                         r=[candb, qselb, actb], w=[actb])

            def mix_out(mc, ps, pb):
                k.op("act", lambda e: e.activation(out=mo[:, mc, :], in_=ps, func=AF.Copy), r=[pb], w=[ob])
            linear_tile(cx, yg, actb, wo, 8, D, mix_out)
            post_norm_add(cx, x[:, :, sl], xb, mo, ob, g[:, 9, :], gb, 1.0)
        ffn_half(cx, x, xb, 2, f2[0], f2[1], f2[2], g[:, 10, :], g[:, 11, :], gb)
        k.dma("sp", outT[:, :, half * 1024:(half + 1) * 1024], x, r=[xb], sb=xb)
        outs.append(xb)
    k.wait_all("sp", outs)
    return nc


def _run(nc, in_maps):
    res = run_bass_kernel_spmd(nc, in_maps, core_ids=list(range(NCORE)))
    return res.results


def kernel_unfused(**inp):
    import ml_dtypes
    f32 = lambda a: np.ascontiguousarray(np.asarray(a, dtype=np.float32))
    I = {k_: f32(v) for k_, v in inp.items()}
    x = I["x"].reshape(16384, D)
    g = gains_layout(I["norm_gains"])
    tok = [slice(c * TOK, (c + 1) * TOK) for c in range(NCORE)]
    grp = lambda lst, b: np.ascontiguousarray(np.concatenate(lst[4 * b:4 * b + 4], axis=0))
    qsel = []
    for c in range(NCORE):
        q_ = np.zeros((128, 4), np.float32); q_[:, c % 4] = 1.0
        qsel.append(q_)
    r1 = _run(build_l1(), [{"xT": fm(x[tok[c]]), "gains": g, "wg": I["ffn1_w_gate"][0], "wu": I["ffn1_w_up"][0],
                            "wd": I["ffn1_w_down"][0]} for c in range(NCORE)])
    x1T = [np.asarray(r["x1T"]) for r in r1]
    h0loc = [np.asarray(r["h0T"]) for r in r1]
    h0all = [grp(h0loc, b) for b in range(2)]
    PA = {k_: I[k_][0] for k_ in I if k_.startswith("a_") or k_.startswith("ab_") or k_.startswith("b_")}
    C = nsa_consts()
    r2a = _run(build_l2a(), [l2a_inputs(h0all[c // 4], (c % 4) // 2, c % 2, PA, C) for c in range(NCORE)])
    r2b = _run(build_l2b(), [l2b_inputs(h0all[c // 4], c % 4, PA) for c in range(NCORE)])
    oaall = [grp([np.asarray(r["oa"]) for r in r2a], b) for b in range(2)]
    yall = [grp([np.asarray(r["y"]) for r in r2b], b) for b in range(2)]
    w_in = I["ab_w_in"][0]
    m3 = []
    for c in range(NCORE):
        m3.append({"x1T": x1T[c], "h0T": h0loc[c], "oaall": oaall[c // 4], "yall": yall[c // 4], "qsel": qsel[c], "gains": g,
                   "normw": np.ascontiguousarray(I["b_norm_w"][0].reshape(8, 128).T),
                   "wz": np.ascontiguousarray(w_in[:, 1304:2328]), "wout": I["ab_w_out"][0],
                   "f2g": I["ffn2_w_gate"][0], "f2u": I["ffn2_w_up"][0], "f2d": I["ffn2_w_down"][0],
                   "f1g": I["ffn1_w_gate"][1], "f1u": I["ffn1_w_up"][1], "f1d": I["ffn1_w_down"][1]})
    r3 = _run(build_l3(), m3)
    x4T = [np.asarray(r["x4T"]) for r in r3]
    h1all = [grp([np.asarray(r["h1T"]) for r in r3], b) for b in range(2)]
    PC = {k_: I[k_][0] for k_ in I if k_.startswith("c_")}
    r4 = _run(build_l4(), [l4_inputs(h1all[c // 4], c % 4, PC) for c in range(NCORE)])
    ygall = [grp([np.asarray(r["ygT"]) for r in r4], b) for b in range(2)]
    m5 = []
    for c in range(NCORE):
        m5.append({"x4T": x4T[c], "ygall": ygall[c // 4], "qsel": qsel[c], "gains": g,
                   "wo": I["c_w_o"][0], "f2g": I["ffn2_w_gate"][1], "f2u": I["ffn2_w_up"][1], "f2d": I["ffn2_w_down"][1]})
    r5 = _run(build_l5(), m5)
    out = np.concatenate([unfm(np.asarray(r["outT"])) for r in r5], axis=0)
    return np.ascontiguousarray(out.reshape(2, SEQ, D).astype(np.float32))


RG = [[0, 1, 2, 3], [4, 5, 6, 7]]


def build_fused(upto=99):
    nc = bass.Bass("TRN2", target_bir_lowering=False, num_devices=NCORE)
    k = K(nc)
    idram = lambda n, sh, dt: nc.dram_tensor(n, list(sh), dt, kind="Internal").ap()
    x1T = idram("i_x1T", [128, 8, TOK], F32)
    h0loc = idram("i_h0loc", [1024, TOK], BF16); h0all = idram("i_h0all", [4096, TOK], BF16)
    oaloc = idram("i_oaloc", [NQB * 128, 256], BF16); oaall = idram("i_oaall", [4 * 4096, 256], BF16)
    yloc = idram("i_yloc", [SEQ, 256], BF16); yall = idram("i_yall", [4 * SEQ, 256], BF16)
    x4T = idram("i_x4T", [128, 8, TOK], F32)
    h1loc = idram("i_h1loc", [1024, TOK], BF16); h1all = idram("i_h1all", [4096, TOK], BF16)
    ygloc = idram("i_ygloc", [1024, TOK], BF16); ygall = idram("i_ygall", [4096, TOK], BF16)
    ccsem = Sem(nc.alloc_semaphore(name="ccsem"), "cc")
    k.dall = [ccsem]; k.dused = []; k.dfree = []

    def allgather(src, dst, wait=True):
        rows, cols = src.shape
        R = (1 << 20) // (cols * mybir.dt.size(src.dtype))
        k.barrier()
        for i in range(rows // R):
            ins = nc.gpsimd.collective_compute("AllGather", ALU.bypass, replica_groups=RG,
                                               ins=[src[i * R:(i + 1) * R, :]], outs=[dst[i * 4 * R:(i + 1) * 4 * R, :]])
            ins.then_inc(ccsem.h, 1)
            ccsem.total += 1
        if wait:
            k.barrier()

    def phase(fn, pre, ext, **kw):
        k.begin_phase()
        cx = Ctx(nc, k)
        fn(nc=nc, cx=cx, io=IO(nc, pre, ext), **kw)
        k.end_phase()

    steps = [lambda: phase(build_l1, "l1_", {"x1T": x1T, "h0T": h0loc}),
             lambda: allgather(h0loc, h0all),
             lambda: phase(build_l2a, "l2a_", {"hall": h0all, "oa": oaloc}),
             lambda: allgather(oaloc, oaall, wait=False),
             lambda: phase(build_l2b, "l2b_", {"hall": h0all, "y": yloc}),
             lambda: allgather(yloc, yall),
             lambda: phase(build_l3, "l3_", {"x1T": x1T, "h0T": h0loc, "oaall": oaall, "yall": yall, "x4T": x4T, "h1T": h1loc}),
             lambda: allgather(h1loc, h1all),
             lambda: phase(build_l4, "l4_", {"hall": h1all, "ygT": ygloc}),
             lambda: allgather(ygloc, ygall),
             lambda: phase(build_l5, "l5_", {"x4T": x4T, "ygall": ygall})]
    for st in steps[:upto]:
        st()
    if upto < len(steps):
        nc.dram_tensor("l5_outT", [128, 8, TOK], F32, kind="ExternalOutput")
    return nc


def kernel(**inp):
    f32 = lambda a: np.ascontiguousarray(np.asarray(a, dtype=np.float32))
    I = {k_: f32(v) for k_, v in inp.items()}
    x = I["x"].reshape(16384, D)
    g = gains_layout(I["norm_gains"])
    PA = {k_: I[k_][0] for k_ in I if k_.startswith("a_") or k_.startswith("ab_") or k_.startswith("b_")}
    PC = {k_: I[k_][0] for k_ in I if k_.startswith("c_")}
    C = nsa_consts()
    w_in = I["ab_w_in"][0]
    in_maps = []
    for c in range(NCORE):
        m = {}
        qs = np.zeros((128, 4), np.float32); qs[:, c % 4] = 1.0
        m.update({"l1_" + k_: v for k_, v in {"xT": fm(x[c * TOK:(c + 1) * TOK]), "gains": g, "wg": I["ffn1_w_gate"][0],
                                             "wu": I["ffn1_w_up"][0], "wd": I["ffn1_w_down"][0]}.items()})
        a = l2a_inputs(None, (c % 4) // 2, c % 2, PA, C); a.pop("hall")
        m.update({"l2a_" + k_: v for k_, v in a.items()})
        b_ = l2b_inputs(None, c % 4, PA); b_.pop("hall")
        m.update({"l2b_" + k_: v for k_, v in b_.items()})
        m.update({"l3_" + k_: v for k_, v in {"qsel": qs, "gains": g,
                  "normw": np.ascontiguousarray(I["b_norm_w"][0].reshape(8, 128).T),
                  "wz": np.ascontiguousarray(w_in[:, 1304:2328]), "wout": I["ab_w_out"][0],
                  "f2g": I["ffn2_w_gate"][0], "f2u": I["ffn2_w_up"][0], "f2d": I["ffn2_w_down"][0],
                  "f1g": I["ffn1_w_gate"][1], "f1u": I["ffn1_w_up"][1], "f1d": I["ffn1_w_down"][1]}.items()})
        d4 = l4_inputs(None, c % 4, PC); d4.pop("hall")
        m.update({"l4_" + k_: v for k_, v in d4.items()})
        m.update({"l5_" + k_: v for k_, v in {"qsel": qs, "gains": g, "wo": I["c_w_o"][0], "f2g": I["ffn2_w_gate"][1],
                                             "f2u": I["ffn2_w_up"][1], "f2d": I["ffn2_w_down"][1]}.items()})
        in_maps.append(m)
    import os
    upto = int(os.environ.get('FUSED_UPTO', '99'))
    nc_ = build_fused(upto)
    if upto < 99:
        names = {a.memorylocations[0].name for a in nc_.allocations if hasattr(a, 'memorylocations') and a.memorylocations}
        in_maps = [{k_: v for k_, v in m.items() if k_ in names} for m in in_maps]
    res = _run(nc_, in_maps)
    out = np.concatenate([unfm(np.asarray(r["l5_outT"])) for r in res], axis=0)
    return np.ascontiguousarray(out.reshape(2, SEQ, D).astype(np.float32))
```
